# Optimizing a Trainium2 kernel written in Bass

```python
import math
import jax, jax.numpy as jnp
from jax import lax
import numpy as np

D_MODEL = 1024
BATCH = 8
SEQ = 2048
DEPTH = 2
DEC_BATCH = 128
DEC_SEQ = 8
PAST_LEN = 16384
PAGE_SIZE = 128

N_EVEN = (DEPTH + 1) // 2
N_ODD = DEPTH // 2
CONV_A_DIM = 512
CONV_A_WIDTH = 31
LN_EPS = 1e-5
RET_HEADS = 4
RET_QK_DIM = 128
RET_V_DIM = 128
RET_QK = RET_HEADS * RET_QK_DIM
RET_DIM = RET_HEADS * RET_V_DIM
RET_CHUNK = 128
ROPE_BASE = 10000.0
GN_EPS = 1e-5
IN_AB_DIM = 2 * CONV_A_DIM + 2 * RET_QK + 2 * RET_DIM
SPLIT_AB = (CONV_A_DIM, 2 * CONV_A_DIM, 2 * CONV_A_DIM + RET_QK, 2 * CONV_A_DIM + 2 * RET_QK, 2 * CONV_A_DIM + 2 * RET_QK + RET_DIM)
MIX_AB_DIM = CONV_A_DIM + RET_DIM
LRU_DIM = 1024
LRU_BLOCKS = 8
LRU_BLOCK = LRU_DIM // LRU_BLOCKS
LRU_CONV_WIDTH = 4
LRU_C = 8.0
FFN_DIM = 2816
FFN_CONV_WIDTH = 3
RMS_EPS = 1e-6

kernel_name = "hybrid_conformerconv_retention_rglru_convffn_step"


def _rmsnorm(x, g):
    xf = x.astype(jnp.float32)
    y = xf * lax.rsqrt(jnp.mean(xf * xf, axis=-1, keepdims=True) + RMS_EPS) * g.astype(jnp.float32)
    return y.astype(x.dtype)


def _layernorm(x, g, b):
    xf = x.astype(jnp.float32)
    mu = jnp.mean(xf, axis=-1, keepdims=True)
    var = jnp.mean(jnp.square(xf - mu), axis=-1, keepdims=True)
    y = (xf - mu) * lax.rsqrt(var + LN_EPS) * g.astype(jnp.float32) + b.astype(jnp.float32)
    return y.astype(x.dtype)


def _causal_dwconv(x, buf, w, b):
    width = w.shape[0]
    xp = jnp.concatenate([buf.astype(x.dtype), x], axis=1)
    y = lax.conv_general_dilated(xp, w[:, None, :].astype(x.dtype), window_strides=(1,), padding='VALID',
                                 dimension_numbers=('NWC', 'WIO', 'NWC'), feature_group_count=x.shape[-1])
    return y + b.astype(x.dtype), xp[:, xp.shape[1] - (width - 1):]


def _rope(x, pos):
    d = x.shape[-1]
    inv_freq = ROPE_BASE ** (-jnp.arange(0, d, 2, dtype=jnp.float32) / d)
    ang = pos.astype(jnp.float32)[:, None] * inv_freq[None, :]
    cos = jnp.cos(ang)[None, :, None, :]
    sin = jnp.sin(ang)[None, :, None, :]
    xf = x.astype(jnp.float32)
    x1, x2 = xf[..., : d // 2], xf[..., d // 2:]
    return jnp.concatenate([x1 * cos - x2 * sin, x2 * cos + x1 * sin], axis=-1)


def _retention(q, k, v, s0):
    bsz, t, nh, dk = q.shape
    dv = v.shape[-1]
    c = RET_CHUNK if t % RET_CHUNK == 0 else t
    n = t // c
    log_gamma = jnp.log(1.0 - 2.0 ** (-5.0 - jnp.arange(nh, dtype=jnp.float32)))
    idx = jnp.arange(c, dtype=jnp.float32)
    rel = idx[:, None] - idx[None, :]
    decay_mask = jnp.where(rel >= 0, jnp.exp(jnp.maximum(rel, 0.0)[None] * log_gamma[:, None, None]), 0.0)
    q_decay = jnp.exp((idx + 1.0)[None, :] * log_gamma[:, None])
    k_decay = jnp.exp((c - 1.0 - idx)[None, :] * log_gamma[:, None])
    chunk_decay = jnp.exp(c * log_gamma)

    def to_chunks(a):
        return a.astype(jnp.float32).reshape(bsz, n, c, nh, a.shape[-1]).transpose(1, 0, 3, 2, 4)

    def step(s, inp):
        qb, kb, vb = inp
        scores = jnp.einsum('bhid,bhjd->bhij', qb, kb) * decay_mask
        inner = jnp.einsum('bhij,bhjv->bhiv', scores, vb)
        cross = jnp.einsum('bhid,bhdv->bhiv', qb, s) * q_decay[None, :, :, None]
        s_new = s * chunk_decay[None, :, None, None] + jnp.einsum('bhjd,bhjv->bhdv', kb * k_decay[None, :, :, None], vb)
        return s_new, inner + cross

    s_fin, o = lax.scan(step, s0.astype(jnp.float32), (to_chunks(q), to_chunks(k), to_chunks(v)))
    o = o.transpose(1, 0, 3, 2, 4).reshape(bsz, t, nh, dv)
    return o, s_fin


def _even_mixer(h, conv_buf, ret_state, pos0, w_in, conv_w, conv_b, ln_g, ln_b, gn_g, w_out):
    bsz, t, _ = h.shape
    z = h @ w_in
    a_lin, a_gate, q, k, v, g = jnp.split(z, SPLIT_AB, axis=-1)
    u = a_lin * jax.nn.sigmoid(a_gate)
    cv, new_conv_buf = _causal_dwconv(u, conv_buf, conv_w, conv_b)
    ya = jax.nn.silu(_layernorm(cv, ln_g, ln_b))
    pos = pos0 + jnp.arange(t, dtype=jnp.int32)
    qh = _rope(q.reshape(bsz, t, RET_HEADS, RET_QK_DIM), pos)
    kh = _rope(k.reshape(bsz, t, RET_HEADS, RET_QK_DIM), pos) * (RET_QK_DIM ** -0.5)
    vh = v.reshape(bsz, t, RET_HEADS, RET_V_DIM)
    o, new_ret = _retention(qh, kh, vh, ret_state)
    mu = jnp.mean(o, axis=-1, keepdims=True)
    var = jnp.mean(jnp.square(o - mu), axis=-1, keepdims=True)
    o = ((o - mu) * lax.rsqrt(var + GN_EPS)).reshape(bsz, t, RET_DIM) * gn_g.astype(jnp.float32)
    yb = o.astype(h.dtype) * jax.nn.silu(g)
    y = jnp.concatenate([ya, yb], axis=-1) @ w_out
    return y, new_conv_buf, new_ret


def _lru_combine(left, right):
    a1, b1 = left
    a2, b2 = right
    return a1 * a2, a2 * b1 + b2


def _odd_mixer(h, conv_buf, h0, w_in, conv_w, conv_b, w_a, b_a, w_x, b_x, lam, w_out):
    bsz, t, _ = h.shape
    z = h @ w_in
    gate_br, rec_br = jnp.split(z, 2, axis=-1)
    xc, new_conv_buf = _causal_dwconv(rec_br, conv_buf, conv_w, conv_b)
    xb = xc.reshape(bsz, t, LRU_BLOCKS, LRU_BLOCK)
    r = jax.nn.sigmoid(jnp.einsum('btnc,ncd->btnd', xb, w_a).reshape(bsz, t, LRU_DIM) + b_a).astype(jnp.float32)
    i = jax.nn.sigmoid(jnp.einsum('btnc,ncd->btnd', xb, w_x).reshape(bsz, t, LRU_DIM) + b_x).astype(jnp.float32)
    log_a = -LRU_C * r * jax.nn.softplus(-lam.astype(jnp.float32))
    a = jnp.exp(log_a)
    bterm = jnp.sqrt(jnp.maximum(1.0 - a * a, 0.0)) * (i * xc.astype(jnp.float32))
    a_cum, b_cum = lax.associative_scan(_lru_combine, (a, bterm), axis=1)
    hs = a_cum * h0.astype(jnp.float32)[:, None, :] + b_cum
    y = (hs.astype(h.dtype) * jax.nn.gelu(gate_br)) @ w_out
    return y, new_conv_buf, hs[:, -1]


def _conv_ffn(h, buf, w_up, conv_w, conv_b, w_down):
    z = h @ w_up
    zc, new_buf = _causal_dwconv(z, buf, conv_w, conv_b)
    g, u = jnp.split(zc, 2, axis=-1)
    return (jax.nn.gelu(g) * u) @ w_down, new_buf


def _trunk(x, conv_a_buf, ret_st, lru_conv_buf, lru_h, ffn_buf, pos0, p):
    new_conv_a, new_ret, new_lru_conv, new_lru_h, new_ffn = [], [], [], [], []
    for l in range(DEPTH):
        hn = _rmsnorm(x, p['norm_mix'][l])
        if l % 2 == 0:
            e = l // 2
            y, cb, rs = _even_mixer(hn, conv_a_buf[e], ret_st[e], pos0, p['w_in_ab'][e], p['conv_a_w'][e], p['conv_a_b'][e],
                                    p['ln_a_g'][e], p['ln_a_b'][e], p['gn_ret_g'][e], p['w_out_ab'][e])
            new_conv_a.append(cb.astype(x.dtype))
            new_ret.append(rs.astype(x.dtype))
        else:
            o = l // 2
            y, cb, hl = _odd_mixer(hn, lru_conv_buf[o], lru_h[o], p['w_in_c'][o], p['conv_c_w'][o], p['conv_c_b'][o],
                                   p['w_lru_a'][o], p['b_lru_a'][o], p['w_lru_x'][o], p['b_lru_x'][o], p['lru_lambda'][o], p['w_out_c'][o])
            new_lru_conv.append(cb.astype(x.dtype))
            new_lru_h.append(hl.astype(x.dtype))
        x = x + y
        hn = _rmsnorm(x, p['norm_ffn'][l])
        y, fb = _conv_ffn(hn, ffn_buf[l], p['w_ffn_up'][l], p['ffn_conv_w'][l], p['ffn_conv_b'][l], p['w_ffn_down'][l])
        new_ffn.append(fb.astype(x.dtype))
        x = x + y
    x = _rmsnorm(x, p['norm_final'])
    return x, jnp.stack(new_conv_a), jnp.stack(new_ret), jnp.stack(new_lru_conv), jnp.stack(new_lru_h), jnp.stack(new_ffn)


def setup_inputs(seed: int = 0) -> dict:
    key = jax.random.key(seed)
    ks = iter(jax.random.split(key, 40))
    f32 = jnp.float32

    def nrm(shape, scale):
        return jax.random.normal(next(ks), shape, f32) * scale

    u = jax.random.uniform(next(ks), (N_ODD, LRU_DIM), f32, minval=0.9, maxval=0.999)
    s = u ** (1.0 / LRU_C)
    lam = jnp.log(s) - jnp.log1p(-s)
    return {
        'x_prompt': nrm((BATCH, SEQ, D_MODEL), 1.0),
        'x_sample': nrm((DEC_BATCH, DEC_SEQ, D_MODEL), 1.0),
        'state_conv_a': nrm((N_EVEN, DEC_BATCH, CONV_A_WIDTH - 1, CONV_A_DIM), 0.5),
        'state_ret': nrm((N_EVEN, DEC_BATCH, RET_HEADS, RET_QK_DIM, RET_V_DIM), 0.5),
        'state_lru_conv': nrm((N_ODD, DEC_BATCH, LRU_CONV_WIDTH - 1, LRU_DIM), 1.0),
        'state_lru_h': nrm((N_ODD, DEC_BATCH, LRU_DIM), 0.5),
        'state_ffn_conv': nrm((DEPTH, DEC_BATCH, FFN_CONV_WIDTH - 1, 2 * FFN_DIM), 1.0),
        'norm_mix': 1.0 + nrm((DEPTH, D_MODEL), 0.02),
        'norm_ffn': 1.0 + nrm((DEPTH, D_MODEL), 0.02),
        'norm_final': 1.0 + nrm((D_MODEL,), 0.02),
        'w_in_ab': nrm((N_EVEN, D_MODEL, IN_AB_DIM), D_MODEL ** -0.5),
        'conv_a_w': nrm((N_EVEN, CONV_A_WIDTH, CONV_A_DIM), CONV_A_WIDTH ** -0.5),
        'conv_a_b': nrm((N_EVEN, CONV_A_DIM), 0.02),
        'ln_a_g': 1.0 + nrm((N_EVEN, CONV_A_DIM), 0.02),
        'ln_a_b': nrm((N_EVEN, CONV_A_DIM), 0.02),
        'gn_ret_g': 1.0 + nrm((N_EVEN, RET_DIM), 0.02),
        'w_out_ab': nrm((N_EVEN, MIX_AB_DIM, D_MODEL), MIX_AB_DIM ** -0.5),
        'w_in_c': nrm((N_ODD, D_MODEL, 2 * LRU_DIM), D_MODEL ** -0.5),
        'conv_c_w': nrm((N_ODD, LRU_CONV_WIDTH, LRU_DIM), LRU_CONV_WIDTH ** -0.5),
        'conv_c_b': nrm((N_ODD, LRU_DIM), 0.02),
        'w_lru_a': nrm((N_ODD, LRU_BLOCKS, LRU_BLOCK, LRU_BLOCK), LRU_BLOCK ** -0.5),
        'b_lru_a': nrm((N_ODD, LRU_DIM), 0.02),
        'w_lru_x': nrm((N_ODD, LRU_BLOCKS, LRU_BLOCK, LRU_BLOCK), LRU_BLOCK ** -0.5),
        'b_lru_x': nrm((N_ODD, LRU_DIM), 0.02),
        'lru_lambda': lam,
        'w_out_c': nrm((N_ODD, LRU_DIM, D_MODEL), LRU_DIM ** -0.5),
        'w_ffn_up': nrm((DEPTH, D_MODEL, 2 * FFN_DIM), D_MODEL ** -0.5),
        'ffn_conv_w': nrm((DEPTH, FFN_CONV_WIDTH, 2 * FFN_DIM), FFN_CONV_WIDTH ** -0.5),
        'ffn_conv_b': nrm((DEPTH, 2 * FFN_DIM), 0.02),
        'w_ffn_down': nrm((DEPTH, FFN_DIM, D_MODEL), FFN_DIM ** -0.5),
    }


def reference(x_prompt, x_sample, state_conv_a, state_ret, state_lru_conv, state_lru_h, state_ffn_conv,
              norm_mix, norm_ffn, norm_final, w_in_ab, conv_a_w, conv_a_b, ln_a_g, ln_a_b, gn_ret_g, w_out_ab,
              w_in_c, conv_c_w, conv_c_b, w_lru_a, b_lru_a, w_lru_x, b_lru_x, lru_lambda, w_out_c,
              w_ffn_up, ffn_conv_w, ffn_conv_b, w_ffn_down):
    p = {'norm_mix': norm_mix, 'norm_ffn': norm_ffn, 'norm_final': norm_final,
         'w_in_ab': w_in_ab, 'conv_a_w': conv_a_w, 'conv_a_b': conv_a_b, 'ln_a_g': ln_a_g, 'ln_a_b': ln_a_b,
         'gn_ret_g': gn_ret_g, 'w_out_ab': w_out_ab, 'w_in_c': w_in_c, 'conv_c_w': conv_c_w, 'conv_c_b': conv_c_b,
         'w_lru_a': w_lru_a, 'b_lru_a': b_lru_a, 'w_lru_x': w_lru_x, 'b_lru_x': b_lru_x, 'lru_lambda': lru_lambda,
         'w_out_c': w_out_c, 'w_ffn_up': w_ffn_up, 'ffn_conv_w': ffn_conv_w, 'ffn_conv_b': ffn_conv_b,
         'w_ffn_down': w_ffn_down}
    bp = x_prompt.shape[0]
    dt = x_prompt.dtype
    y_prompt, p_conv_a, p_ret, p_lru_conv, p_lru_h, p_ffn = _trunk(
        x_prompt,
        jnp.zeros((N_EVEN, bp, CONV_A_WIDTH - 1, CONV_A_DIM), dt),
        jnp.zeros((N_EVEN, bp, RET_HEADS, RET_QK_DIM, RET_V_DIM), jnp.float32),
        jnp.zeros((N_ODD, bp, LRU_CONV_WIDTH - 1, LRU_DIM), dt),
        jnp.zeros((N_ODD, bp, LRU_DIM), jnp.float32),
        jnp.zeros((DEPTH, bp, FFN_CONV_WIDTH - 1, 2 * FFN_DIM), dt),
        0, p)
    y_sample, s_conv_a, s_ret, s_lru_conv, s_lru_h, s_ffn = _trunk(
        x_sample, state_conv_a, state_ret, state_lru_conv, state_lru_h, state_ffn_conv, PAST_LEN, p)
    return (y_prompt, y_sample, p_conv_a, p_ret, p_lru_conv, p_lru_h, p_ffn, s_conv_a, s_ret, s_lru_conv, s_lru_h, s_ffn)
```

```python
import math
from contextlib import ExitStack

import numpy as np
import concourse.bass as bass
import concourse.mybir as mybir
from concourse.bass_utils import run_bass_kernel_spmd

F32 = mybir.dt.float32
BF16 = mybir.dt.bfloat16
AF = mybir.ActivationFunctionType
ALU = mybir.AluOpType

D = 1024
KC = 8
SEQ = 2048
NSQ = 16
TS = 8
PAST = 16384
H = 3
SW = H + TS
FF = 2816
NJ = 22
HALVES = [(0, 896, False), (896, 1152, True)]
GROUPS = [(0, 8), (8, 7), (15, 7)]
RMS_EPS = 1e-6
LN_EPS = 1e-5
GN_EPS = 1e-5
NSLOT = 12
PF = 3
NTF = 12
NTB = 8
STAGE = 4
import os as _os
_LAT = float(_os.environ.get('SCHED_LAT', '500'))
_EPS = float(_os.environ.get('SCHED_EPS', '150'))
_TBL_INIT = None


class Reg:
    __slots__ = ("name", "w", "rd", "excl")

    def __init__(self, name, excl=False):
        self.name = name
        self.w = None
        self.rd = {}
        self.excl = excl


class Buf:
    def __init__(self, t, name, ncols=None, G=256, excl=False):
        self.t = t
        self.name = name
        self.G = G
        n = 1 if ncols is None else (ncols + G - 1) // G
        self.regs = [Reg(f"{name}.{i}", excl) for i in range(n)]
        self.ncols = ncols

    def r(self, c0=None, c1=None):
        if self.ncols is None or c0 is None:
            return list(self.regs)
        return self.regs[c0 // self.G:(c1 - 1) // self.G + 1]


class _Rec:
    def __init__(self):
        self.call = None

    def __getattr__(self, name):
        def f(*a, **k):
            self.call = (name, a, k)
            return self
        return f


class _Node:
    __slots__ = ("idx", "eng", "call", "deps", "dur", "lat", "tbl", "tok", "is_dma", "succ", "ndep", "ready")


_ACT_TBL = {}


def _free_elems(ap):
    try:
        sh = ap.shape
        n = 1
        for d in sh[1:]:
            n *= int(d)
        return n
    except Exception:
        return 512


class Sched:
    def __init__(self, nc, ndma=6):
        self.nc = nc
        self.E = {"pe": nc.tensor, "act": nc.scalar, "dve": nc.vector, "pool": nc.gpsimd, "sp": nc.sync}
        self.sem = {}
        self.cnt = {}
        self.waited = {e: {} for e in self.E}
        for e in self.E:
            self.sem[e] = nc.alloc_semaphore(name=f"c_{e}")
            self.cnt[e] = 0
        self.dsem = {}
        for q in ("sp", "pool"):
            self.dsem[q] = [[nc.alloc_semaphore(name=f"d_{q}{i}"), 0] for i in range(ndma)]
        self.drr = {"sp": 0, "pool": 0}
        self.nodes = []
        self.nidx = 0
        self.reorder = True
        self.marks = []

    def _record(self, node, reads, writes):
        deps = {}

        def add(n, raw):
            if n is None:
                return
            if deps.get(n.idx, (None, False))[1] is False:
                deps[n.idx] = (n, raw or deps.get(n.idx, (None, False))[1])

        for r in reads:
            add(r.w, True)
            if r.excl:
                for n in r.rd.values():
                    add(n, False)
        for r in writes:
            add(r.w, r.excl)
            for n in r.rd.values():
                add(n, False)
        node.deps = list(deps.values())
        for r in writes:
            r.w = node
            r.rd = {}
        for r in reads:
            if r.excl:
                r.w = node
                r.rd = {}
            else:
                r.rd[node.idx] = node
        self.nodes.append(node)

    def op(self, e, fn, reads=(), writes=(), dur=None, tbl=None):
        rec = _Rec()
        fn(rec)
        n = _Node()
        n.idx = self.nidx
        self.nidx += 1
        n.eng = e
        n.call = rec.call
        n.is_dma = False
        n.tbl = tbl
        n.tok = None
        if dur is None:
            name, a, k = rec.call
            out = k.get("out", a[0] if a else None)
            fe = _free_elems(out) if out is not None else 512
            if e == "pe":
                dur = 10.0 + fe / 2.4
            elif e == "act":
                dur = 280.0 + fe / 1.1
                if name == "activation":
                    f = k.get("func")
                    tbl = _ACT_TBL.get(f, None)
                    n.tbl = tbl
            elif e == "dve":
                dur = 180.0 + fe / 0.9
                if name == "reciprocal":
                    dur = 120.0 + 4 * fe / 0.96
                elif name in ("memset",):
                    dur = 100.0 + fe / 3.0
            else:
                dur = 300.0
        n.dur = dur
        n.lat = 60.0 if e == "pe" else _LAT
        self._record(n, reads, writes)

    def dma(self, q, out, in_, reads=(), writes=(), nbytes=None):
        n = _Node()
        n.idx = self.nidx
        self.nidx += 1
        n.eng = q
        n.call = (out, in_)
        n.is_dma = True
        n.tbl = None
        n.tok = None
        if nbytes is None:
            try:
                nb = 1
                for d in out.shape:
                    nb *= int(d)
                nbytes = nb * 4
            except Exception:
                nbytes = 65536
        n.dur = 1000.0 if q == "pool" else 150.0
        n.lat = 2000.0 + nbytes / 150.0
        self._record(n, reads, writes)

    def _wait(self, e, key, sem, val):
        if self.waited[e].get(key, 0) >= val:
            return
        self.E[e].wait_ge(sem, val)
        self.waited[e][key] = val

    def _emit(self, n):
        e = n.eng
        toks = {}
        for (d, raw) in n.deps:
            key, sem, val = d.tok
            if key == e:
                if e in ("pe", "pool", "sp"):
                    continue
                if not raw:
                    continue
                if val <= self.cnt[e] - 2:
                    continue
            if toks.get(key, (None, 0))[1] < val:
                toks[key] = (sem, val)
        if n.is_dma:
            i = self.drr[e]
            self.drr[e] = (i + 1) % len(self.dsem[e])
            ent = self.dsem[e][i]
            key = f"d_{e}{i}"
            if ent[1] > 0:
                self._wait(e, key, ent[0], 16 * ent[1])
            for k2, (sem, val) in toks.items():
                self._wait(e, k2, sem, val)
            out, in_ = n.call
            self.E[e].dma_start(out=out, in_=in_).then_inc(ent[0], 16)
            ent[1] += 1
            n.tok = (key, ent[0], 16 * ent[1])
        else:
            for k2, (sem, val) in toks.items():
                self._wait(e, k2, sem, val)
            name, a, k = n.call
            ins = getattr(self.E[e], name)(*a, **k)
            self.cnt[e] += 1
            ins.then_inc(self.sem[e], 1)
            n.tok = (e, self.sem[e], self.cnt[e])
        n.deps = None
        n.call = None

    def flush(self):
        nodes = self.nodes
        self.nodes = []
        if not nodes:
            return
        if not self.reorder:
            for n in nodes:
                self._emit(n)
            return
        inwin = {n.idx: n for n in nodes}
        for n in nodes:
            n.succ = []
            n.ndep = 0
            n.ready = 0.0
        for n in nodes:
            for (d, raw) in n.deps:
                if d.idx in inwin:
                    d.succ.append(n)
                    n.ndep += 1
        bl = {}
        for n in reversed(nodes):
            m = 0.0
            for s_ in n.succ:
                v = n.lat + bl[s_.idx]
                if v > m:
                    m = v
            bl[n.idx] = n.dur + m
        free = {e: 0.0 for e in self.E}
        last_tbl = None
        ready = {e: [] for e in self.E}
        for n in nodes:
            if n.ndep == 0:
                ready[n.eng].append(n)
        left = len(nodes)
        EPS = _EPS
        while left:
            best = None
            for e, lst in ready.items():
                if not lst:
                    continue
                mr = min(x.ready for x in lst)
                t_e = max(free[e], mr)
                if best is None or t_e < best[0]:
                    best = (t_e, e)
            t_e, e = best
            lst = ready[e]
            cands = [x for x in lst if x.ready <= t_e + EPS]
            if e in ("sp", "pool"):
                n = min(cands, key=lambda x: x.idx)
            else:
                if e == "act" and last_tbl is not None:
                    same = [x for x in lst if x.ready <= t_e + 1500.0 and (x.tbl is None or x.tbl == last_tbl)]
                    if same:
                        cands = same
                n = max(cands, key=lambda x: (bl[x.idx], -x.idx))
            lst.remove(n)
            st = max(free[e], n.ready)
            if e == "act" and n.tbl is not None:
                if last_tbl is not None and n.tbl != last_tbl:
                    st += 1300.0
                last_tbl = n.tbl
            end = st + n.dur
            free[e] = end
            fin_t = end + n.lat
            self._emit(n)
            left -= 1
            for s_ in n.succ:
                if fin_t > s_.ready:
                    s_.ready = fin_t
                s_.ndep -= 1
                if s_.ndep == 0:
                    ready[s_.eng].append(s_)
            n.succ = None

    def barrier(self):
        self.flush()
        self.marks.append(dict(self.cnt))
        for e in self.E:
            for e2 in ("pe", "act", "dve", "pool"):
                if e2 != e and self.cnt[e2] > 0:
                    self._wait(e, e2, self.sem[e2], self.cnt[e2])
            for q in ("sp", "pool"):
                for i, ent in enumerate(self.dsem[q]):
                    if ent[1] > 0:
                        self._wait(e, f"d_{q}{i}", ent[0], 16 * ent[1])


class Pool:
    def __init__(self, es, nc, name, n, shape, dt, excl=False, space="sbuf"):
        self.items = []
        for i in range(n):
            if space == "sbuf":
                t = es.enter_context(nc.sbuf_tensor(f"{name}{i}", shape, dt))
            else:
                t = es.enter_context(nc.psum_tensor(f"{name}{i}", shape, dt))
            self.items.append((t, Buf(t, f"{name}{i}", excl=excl)))
        self.i = 0

    def next(self):
        it = self.items[self.i]
        self.i = (self.i + 1) % len(self.items)
        return it


def _init_tbl():
    _ACT_TBL.update({AF.Gelu_apprx_tanh: "gelu", AF.Sigmoid: "sig", AF.Silu: "silu", AF.Exp: "exp", AF.Ln: "exp",
                     AF.Sqrt: "sqrt", AF.Tanh: "exp"})


def _gammas():
    lg = np.log(np.float32(1.0) - np.float32(2.0) ** (-5.0 - np.arange(4, dtype=np.float32))).astype(np.float32)
    return lg


def _half_geom(hi):
    t0, npr, has_s = HALVES[hi]
    sc0 = H + npr
    nt = sc0 + (NSQ * SW if has_s else 0)
    return t0, npr, has_s, sc0, nt


def _blocks0(hi):
    t0, npr, has_s, sc0, nt = _half_geom(hi)
    out = []
    c = H
    while c < nt:
        l = min(512, nt - c)
        out.append((c, l))
        c += l
    return out


def _blocks3(hi):
    t0, npr, has_s, sc0, nt = _half_geom(hi)
    out = []
    c = 0
    while True:
        l = min(512, nt - c)
        out.append((c, l))
        if c + l >= nt:
            break
        c += l - H
    return out


def _host_consts():
    c = {}
    c["ident"] = np.eye(128, dtype=np.float32)
    lg = _gammas()
    inv_freq = (np.float32(10000.0) ** (-np.arange(0, 128, 2, dtype=np.float32) / np.float32(128))).astype(np.float32)
    for hi in range(2):
        t0, npr, has_s, sc0, nt = _half_geom(hi)
        pos = np.zeros(nt, np.float32)
        valid = np.zeros(nt, bool)
        pos[H:H + npr] = np.arange(t0, t0 + npr, dtype=np.float32)
        valid[H:H + npr] = True
        if has_s:
            for s in range(NSQ):
                b = sc0 + SW * s + H
                pos[b:b + TS] = np.arange(PAST, PAST + TS, dtype=np.float32)
                valid[b:b + TS] = True
        ang = (pos[:, None] * inv_freq[None, :]).astype(np.float32)
        cs = np.cos(ang).astype(np.float32)
        sn = np.sin(ang).astype(np.float32)
        cosT = np.concatenate([cs.T, cs.T], axis=0)
        sinT = np.concatenate([-sn.T, sn.T], axis=0)
        cosT[:, ~valid] = 0
        sinT[:, ~valid] = 0
        c[f"rope{hi}"] = np.ascontiguousarray(np.stack([cosT, sinT], axis=1).astype(np.float32))
    idx = np.arange(128, dtype=np.float32)
    maskP = np.zeros((128, 4, 128), np.float32)
    qdP = np.zeros((128, 4, 128), np.float32)
    kdecP = np.zeros((128, 4), np.float32)
    maskS = np.zeros((128, 4, 128), np.float32)
    qd8 = np.zeros((128, 4, 128), np.float32)
    kdec8 = np.zeros((128, 4), np.float32)
    causal = (idx[:, None] <= idx[None, :]).astype(np.float32)
    sj = np.arange(128) // 8
    jj = (np.arange(128) % 8).astype(np.float32)
    for h in range(4):
        maskP[:, h, :] = causal * np.exp(np.float32(-128.0) * lg[h]).astype(np.float32)
        qdP[:, h, :] = np.exp((idx + 1.0) * lg[h])[None, :]
        kdecP[:, h] = np.exp((127.0 - idx) * lg[h])
        rel = jj[None, :] - jj[:, None]
        m = np.where((sj[:, None] == sj[None, :]) & (rel >= 0), np.exp(np.maximum(rel, 0) * lg[h]), 0.0)
        maskS[:, h, :] = m
        qd8[:, h, :] = np.exp((jj + 1.0) * lg[h])[None, :]
        kdec8[:, h] = np.exp((7.0 - jj) * lg[h])
    c["maskP"] = maskP
    c["qdP"] = qdP
    c["maskS"] = maskS.astype(np.float32)
    c["qd8"] = qd8
    rc = np.zeros((128, 32), np.float32)
    rc[:, 0:4] = kdecP
    rc[:, 4:8] = kdec8
    rc[:, 8:24] = (sj[:, None] == np.arange(16)[None, :]).astype(np.float32)
    c["rcon"] = rc
    t0, npr, has_s, sc0, nt = _half_geom(1)
    b3 = _blocks3(1)
    lc0, ll = b3[-1]
    sel2 = np.zeros((32, ll), np.float32)
    sel3 = np.zeros((48, ll), np.float32)
    for s in range(NSQ):
        for r in range(2):
            sel2[2 * s + r, sc0 + SW * s + 1 + r - lc0] = 1
        for r in range(3):
            sel3[3 * s + r, sc0 + SW * s + r - lc0] = 1
    c["sel2"] = sel2
    c["sel3"] = sel3
    return c


def _fm(v):
    return np.ascontiguousarray(v.reshape(-1, 128).T)


PAR = {}


def _pack_params(inp):
    cols = []
    off = 0

    def add(name, arr):
        nonlocal off
        arr = np.ascontiguousarray(arr, dtype=np.float32).reshape(128, -1)
        PAR[name] = (off, arr.shape[1])
        cols.append(arr)
        off += arr.shape[1]

    for l in range(2):
        add(f"nmix{l}", _fm(inp["norm_mix"][l]))
        add(f"nffn{l}", _fm(inp["norm_ffn"][l]))
    add("nfin", _fm(inp["norm_final"]))
    add("cab", _fm(inp["conv_a_b"][0]))
    add("lng", _fm(inp["ln_a_g"][0]))
    add("lnb", _fm(inp["ln_a_b"][0]))
    add("gng", _fm(inp["gn_ret_g"][0]))
    add("caw", inp["conv_a_w"][0].reshape(31, 4, 128).transpose(2, 1, 0))
    add("ccw", inp["conv_c_w"][0].reshape(4, 8, 128).transpose(2, 1, 0))
    add("ccb", _fm(inp["conv_c_b"][0]))
    add("ba", _fm(inp["b_lru_a"][0]))
    add("bx", _fm(inp["b_lru_x"][0]))
    add("lam", _fm(inp["lru_lambda"][0]))
    for l in range(2):
        add(f"fcw{l}", inp["ffn_conv_w"][l].reshape(3, 44, 128).transpose(2, 1, 0))
        add(f"fcb{l}", _fm(inp["ffn_conv_b"][l]))
    return np.ascontiguousarray(np.concatenate(cols, axis=1)), off


def build(npar):
    nc = bass.Bass("TRN2", target_bir_lowering=False)

    def din(name, shape):
        return nc.dram_tensor(name, list(shape), F32, kind="ExternalInput").ap()

    def dout(name, shape):
        return nc.dram_tensor(name, list(shape), F32, kind="ExternalOutput").ap()

    xp_d = din("xp", (SEQ, D))
    xs_d = din("xs", (NSQ * TS, D))
    sca_d = din("st_conv_a", (NSQ * 30, 512))
    sret_d = din("st_ret", (NSQ, 4, 128, 128))
    slc_d = din("st_lru_conv", (NSQ * 3, D))
    slh_d = din("st_lru_h", (NSQ, D))
    sffn_d = din("st_ffn", (2, NSQ * 2, 2 * FF))
    w_in_ab = din("w_in_ab", (D, 3072))
    w_sw = din("w_sw", (D, 1024))
    w_out_ab = din("w_out_ab", (D, D))
    w_in_c = din("w_in_c", (D, 2048))
    w_lru_a = din("w_lru_a", (8, 128, 128))
    w_lru_x = din("w_lru_x", (8, 128, 128))
    w_out_c = din("w_out_c", (D, D))
    w_up = din("w_ffn_up", (2, D, 2 * FF))
    w_dn = din("w_ffn_down", (2, FF, D))
    par_d = din("params", (128, npar))
    ident_d = din("ident", (128, 128))
    rope_d = [din(f"rope{hi}", (128, 2, _half_geom(hi)[4])) for hi in range(2)]
    maskP_d = din("maskP", (128, 4, 128))
    qdP_d = din("qdP", (128, 4, 128))
    maskS_d = din("maskS", (128, 4, 128))
    qd8_d = din("qd8", (128, 4, 128))
    rcon_d = din("rcon", (128, 32))
    b3_1 = _blocks3(1)
    sel2_d = din("sel2", (32, b3_1[-1][1]))
    sel3_d = din("sel3", (48, b3_1[-1][1]))

    yp_d = dout("y_p", (SEQ, D))
    ys_d = dout("y_s", (NSQ * TS, D))
    o_pca = dout("p_conv_a", (30, 512))
    o_pret = dout("p_ret", (4, 128, 128))
    o_plc = dout("p_lru_conv", (3, D))
    o_plh = dout("p_lru_h", (1, D))
    o_pffn = dout("p_ffn", (2, 2, 2 * FF))
    o_sca = dout("s_conv_a", (NSQ * 30, 512))
    o_sret = dout("s_ret", (NSQ, 4, 128, 128))
    o_slc = dout("s_lru_conv", (NSQ * 3, D))
    o_slh = dout("s_lru_h", (NSQ, D))
    o_sffn = dout("s_ffn", (2, NSQ * 2, 2 * FF))

    lg = _gammas()
    gC = [float(np.exp(np.float32(128.0) * lg[h])) for h in range(4)]
    g8 = [float(np.exp(np.float32(8.0) * lg[h])) for h in range(4)]

    _init_tbl()
    S = Sched(nc)
    NTMAX = _half_geom(1)[4]

    with ExitStack() as es:
        def sb(name, shape, dt=F32):
            return es.enter_context(nc.sbuf_tensor("sb_" + name, list(shape), dt))

        xT = sb("xT", (128, KC, NTMAX))
        xn = sb("xn", (128, KC, NTMAX), BF16)
        BxT = Buf(xT, "xT", NTMAX)
        Bxn = Buf(xn, "xn", NTMAX)
        par = sb("par", (128, npar))
        Bpar = Buf(par, "par")
        identf = sb("identf", (128, 128))
        identb = sb("identb", (128, 128), BF16)
        Bid = Buf(identf, "identf")
        Bidb = Buf(identb, "identb")
        ones = sb("ones", (128, 3, 128), BF16)
        Bones = Buf(ones, "ones")
        maskP = sb("maskP", (128, 4, 128))
        qdP = sb("qdP", (128, 4, 128))
        maskS = sb("maskS", (128, 4, 128))
        qd8 = sb("qd8", (128, 4, 128))
        rcon = sb("rcon", (128, 32))
        Bcon = Buf(maskP, "retconst")
        sel2 = sb("sel2", (32, b3_1[-1][1]), BF16)
        sel3 = sb("sel3", (48, b3_1[-1][1]), BF16)
        Bsel = Buf(sel2, "sel")
        lruc = sb("lruc", (128, 8, 2))
        hbias = sb("hbias", (128, 2, 8))
        Blruc = Buf(lruc, "lruc")
        cxn = sb("cxn", (128, 3, KC, H), BF16)
        Bcxn = Buf(cxn, "cxn")
        cu = sb("cu", (128, 4, 30), BF16)
        Bcu = Buf(cu, "cu")
        cS = sb("cS", (128, 4, 128))
        BcS = Buf(cS, "cS")
        ch = sb("ch", (128, 8))
        Bch = Buf(ch, "ch")

        slots = Pool(es, nc, "wsl", NSLOT, (128, 1024), BF16)
        TF = Pool(es, nc, "tf", NTF, (128, 512), F32)
        TB = Pool(es, nc, "tb", NTB, (128, 512), BF16)
        PS = Pool(es, nc, "ps", 6, (128, 512), F32, excl=True, space="psum")
        PSL = Pool(es, nc, "psl", 2, (128, 512), F32, excl=True, space="psum")
        _ps6 = list(PS.items)
        _ps8 = list(PS.items) + list(PSL.items)

        def ps_mode(n):
            PS.items = _ps8 if n == 8 else _ps6
            PS.i = 0

        def P(name, k=None):
            o, n = PAR[name]
            if k is None:
                return par[:, o:o + n]
            return par[:, o + k:o + k + 1]

        S.dma("sp", par[:], par_d, writes=Bpar.r())
        S.dma("sp", identf[:], ident_d, writes=Bid.r())
        S.dma("pool", identb[:], ident_d, writes=Bidb.r())
        S.dma("sp", maskP[:], maskP_d, writes=Bcon.r())
        S.dma("sp", qdP[:], qdP_d, writes=Bcon.r())
        S.dma("sp", maskS[:], maskS_d, writes=Bcon.r())
        S.dma("sp", qd8[:], qd8_d, writes=Bcon.r())
        S.dma("sp", rcon[:], rcon_d, writes=Bcon.r())
        S.dma("pool", sel2[:], sel2_d, writes=Bsel.r())
        S.dma("pool", sel3[:], sel3_d, writes=Bsel.r())
        S.op("dve", lambda e: e.memset(ones[:, 0, :], 1.0 / 1024), writes=Bones.r())
        S.op("dve", lambda e: e.memset(ones[:, 1, :], 1.0 / 512), writes=Bones.r())
        S.op("dve", lambda e: e.memset(ones[:, 2, :], 1.0 / 128), writes=Bones.r())
        S.op("act", lambda e: e.activation(out=lruc[:, :, 0], in_=P("lam"), func=AF.Exp, scale=-1.0),
             reads=Bpar.r(), writes=Blruc.r())
        S.op("act", lambda e: e.activation(out=lruc[:, :, 0], in_=lruc[:, :, 0], func=AF.Ln, bias=1.0, scale=1.0),
             reads=Blruc.r(), writes=Blruc.r())
        S.op("dve", lambda e: e.tensor_scalar(out=lruc[:, :, 1], in0=lruc[:, :, 0], scalar1=-8.0, scalar2=None, op0=ALU.mult),
             reads=Blruc.r(), writes=Blruc.r())
        S.op("dve", lambda e: e.tensor_scalar(out=lruc[:, :, 0], in0=lruc[:, :, 0], scalar1=-4.0, scalar2=None, op0=ALU.mult),
             reads=Blruc.r(), writes=Blruc.r())
        S.op("dve", lambda e: e.tensor_scalar(out=hbias[:, 0, :], in0=P("ba"), scalar1=0.5, scalar2=None, op0=ALU.mult),
             reads=Bpar.r(), writes=Blruc.r())
        S.op("dve", lambda e: e.tensor_scalar(out=hbias[:, 1, :], in0=P("bx"), scalar1=0.5, scalar2=None, op0=ALU.mult),
             reads=Bpar.r(), writes=Blruc.r())

        def wap_cols(w2d, c0, ncols=128):
            return w2d.rearrange("(k p) c -> p k c", p=128)[:, :, c0:c0 + ncols]

        def weight_plan():
            for hi in range(2):
                if STAGE < 1:
                    continue
                for c in range(4):
                    yield ("alin", c), wap_cols(w_in_ab, 128 * c)
                    yield ("agate", c), wap_cols(w_in_ab, 512 + 128 * c)
                for h in range(4):
                    yield ("q", h), wap_cols(w_in_ab, 1024 + 128 * h)
                    yield ("qs", h), wap_cols(w_sw, 128 * h)
                    yield ("k", h), wap_cols(w_in_ab, 1536 + 128 * h)
                    yield ("ks", h), wap_cols(w_sw, 512 + 128 * h)
                    yield ("v", h), wap_cols(w_in_ab, 2048 + 128 * h)
                    yield ("g", h), wap_cols(w_in_ab, 2560 + 128 * h)
                for o in range(8):
                    yield ("wo0", o), wap_cols(w_out_ab, 128 * o)
                for l in range(2):
                    if l == 0 and STAGE < 2:
                        continue
                    if l == 1 and STAGE < 3:
                        continue
                    if l == 1:
                        for n in range(8):
                            yield ("gate", n), wap_cols(w_in_c, 128 * n)
                            yield ("rec", n), wap_cols(w_in_c, 1024 + 128 * n)
                        for o in range(8):
                            yield ("wo1", o), wap_cols(w_out_c, 128 * o)
                        if STAGE < 4:
                            continue
                    for (j0, jn) in GROUPS:
                        for j in range(j0, j0 + jn):
                            yield ("upg", l, j), wap_cols(w_up[l], 128 * j)
                            yield ("upu", l, j), wap_cols(w_up[l], FF + 128 * j)
                        for j in range(j0, j0 + jn):
                            yield ("wd", l, j), w_dn[l][128 * j:128 * j + 128, :]

        plan = list(weight_plan())
        wstate = {"issued": 0, "used": 0}
        wslot_of = {}

        def w_issue(upto):
            while wstate["issued"] < min(upto, len(plan)):
                i = wstate["issued"]
                name, ap = plan[i]
                t, b = slots.items[i % NSLOT]
                if len(ap.shape) == 3:
                    dst = t[:].rearrange("p (k c) -> p k c", c=128)
                else:
                    dst = t[:]
                S.dma("pool", dst, ap, writes=b.r())
                wslot_of[i] = (t, b)
                wstate["issued"] += 1

        def w_next(name):
            i = wstate["used"]
            assert plan[i][0] == name, (plan[i][0], name)
            w_issue(i + 1 + PF)
            wstate["used"] += 1
            t, b = wslot_of[i]
            return i, t, b

        def w_check(i):
            assert wstate["issued"] <= i + NSLOT, ("weight evicted", plan[i][0])

        def mm(ps, lhsT, rhs, start, stop, reads, pb):
            S.op("pe", lambda e: e.matmul(ps, lhsT, rhs, start=start, stop=stop), reads=reads, writes=pb.r())

        def proj_block(ps, pb, wi, wt, wb, rhs_buf, rhs_B, c0, ln, nk=KC, last_stop=True):
            w_check(wi)
            w3 = wt[:].rearrange("p (k c) -> p k c", c=128)
            for k in range(nk):
                mm(ps[:, 0:ln], w3[:, k, :], rhs_buf[:, k, c0:c0 + ln], k == 0, (k == nk - 1) and last_stop,
                   wb.r() + rhs_B.r(c0, c0 + ln), pb)

        def rmsnorm(hi, gname, phase_idx):
            t0, npr, has_s, sc0, nt = _half_geom(hi)
            for (c0, ln) in _blocks0(hi):
                ps, pb = PS.next()
                for k in range(KC):
                    sq, sqb = TB.next()
                    if k % 3 != 2:
                        S.op("act", lambda e: e.activation(out=sq[:, 0:ln], in_=xT[:, k, c0:c0 + ln], func=AF.Square),
                             reads=BxT.r(c0, c0 + ln), writes=sqb.r())
                    else:
                        S.op("dve", lambda e: e.tensor_tensor(out=sq[:, 0:ln], in0=xT[:, k, c0:c0 + ln], in1=xT[:, k, c0:c0 + ln], op=ALU.mult),
                             reads=BxT.r(c0, c0 + ln), writes=sqb.r())
                    mm(ps[:, 0:ln], ones[:, 0, :], sq[:, 0:ln], k == 0, k == KC - 1, sqb.r() + Bones.r(), pb)
                sd, sdb = TF.next()
                S.op("act", lambda e: e.activation(out=sd[:, 0:ln], in_=ps[:, 0:ln], func=AF.Ln, bias=RMS_EPS, scale=1.0),
                     reads=pb.r(), writes=sdb.r())
                rs, rsb = TF.next()
                S.op("act", lambda e: e.activation(out=rs[:, 0:ln], in_=sd[:, 0:ln], func=AF.Exp, scale=-0.5), reads=sdb.r(), writes=rsb.r())
                for k in range(KC):
                    S.op("dve", lambda e: e.scalar_tensor_tensor(out=xn[:, k, c0:c0 + ln], in0=xT[:, k, c0:c0 + ln],
                                                                 scalar=P(gname, k), in1=rs[:, 0:ln],
                                                                 op0=ALU.mult, op1=ALU.mult),
                         reads=BxT.r(c0, c0 + ln) + rsb.r() + Bpar.r(), writes=Bxn.r(c0, c0 + ln))
            if has_s:
                for k in range(KC):
                    v = xn[:, k, sc0:nt].rearrange("p (s c) -> p s c", c=SW)[:, :, 0:H]
                    S.op("dve", lambda e: e.memset(v, 0.0), writes=Bxn.r(sc0, nt))
            if phase_idx is not None:
                if hi == 0:
                    S.op("dve", lambda e: e.tensor_copy(out=cxn[:, phase_idx, :, :], in_=xn[:, :, nt - H:nt]),
                         reads=Bxn.r(nt - H, nt), writes=Bcxn.r())
                else:
                    S.op("dve", lambda e: e.tensor_copy(out=xn[:, :, 0:H], in_=cxn[:, phase_idx, :, :]),
                         reads=Bcxn.r(), writes=Bxn.r(0, H))
            elif hi == 0 or True:
                S.op("dve", lambda e: e.memset(xn[:, :, 0:H], 0.0), writes=Bxn.r(0, H))

        def out_proj(hi, wname, mix, Bmix):
            ws = [w_next((wname, o)) for o in range(8)]
            for (c0, ln) in _blocks0(hi):
                for o in range(8):
                    wi, wt, wb = ws[o]
                    ps, pb = PS.next()
                    proj_block(ps, pb, wi, wt, wb, mix, Bmix, c0, ln)
                    S.op("dve", lambda e: e.tensor_tensor(out=xT[:, o, c0:c0 + ln], in0=xT[:, o, c0:c0 + ln],
                                                          in1=ps[:, 0:ln], op=ALU.add),
                         reads=pb.r() + BxT.r(c0, c0 + ln), writes=BxT.r(c0, c0 + ln))

        def transpose_out(src_aps, nrows, dsts):
            ps, pb = PS.next()
            n = len(src_aps)
            for i, (ap, rr) in enumerate(src_aps):
                S.op("pe", lambda e: e.transpose(ps[0:nrows, 128 * i:128 * i + 128], ap, identf[:]),
                     reads=rr + Bid.r(), writes=pb.r())
            st, stb = TF.next()
            S.op("act", lambda e: e.activation(out=st[0:nrows, 0:128 * n], in_=ps[0:nrows, 0:128 * n], func=AF.Copy),
                 reads=pb.r(), writes=stb.r())
            for (dap, r0, r1) in dsts:
                S.dma("sp", dap, st[r0:r1, 0:128 * n], reads=stb.r())

        def load_x(hi):
            t0, npr, has_s, sc0, nt = _half_geom(hi)
            S.op("dve", lambda e: e.memset(xT[:, :, 0:H], 0.0), writes=BxT.r(0, H))
            if has_s:
                for k in range(KC):
                    v = xT[:, k, sc0:nt].rearrange("p (s c) -> p s c", c=SW)[:, :, 0:H]
                    S.op("dve", lambda e: e.memset(v, 0.0), writes=BxT.r(sc0, nt))
            ntile = npr // 128 + (1 if has_s else 0)
            for ti in range(ntile):
                xi, xib = TF.next()
                xi2, xib2 = TF.next()
                is_s = ti == npr // 128
                src = xs_d if is_s else xp_d[t0 + 128 * ti:t0 + 128 * ti + 128, :]
                S.dma("sp", xi[:], src[:, 0:512], writes=xib.r())
                S.dma("sp", xi2[:], src[:, 512:1024], writes=xib2.r())
                for half, (xt_, xb_) in enumerate(((xi, xib), (xi2, xib2))):
                    ps, pb = PS.next()
                    for q in range(4):
                        S.op("pe", lambda e: e.transpose(ps[:, 128 * q:128 * q + 128], xt_[:, 128 * q:128 * q + 128], identf[:]),
                             reads=xb_.r() + Bid.r(), writes=pb.r())
                    eng = "act" if half == 0 else "dve"
                    if not is_s:
                        c0 = H + 128 * ti
                        dst = xT[:, 4 * half:4 * half + 4, c0:c0 + 128]
                        src_ps = ps[:, :].rearrange("p (k c) -> p k c", c=128)
                        if eng == "act":
                            S.op("act", lambda e: e.activation(out=dst, in_=src_ps, func=AF.Copy), reads=pb.r(), writes=BxT.r(c0, c0 + 128))
                        else:
                            S.op("dve", lambda e: e.tensor_copy(out=dst, in_=src_ps), reads=pb.r(), writes=BxT.r(c0, c0 + 128))
                    else:
                        for q in range(4):
                            k = 4 * half + q
                            dst = xT[:, k, sc0:nt].rearrange("p (s c) -> p s c", c=SW)[:, :, H:SW]
                            src_ps = ps[:, 128 * q:128 * q + 128].rearrange("p (s c) -> p s c", c=TS)
                            if eng == "act":
                                S.op("act", lambda e: e.activation(out=dst, in_=src_ps, func=AF.Copy), reads=pb.r(), writes=BxT.r(sc0, nt))
                            else:
                                S.op("dve", lambda e: e.tensor_copy(out=dst, in_=src_ps), reads=pb.r(), writes=BxT.r(sc0, nt))

        def final_out(hi):
            t0, npr, has_s, sc0, nt = _half_geom(hi)
            ntile = npr // 128 + (1 if has_s else 0)
            for ti in range(ntile):
                is_s = ti == npr // 128
                if not is_s:
                    c0, c1 = H + 128 * ti, H + 128 * ti + 128

                    def cols(t3, k):
                        return t3[:, k, c0:c1]
                else:
                    c0, c1 = sc0, nt

                    def cols(t3, k):
                        return t3[:, k, sc0:nt].rearrange("p (s c) -> p s c", c=SW)[:, :, H:SW]
                ps, pb = PS.next()
                for k in range(KC):
                    sq, sqb = TB.next()
                    sqv = sq[:, 0:128] if not is_s else sq[:, 0:128].rearrange("p (s c) -> p s c", c=TS)
                    S.op("act", lambda e: e.activation(out=sqv, in_=cols(xT, k), func=AF.Square),
                         reads=BxT.r(c0, c1), writes=sqb.r())
                    mm(ps[:, 0:128], ones[:, 0, :], sq[:, 0:128], k == 0, k == KC - 1, sqb.r() + Bones.r(), pb)
                sd, sdb = TF.next()
                S.op("act", lambda e: e.activation(out=sd[:, 0:128], in_=ps[:, 0:128], func=AF.Ln, bias=RMS_EPS, scale=1.0),
                     reads=pb.r(), writes=sdb.r())
                rs, rsb = TF.next()
                S.op("act", lambda e: e.activation(out=rs[:, 0:128], in_=sd[:, 0:128], func=AF.Exp, scale=-0.5), reads=sdb.r(), writes=rsb.r())
                rsv = rs[:, 0:128] if not is_s else rs[:, 0:128].rearrange("p (s c) -> p s c", c=TS)
                ya, yab = TF.next()
                yb_, ybb = TF.next()
                for k in range(KC):
                    yt = ya if k < 4 else yb_
                    ytb = yab if k < 4 else ybb
                    o = yt[:, 128 * (k % 4):128 * (k % 4) + 128]
                    if is_s:
                        o = o.rearrange("p (s c) -> p s c", c=TS)
                    S.op("dve", lambda e: e.scalar_tensor_tensor(out=o, in0=cols(xT, k), scalar=P("nfin", k), in1=rsv,
                                                                 op0=ALU.mult, op1=ALU.mult),
                         reads=BxT.r(c0, c1) + rsb.r() + Bpar.r(), writes=ytb.r())
                dst_rows = ys_d if is_s else yp_d[t0 + 128 * ti:t0 + 128 * ti + 128, :]
                for half, (yt, ytb) in enumerate(((ya, yab), (yb_, ybb))):
                    ps2, pb2 = PS.next()
                    for q in range(4):
                        S.op("pe", lambda e: e.transpose(ps2[:, 128 * q:128 * q + 128], yt[:, 128 * q:128 * q + 128], identf[:]),
                             reads=ytb.r() + Bid.r(), writes=pb2.r())
                    st, stb = TF.next()
                    if half == 0:
                        S.op("act", lambda e: e.activation(out=st[:, :], in_=ps2[:, :], func=AF.Copy), reads=pb2.r(), writes=stb.r())
                    else:
                        S.op("dve", lambda e: e.tensor_copy(out=st[:, :], in_=ps2[:, :]), reads=pb2.r(), writes=stb.r())
                    S.dma("sp", dst_rows[:, 512 * half:512 * half + 512], st[:, :], reads=stb.r())

        SH = {}

        def shget(ph, key, make):
            if SH.get("on"):
                if key not in SH:
                    SH[key] = make(SH["ph"])
                return SH[key]
            return make(ph)

        def mk_buf(p, name, shape, dt, ncols=None):
            t = p.enter_context(nc.sbuf_tensor(name, list(shape), dt))
            return t, Buf(t, name, ncols)

        def ffn(hi, l, tail):
            t0, npr, has_s, sc0, nt = _half_geom(hi)
            b3 = _blocks3(hi)
            ps_mode(8)
            with ExitStack() as ph_:
                ph = SH["ph"] if SH.get("on") else ph_
                jmax = max(g[1] for g in GROUPS)
                assert jmax == KC
                tag = "" if SH.get("on") else f"_{l}"
                act, Bact = shget(ph, ("mixact", hi), lambda p: mk_buf(p, f"mixact_{hi}{tag}", [128, jmax, nt], BF16, nt))
                FLT = shget(ph, ("LT", hi), lambda p: Pool(p, nc, f"lt{hi}_", 15, (128, 512), F32)) if SH.get("on") else None
                if has_s:
                    stt, Bstt = shget(ph, ("stt", hi), lambda p: mk_buf(p, f"stt_{hi}{tag}", [32, 2, jmax * 128], BF16))
                    zt, Bzt = shget(ph, ("zt", hi), lambda p: mk_buf(p, f"zt_{hi}{tag}", [128, 2, jmax, 34], F32))
                for (j0, jn) in GROUPS:
                    if has_s:
                        S.dma("pool", stt[:, 0, 0:jn * 128], sffn_d[l][:, 128 * j0:128 * (j0 + jn)], writes=Bstt.r())
                        S.dma("pool", stt[:, 1, 0:jn * 128], sffn_d[l][:, FF + 128 * j0:FF + 128 * (j0 + jn)], writes=Bstt.r())
                    for j in range(j0, j0 + jn):
                        jl = j - j0
                        zc = {}
                        for which, wname in ((0, "upg"), (1, "upu")):
                            wi, wt, wb = w_next((wname, l, j))
                            fch = j + NJ * which
                            o_w, _ = PAR[f"fcw{l}"]
                            o_b, _ = PAR[f"fcb{l}"]
                            wcol = lambda tap: par[:, o_w + 3 * fch + tap:o_w + 3 * fch + tap + 1]
                            bcol = par[:, o_b + fch:o_b + fch + 1]
                            zc[which] = []
                            for bi, (c0, ln) in enumerate(b3):
                                last = bi == len(b3) - 1
                                ps, pb = PS.next()
                                inj = has_s and last
                                proj_block(ps, pb, wi, wt, wb, xn, Bxn, c0, ln, last_stop=not inj)
                                if inj:
                                    mm(ps[:, 0:ln], stt[:, which, 128 * jl:128 * jl + 128], sel2[:, 0:ln], False, True,
                                       Bstt.r() + Bsel.r(), pb)
                                acc, accb = TF.next()
                                lo = ln - H
                                S.op("act", lambda e: e.activation(out=acc[:, 0:lo], in_=ps[:, H:ln], func=AF.Identity,
                                                                   scale=wcol(2), bias=bcol),
                                     reads=pb.r() + Bpar.r(), writes=accb.r())
                                if which == 0 and FLT is not None:
                                    tb_, tbb_ = FLT.next()
                                    S.op("act", lambda e: e.activation(out=tb_[:, 0:lo], in_=ps[:, H - 1:ln - 1], func=AF.Copy, scale=wcol(1)),
                                         reads=pb.r() + Bpar.r(), writes=tbb_.r())
                                    S.op("dve", lambda e: e.scalar_tensor_tensor(out=acc[:, 0:lo], in0=ps[:, H - 2:ln - 2], scalar=wcol(0),
                                                                                 in1=acc[:, 0:lo], op0=ALU.mult, op1=ALU.add),
                                         reads=pb.r() + Bpar.r() + accb.r(), writes=accb.r())
                                    S.op("pool", lambda e: e.tensor_tensor(out=acc[:, 0:lo], in0=acc[:, 0:lo], in1=tb_[:, 0:lo], op=ALU.add),
                                         reads=accb.r() + tbb_.r(), writes=accb.r(), dur=150.0 + 2.2 * lo)
                                else:
                                    S.op("dve", lambda e: e.scalar_tensor_tensor(out=acc[:, 0:lo], in0=ps[:, H - 1:ln - 1], scalar=wcol(1),
                                                                                 in1=acc[:, 0:lo], op0=ALU.mult, op1=ALU.add),
                                         reads=pb.r() + Bpar.r() + accb.r(), writes=accb.r())
                                    S.op("dve", lambda e: e.scalar_tensor_tensor(out=acc[:, 0:lo], in0=ps[:, H - 2:ln - 2], scalar=wcol(0),
                                                                                 in1=acc[:, 0:lo], op0=ALU.mult, op1=ALU.add),
                                         reads=pb.r() + Bpar.r() + accb.r(), writes=accb.r())
                                if has_s and last:
                                    pl = sc0 - c0
                                    S.op("act", lambda e: e.activation(out=zt[:, which, jl, 0:2], in_=ps[:, pl - 2:pl], func=AF.Copy),
                                         reads=pb.r(), writes=Bzt.r())
                                    sv = ps[:, pl:ln].rearrange("p (s c) -> p s c", c=SW)[:, :, H + 6:H + 8]
                                    dv = zt[:, which, jl, 2:34].rearrange("p (s c) -> p s c", c=2)
                                    S.op("act", lambda e: e.activation(out=dv, in_=sv, func=AF.Copy), reads=pb.r(), writes=Bzt.r())
                                zc[which].append((acc, accb, c0 + H, lo))
                        for (ag, agb, oc, lo), (au, aub, _, _) in zip(zc[0], zc[1]):
                            S.op("act", lambda e: e.activation(out=ag[:, 0:lo], in_=ag[:, 0:lo], func=AF.Gelu_apprx_tanh),
                                 reads=agb.r(), writes=agb.r())
                            S.op("dve", lambda e: e.tensor_tensor(out=act[:, jl, oc:oc + lo], in0=ag[:, 0:lo], in1=au[:, 0:lo], op=ALU.mult),
                                 reads=agb.r() + aub.r(), writes=Bact.r(oc, oc + lo))
                    if has_s:
                        for which in range(2):
                            for q0 in range(0, jn, 4):
                                qn = min(4, jn - q0)
                                f0 = FF * which + 128 * (j0 + q0)
                                srcs = [(zt[:, which, q0 + i, :], Bzt.r()) for i in range(qn)]
                                transpose_out(srcs, 34, [(o_pffn[l][:, f0:f0 + 128 * qn], 0, 2),
                                                         (o_sffn[l][:, f0:f0 + 128 * qn], 2, 34)])
                    wds = [w_next(("wd", l, j)) for j in range(j0, j0 + jn)]
                    for bi0, (c0, ln) in enumerate(_blocks0(hi)):
                        if bi0 == 0:
                            NOPEN = 4
                            pss = [PS.next() for o in range(NOPEN)]
                            for o in range(NOPEN):
                                ps, pb = pss[o]
                                for jl in range(jn - 1):
                                    wi, wt, wb = wds[jl]
                                    w_check(wi)
                                    mm(ps[:, 0:ln], wt[:, 128 * o:128 * o + 128], act[:, jl, c0:c0 + ln], jl == 0, False,
                                       wb.r() + Bact.r(c0, c0 + ln), pb)
                            for o in range(NOPEN):
                                ps, pb = pss[o]
                                wi, wt, wb = wds[jn - 1]
                                mm(ps[:, 0:ln], wt[:, 128 * o:128 * o + 128], act[:, jn - 1, c0:c0 + ln], False, True,
                                   wb.r() + Bact.r(c0, c0 + ln), pb)
                                S.op("dve", lambda e: e.tensor_tensor(out=xT[:, o, c0:c0 + ln], in0=xT[:, o, c0:c0 + ln],
                                                                      in1=ps[:, 0:ln], op=ALU.add),
                                     reads=pb.r() + BxT.r(c0, c0 + ln), writes=BxT.r(c0, c0 + ln))
                        for o in range(NOPEN if bi0 == 0 else 0, 8):
                            ps, pb = PS.next()
                            for jl in range(jn):
                                wi, wt, wb = wds[jl]
                                w_check(wi)
                                mm(ps[:, 0:ln], wt[:, 128 * o:128 * o + 128], act[:, jl, c0:c0 + ln], jl == 0, jl == jn - 1,
                                   wb.r() + Bact.r(c0, c0 + ln), pb)
                            S.op("dve", lambda e: e.tensor_tensor(out=xT[:, o, c0:c0 + ln], in0=xT[:, o, c0:c0 + ln],
                                                                  in1=ps[:, 0:ln], op=ALU.add),
                                 reads=pb.r() + BxT.r(c0, c0 + ln), writes=BxT.r(c0, c0 + ln))
                tail()
                if not SH.get("on"):
                    S.barrier()

        def lru_mixer(hi, tail):
            t0, npr, has_s, sc0, nt = _half_geom(hi)
            b3 = _blocks3(hi)
            ps_mode(8)
            with ExitStack() as ph_:
                ph = SH["ph"] if SH.get("on") else ph_
                LT = shget(ph, ("LT", hi), lambda p: Pool(p, nc, f"lt{hi}_", 15, (128, 512), F32))
                mix, Bmix = shget(ph, ("mixact", hi), lambda p: mk_buf(p, f"mixact_{hi}", [128, KC, nt], BF16, nt))
                wax = ph.enter_context(nc.sbuf_tensor(f"wax_{hi}", [128, 2, 8, 128], BF16))
                Bwax = Buf(wax, "wax")
                S.dma("pool", wax[:, 0, :, :], w_lru_a.rearrange("n c d -> c n d"), writes=Bwax.r())
                S.dma("pool", wax[:, 1, :, :], w_lru_x.rearrange("n c d -> c n d"), writes=Bwax.r())
                if has_s:
                    st3 = ph.enter_context(nc.sbuf_tensor(f"st3_{hi}", [48, D], BF16))
                    Bst3 = Buf(st3, "st3")
                    S.dma("pool", st3[:], slc_d, writes=Bst3.r())
                    h0in = ph.enter_context(nc.sbuf_tensor(f"h0in_{hi}", [16, D], F32))
                    Bh0in = Buf(h0in, "h0in")
                    S.dma("sp", h0in[:], slh_d, writes=Bh0in.r())
                    h0T = ph.enter_context(nc.sbuf_tensor(f"h0T_{hi}", [128, 8, 16], F32))
                    Bh0T = Buf(h0T, "h0T")
                    ps, pb = PS.next()
                    for n in range(8):
                        S.op("pe", lambda e: e.transpose(ps[:, 16 * n:16 * n + 16], h0in[:, 128 * n:128 * n + 128], identf[0:16, 0:16]),
                             reads=Bh0in.r() + Bid.r(), writes=pb.r())
                    S.op("act", lambda e: e.activation(out=h0T[:].rearrange("p n s -> p (n s)"), in_=ps[:, 0:128], func=AF.Copy),
                         reads=pb.r(), writes=Bh0T.r())
                    rt = ph.enter_context(nc.sbuf_tensor(f"rt_{hi}", [128, 8, 51], F32))
                    Brt = Buf(rt, "rt")
                    ht = ph.enter_context(nc.sbuf_tensor(f"ht_{hi}", [128, 8, 17], F32))
                    Bht = Buf(ht, "ht")
                hprev = ph.enter_context(nc.sbuf_tensor(f"hprev_{hi}", [128, 8, 4], F32))
                Bhp = Buf(hprev, "hprev")
                for n in range(8):
                    wgi, wgt, wgb = w_next(("gate", n))
                    wri, wrt, wrb = w_next(("rec", n))
                    o_w, _ = PAR["ccw"]
                    wcol = lambda tap: par[:, o_w + 4 * n + tap:o_w + 4 * n + tap + 1]
                    units = []
                    for bi, (c0, ln) in enumerate(b3):
                        last = bi == len(b3) - 1
                        lo = ln - H
                        oc = c0 + H
                        psg, pbg = PS.next()
                        proj_block(psg, pbg, wgi, wgt, wgb, xn, Bxn, c0, ln)
                        S.op("act", lambda e: e.activation(out=mix[:, n, oc:oc + lo], in_=psg[:, H:ln], func=AF.Gelu_apprx_tanh),
                             reads=pbg.r(), writes=Bmix.r(oc, oc + lo))
                        psr, pbr = PS.next()
                        inj = has_s and last
                        proj_block(psr, pbr, wri, wrt, wrb, xn, Bxn, c0, ln, last_stop=not inj)
                        if inj:
                            mm(psr[:, 0:ln], st3[:, 128 * n:128 * n + 128], sel3[:, 0:ln], False, True, Bst3.r() + Bsel.r(), pbr)
                        xc, xcb = LT.next()
                        S.op("dve", lambda e: e.tensor_scalar(out=xc[:, 0:lo], in0=psr[:, H:ln], scalar1=wcol(3), scalar2=P("ccb", n),
                                                              op0=ALU.mult, op1=ALU.add),
                             reads=pbr.r() + Bpar.r(), writes=xcb.r())
                        for tap in (2, 1, 0):
                            sh = 3 - tap
                            S.op("dve", lambda e: e.scalar_tensor_tensor(out=xc[:, 0:lo], in0=psr[:, H - sh:ln - sh], scalar=wcol(tap),
                                                                         in1=xc[:, 0:lo], op0=ALU.mult, op1=ALU.add),
                                 reads=pbr.r() + Bpar.r() + xcb.r(), writes=xcb.r())
                        if has_s and last:
                            pl = sc0 - c0
                            S.op("act", lambda e: e.activation(out=rt[:, n, 0:3], in_=psr[:, pl - 3:pl], func=AF.Copy),
                                 reads=pbr.r(), writes=Brt.r())
                            sv = psr[:, pl:ln].rearrange("p (s c) -> p s c", c=SW)[:, :, H + 5:H + 8]
                            dv = rt[:, n, 3:51].rearrange("p (s c) -> p s c", c=3)
                            S.op("act", lambda e: e.activation(out=dv, in_=sv, func=AF.Copy), reads=pbr.r(), writes=Brt.r())
                        units.append((bi, c0, ln, last, lo, oc, xc, xcb))
                    for (bi, c0, ln, last, lo, oc, xc, xcb) in units:
                        xb, xbb = TB.next()
                        S.op("dve", lambda e: e.tensor_copy(out=xb[:, 0:lo], in_=xc[:, 0:lo]), reads=xcb.r(), writes=xbb.r())
                        psa, pba = PS.next()
                        mm(psa[:, 0:lo], wax[:, 0, n, :], xb[:, 0:lo], True, True, Bwax.r() + xbb.r(), pba)
                        psx, pbx = PS.next()
                        mm(psx[:, 0:lo], wax[:, 1, n, :], xb[:, 0:lo], True, True, Bwax.r() + xbb.r(), pbx)
                        A, Ab = LT.next()
                        I, Ib = TF.next()
                        S2, S2b = TF.next()
                        S.op("act", lambda e: e.activation(out=A[:, 0:lo], in_=psa[:, 0:lo], func=AF.Tanh, bias=hbias[:, 0, n:n + 1], scale=0.5),
                             reads=pba.r() + Blruc.r(), writes=Ab.r())
                        S.op("act", lambda e: e.activation(out=I[:, 0:lo], in_=psx[:, 0:lo], func=AF.Tanh, bias=hbias[:, 1, n:n + 1], scale=0.5),
                             reads=pbx.r() + Blruc.r(), writes=Ib.r())
                        S.op("act", lambda e: e.activation(out=S2[:, 0:lo], in_=A[:, 0:lo], func=AF.Exp, scale=lruc[:, n, 1:2], bias=lruc[:, n, 1:2]),
                             reads=Ab.r() + Blruc.r(), writes=S2b.r())
                        S.op("act", lambda e: e.activation(out=A[:, 0:lo], in_=A[:, 0:lo], func=AF.Exp, scale=lruc[:, n, 0:1], bias=lruc[:, n, 0:1]),
                             reads=Ab.r() + Blruc.r(), writes=Ab.r())
                        S.op("dve", lambda e: e.scalar_tensor_tensor(out=I[:, 0:lo], in0=I[:, 0:lo], scalar=1.0, in1=xc[:, 0:lo], op0=ALU.add, op1=ALU.mult),
                             reads=Ib.r() + xcb.r(), writes=Ib.r())
                        S.op("act", lambda e: e.activation(out=S2[:, 0:lo], in_=S2[:, 0:lo], func=AF.Sqrt, scale=-0.25, bias=0.25),
                             reads=S2b.r(), writes=S2b.r())
                        S.op("dve", lambda e: e.tensor_tensor(out=I[:, 0:lo], in0=I[:, 0:lo], in1=S2[:, 0:lo], op=ALU.mult),
                             reads=Ib.r() + S2b.r(), writes=Ib.r())
                        if has_s and last:
                            pl = sc0 - oc
                            av = A[:, pl:lo].rearrange("p (s c) -> p s c", c=SW)[:, :, 0:H]
                            bv = I[:, pl:lo].rearrange("p (s c) -> p s c", c=SW)
                            S.op("dve", lambda e: e.memset(av, 0.0), writes=Ab.r())
                            S.op("dve", lambda e: e.memset(bv[:, :, 0:H - 1], 0.0), writes=Ib.r())
                            S.op("dve", lambda e: e.tensor_copy(out=bv[:, :, H - 1:H], in_=h0T[:, n, :].unsqueeze(2)),
                                 reads=Bh0T.r(), writes=Ib.r())
                        if bi == 0:
                            init = 0.0 if hi == 0 else ch[:, n:n + 1]
                            ird = [] if hi == 0 else Bch.r()
                        else:
                            init = hprev[:, n, bi - 1:bi]
                            ird = Bhp.r()
                        S.op("dve", lambda e: e.tensor_tensor_scan(out=xc[:, 0:lo], data0=A[:, 0:lo], data1=I[:, 0:lo], initial=init,
                                                                   op0=ALU.mult, op1=ALU.add),
                             reads=Ab.r() + Ib.r() + ird + xcb.r(), writes=xcb.r(), dur=200.0 + 2.3 * lo)
                        S.op("act", lambda e: e.activation(out=hprev[:, n, bi:bi + 1], in_=xc[:, lo - 1:lo], func=AF.Copy),
                             reads=xcb.r(), writes=Bhp.r())
                        if last:
                            if hi == 0:
                                S.op("act", lambda e: e.activation(out=ch[:, n:n + 1], in_=xc[:, lo - 1:lo], func=AF.Copy),
                                     reads=xcb.r(), writes=Bch.r())
                            else:
                                pl = sc0 - oc
                                S.op("act", lambda e: e.activation(out=ht[:, n, 0:1], in_=xc[:, pl - 1:pl], func=AF.Copy),
                                     reads=xcb.r(), writes=Bht.r())
                                sv = xc[:, pl:lo].rearrange("p (s c) -> p s c", c=SW)[:, :, SW - 1:SW]
                                S.op("act", lambda e: e.activation(out=ht[:, n, 1:17].unsqueeze(2), in_=sv, func=AF.Copy),
                                     reads=xcb.r(), writes=Bht.r())
                        S.op("dve", lambda e: e.tensor_tensor(out=mix[:, n, oc:oc + lo], in0=xc[:, 0:lo], in1=mix[:, n, oc:oc + lo], op=ALU.mult),
                             reads=xcb.r() + Bmix.r(oc, oc + lo), writes=Bmix.r(oc, oc + lo))
                if has_s:
                    for q0 in (0, 4):
                        transpose_out([(rt[:, q0 + i, :], Brt.r()) for i in range(4)], 51,
                                      [(o_plc[:, 512 * (q0 // 4):512 * (q0 // 4) + 512], 0, 3),
                                       (o_slc[:, 512 * (q0 // 4):512 * (q0 // 4) + 512], 3, 51)])
                        transpose_out([(ht[:, q0 + i, :], Bht.r()) for i in range(4)], 17,
                                      [(o_plh[:, 512 * (q0 // 4):512 * (q0 // 4) + 512], 0, 1),
                                       (o_slh[:, 512 * (q0 // 4):512 * (q0 // 4) + 512], 1, 17)])
                out_proj(hi, "wo1", mix, Bmix)
                tail()
                if not SH.get("on"):
                    S.barrier()

        def gn_block_3d(ps_o, pb_o, width, cn, qd_ap, dst_fn, c_lo, c_hi, h, Bmix, BSG):
            o, ob = TF.next()
            if qd_ap is not None:
                S.op("dve", lambda e: e.tensor_tensor(out=o[:, 0:width].rearrange("p (a b) -> p a b", b=128),
                                                      in0=ps_o[:, 0:width].rearrange("p (a b) -> p a b", b=128), in1=qd_ap, op=ALU.mult),
                     reads=pb_o.r() + Bcon.r(), writes=ob.r())
            else:
                S.op("act", lambda e: e.activation(out=o[:, 0:width], in_=ps_o[:, 0:width], func=AF.Copy), reads=pb_o.r(), writes=ob.r())
            o16, o16b = TB.next()
            S.op("act", lambda e: e.activation(out=o16[:, 0:width], in_=o[:, 0:width], func=AF.Copy), reads=ob.r(), writes=o16b.r())
            q16, q16b = TB.next()
            S.op("act", lambda e: e.activation(out=q16[:, 0:width], in_=o[:, 0:width], func=AF.Square), reads=ob.r(), writes=q16b.r())
            psm, pbm = PSL.next()
            mm(psm[:, 0:width], ones[:, 2, :], o16[:, 0:width], True, True, o16b.r() + Bones.r(), pbm)
            psq, pbq = PSL.next()
            mm(psq[:, 0:width], ones[:, 2, :], q16[:, 0:width], True, True, q16b.r() + Bones.r(), pbq)
            mean, meanb = TF.next()
            S.op("act", lambda e: e.activation(out=mean[:, 0:width], in_=psm[:, 0:width], func=AF.Copy), reads=pbm.r(), writes=meanb.r())
            var, varb = TF.next()
            S.op("dve", lambda e: e.tensor_tensor(out=var[:, 0:width], in0=mean[:, 0:width], in1=mean[:, 0:width], op=ALU.mult),
                 reads=meanb.r(), writes=varb.r())
            S.op("dve", lambda e: e.tensor_tensor(out=var[:, 0:width], in0=psq[:, 0:width], in1=var[:, 0:width], op=ALU.subtract),
                 reads=pbq.r() + varb.r(), writes=varb.r())
            S.op("act", lambda e: e.activation(out=var[:, 0:width], in_=var[:, 0:width], func=AF.Ln, bias=GN_EPS, scale=1.0),
                 reads=varb.r(), writes=varb.r())
            rs, rsb = TF.next()
            S.op("act", lambda e: e.activation(out=rs[:, 0:width], in_=var[:, 0:width], func=AF.Exp, scale=-0.5), reads=varb.r(), writes=rsb.r())
            S.op("dve", lambda e: e.tensor_tensor(out=o[:, 0:width], in0=o[:, 0:width], in1=mean[:, 0:width], op=ALU.subtract),
                 reads=ob.r() + meanb.r(), writes=ob.r())
            S.op("dve", lambda e: e.tensor_tensor(out=o[:, 0:width], in0=o[:, 0:width], in1=rs[:, 0:width], op=ALU.mult),
                 reads=ob.r() + rsb.r(), writes=ob.r())
            dst, sgv, ov = dst_fn(o)
            S.op("dve", lambda e: e.scalar_tensor_tensor(out=dst, in0=ov, scalar=P("gng", h), in1=sgv, op0=ALU.mult, op1=ALU.mult),
                 reads=ob.r() + BSG.r(c_lo, c_hi) + Bpar.r(), writes=Bmix.r(c_lo, c_hi))


        def even_mixer(hi, tail):
            t0, npr, has_s, sc0, nt = _half_geom(hi)
            b0 = _blocks0(hi)
            nch = npr // 128
            UW = 30 + npr + (NSQ * 38 if has_s else 0)
            us0 = 30 + npr
            ps_mode(6)
            with ExitStack() as ph:
                mix = ph.enter_context(nc.sbuf_tensor(f"mix_{hi}", [128, KC, nt], BF16))
                Bmix = Buf(mix, "mix", nt)
                S.op("dve", lambda e: e.memset(mix[:, :, :], 0.0), writes=Bmix.r())
                with ExitStack() as pa:
                    uT = pa.enter_context(nc.sbuf_tensor(f"uT_{hi}", [128, 4, UW], BF16))
                    BuTc = [Buf(uT, f"uT{c_}", UW) for c_ in range(4)]

                    class _AllU:
                        def r(self, c0=None, c1=None):
                            out = []
                            for b_ in BuTc:
                                out += b_.r(c0, c1)
                            return out
                    BuT = _AllU()
                    diag = pa.enter_context(nc.sbuf_tensor(f"diag_{hi}", [128, 4, 31, 128], BF16))
                    Bdiagc = [Buf(diag, f"diag{c_}") for c_ in range(4)]
                    ut = pa.enter_context(nc.sbuf_tensor(f"ut_{hi}", [128, 4, 30], F32))
                    But = Buf(ut, "ut")
                    o_w, _ = PAR["caw"]
                    for c in range(4):
                        for k in range(31):
                            S.op("dve", lambda e: e.tensor_scalar(out=diag[:, c, k, :], in0=identb[:], scalar1=par[:, o_w + 31 * c + k:o_w + 31 * c + k + 1],
                                                                  scalar2=None, op0=ALU.mult),
                                 reads=Bidb.r() + Bpar.r(), writes=Bdiagc[c].r())
                    if hi == 0:
                        S.op("dve", lambda e: e.memset(uT[:, :, 0:30], 0.0), writes=BuT.r(0, 30))
                    else:
                        S.op("dve", lambda e: e.tensor_copy(out=uT[:, :, 0:30], in_=cu[:, :, :]), reads=Bcu.r(), writes=BuT.r(0, 30))
                    if has_s:
                        us = pa.enter_context(nc.sbuf_tensor(f"us_{hi}", [128, 4, 128], F32))
                        Bus = Buf(us, "us")
                        for q in range(4):
                            si, sib = TF.next()
                            S.dma("sp", si[0:120, :], sca_d[120 * q:120 * q + 120, :], writes=sib.r())
                            ps, pb = PS.next()
                            for c in range(4):
                                S.op("pe", lambda e: e.transpose(ps[:, 128 * c:128 * c + 120], si[0:120, 128 * c:128 * c + 128], identf[0:120, 0:120]),
                                     reads=sib.r() + Bid.r(), writes=pb.r())
                            for c in range(4):
                                dst = uT[:, c, us0 + 38 * 4 * q:us0 + 38 * 4 * (q + 1)].rearrange("p (s c) -> p s c", c=38)[:, :, 0:30]
                                src = ps[:, 128 * c:128 * c + 120].rearrange("p (s c) -> p s c", c=30)
                                S.op("act", lambda e: e.activation(out=dst, in_=src, func=AF.Copy), reads=pb.r(), writes=BuTc[c].r(us0, UW))
                        S.dma("sp", o_sca.rearrange("(s r) f -> s r f", r=30)[:, 0:22, :],
                              sca_d.rearrange("(s r) f -> s r f", r=30)[:, 8:30, :])
                    for c in range(4):
                        wli, wlt, wlb = w_next(("alin", c))
                        wgi, wgt, wgb = w_next(("agate", c))
                        for bi, (c0, ln) in enumerate(b0):
                            psl, pbl = PS.next()
                            proj_block(psl, pbl, wli, wlt, wlb, xn, Bxn, c0, ln)
                            psg, pbg = PS.next()
                            proj_block(psg, pbg, wgi, wgt, wgb, xn, Bxn, c0, ln)
                            sg, sgb = TF.next()
                            S.op("act", lambda e: e.activation(out=sg[:, 0:ln], in_=psg[:, 0:ln], func=AF.Sigmoid),
                                 reads=pbg.r(), writes=sgb.r())
                            p1 = min(c0 + ln, sc0)
                            pn = p1 - c0
                            if pn > 0:
                                uc = 30 + (c0 - H)
                                S.op("dve", lambda e: e.tensor_tensor(out=uT[:, c, uc:uc + pn], in0=psl[:, 0:pn], in1=sg[:, 0:pn], op=ALU.mult),
                                     reads=pbl.r() + sgb.r(), writes=BuTc[c].r(uc, uc + pn))
                                if p1 == sc0:
                                    S.op("dve", lambda e: e.tensor_tensor(out=ut[:, c, :], in0=psl[:, pn - 30:pn], in1=sg[:, pn - 30:pn], op=ALU.mult),
                                         reads=pbl.r() + sgb.r(), writes=But.r())
                            if has_s and c0 + ln > sc0:
                                pl = sc0 - c0
                                lv = psl[:, pl:ln].rearrange("p (s c) -> p s c", c=SW)[:, :, H:SW]
                                gv = sg[:, pl:ln].rearrange("p (s c) -> p s c", c=SW)[:, :, H:SW]
                                dv = uT[:, c, us0:UW].rearrange("p (s c) -> p s c", c=38)[:, :, 30:38]
                                S.op("dve", lambda e: e.tensor_tensor(out=dv, in0=lv, in1=gv, op=ALU.mult),
                                     reads=pbl.r() + sgb.r(), writes=BuTc[c].r(us0, UW))
                                S.op("dve", lambda e: e.tensor_tensor(out=us[:, c, :].rearrange("p (s c) -> p s c", c=TS), in0=lv, in1=gv, op=ALU.mult),
                                     reads=pbl.r() + sgb.r(), writes=Bus.r())
                    if hi == 0:
                        S.op("act", lambda e: e.activation(out=cu[:, :, :], in_=uT[:, :, us0 - 30:us0], func=AF.Copy), reads=BuT.r(us0 - 30, us0), writes=Bcu.r())
                    else:
                        transpose_out([(ut[:, c, :], But.r()) for c in range(4)], 30, [(o_pca[:, :], 0, 30)])
                        ps, pb = PS.next()
                        for c in range(4):
                            S.op("pe", lambda e: e.transpose(ps[:, 128 * c:128 * c + 128], us[:, c, :], identf[:]),
                                 reads=Bus.r() + Bid.r(), writes=pb.r())
                        st, stb = TF.next()
                        S.op("act", lambda e: e.activation(out=st[:, :], in_=ps[:, :], func=AF.Copy), reads=pb.r(), writes=stb.r())
                        osv = o_sca.rearrange("(s r) f -> s r f", r=30)
                        for s in range(NSQ):
                            S.dma("sp", osv[s, 22:30, :], st[8 * s:8 * s + 8, :], reads=stb.r())
                    for bi, (c0, ln) in enumerate(b0):
                        segs = []
                        p1 = min(c0 + ln, sc0)
                        if p1 > c0:
                            segs.append(("p", c0, p1 - c0))
                        if has_s and c0 + ln > sc0:
                            segs.append(("s", sc0, NSQ * TS))
                        for kind, s0, sl in segs:
                            cvs = []
                            psm, pbm = PSL.next()
                            psq, pbq = PSL.next()
                            for c in range(4):
                                ps, pb = PS.next()
                                for k in range(31):
                                    if kind == "p":
                                        ub = (s0 - H) + k
                                        rhs = uT[:, c, ub:ub + sl]
                                        urd = BuTc[c].r(ub, ub + sl)
                                    else:
                                        rhs = uT[:, c, us0:UW].rearrange("p (s c) -> p s c", c=38)[:, :, k:k + TS]
                                        urd = BuTc[c].r(us0, UW)
                                    mm(ps[:, 0:sl], diag[:, c, k, :], rhs, k == 0, k == 30, urd + Bdiagc[c].r(), pb)
                                cv, cvb = TF.next()
                                S.op("act", lambda e: e.activation(out=cv[:, 0:sl], in_=ps[:, 0:sl], func=AF.Identity, bias=P("cab", c), scale=1.0),
                                     reads=pb.r() + Bpar.r(), writes=cvb.r())
                                c16, c16b = TB.next()
                                S.op("act", lambda e: e.activation(out=c16[:, 0:sl], in_=cv[:, 0:sl], func=AF.Copy), reads=cvb.r(), writes=c16b.r())
                                q16, q16b = TB.next()
                                S.op("act", lambda e: e.activation(out=q16[:, 0:sl], in_=cv[:, 0:sl], func=AF.Square), reads=cvb.r(), writes=q16b.r())
                                mm(psm[:, 0:sl], ones[:, 1, :], c16[:, 0:sl], c == 0, c == 3, c16b.r() + Bones.r(), pbm)
                                mm(psq[:, 0:sl], ones[:, 1, :], q16[:, 0:sl], c == 0, c == 3, q16b.r() + Bones.r(), pbq)
                                cvs.append((cv, cvb))
                            mean, meanb = TF.next()
                            S.op("act", lambda e: e.activation(out=mean[:, 0:sl], in_=psm[:, 0:sl], func=AF.Copy), reads=pbm.r(), writes=meanb.r())
                            var, varb = TF.next()
                            S.op("dve", lambda e: e.tensor_tensor(out=var[:, 0:sl], in0=mean[:, 0:sl], in1=mean[:, 0:sl], op=ALU.mult),
                                 reads=meanb.r(), writes=varb.r())
                            S.op("dve", lambda e: e.tensor_tensor(out=var[:, 0:sl], in0=psq[:, 0:sl], in1=var[:, 0:sl], op=ALU.subtract),
                                 reads=pbq.r() + varb.r(), writes=varb.r())
                            S.op("act", lambda e: e.activation(out=var[:, 0:sl], in_=var[:, 0:sl], func=AF.Ln, bias=LN_EPS, scale=1.0),
                                 reads=varb.r(), writes=varb.r())
                            rs, rsb = TF.next()
                            S.op("act", lambda e: e.activation(out=rs[:, 0:sl], in_=var[:, 0:sl], func=AF.Exp, scale=-0.5), reads=varb.r(), writes=rsb.r())
                            for c in range(4):
                                cv, cvb = cvs[c]
                                S.op("dve", lambda e: e.tensor_tensor(out=cv[:, 0:sl], in0=cv[:, 0:sl], in1=mean[:, 0:sl], op=ALU.subtract),
                                     reads=cvb.r() + meanb.r(), writes=cvb.r())
                                S.op("dve", lambda e: e.tensor_tensor(out=cv[:, 0:sl], in0=cv[:, 0:sl], in1=rs[:, 0:sl], op=ALU.mult),
                                     reads=cvb.r() + rsb.r(), writes=cvb.r())
                                if kind == "p":
                                    dst = mix[:, c, s0:s0 + sl]
                                    src = cv[:, 0:sl]
                                    wr = Bmix.r(s0, s0 + sl)
                                else:
                                    dst = mix[:, c, sc0:nt].rearrange("p (s c) -> p s c", c=SW)[:, :, H:SW]
                                    src = cv[:, 0:sl].rearrange("p (s c) -> p s c", c=TS)
                                    wr = Bmix.r(sc0, nt)
                                S.op("act", lambda e: e.activation(out=dst, in_=src, func=AF.Silu, scale=P("lng", c), bias=P("lnb", c)),
                                     reads=cvb.r() + Bpar.r(), writes=wr)
                    S.barrier()
                with ExitStack() as pbk:
                    rope = pbk.enter_context(nc.sbuf_tensor(f"rope_{hi}", [128, 2, nt], F32))
                    Brope = Buf(rope, "rope")
                    S.dma("sp", rope[:], rope_d[hi], writes=Brope.r())
                    QT = pbk.enter_context(nc.sbuf_tensor(f"QT_{hi}", [128, nt], BF16))
                    KT = pbk.enter_context(nc.sbuf_tensor(f"KT_{hi}", [128, nt], BF16))
                    VT = pbk.enter_context(nc.sbuf_tensor(f"VT_{hi}", [128, nt], BF16))
                    SG = pbk.enter_context(nc.sbuf_tensor(f"SG_{hi}", [128, nt], F32))
                    BQT, BKT, BVT, BSG = Buf(QT, "QT", nt), Buf(KT, "KT", nt), Buf(VT, "VT", nt), Buf(SG, "SG", nt)
                    ntl = nch + (1 if has_s else 0)
                    Ktok = pbk.enter_context(nc.sbuf_tensor(f"Ktok_{hi}", [128, ntl, 128], BF16))
                    Vdec = pbk.enter_context(nc.sbuf_tensor(f"Vdec_{hi}", [128, ntl, 128], BF16))
                    BKtok, BVdec = Buf(Ktok, "Ktok"), Buf(Vdec, "Vdec")
                    Sbf = pbk.enter_context(nc.sbuf_tensor(f"Sbf_{hi}", [128, nch + 1, 128], BF16))
                    BSbf = [Buf(Sbf, f"Sbf{i}") for i in range(nch + 1)]
                    Sf = pbk.enter_context(nc.sbuf_tensor(f"Sf_{hi}", [128, 128], F32))
                    BSf = Buf(Sf, "Sf")
                    if has_s:
                        S0f = pbk.enter_context(nc.sbuf_tensor(f"S0f_{hi}", [128, NSQ, 128], F32))
                        BS0f = Buf(S0f, "S0f")
                        S0b = pbk.enter_context(nc.sbuf_tensor(f"S0b_{hi}", [128, NSQ, 128], BF16))
                        BS0b = Buf(S0b, "S0b")
                        Vbd = pbk.enter_context(nc.sbuf_tensor(f"Vbd_{hi}", [128, NSQ, 128], BF16))
                        BVbd = Buf(Vbd, "Vbd")
                        Kds = pbk.enter_context(nc.sbuf_tensor(f"Kds_{hi}", [128, 128], BF16))
                        BKds = Buf(Kds, "Kds")
                        Qds = pbk.enter_context(nc.sbuf_tensor(f"Qds_{hi}", [128, 128], BF16))
                        BQds = Buf(Qds, "Qds")
                        cmpS = pbk.enter_context(nc.sbuf_tensor(f"cmpS_{hi}", [128, 3, 128], BF16))
                        BcmpS = Buf(cmpS, "cmpS")
                    dk_scale = 128.0 ** -0.5
                    for h in range(4):
                        ws = {nm: w_next((nm, h)) for nm in ("q", "qs", "k", "ks", "v", "g")}
                        if has_s:
                            S.dma("sp", S0f[:], sret_d[:, h, :, :].rearrange("s d v -> d s v"), writes=BS0f.r())
                            S.dma("pool", S0b[:], sret_d[:, h, :, :].rearrange("s d v -> d s v"), writes=BS0b.r())
                        for (c0, ln) in b0:
                            for (dstT, Bd, n1, n2, sc) in ((QT, BQT, "q", "qs", 1.0), (KT, BKT, "k", "ks", dk_scale)):
                                ps1, pb1 = PS.next()
                                proj_block(ps1, pb1, *ws[n1], xn, Bxn, c0, ln)
                                ps2, pb2 = PS.next()
                                proj_block(ps2, pb2, *ws[n2], xn, Bxn, c0, ln)
                                t1, t1b = TF.next()
                                t2, t2b = TF.next()
                                S.op("dve", lambda e: e.scalar_tensor_tensor(out=t1[:, 0:ln], in0=ps1[:, 0:ln], scalar=sc, in1=rope[:, 0, c0:c0 + ln],
                                                                             op0=ALU.mult, op1=ALU.mult),
                                     reads=pb1.r() + Brope.r(), writes=t1b.r())
                                S.op("dve", lambda e: e.scalar_tensor_tensor(out=t2[:, 0:ln], in0=ps2[:, 0:ln], scalar=sc, in1=rope[:, 1, c0:c0 + ln],
                                                                             op0=ALU.mult, op1=ALU.mult),
                                     reads=pb2.r() + Brope.r(), writes=t2b.r())
                                S.op("dve", lambda e: e.tensor_tensor(out=dstT[:, c0:c0 + ln], in0=t1[:, 0:ln], in1=t2[:, 0:ln], op=ALU.add),
                                     reads=t1b.r() + t2b.r(), writes=Bd.r(c0, c0 + ln))
                            psv, pbv = PS.next()
                            proj_block(psv, pbv, *ws["v"], xn, Bxn, c0, ln)
                            S.op("act", lambda e: e.activation(out=VT[:, c0:c0 + ln], in_=psv[:, 0:ln], func=AF.Copy), reads=pbv.r(), writes=BVT.r(c0, c0 + ln))
                            psg, pbg = PS.next()
                            proj_block(psg, pbg, *ws["g"], xn, Bxn, c0, ln)
                            S.op("act", lambda e: e.activation(out=SG[:, c0:c0 + ln], in_=psg[:, 0:ln], func=AF.Silu), reads=pbg.r(), writes=BSG.r(c0, c0 + ln))

                        if has_s:
                            for (srcT, Bs, j_) in ((QT, BQT, 0), (KT, BKT, 1), (VT, BVT, 2)):
                                S.op("act", lambda e: e.activation(out=cmpS[:, j_, :].rearrange("p (s c) -> p s c", c=TS),
                                                                   in_=srcT[:, sc0:nt].rearrange("p (s c) -> p s c", c=SW)[:, :, H:SW], func=AF.Copy),
                                     reads=Bs.r(sc0, nt), writes=BcmpS.r())
                        cmp_idx = {id(QT): 0, id(KT): 1, id(VT): 2}

                        def tcols(tsr, i):
                            if i < nch:
                                return tsr[:, H + 128 * i:H + 128 * i + 128]
                            return cmpS[:, cmp_idx[id(tsr)], :]

                        def tregs(B, i):
                            if i < nch:
                                return B.r(H + 128 * i, H + 128 * i + 128)
                            return BcmpS.r()
                        for i0 in range(0, ntl, 8):
                            i1 = min(ntl, i0 + 8)
                            for (srcT, Bs, dstk) in ((KT, BKT, "k"), (VT, BVT, "v")):
                                ps, pb = PS.next()
                                psb = ps[:].bitcast(BF16)
                                for i in range(i0, i1):
                                    S.op("pe", lambda e: e.transpose(psb[:, 128 * (i - i0):128 * (i - i0) + 128], tcols(srcT, i), identb[:]),
                                         reads=tregs(Bs, i) + Bidb.r(), writes=pb.r())
                                npz = min(i1, nch) - i0
                                if dstk == "k":
                                    if npz > 0:
                                        S.op("act", lambda e: e.activation(out=Ktok[:, i0:i0 + npz, :].rearrange("p a b -> p (a b)"),
                                                                           in_=psb[:, 0:128 * npz], func=AF.Copy),
                                             reads=pb.r(), writes=BKtok.r())
                                    if has_s and i1 == ntl:
                                        off = 128 * (nch - i0)
                                        S.op("act", lambda e: e.activation(out=Kds[:, :], in_=psb[:, off:off + 128], func=AF.Copy, scale=rcon[:, 4 + h:5 + h]),
                                             reads=pb.r() + Bcon.r(), writes=BKds.r())
                                else:
                                    if npz > 0:
                                        S.op("act", lambda e: e.activation(out=Vdec[:, i0:i0 + npz, :].rearrange("p a b -> p (a b)"),
                                                                           in_=psb[:, 0:128 * npz], func=AF.Copy, scale=rcon[:, h:h + 1]),
                                             reads=pb.r() + Bcon.r(), writes=BVdec.r())
                                    if has_s and i1 == ntl:
                                        off = 128 * (nch - i0)
                                        S.op("act", lambda e: e.activation(out=Vdec[:, nch, :], in_=psb[:, off:off + 128], func=AF.Copy),
                                             reads=pb.r(), writes=BVdec.r())
                        if hi == 0:
                            S.op("dve", lambda e: e.memset(Sf[:, :], 0.0), writes=BSf.r())
                            S.op("dve", lambda e: e.memset(Sbf[:, 0, :], 0.0), writes=BSbf[0].r())
                        else:
                            S.op("dve", lambda e: e.tensor_copy(out=Sf[:, :], in_=cS[:, h, :]), reads=BcS.r(), writes=BSf.r())
                            S.op("act", lambda e: e.activation(out=Sbf[:, 0, :], in_=cS[:, h, :], func=AF.Copy), reads=BcS.r(), writes=BSbf[0].r())

                        for cg in range(0, nch, 4):
                            cn = min(4, nch - cg)
                            ps_o, pb_o = PSL.next()
                            for ci in range(cg, cg + cn):
                                cc0 = H + 128 * ci
                                ps_u, pb_u = PS.next()
                                mm(ps_u[:, 0:128], Ktok[:, ci, :], Vdec[:, ci, :], True, True, BKtok.r() + BVdec.r(), pb_u)
                                S.op("dve", lambda e: e.scalar_tensor_tensor(out=Sf[:, :], in0=Sf[:, :], scalar=gC[h], in1=ps_u[:, 0:128],
                                                                             op0=ALU.mult, op1=ALU.add),
                                     reads=pb_u.r() + BSf.r(), writes=BSf.r())
                                S.op("act", lambda e: e.activation(out=Sbf[:, ci + 1, :], in_=Sf[:, :], func=AF.Copy), reads=BSf.r(), writes=BSbf[ci + 1].r())
                                ps_s, pb_s = PS.next()
                                mm(ps_s[:, 0:128], KT[:, cc0:cc0 + 128], QT[:, cc0:cc0 + 128], True, True,
                                   BKT.r(cc0, cc0 + 128) + BQT.r(cc0, cc0 + 128), pb_s)
                                scm, scmb = TB.next()
                                S.op("dve", lambda e: e.tensor_tensor(out=scm[:, 0:128], in0=ps_s[:, 0:128], in1=maskP[:, h, :], op=ALU.mult),
                                     reads=pb_s.r() + Bcon.r(), writes=scmb.r())
                                oc_ = 128 * (ci - cg)
                                mm(ps_o[:, oc_:oc_ + 128], Vdec[:, ci, :], scm[:, 0:128], True, False, BVdec.r() + scmb.r(), pb_o)
                                mm(ps_o[:, oc_:oc_ + 128], Sbf[:, ci, :], QT[:, cc0:cc0 + 128], False, True,
                                   BSbf[ci].r() + BQT.r(cc0, cc0 + 128), pb_o)
                            w_ = 128 * cn
                            g0 = H + 128 * cg
                            qd_ap = qdP[:, h, :].unsqueeze(1).broadcast_to([128, cn, 128])

                            def dst_fn(o, g0=g0, w_=w_):
                                return mix[:, 4 + h, g0:g0 + w_], SG[:, g0:g0 + w_], o[:, 0:w_]
                            gn_block_3d(ps_o, pb_o, w_, cn, qd_ap, dst_fn, g0, g0 + w_, h, Bmix, BSG)
                        if hi == 0:
                            S.op("dve", lambda e: e.tensor_copy(out=cS[:, h, :], in_=Sf[:, :]), reads=BSf.r(), writes=BcS.r())
                        else:
                            S.dma("sp", o_pret[h, :, :], Sf[:, :], reads=BSf.r())
                        if has_s:
                            qs_v = tcols(QT, nch)
                            ks_v = tcols(KT, nch)
                            S.op("dve", lambda e: e.tensor_tensor(out=Qds[:, :], in0=qs_v, in1=qd8[:, h, :], op=ALU.mult),
                                 reads=BcmpS.r() + Bcon.r(), writes=BQds.r())
                            ps_s, pb_s = PS.next()
                            mm(ps_s[:, 0:128], ks_v, qs_v, True, True, BcmpS.r(), pb_s)
                            scm, scmb = TB.next()
                            S.op("dve", lambda e: e.tensor_tensor(out=scm[:, 0:128], in0=ps_s[:, 0:128], in1=maskS[:, h, :], op=ALU.mult),
                                 reads=pb_s.r() + Bcon.r(), writes=scmb.r())
                            ps_o, pb_o = PSL.next()
                            mm(ps_o[:, 0:128], Vdec[:, nch, :], scm[:, 0:128], True, False, BVdec.r() + scmb.r(), pb_o)
                            for s in range(NSQ):
                                mm(ps_o[:, 8 * s:8 * s + 8], S0b[:, s, :], Qds[:, 8 * s:8 * s + 8], False, s == NSQ - 1,
                                   BS0b.r() + BQds.r(), pb_o)

                            def dst_fn_s(o):
                                return (mix[:, 4 + h, sc0:nt].rearrange("p (s c) -> p s c", c=SW)[:, :, H:SW],
                                        SG[:, sc0:nt].rearrange("p (s c) -> p s c", c=SW)[:, :, H:SW],
                                        o[:, 0:128].rearrange("p (s c) -> p s c", c=TS))
                            gn_block_3d(ps_o, pb_o, 128, None, None, dst_fn_s, sc0, nt, h, Bmix, BSG)
                            S.op("dve", lambda e: e.tensor_tensor(out=Vbd[:, :, :], in0=Vdec[:, nch, :].unsqueeze(1).broadcast_to([128, NSQ, 128]),
                                                                  in1=rcon[:, 8:24].unsqueeze(2).broadcast_to([128, NSQ, 128]), op=ALU.mult),
                                 reads=BVdec.r() + Bcon.r(), writes=BVbd.r())
                            for q in range(4):
                                ps_u, pb_u = PS.next()
                                mm(ps_u[:, :], Kds[:, :], Vbd[:, 4 * q:4 * q + 4, :], True, True, BKds.r() + BVbd.r(), pb_u)
                                sn, snb = TF.next()
                                S.op("dve", lambda e: e.scalar_tensor_tensor(out=sn[:, :], in0=S0f[:, 4 * q:4 * q + 4, :].rearrange("p a b -> p (a b)"),
                                                                             scalar=g8[h], in1=ps_u[:, :], op0=ALU.mult, op1=ALU.add),
                                     reads=pb_u.r() + BS0f.r(), writes=snb.r())
                                S.dma("sp", o_sret[4 * q:4 * q + 4, h, :, :].rearrange("s d v -> d s v"),
                                      sn[:, :].rearrange("p (s v) -> p s v", v=128), reads=snb.r())
                    out_proj(hi, "wo0", mix, Bmix)
                    tail()
                    S.barrier()

        for hi in range(2):
            load_x(hi)
            rmsnorm(hi, "nmix0", None)
            even_mixer(hi, lambda hi=hi: rmsnorm(hi, "nffn0", 0))
            with ExitStack() as shared:
                SH.clear()
                SH["on"] = True
                SH["ph"] = shared
                ffn(hi, 0, lambda hi=hi: rmsnorm(hi, "nmix1", 1))
                lru_mixer(hi, lambda hi=hi: rmsnorm(hi, "nffn1", 2))
                ffn(hi, 1, lambda hi=hi: final_out(hi))
                SH["on"] = False
                S.barrier()
        S.barrier()
        build.marks = S.marks

    return nc


_NC_CACHE = {}


def _core_inputs(inp, params, consts, w_sw, c):
    m = {
        "xp": np.ascontiguousarray(inp["x_prompt"][c]),
        "xs": np.ascontiguousarray(inp["x_sample"][NSQ * c:NSQ * (c + 1)].reshape(NSQ * TS, D)),
        "st_conv_a": np.ascontiguousarray(inp["state_conv_a"][0, NSQ * c:NSQ * (c + 1)].reshape(NSQ * 30, 512)),
        "st_ret": np.ascontiguousarray(inp["state_ret"][0, NSQ * c:NSQ * (c + 1)]),
        "st_lru_conv": np.ascontiguousarray(inp["state_lru_conv"][0, NSQ * c:NSQ * (c + 1)].reshape(NSQ * 3, D)),
        "st_lru_h": np.ascontiguousarray(inp["state_lru_h"][0, NSQ * c:NSQ * (c + 1)]),
        "st_ffn": np.ascontiguousarray(inp["state_ffn_conv"][:, NSQ * c:NSQ * (c + 1)].reshape(2, NSQ * 2, 2 * FF)),
        "w_in_ab": inp["w_in_ab"][0],
        "w_sw": w_sw,
        "w_out_ab": inp["w_out_ab"][0],
        "w_in_c": inp["w_in_c"][0],
        "w_lru_a": inp["w_lru_a"][0],
        "w_lru_x": inp["w_lru_x"][0],
        "w_out_c": inp["w_out_c"][0],
        "w_ffn_up": inp["w_ffn_up"],
        "w_ffn_down": inp["w_ffn_down"],
        "params": params,
    }
    m.update(consts)
    return m


def _prep(inputs):
    inp = {k: np.asarray(v, dtype=np.float32) for k, v in inputs.items()}
    params, npar = _pack_params(inp)
    consts = _host_consts()
    wqk = inp["w_in_ab"][0][:, 1024:2048].reshape(D, 8, 2, 64)
    w_sw = np.ascontiguousarray(wqk[:, :, ::-1, :].reshape(D, 1024))
    return inp, params, npar, consts, w_sw


def _assemble(res, ncores):
    def g(name):
        return [np.asarray(r[name]) for r in res]
    y_p = np.stack(g("y_p"), 0)
    y_s = np.concatenate([a.reshape(NSQ, TS, D) for a in g("y_s")], 0)
    p_ca = np.stack(g("p_conv_a"), 0)[None]
    p_ret = np.stack(g("p_ret"), 0)[None]
    p_lc = np.stack(g("p_lru_conv"), 0)[None]
    p_lh = np.stack([a.reshape(D) for a in g("p_lru_h")], 0)[None]
    p_ffn = np.stack(g("p_ffn"), 1)
    s_ca = np.concatenate([a.reshape(NSQ, 30, 512) for a in g("s_conv_a")], 0)[None]
    s_ret = np.concatenate(g("s_ret"), 0)[None]
    s_lc = np.concatenate([a.reshape(NSQ, 3, D) for a in g("s_lru_conv")], 0)[None]
    s_lh = np.concatenate(g("s_lru_h"), 0)[None]
    s_ffn = np.concatenate([a.reshape(2, NSQ, 2, 2 * FF) for a in g("s_ffn")], 1)
    outs = (y_p, y_s, p_ca, p_ret, p_lc, p_lh, p_ffn, s_ca, s_ret, s_lc, s_lh, s_ffn)
    return tuple(np.ascontiguousarray(o, dtype=np.float32) for o in outs)


def kernel(**inputs):
    inp, params, npar, consts, w_sw = _prep(inputs)
    ncores = 8
    nc = build(npar)
    in_maps = [_core_inputs(inp, params, consts, w_sw, c) for c in range(ncores)]
    res = run_bass_kernel_spmd(nc, in_maps, core_ids=list(range(ncores)))
    return _assemble(res.results, ncores)
```

```python
import math
from contextlib import ExitStack

import numpy as np
import concourse.bass as bass
import concourse.mybir as mybir
from concourse.bass_utils import run_bass_kernel_spmd

F32 = mybir.dt.float32
BF16 = mybir.dt.bfloat16
AF = mybir.ActivationFunctionType
ALU = mybir.AluOpType

D = 1024
KC = 8
SEQ = 2048
NSQ = 16
TS = 8
PAST = 16384
H = 3
SW = H + TS
FF = 2816
NJ = 22
HALVES = [(0, 896, False), (896, 1152, True)]
GROUPS = [(0, 8), (8, 7), (15, 7)]
RMS_EPS = 1e-6
LN_EPS = 1e-5
GN_EPS = 1e-5
NSLOT = 12
PF = 3
NTF = 12
NTB = 8
STAGE = 4
import os as _os
_LAT = float(_os.environ.get('SCHED_LAT', '500'))
_EPS = float(_os.environ.get('SCHED_EPS', '150'))
_TBL_INIT = None


class Reg:
    __slots__ = ("name", "w", "rd", "excl")

    def __init__(self, name, excl=False):
        self.name = name
        self.w = None
        self.rd = {}
        self.excl = excl


class Buf:
    def __init__(self, t, name, ncols=None, G=256, excl=False):
        self.t = t
        self.name = name
        self.G = G
        n = 1 if ncols is None else (ncols + G - 1) // G
        self.regs = [Reg(f"{name}.{i}", excl) for i in range(n)]
        self.ncols = ncols

    def r(self, c0=None, c1=None):
        if self.ncols is None or c0 is None:
            return list(self.regs)
        return self.regs[c0 // self.G:(c1 - 1) // self.G + 1]


class _Rec:
    def __init__(self):
        self.call = None

    def __getattr__(self, name):
        def f(*a, **k):
            self.call = (name, a, k)
            return self
        return f


class _Node:
    __slots__ = ("idx", "eng", "call", "deps", "dur", "lat", "tbl", "tok", "is_dma", "succ", "ndep", "ready")


_ACT_TBL = {}


def _free_elems(ap):
    try:
        sh = ap.shape
        n = 1
        for d in sh[1:]:
            n *= int(d)
        return n
    except Exception:
        return 512


class Sched:
    def __init__(self, nc, ndma=6):
        self.nc = nc
        self.E = {"pe": nc.tensor, "act": nc.scalar, "dve": nc.vector, "pool": nc.gpsimd, "sp": nc.sync}
        self.sem = {}
        self.cnt = {}
        self.waited = {e: {} for e in self.E}
        for e in self.E:
            self.sem[e] = nc.alloc_semaphore(name=f"c_{e}")
            self.cnt[e] = 0
        self.dsem = {}
        for q in ("sp", "pool"):
            self.dsem[q] = [[nc.alloc_semaphore(name=f"d_{q}{i}"), 0] for i in range(ndma)]
        self.drr = {"sp": 0, "pool": 0}
        self.nodes = []
        self.nidx = 0
        self.reorder = True
        self.marks = []

    def _record(self, node, reads, writes):
        deps = {}

        def add(n, raw):
            if n is None:
                return
            if deps.get(n.idx, (None, False))[1] is False:
                deps[n.idx] = (n, raw or deps.get(n.idx, (None, False))[1])

        for r in reads:
            add(r.w, True)
            if r.excl:
                for n in r.rd.values():
                    add(n, False)
        for r in writes:
            add(r.w, r.excl)
            for n in r.rd.values():
                add(n, False)
        node.deps = list(deps.values())
        for r in writes:
            r.w = node
            r.rd = {}
        for r in reads:
            if r.excl:
                r.w = node
                r.rd = {}
            else:
                r.rd[node.idx] = node
        self.nodes.append(node)

    def op(self, e, fn, reads=(), writes=(), dur=None, tbl=None):
        rec = _Rec()
        fn(rec)
        n = _Node()
        n.idx = self.nidx
        self.nidx += 1
        n.eng = e
        n.call = rec.call
        n.is_dma = False
        n.tbl = tbl
        n.tok = None
        if dur is None:
            name, a, k = rec.call
            out = k.get("out", a[0] if a else None)
            fe = _free_elems(out) if out is not None else 512
            if e == "pe":
                dur = 10.0 + fe / 2.4
            elif e == "act":
                dur = 280.0 + fe / 1.1
                if name == "activation":
                    f = k.get("func")
                    tbl = _ACT_TBL.get(f, None)
                    n.tbl = tbl
            elif e == "dve":
                dur = 180.0 + fe / 0.9
                if name == "reciprocal":
                    dur = 120.0 + 4 * fe / 0.96
                elif name in ("memset",):
                    dur = 100.0 + fe / 3.0
            else:
                dur = 300.0
        n.dur = dur
        n.lat = 60.0 if e == "pe" else _LAT
        self._record(n, reads, writes)

    def dma(self, q, out, in_, reads=(), writes=(), nbytes=None):
        n = _Node()
        n.idx = self.nidx
        self.nidx += 1
        n.eng = q
        n.call = (out, in_)
        n.is_dma = True
        n.tbl = None
        n.tok = None
        if nbytes is None:
            try:
                nb = 1
                for d in out.shape:
                    nb *= int(d)
                nbytes = nb * 4
            except Exception:
                nbytes = 65536
        n.dur = 1000.0 if q == "pool" else 150.0
        n.lat = 2000.0 + nbytes / 150.0
        self._record(n, reads, writes)

    def _wait(self, e, key, sem, val):
        if self.waited[e].get(key, 0) >= val:
            return
        self.E[e].wait_ge(sem, val)
        self.waited[e][key] = val

    def _emit(self, n):
        e = n.eng
        toks = {}
        for (d, raw) in n.deps:
            key, sem, val = d.tok
            if key == e:
                if e in ("pe", "pool", "sp"):
                    continue
                if not raw:
                    continue
                if val <= self.cnt[e] - 2:
                    continue
            if toks.get(key, (None, 0))[1] < val:
                toks[key] = (sem, val)
        if n.is_dma:
            i = self.drr[e]
            self.drr[e] = (i + 1) % len(self.dsem[e])
            ent = self.dsem[e][i]
            key = f"d_{e}{i}"
            if ent[1] > 0:
                self._wait(e, key, ent[0], 16 * ent[1])
            for k2, (sem, val) in toks.items():
                self._wait(e, k2, sem, val)
            out, in_ = n.call
            self.E[e].dma_start(out=out, in_=in_).then_inc(ent[0], 16)
            ent[1] += 1
            n.tok = (key, ent[0], 16 * ent[1])
        else:
            for k2, (sem, val) in toks.items():
                self._wait(e, k2, sem, val)
            name, a, k = n.call
            ins = getattr(self.E[e], name)(*a, **k)
            self.cnt[e] += 1
            ins.then_inc(self.sem[e], 1)
            n.tok = (e, self.sem[e], self.cnt[e])
        n.deps = None
        n.call = None

    def flush(self):
        nodes = self.nodes
        self.nodes = []
        if not nodes:
            return
        if not self.reorder:
            for n in nodes:
                self._emit(n)
            return
        inwin = {n.idx: n for n in nodes}
        for n in nodes:
            n.succ = []
            n.ndep = 0
            n.ready = 0.0
        for n in nodes:
            for (d, raw) in n.deps:
                if d.idx in inwin:
                    d.succ.append(n)
                    n.ndep += 1
        bl = {}
        for n in reversed(nodes):
            m = 0.0
            for s_ in n.succ:
                v = n.lat + bl[s_.idx]
                if v > m:
                    m = v
            bl[n.idx] = n.dur + m
        free = {e: 0.0 for e in self.E}
        last_tbl = None
        ready = {e: [] for e in self.E}
        for n in nodes:
            if n.ndep == 0:
                ready[n.eng].append(n)
        left = len(nodes)
        EPS = _EPS
        while left:
            best = None
            for e, lst in ready.items():
                if not lst:
                    continue
                mr = min(x.ready for x in lst)
                t_e = max(free[e], mr)
                if best is None or t_e < best[0]:
                    best = (t_e, e)
            t_e, e = best
            lst = ready[e]
            cands = [x for x in lst if x.ready <= t_e + EPS]
            if e in ("sp", "pool"):
                n = min(cands, key=lambda x: x.idx)
            else:
                if e == "act" and last_tbl is not None:
                    same = [x for x in lst if x.ready <= t_e + 1500.0 and (x.tbl is None or x.tbl == last_tbl)]
                    if same:
                        cands = same
                n = max(cands, key=lambda x: (bl[x.idx], -x.idx))
            lst.remove(n)
            st = max(free[e], n.ready)
            if e == "act" and n.tbl is not None:
                if last_tbl is not None and n.tbl != last_tbl:
                    st += 1300.0
                last_tbl = n.tbl
            end = st + n.dur
            free[e] = end
            fin_t = end + n.lat
            self._emit(n)
            left -= 1
            for s_ in n.succ:
                if fin_t > s_.ready:
                    s_.ready = fin_t
                s_.ndep -= 1
                if s_.ndep == 0:
                    ready[s_.eng].append(s_)
            n.succ = None

    def barrier(self):
        self.flush()
        self.marks.append(dict(self.cnt))
        for e in self.E:
            for e2 in ("pe", "act", "dve", "pool"):
                if e2 != e and self.cnt[e2] > 0:
                    self._wait(e, e2, self.sem[e2], self.cnt[e2])
            for q in ("sp", "pool"):
                for i, ent in enumerate(self.dsem[q]):
                    if ent[1] > 0:
                        self._wait(e, f"d_{q}{i}", ent[0], 16 * ent[1])


class Pool:
    def __init__(self, es, nc, name, n, shape, dt, excl=False, space="sbuf"):
        self.items = []
        for i in range(n):
            if space == "sbuf":
                t = es.enter_context(nc.sbuf_tensor(f"{name}{i}", shape, dt))
            else:
                t = es.enter_context(nc.psum_tensor(f"{name}{i}", shape, dt))
            self.items.append((t, Buf(t, f"{name}{i}", excl=excl)))
        self.i = 0

    def next(self):
        it = self.items[self.i]
        self.i = (self.i + 1) % len(self.items)
        return it


def _init_tbl():
    _ACT_TBL.update({AF.Gelu_apprx_tanh: "gelu", AF.Sigmoid: "sig", AF.Silu: "silu", AF.Exp: "exp", AF.Ln: "exp",
                     AF.Sqrt: "sqrt", AF.Tanh: "exp"})


def _gammas():
    lg = np.log(np.float32(1.0) - np.float32(2.0) ** (-5.0 - np.arange(4, dtype=np.float32))).astype(np.float32)
    return lg


def _half_geom(hi):
    t0, npr, has_s = HALVES[hi]
    sc0 = H + npr
    nt = sc0 + (NSQ * SW if has_s else 0)
    return t0, npr, has_s, sc0, nt


def _blocks0(hi):
    t0, npr, has_s, sc0, nt = _half_geom(hi)
    out = []
    c = H
    while c < nt:
        l = min(512, nt - c)
        out.append((c, l))
        c += l
    return out


def _blocks3(hi):
    t0, npr, has_s, sc0, nt = _half_geom(hi)
    out = []
    c = 0
    while True:
        l = min(512, nt - c)
        out.append((c, l))
        if c + l >= nt:
            break
        c += l - H
    return out


def _host_consts():
    c = {}
    c["ident"] = np.eye(128, dtype=np.float32)
    lg = _gammas()
    inv_freq = (np.float32(10000.0) ** (-np.arange(0, 128, 2, dtype=np.float32) / np.float32(128))).astype(np.float32)
    for hi in range(2):
        t0, npr, has_s, sc0, nt = _half_geom(hi)
        pos = np.zeros(nt, np.float32)
        valid = np.zeros(nt, bool)
        pos[H:H + npr] = np.arange(t0, t0 + npr, dtype=np.float32)
        valid[H:H + npr] = True
        if has_s:
            for s in range(NSQ):
                b = sc0 + SW * s + H
                pos[b:b + TS] = np.arange(PAST, PAST + TS, dtype=np.float32)
                valid[b:b + TS] = True
        ang = (pos[:, None] * inv_freq[None, :]).astype(np.float32)
        cs = np.cos(ang).astype(np.float32)
        sn = np.sin(ang).astype(np.float32)
        cosT = np.concatenate([cs.T, cs.T], axis=0)
        sinT = np.concatenate([-sn.T, sn.T], axis=0)
        cosT[:, ~valid] = 0
        sinT[:, ~valid] = 0
        c[f"rope{hi}"] = np.ascontiguousarray(np.stack([cosT, sinT], axis=1).astype(np.float32))
    idx = np.arange(128, dtype=np.float32)
    maskP = np.zeros((128, 4, 128), np.float32)
    qdP = np.zeros((128, 4, 128), np.float32)
    kdecP = np.zeros((128, 4), np.float32)
    maskS = np.zeros((128, 4, 128), np.float32)
    qd8 = np.zeros((128, 4, 128), np.float32)
    kdec8 = np.zeros((128, 4), np.float32)
    causal = (idx[:, None] <= idx[None, :]).astype(np.float32)
    sj = np.arange(128) // 8
    jj = (np.arange(128) % 8).astype(np.float32)
    for h in range(4):
        maskP[:, h, :] = causal * np.exp(np.float32(-128.0) * lg[h]).astype(np.float32)
        qdP[:, h, :] = np.exp((idx + 1.0) * lg[h])[None, :]
        kdecP[:, h] = np.exp((127.0 - idx) * lg[h])
        rel = jj[None, :] - jj[:, None]
        m = np.where((sj[:, None] == sj[None, :]) & (rel >= 0), np.exp(np.maximum(rel, 0) * lg[h]), 0.0)
        maskS[:, h, :] = m
        qd8[:, h, :] = np.exp((jj + 1.0) * lg[h])[None, :]
        kdec8[:, h] = np.exp((7.0 - jj) * lg[h])
    c["maskP"] = maskP
    c["qdP"] = qdP
    c["maskS"] = maskS.astype(np.float32)
    c["qd8"] = qd8
    rc = np.zeros((128, 32), np.float32)
    rc[:, 0:4] = kdecP
    rc[:, 4:8] = kdec8
    rc[:, 8:24] = (sj[:, None] == np.arange(16)[None, :]).astype(np.float32)
    c["rcon"] = rc
    t0, npr, has_s, sc0, nt = _half_geom(1)
    b3 = _blocks3(1)
    lc0, ll = b3[-1]
    sel2 = np.zeros((32, ll), np.float32)
    sel3 = np.zeros((48, ll), np.float32)
    for s in range(NSQ):
        for r in range(2):
            sel2[2 * s + r, sc0 + SW * s + 1 + r - lc0] = 1
        for r in range(3):
            sel3[3 * s + r, sc0 + SW * s + r - lc0] = 1
    c["sel2"] = sel2
    c["sel3"] = sel3
    return c


def _fm(v):
    return np.ascontiguousarray(v.reshape(-1, 128).T)


PAR = {}


def _pack_params(inp):
    cols = []
    off = 0

    def add(name, arr):
        nonlocal off
        arr = np.ascontiguousarray(arr, dtype=np.float32).reshape(128, -1)
        PAR[name] = (off, arr.shape[1])
        cols.append(arr)
        off += arr.shape[1]

    for l in range(2):
        add(f"nmix{l}", _fm(inp["norm_mix"][l]))
        add(f"nffn{l}", _fm(inp["norm_ffn"][l]))
    add("nfin", _fm(inp["norm_final"]))
    add("cab", _fm(inp["conv_a_b"][0]))
    add("lng", _fm(inp["ln_a_g"][0]))
    add("lnb", _fm(inp["ln_a_b"][0]))
    add("gng", _fm(inp["gn_ret_g"][0]))
    add("caw", inp["conv_a_w"][0].reshape(31, 4, 128).transpose(2, 1, 0))
    add("ccw", inp["conv_c_w"][0].reshape(4, 8, 128).transpose(2, 1, 0))
    add("ccb", _fm(inp["conv_c_b"][0]))
    add("ba", _fm(inp["b_lru_a"][0]))
    add("bx", _fm(inp["b_lru_x"][0]))
    add("lam", _fm(inp["lru_lambda"][0]))
    for l in range(2):
        add(f"fcw{l}", inp["ffn_conv_w"][l].reshape(3, 44, 128).transpose(2, 1, 0))
        add(f"fcb{l}", _fm(inp["ffn_conv_b"][l]))
    return np.ascontiguousarray(np.concatenate(cols, axis=1)), off


def build(npar):
    nc = bass.Bass("TRN2", target_bir_lowering=False)

    def din(name, shape):
        return nc.dram_tensor(name, list(shape), F32, kind="ExternalInput").ap()

    def dout(name, shape):
        return nc.dram_tensor(name, list(shape), F32, kind="ExternalOutput").ap()

    xp_d = din("xp", (SEQ, D))
    xs_d = din("xs", (NSQ * TS, D))
    sca_d = din("st_conv_a", (NSQ * 30, 512))
    sret_d = din("st_ret", (NSQ, 4, 128, 128))
    slc_d = din("st_lru_conv", (NSQ * 3, D))
    slh_d = din("st_lru_h", (NSQ, D))
    sffn_d = din("st_ffn", (2, NSQ * 2, 2 * FF))
    w_in_ab = din("w_in_ab", (D, 3072))
    w_sw = din("w_sw", (D, 1024))
    w_out_ab = din("w_out_ab", (D, D))
    w_in_c = din("w_in_c", (D, 2048))
    w_lru_a = din("w_lru_a", (8, 128, 128))
    w_lru_x = din("w_lru_x", (8, 128, 128))
    w_out_c = din("w_out_c", (D, D))
    w_up = din("w_ffn_up", (2, D, 2 * FF))
    w_dn = din("w_ffn_down", (2, FF, D))
    par_d = din("params", (128, npar))
    ident_d = din("ident", (128, 128))
    rope_d = [din(f"rope{hi}", (128, 2, _half_geom(hi)[4])) for hi in range(2)]
    maskP_d = din("maskP", (128, 4, 128))
    qdP_d = din("qdP", (128, 4, 128))
    maskS_d = din("maskS", (128, 4, 128))
    qd8_d = din("qd8", (128, 4, 128))
    rcon_d = din("rcon", (128, 32))
    b3_1 = _blocks3(1)
    sel2_d = din("sel2", (32, b3_1[-1][1]))
    sel3_d = din("sel3", (48, b3_1[-1][1]))

    yp_d = dout("y_p", (SEQ, D))
    ys_d = dout("y_s", (NSQ * TS, D))
    o_pca = dout("p_conv_a", (30, 512))
    o_pret = dout("p_ret", (4, 128, 128))
    o_plc = dout("p_lru_conv", (3, D))
    o_plh = dout("p_lru_h", (1, D))
    o_pffn = dout("p_ffn", (2, 2, 2 * FF))
    o_sca = dout("s_conv_a", (NSQ * 30, 512))
    o_sret = dout("s_ret", (NSQ, 4, 128, 128))
    o_slc = dout("s_lru_conv", (NSQ * 3, D))
    o_slh = dout("s_lru_h", (NSQ, D))
    o_sffn = dout("s_ffn", (2, NSQ * 2, 2 * FF))

    lg = _gammas()
    gC = [float(np.exp(np.float32(128.0) * lg[h])) for h in range(4)]
    g8 = [float(np.exp(np.float32(8.0) * lg[h])) for h in range(4)]

    _init_tbl()
    S = Sched(nc)
    NTMAX = _half_geom(1)[4]

    with ExitStack() as es:
        def sb(name, shape, dt=F32):
            return es.enter_context(nc.sbuf_tensor("sb_" + name, list(shape), dt))

        xT = sb("xT", (128, KC, NTMAX))
        xn = sb("xn", (128, KC, NTMAX), BF16)
        BxT = Buf(xT, "xT", NTMAX)
        Bxn = Buf(xn, "xn", NTMAX)
        par = sb("par", (128, npar))
        Bpar = Buf(par, "par")
        identf = sb("identf", (128, 128))
        identb = sb("identb", (128, 128), BF16)
        Bid = Buf(identf, "identf")
        Bidb = Buf(identb, "identb")
        ones = sb("ones", (128, 3, 128), BF16)
        Bones = Buf(ones, "ones")
        maskP = sb("maskP", (128, 4, 128))
        qdP = sb("qdP", (128, 4, 128))
        maskS = sb("maskS", (128, 4, 128))
        qd8 = sb("qd8", (128, 4, 128))
        rcon = sb("rcon", (128, 32))
        Bcon = Buf(maskP, "retconst")
        sel2 = sb("sel2", (32, b3_1[-1][1]), BF16)
        sel3 = sb("sel3", (48, b3_1[-1][1]), BF16)
        Bsel = Buf(sel2, "sel")
        lruc = sb("lruc", (128, 8, 2))
        hbias = sb("hbias", (128, 2, 8))
        Blruc = Buf(lruc, "lruc")
        cxn = sb("cxn", (128, 3, KC, H), BF16)
        Bcxn = Buf(cxn, "cxn")
        cu = sb("cu", (128, 4, 30), BF16)
        Bcu = Buf(cu, "cu")
        cS = sb("cS", (128, 4, 128))
        BcS = Buf(cS, "cS")
        ch = sb("ch", (128, 8))
        Bch = Buf(ch, "ch")

        slots = Pool(es, nc, "wsl", NSLOT, (128, 1024), BF16)
        TF = Pool(es, nc, "tf", NTF, (128, 512), F32)
        TB = Pool(es, nc, "tb", NTB, (128, 512), BF16)
        PS = Pool(es, nc, "ps", 6, (128, 512), F32, excl=True, space="psum")
        PSL = Pool(es, nc, "psl", 2, (128, 512), F32, excl=True, space="psum")
        _ps6 = list(PS.items)
        _ps8 = list(PS.items) + list(PSL.items)

        def ps_mode(n):
            PS.items = _ps8 if n == 8 else _ps6
            PS.i = 0

        def P(name, k=None):
            o, n = PAR[name]
            if k is None:
                return par[:, o:o + n]
            return par[:, o + k:o + k + 1]

        S.dma("sp", par[:], par_d, writes=Bpar.r())
        S.dma("sp", identf[:], ident_d, writes=Bid.r())
        S.dma("pool", identb[:], ident_d, writes=Bidb.r())
        S.dma("sp", maskP[:], maskP_d, writes=Bcon.r())
        S.dma("sp", qdP[:], qdP_d, writes=Bcon.r())
        S.dma("sp", maskS[:], maskS_d, writes=Bcon.r())
        S.dma("sp", qd8[:], qd8_d, writes=Bcon.r())
        S.dma("sp", rcon[:], rcon_d, writes=Bcon.r())
        S.dma("pool", sel2[:], sel2_d, writes=Bsel.r())
        S.dma("pool", sel3[:], sel3_d, writes=Bsel.r())
        S.op("dve", lambda e: e.memset(ones[:, 0, :], 1.0 / 1024), writes=Bones.r())
        S.op("dve", lambda e: e.memset(ones[:, 1, :], 1.0 / 512), writes=Bones.r())
        S.op("dve", lambda e: e.memset(ones[:, 2, :], 1.0 / 128), writes=Bones.r())
        S.op("act", lambda e: e.activation(out=lruc[:, :, 0], in_=P("lam"), func=AF.Exp, scale=-1.0),
             reads=Bpar.r(), writes=Blruc.r())
        S.op("act", lambda e: e.activation(out=lruc[:, :, 0], in_=lruc[:, :, 0], func=AF.Ln, bias=1.0, scale=1.0),
             reads=Blruc.r(), writes=Blruc.r())
        S.op("dve", lambda e: e.tensor_scalar(out=lruc[:, :, 1], in0=lruc[:, :, 0], scalar1=-8.0, scalar2=None, op0=ALU.mult),
             reads=Blruc.r(), writes=Blruc.r())
        S.op("dve", lambda e: e.tensor_scalar(out=lruc[:, :, 0], in0=lruc[:, :, 0], scalar1=-4.0, scalar2=None, op0=ALU.mult),
             reads=Blruc.r(), writes=Blruc.r())
        S.op("dve", lambda e: e.tensor_scalar(out=hbias[:, 0, :], in0=P("ba"), scalar1=0.5, scalar2=None, op0=ALU.mult),
             reads=Bpar.r(), writes=Blruc.r())
        S.op("dve", lambda e: e.tensor_scalar(out=hbias[:, 1, :], in0=P("bx"), scalar1=0.5, scalar2=None, op0=ALU.mult),
             reads=Bpar.r(), writes=Blruc.r())

        def wap_cols(w2d, c0, ncols=128):
            return w2d.rearrange("(k p) c -> p k c", p=128)[:, :, c0:c0 + ncols]

        def weight_plan():
            for hi in range(2):
                if STAGE < 1:
                    continue
                for c in range(4):
                    yield ("alin", c), wap_cols(w_in_ab, 128 * c)
                    yield ("agate", c), wap_cols(w_in_ab, 512 + 128 * c)
                for h in range(4):
                    yield ("q", h), wap_cols(w_in_ab, 1024 + 128 * h)
                    yield ("qs", h), wap_cols(w_sw, 128 * h)
                    yield ("k", h), wap_cols(w_in_ab, 1536 + 128 * h)
                    yield ("ks", h), wap_cols(w_sw, 512 + 128 * h)
                    yield ("v", h), wap_cols(w_in_ab, 2048 + 128 * h)
                    yield ("g", h), wap_cols(w_in_ab, 2560 + 128 * h)
                for o in range(8):
                    yield ("wo0", o), wap_cols(w_out_ab, 128 * o)
                for l in range(2):
                    if l == 0 and STAGE < 2:
                        continue
                    if l == 1 and STAGE < 3:
                        continue
                    if l == 1:
                        for n in range(8):
                            yield ("gate", n), wap_cols(w_in_c, 128 * n)
                            yield ("rec", n), wap_cols(w_in_c, 1024 + 128 * n)
                        for o in range(8):
                            yield ("wo1", o), wap_cols(w_out_c, 128 * o)
                        if STAGE < 4:
                            continue
                    for (j0, jn) in GROUPS:
                        for j in range(j0, j0 + jn):
                            yield ("upg", l, j), wap_cols(w_up[l], 128 * j)
                            yield ("upu", l, j), wap_cols(w_up[l], FF + 128 * j)
                        for j in range(j0, j0 + jn):
                            yield ("wd", l, j), w_dn[l][128 * j:128 * j + 128, :]

        plan = list(weight_plan())
        wstate = {"issued": 0, "used": 0}
        wslot_of = {}

        def w_issue(upto):
            while wstate["issued"] < min(upto, len(plan)):
                i = wstate["issued"]
                name, ap = plan[i]
                t, b = slots.items[i % NSLOT]
                if len(ap.shape) == 3:
                    dst = t[:].rearrange("p (k c) -> p k c", c=128)
                else:
                    dst = t[:]
                S.dma("pool", dst, ap, writes=b.r())
                wslot_of[i] = (t, b)
                wstate["issued"] += 1

        def w_next(name):
            i = wstate["used"]
            assert plan[i][0] == name, (plan[i][0], name)
            w_issue(i + 1 + PF)
            wstate["used"] += 1
            t, b = wslot_of[i]
            return i, t, b

        def w_check(i):
            assert wstate["issued"] <= i + NSLOT, ("weight evicted", plan[i][0])

        def mm(ps, lhsT, rhs, start, stop, reads, pb):
            S.op("pe", lambda e: e.matmul(ps, lhsT, rhs, start=start, stop=stop), reads=reads, writes=pb.r())

        def proj_block(ps, pb, wi, wt, wb, rhs_buf, rhs_B, c0, ln, nk=KC, last_stop=True):
            w_check(wi)
            w3 = wt[:].rearrange("p (k c) -> p k c", c=128)
            for k in range(nk):
                mm(ps[:, 0:ln], w3[:, k, :], rhs_buf[:, k, c0:c0 + ln], k == 0, (k == nk - 1) and last_stop,
                   wb.r() + rhs_B.r(c0, c0 + ln), pb)

        def rmsnorm(hi, gname, phase_idx):
            t0, npr, has_s, sc0, nt = _half_geom(hi)
            for (c0, ln) in _blocks0(hi):
                ps, pb = PS.next()
                for k in range(KC):
                    sq, sqb = TB.next()
                    if k % 3 != 2:
                        S.op("act", lambda e: e.activation(out=sq[:, 0:ln], in_=xT[:, k, c0:c0 + ln], func=AF.Square),
                             reads=BxT.r(c0, c0 + ln), writes=sqb.r())
                    else:
                        S.op("dve", lambda e: e.tensor_tensor(out=sq[:, 0:ln], in0=xT[:, k, c0:c0 + ln], in1=xT[:, k, c0:c0 + ln], op=ALU.mult),
                             reads=BxT.r(c0, c0 + ln), writes=sqb.r())
                    mm(ps[:, 0:ln], ones[:, 0, :], sq[:, 0:ln], k == 0, k == KC - 1, sqb.r() + Bones.r(), pb)
                sd, sdb = TF.next()
                S.op("act", lambda e: e.activation(out=sd[:, 0:ln], in_=ps[:, 0:ln], func=AF.Ln, bias=RMS_EPS, scale=1.0),
                     reads=pb.r(), writes=sdb.r())
                rs, rsb = TF.next()
                S.op("act", lambda e: e.activation(out=rs[:, 0:ln], in_=sd[:, 0:ln], func=AF.Exp, scale=-0.5), reads=sdb.r(), writes=rsb.r())
                for k in range(KC):
                    S.op("dve", lambda e: e.scalar_tensor_tensor(out=xn[:, k, c0:c0 + ln], in0=xT[:, k, c0:c0 + ln],
                                                                 scalar=P(gname, k), in1=rs[:, 0:ln],
                                                                 op0=ALU.mult, op1=ALU.mult),
                         reads=BxT.r(c0, c0 + ln) + rsb.r() + Bpar.r(), writes=Bxn.r(c0, c0 + ln))
            if has_s:
                for k in range(KC):
                    v = xn[:, k, sc0:nt].rearrange("p (s c) -> p s c", c=SW)[:, :, 0:H]
                    S.op("dve", lambda e: e.memset(v, 0.0), writes=Bxn.r(sc0, nt))
            if phase_idx is not None:
                if hi == 0:
                    S.op("dve", lambda e: e.tensor_copy(out=cxn[:, phase_idx, :, :], in_=xn[:, :, nt - H:nt]),
                         reads=Bxn.r(nt - H, nt), writes=Bcxn.r())
                else:
                    S.op("dve", lambda e: e.tensor_copy(out=xn[:, :, 0:H], in_=cxn[:, phase_idx, :, :]),
                         reads=Bcxn.r(), writes=Bxn.r(0, H))
            elif hi == 0 or True:
                S.op("dve", lambda e: e.memset(xn[:, :, 0:H], 0.0), writes=Bxn.r(0, H))

        def out_proj(hi, wname, mix, Bmix):
            ws = [w_next((wname, o)) for o in range(8)]
            for (c0, ln) in _blocks0(hi):
                for o in range(8):
                    wi, wt, wb = ws[o]
                    ps, pb = PS.next()
                    proj_block(ps, pb, wi, wt, wb, mix, Bmix, c0, ln)
                    S.op("dve", lambda e: e.tensor_tensor(out=xT[:, o, c0:c0 + ln], in0=xT[:, o, c0:c0 + ln],
                                                          in1=ps[:, 0:ln], op=ALU.add),
                         reads=pb.r() + BxT.r(c0, c0 + ln), writes=BxT.r(c0, c0 + ln))

        def transpose_out(src_aps, nrows, dsts):
            ps, pb = PS.next()
            n = len(src_aps)
            for i, (ap, rr) in enumerate(src_aps):
                S.op("pe", lambda e: e.transpose(ps[0:nrows, 128 * i:128 * i + 128], ap, identf[:]),
                     reads=rr + Bid.r(), writes=pb.r())
            st, stb = TF.next()
            S.op("act", lambda e: e.activation(out=st[0:nrows, 0:128 * n], in_=ps[0:nrows, 0:128 * n], func=AF.Copy),
                 reads=pb.r(), writes=stb.r())
            for (dap, r0, r1) in dsts:
                S.dma("sp", dap, st[r0:r1, 0:128 * n], reads=stb.r())

        def load_x(hi):
            t0, npr, has_s, sc0, nt = _half_geom(hi)
            S.op("dve", lambda e: e.memset(xT[:, :, 0:H], 0.0), writes=BxT.r(0, H))
            if has_s:
                for k in range(KC):
                    v = xT[:, k, sc0:nt].rearrange("p (s c) -> p s c", c=SW)[:, :, 0:H]
                    S.op("dve", lambda e: e.memset(v, 0.0), writes=BxT.r(sc0, nt))
            ntile = npr // 128 + (1 if has_s else 0)
            for ti in range(ntile):
                xi, xib = TF.next()
                xi2, xib2 = TF.next()
                is_s = ti == npr // 128
                src = xs_d if is_s else xp_d[t0 + 128 * ti:t0 + 128 * ti + 128, :]
                S.dma("sp", xi[:], src[:, 0:512], writes=xib.r())
                S.dma("sp", xi2[:], src[:, 512:1024], writes=xib2.r())
                for half, (xt_, xb_) in enumerate(((xi, xib), (xi2, xib2))):
                    ps, pb = PS.next()
                    for q in range(4):
                        S.op("pe", lambda e: e.transpose(ps[:, 128 * q:128 * q + 128], xt_[:, 128 * q:128 * q + 128], identf[:]),
                             reads=xb_.r() + Bid.r(), writes=pb.r())
                    eng = "act" if half == 0 else "dve"
                    if not is_s:
                        c0 = H + 128 * ti
                        dst = xT[:, 4 * half:4 * half + 4, c0:c0 + 128]
                        src_ps = ps[:, :].rearrange("p (k c) -> p k c", c=128)
                        if eng == "act":
                            S.op("act", lambda e: e.activation(out=dst, in_=src_ps, func=AF.Copy), reads=pb.r(), writes=BxT.r(c0, c0 + 128))
                        else:
                            S.op("dve", lambda e: e.tensor_copy(out=dst, in_=src_ps), reads=pb.r(), writes=BxT.r(c0, c0 + 128))
                    else:
                        for q in range(4):
                            k = 4 * half + q
                            dst = xT[:, k, sc0:nt].rearrange("p (s c) -> p s c", c=SW)[:, :, H:SW]
                            src_ps = ps[:, 128 * q:128 * q + 128].rearrange("p (s c) -> p s c", c=TS)
                            if eng == "act":
                                S.op("act", lambda e: e.activation(out=dst, in_=src_ps, func=AF.Copy), reads=pb.r(), writes=BxT.r(sc0, nt))
                            else:
                                S.op("dve", lambda e: e.tensor_copy(out=dst, in_=src_ps), reads=pb.r(), writes=BxT.r(sc0, nt))

        def final_out(hi):
            t0, npr, has_s, sc0, nt = _half_geom(hi)
            ntile = npr // 128 + (1 if has_s else 0)
            for ti in range(ntile):
                is_s = ti == npr // 128
                if not is_s:
                    c0, c1 = H + 128 * ti, H + 128 * ti + 128

                    def cols(t3, k):
                        return t3[:, k, c0:c1]
                else:
                    c0, c1 = sc0, nt

                    def cols(t3, k):
                        return t3[:, k, sc0:nt].rearrange("p (s c) -> p s c", c=SW)[:, :, H:SW]
                ps, pb = PS.next()
                for k in range(KC):
                    sq, sqb = TB.next()
                    sqv = sq[:, 0:128] if not is_s else sq[:, 0:128].rearrange("p (s c) -> p s c", c=TS)
                    S.op("act", lambda e: e.activation(out=sqv, in_=cols(xT, k), func=AF.Square),
                         reads=BxT.r(c0, c1), writes=sqb.r())
                    mm(ps[:, 0:128], ones[:, 0, :], sq[:, 0:128], k == 0, k == KC - 1, sqb.r() + Bones.r(), pb)
                sd, sdb = TF.next()
                S.op("act", lambda e: e.activation(out=sd[:, 0:128], in_=ps[:, 0:128], func=AF.Ln, bias=RMS_EPS, scale=1.0),
                     reads=pb.r(), writes=sdb.r())
                rs, rsb = TF.next()
                S.op("act", lambda e: e.activation(out=rs[:, 0:128], in_=sd[:, 0:128], func=AF.Exp, scale=-0.5), reads=sdb.r(), writes=rsb.r())
                rsv = rs[:, 0:128] if not is_s else rs[:, 0:128].rearrange("p (s c) -> p s c", c=TS)
                ya, yab = TF.next()
                yb_, ybb = TF.next()
                for k in range(KC):
                    yt = ya if k < 4 else yb_
                    ytb = yab if k < 4 else ybb
                    o = yt[:, 128 * (k % 4):128 * (k % 4) + 128]
                    if is_s:
                        o = o.rearrange("p (s c) -> p s c", c=TS)
                    S.op("dve", lambda e: e.scalar_tensor_tensor(out=o, in0=cols(xT, k), scalar=P("nfin", k), in1=rsv,
                                                                 op0=ALU.mult, op1=ALU.mult),
                         reads=BxT.r(c0, c1) + rsb.r() + Bpar.r(), writes=ytb.r())
                dst_rows = ys_d if is_s else yp_d[t0 + 128 * ti:t0 + 128 * ti + 128, :]
                for half, (yt, ytb) in enumerate(((ya, yab), (yb_, ybb))):
                    ps2, pb2 = PS.next()
                    for q in range(4):
                        S.op("pe", lambda e: e.transpose(ps2[:, 128 * q:128 * q + 128], yt[:, 128 * q:128 * q + 128], identf[:]),
                             reads=ytb.r() + Bid.r(), writes=pb2.r())
                    st, stb = TF.next()
                    if half == 0:
                        S.op("act", lambda e: e.activation(out=st[:, :], in_=ps2[:, :], func=AF.Copy), reads=pb2.r(), writes=stb.r())
                    else:
                        S.op("dve", lambda e: e.tensor_copy(out=st[:, :], in_=ps2[:, :]), reads=pb2.r(), writes=stb.r())
                    S.dma("sp", dst_rows[:, 512 * half:512 * half + 512], st[:, :], reads=stb.r())

        SH = {}

        def shget(ph, key, make):
            if SH.get("on"):
                if key not in SH:
                    SH[key] = make(SH["ph"])
                return SH[key]
            return make(ph)

        def mk_buf(p, name, shape, dt, ncols=None):
            t = p.enter_context(nc.sbuf_tensor(name, list(shape), dt))
            return t, Buf(t, name, ncols)

        def ffn(hi, l, tail):
            t0, npr, has_s, sc0, nt = _half_geom(hi)
            b3 = _blocks3(hi)
            ps_mode(8)
            with ExitStack() as ph_:
                ph = SH["ph"] if SH.get("on") else ph_
                jmax = max(g[1] for g in GROUPS)
                assert jmax == KC
                tag = "" if SH.get("on") else f"_{l}"
                act, Bact = shget(ph, ("mixact", hi), lambda p: mk_buf(p, f"mixact_{hi}{tag}", [128, jmax, nt], BF16, nt))
                FLT = shget(ph, ("LT", hi), lambda p: Pool(p, nc, f"lt{hi}_", 15, (128, 512), F32)) if SH.get("on") else None
                if has_s:
                    stt, Bstt = shget(ph, ("stt", hi), lambda p: mk_buf(p, f"stt_{hi}{tag}", [32, 2, jmax * 128], BF16))
                    zt, Bzt = shget(ph, ("zt", hi), lambda p: mk_buf(p, f"zt_{hi}{tag}", [128, 2, jmax, 34], F32))
                for (j0, jn) in GROUPS:
                    if has_s:
                        S.dma("pool", stt[:, 0, 0:jn * 128], sffn_d[l][:, 128 * j0:128 * (j0 + jn)], writes=Bstt.r())
                        S.dma("pool", stt[:, 1, 0:jn * 128], sffn_d[l][:, FF + 128 * j0:FF + 128 * (j0 + jn)], writes=Bstt.r())
                    for j in range(j0, j0 + jn):
                        jl = j - j0
                        zc = {}
                        for which, wname in ((0, "upg"), (1, "upu")):
                            wi, wt, wb = w_next((wname, l, j))
                            fch = j + NJ * which
                            o_w, _ = PAR[f"fcw{l}"]
                            o_b, _ = PAR[f"fcb{l}"]
                            wcol = lambda tap: par[:, o_w + 3 * fch + tap:o_w + 3 * fch + tap + 1]
                            bcol = par[:, o_b + fch:o_b + fch + 1]
                            zc[which] = []
                            for bi, (c0, ln) in enumerate(b3):
                                last = bi == len(b3) - 1
                                ps, pb = PS.next()
                                inj = has_s and last
                                proj_block(ps, pb, wi, wt, wb, xn, Bxn, c0, ln, last_stop=not inj)
                                if inj:
                                    mm(ps[:, 0:ln], stt[:, which, 128 * jl:128 * jl + 128], sel2[:, 0:ln], False, True,
                                       Bstt.r() + Bsel.r(), pb)
                                acc, accb = TF.next()
                                lo = ln - H
                                S.op("act", lambda e: e.activation(out=acc[:, 0:lo], in_=ps[:, H:ln], func=AF.Identity,
                                                                   scale=wcol(2), bias=bcol),
                                     reads=pb.r() + Bpar.r(), writes=accb.r())
                                if which == 0 and FLT is not None:
                                    tb_, tbb_ = FLT.next()
                                    S.op("act", lambda e: e.activation(out=tb_[:, 0:lo], in_=ps[:, H - 1:ln - 1], func=AF.Copy, scale=wcol(1)),
                                         reads=pb.r() + Bpar.r(), writes=tbb_.r())
                                    S.op("dve", lambda e: e.scalar_tensor_tensor(out=acc[:, 0:lo], in0=ps[:, H - 2:ln - 2], scalar=wcol(0),
                                                                                 in1=acc[:, 0:lo], op0=ALU.mult, op1=ALU.add),
                                         reads=pb.r() + Bpar.r() + accb.r(), writes=accb.r())
                                    S.op("pool", lambda e: e.tensor_tensor(out=acc[:, 0:lo], in0=acc[:, 0:lo], in1=tb_[:, 0:lo], op=ALU.add),
                                         reads=accb.r() + tbb_.r(), writes=accb.r(), dur=150.0 + 2.2 * lo)
                                else:
                                    S.op("dve", lambda e: e.scalar_tensor_tensor(out=acc[:, 0:lo], in0=ps[:, H - 1:ln - 1], scalar=wcol(1),
                                                                                 in1=acc[:, 0:lo], op0=ALU.mult, op1=ALU.add),
                                         reads=pb.r() + Bpar.r() + accb.r(), writes=accb.r())
                                    S.op("dve", lambda e: e.scalar_tensor_tensor(out=acc[:, 0:lo], in0=ps[:, H - 2:ln - 2], scalar=wcol(0),
                                                                                 in1=acc[:, 0:lo], op0=ALU.mult, op1=ALU.add),
                                         reads=pb.r() + Bpar.r() + accb.r(), writes=accb.r())
                                if has_s and last:
                                    pl = sc0 - c0
                                    S.op("act", lambda e: e.activation(out=zt[:, which, jl, 0:2], in_=ps[:, pl - 2:pl], func=AF.Copy),
                                         reads=pb.r(), writes=Bzt.r())
                                    sv = ps[:, pl:ln].rearrange("p (s c) -> p s c", c=SW)[:, :, H + 6:H + 8]
                                    dv = zt[:, which, jl, 2:34].rearrange("p (s c) -> p s c", c=2)
                                    S.op("act", lambda e: e.activation(out=dv, in_=sv, func=AF.Copy), reads=pb.r(), writes=Bzt.r())
                                zc[which].append((acc, accb, c0 + H, lo))
                        for (ag, agb, oc, lo), (au, aub, _, _) in zip(zc[0], zc[1]):
                            S.op("act", lambda e: e.activation(out=ag[:, 0:lo], in_=ag[:, 0:lo], func=AF.Gelu_apprx_tanh),
                                 reads=agb.r(), writes=agb.r())
                            S.op("dve", lambda e: e.tensor_tensor(out=act[:, jl, oc:oc + lo], in0=ag[:, 0:lo], in1=au[:, 0:lo], op=ALU.mult),
                                 reads=agb.r() + aub.r(), writes=Bact.r(oc, oc + lo))
                    if has_s:
                        for which in range(2):
                            for q0 in range(0, jn, 4):
                                qn = min(4, jn - q0)
                                f0 = FF * which + 128 * (j0 + q0)
                                srcs = [(zt[:, which, q0 + i, :], Bzt.r()) for i in range(qn)]
                                transpose_out(srcs, 34, [(o_pffn[l][:, f0:f0 + 128 * qn], 0, 2),
                                                         (o_sffn[l][:, f0:f0 + 128 * qn], 2, 34)])
                    wds = [w_next(("wd", l, j)) for j in range(j0, j0 + jn)]
                    for bi0, (c0, ln) in enumerate(_blocks0(hi)):
                        if bi0 == 0:
                            NOPEN = 4
                            pss = [PS.next() for o in range(NOPEN)]
                            for o in range(NOPEN):
                                ps, pb = pss[o]
                                for jl in range(jn - 1):
                                    wi, wt, wb = wds[jl]
                                    w_check(wi)
                                    mm(ps[:, 0:ln], wt[:, 128 * o:128 * o + 128], act[:, jl, c0:c0 + ln], jl == 0, False,
                                       wb.r() + Bact.r(c0, c0 + ln), pb)
                            for o in range(NOPEN):
                                ps, pb = pss[o]
                                wi, wt, wb = wds[jn - 1]
                                mm(ps[:, 0:ln], wt[:, 128 * o:128 * o + 128], act[:, jn - 1, c0:c0 + ln], False, True,
                                   wb.r() + Bact.r(c0, c0 + ln), pb)
                                S.op("dve", lambda e: e.tensor_tensor(out=xT[:, o, c0:c0 + ln], in0=xT[:, o, c0:c0 + ln],
                                                                      in1=ps[:, 0:ln], op=ALU.add),
                                     reads=pb.r() + BxT.r(c0, c0 + ln), writes=BxT.r(c0, c0 + ln))
                        for o in range(NOPEN if bi0 == 0 else 0, 8):
                            ps, pb = PS.next()
                            for jl in range(jn):
                                wi, wt, wb = wds[jl]
                                w_check(wi)
                                mm(ps[:, 0:ln], wt[:, 128 * o:128 * o + 128], act[:, jl, c0:c0 + ln], jl == 0, jl == jn - 1,
                                   wb.r() + Bact.r(c0, c0 + ln), pb)
                            S.op("dve", lambda e: e.tensor_tensor(out=xT[:, o, c0:c0 + ln], in0=xT[:, o, c0:c0 + ln],
                                                                  in1=ps[:, 0:ln], op=ALU.add),
                                 reads=pb.r() + BxT.r(c0, c0 + ln), writes=BxT.r(c0, c0 + ln))
                tail()
                if not SH.get("on"):
                    S.barrier()

        def lru_mixer(hi, tail):
            t0, npr, has_s, sc0, nt = _half_geom(hi)
            b3 = _blocks3(hi)
            ps_mode(8)
            with ExitStack() as ph_:
                ph = SH["ph"] if SH.get("on") else ph_
                LT = shget(ph, ("LT", hi), lambda p: Pool(p, nc, f"lt{hi}_", 15, (128, 512), F32))
                mix, Bmix = shget(ph, ("mixact", hi), lambda p: mk_buf(p, f"mixact_{hi}", [128, KC, nt], BF16, nt))
                wax = ph.enter_context(nc.sbuf_tensor(f"wax_{hi}", [128, 2, 8, 128], BF16))
                Bwax = Buf(wax, "wax")
                S.dma("pool", wax[:, 0, :, :], w_lru_a.rearrange("n c d -> c n d"), writes=Bwax.r())
                S.dma("pool", wax[:, 1, :, :], w_lru_x.rearrange("n c d -> c n d"), writes=Bwax.r())
                if has_s:
                    st3 = ph.enter_context(nc.sbuf_tensor(f"st3_{hi}", [48, D], BF16))
                    Bst3 = Buf(st3, "st3")
                    S.dma("pool", st3[:], slc_d, writes=Bst3.r())
                    h0in = ph.enter_context(nc.sbuf_tensor(f"h0in_{hi}", [16, D], F32))
                    Bh0in = Buf(h0in, "h0in")
                    S.dma("sp", h0in[:], slh_d, writes=Bh0in.r())
                    h0T = ph.enter_context(nc.sbuf_tensor(f"h0T_{hi}", [128, 8, 16], F32))
                    Bh0T = Buf(h0T, "h0T")
                    ps, pb = PS.next()
                    for n in range(8):
                        S.op("pe", lambda e: e.transpose(ps[:, 16 * n:16 * n + 16], h0in[:, 128 * n:128 * n + 128], identf[0:16, 0:16]),
                             reads=Bh0in.r() + Bid.r(), writes=pb.r())
                    S.op("act", lambda e: e.activation(out=h0T[:].rearrange("p n s -> p (n s)"), in_=ps[:, 0:128], func=AF.Copy),
                         reads=pb.r(), writes=Bh0T.r())
                    rt = ph.enter_context(nc.sbuf_tensor(f"rt_{hi}", [128, 8, 51], F32))
                    Brt = Buf(rt, "rt")
                    ht = ph.enter_context(nc.sbuf_tensor(f"ht_{hi}", [128, 8, 17], F32))
                    Bht = Buf(ht, "ht")
                hprev = ph.enter_context(nc.sbuf_tensor(f"hprev_{hi}", [128, 8, 4], F32))
                Bhp = Buf(hprev, "hprev")
                for n in range(8):
                    wgi, wgt, wgb = w_next(("gate", n))
                    wri, wrt, wrb = w_next(("rec", n))
                    o_w, _ = PAR["ccw"]
                    wcol = lambda tap: par[:, o_w + 4 * n + tap:o_w + 4 * n + tap + 1]
                    units = []
                    for bi, (c0, ln) in enumerate(b3):
                        last = bi == len(b3) - 1
                        lo = ln - H
                        oc = c0 + H
                        psg, pbg = PS.next()
                        proj_block(psg, pbg, wgi, wgt, wgb, xn, Bxn, c0, ln)
                        S.op("act", lambda e: e.activation(out=mix[:, n, oc:oc + lo], in_=psg[:, H:ln], func=AF.Gelu_apprx_tanh),
                             reads=pbg.r(), writes=Bmix.r(oc, oc + lo))
                        psr, pbr = PS.next()
                        inj = has_s and last
                        proj_block(psr, pbr, wri, wrt, wrb, xn, Bxn, c0, ln, last_stop=not inj)
                        if inj:
                            mm(psr[:, 0:ln], st3[:, 128 * n:128 * n + 128], sel3[:, 0:ln], False, True, Bst3.r() + Bsel.r(), pbr)
                        xc, xcb = LT.next()
                        S.op("dve", lambda e: e.tensor_scalar(out=xc[:, 0:lo], in0=psr[:, H:ln], scalar1=wcol(3), scalar2=P("ccb", n),
                                                              op0=ALU.mult, op1=ALU.add),
                             reads=pbr.r() + Bpar.r(), writes=xcb.r())
                        for tap in (2, 1, 0):
                            sh = 3 - tap
                            S.op("dve", lambda e: e.scalar_tensor_tensor(out=xc[:, 0:lo], in0=psr[:, H - sh:ln - sh], scalar=wcol(tap),
                                                                         in1=xc[:, 0:lo], op0=ALU.mult, op1=ALU.add),
                                 reads=pbr.r() + Bpar.r() + xcb.r(), writes=xcb.r())
                        if has_s and last:
                            pl = sc0 - c0
                            S.op("act", lambda e: e.activation(out=rt[:, n, 0:3], in_=psr[:, pl - 3:pl], func=AF.Copy),
                                 reads=pbr.r(), writes=Brt.r())
                            sv = psr[:, pl:ln].rearrange("p (s c) -> p s c", c=SW)[:, :, H + 5:H + 8]
                            dv = rt[:, n, 3:51].rearrange("p (s c) -> p s c", c=3)
                            S.op("act", lambda e: e.activation(out=dv, in_=sv, func=AF.Copy), reads=pbr.r(), writes=Brt.r())
                        units.append((bi, c0, ln, last, lo, oc, xc, xcb))
                    for (bi, c0, ln, last, lo, oc, xc, xcb) in units:
                        xb, xbb = TB.next()
                        S.op("dve", lambda e: e.tensor_copy(out=xb[:, 0:lo], in_=xc[:, 0:lo]), reads=xcb.r(), writes=xbb.r())
                        psa, pba = PS.next()
                        mm(psa[:, 0:lo], wax[:, 0, n, :], xb[:, 0:lo], True, True, Bwax.r() + xbb.r(), pba)
                        psx, pbx = PS.next()
                        mm(psx[:, 0:lo], wax[:, 1, n, :], xb[:, 0:lo], True, True, Bwax.r() + xbb.r(), pbx)
                        A, Ab = LT.next()
                        I, Ib = TF.next()
                        S2, S2b = TF.next()
                        S.op("act", lambda e: e.activation(out=A[:, 0:lo], in_=psa[:, 0:lo], func=AF.Tanh, bias=hbias[:, 0, n:n + 1], scale=0.5),
                             reads=pba.r() + Blruc.r(), writes=Ab.r())
                        S.op("act", lambda e: e.activation(out=I[:, 0:lo], in_=psx[:, 0:lo], func=AF.Tanh, bias=hbias[:, 1, n:n + 1], scale=0.5),
                             reads=pbx.r() + Blruc.r(), writes=Ib.r())
                        S.op("act", lambda e: e.activation(out=S2[:, 0:lo], in_=A[:, 0:lo], func=AF.Exp, scale=lruc[:, n, 1:2], bias=lruc[:, n, 1:2]),
                             reads=Ab.r() + Blruc.r(), writes=S2b.r())
                        S.op("act", lambda e: e.activation(out=A[:, 0:lo], in_=A[:, 0:lo], func=AF.Exp, scale=lruc[:, n, 0:1], bias=lruc[:, n, 0:1]),
                             reads=Ab.r() + Blruc.r(), writes=Ab.r())
                        S.op("dve", lambda e: e.scalar_tensor_tensor(out=I[:, 0:lo], in0=I[:, 0:lo], scalar=1.0, in1=xc[:, 0:lo], op0=ALU.add, op1=ALU.mult),
                             reads=Ib.r() + xcb.r(), writes=Ib.r())
                        S.op("act", lambda e: e.activation(out=S2[:, 0:lo], in_=S2[:, 0:lo], func=AF.Sqrt, scale=-0.25, bias=0.25),
                             reads=S2b.r(), writes=S2b.r())
                        S.op("dve", lambda e: e.tensor_tensor(out=I[:, 0:lo], in0=I[:, 0:lo], in1=S2[:, 0:lo], op=ALU.mult),
                             reads=Ib.r() + S2b.r(), writes=Ib.r())
                        if has_s and last:
                            pl = sc0 - oc
                            av = A[:, pl:lo].rearrange("p (s c) -> p s c", c=SW)[:, :, 0:H]
                            bv = I[:, pl:lo].rearrange("p (s c) -> p s c", c=SW)
                            S.op("dve", lambda e: e.memset(av, 0.0), writes=Ab.r())
                            S.op("dve", lambda e: e.memset(bv[:, :, 0:H - 1], 0.0), writes=Ib.r())
                            S.op("dve", lambda e: e.tensor_copy(out=bv[:, :, H - 1:H], in_=h0T[:, n, :].unsqueeze(2)),
                                 reads=Bh0T.r(), writes=Ib.r())
                        if bi == 0:
                            init = 0.0 if hi == 0 else ch[:, n:n + 1]
                            ird = [] if hi == 0 else Bch.r()
                        else:
                            init = hprev[:, n, bi - 1:bi]
                            ird = Bhp.r()
                        S.op("dve", lambda e: e.tensor_tensor_scan(out=xc[:, 0:lo], data0=A[:, 0:lo], data1=I[:, 0:lo], initial=init,
                                                                   op0=ALU.mult, op1=ALU.add),
                             reads=Ab.r() + Ib.r() + ird + xcb.r(), writes=xcb.r(), dur=200.0 + 2.3 * lo)
                        S.op("act", lambda e: e.activation(out=hprev[:, n, bi:bi + 1], in_=xc[:, lo - 1:lo], func=AF.Copy),
                             reads=xcb.r(), writes=Bhp.r())
                        if last:
                            if hi == 0:
                                S.op("act", lambda e: e.activation(out=ch[:, n:n + 1], in_=xc[:, lo - 1:lo], func=AF.Copy),
                                     reads=xcb.r(), writes=Bch.r())
                            else:
                                pl = sc0 - oc
                                S.op("act", lambda e: e.activation(out=ht[:, n, 0:1], in_=xc[:, pl - 1:pl], func=AF.Copy),
                                     reads=xcb.r(), writes=Bht.r())
                                sv = xc[:, pl:lo].rearrange("p (s c) -> p s c", c=SW)[:, :, SW - 1:SW]
                                S.op("act", lambda e: e.activation(out=ht[:, n, 1:17].unsqueeze(2), in_=sv, func=AF.Copy),
                                     reads=xcb.r(), writes=Bht.r())
                        S.op("dve", lambda e: e.tensor_tensor(out=mix[:, n, oc:oc + lo], in0=xc[:, 0:lo], in1=mix[:, n, oc:oc + lo], op=ALU.mult),
                             reads=xcb.r() + Bmix.r(oc, oc + lo), writes=Bmix.r(oc, oc + lo))
                if has_s:
                    for q0 in (0, 4):
                        transpose_out([(rt[:, q0 + i, :], Brt.r()) for i in range(4)], 51,
                                      [(o_plc[:, 512 * (q0 // 4):512 * (q0 // 4) + 512], 0, 3),
                                       (o_slc[:, 512 * (q0 // 4):512 * (q0 // 4) + 512], 3, 51)])
                        transpose_out([(ht[:, q0 + i, :], Bht.r()) for i in range(4)], 17,
                                      [(o_plh[:, 512 * (q0 // 4):512 * (q0 // 4) + 512], 0, 1),
                                       (o_slh[:, 512 * (q0 // 4):512 * (q0 // 4) + 512], 1, 17)])
                out_proj(hi, "wo1", mix, Bmix)
                tail()
                if not SH.get("on"):
                    S.barrier()

        def gn_block_3d(ps_o, pb_o, width, cn, qd_ap, dst_fn, c_lo, c_hi, h, Bmix, BSG):
            o, ob = TF.next()
            if qd_ap is not None:
                S.op("dve", lambda e: e.tensor_tensor(out=o[:, 0:width].rearrange("p (a b) -> p a b", b=128),
                                                      in0=ps_o[:, 0:width].rearrange("p (a b) -> p a b", b=128), in1=qd_ap, op=ALU.mult),
                     reads=pb_o.r() + Bcon.r(), writes=ob.r())
            else:
                S.op("act", lambda e: e.activation(out=o[:, 0:width], in_=ps_o[:, 0:width], func=AF.Copy), reads=pb_o.r(), writes=ob.r())
            o16, o16b = TB.next()
            S.op("act", lambda e: e.activation(out=o16[:, 0:width], in_=o[:, 0:width], func=AF.Copy), reads=ob.r(), writes=o16b.r())
            q16, q16b = TB.next()
            S.op("act", lambda e: e.activation(out=q16[:, 0:width], in_=o[:, 0:width], func=AF.Square), reads=ob.r(), writes=q16b.r())
            psm, pbm = PSL.next()
            mm(psm[:, 0:width], ones[:, 2, :], o16[:, 0:width], True, True, o16b.r() + Bones.r(), pbm)
            psq, pbq = PSL.next()
            mm(psq[:, 0:width], ones[:, 2, :], q16[:, 0:width], True, True, q16b.r() + Bones.r(), pbq)
            mean, meanb = TF.next()
            S.op("act", lambda e: e.activation(out=mean[:, 0:width], in_=psm[:, 0:width], func=AF.Copy), reads=pbm.r(), writes=meanb.r())
            var, varb = TF.next()
            S.op("dve", lambda e: e.tensor_tensor(out=var[:, 0:width], in0=mean[:, 0:width], in1=mean[:, 0:width], op=ALU.mult),
                 reads=meanb.r(), writes=varb.r())
            S.op("dve", lambda e: e.tensor_tensor(out=var[:, 0:width], in0=psq[:, 0:width], in1=var[:, 0:width], op=ALU.subtract),
                 reads=pbq.r() + varb.r(), writes=varb.r())
            S.op("dve", lambda e: e.tensor_scalar_max(out=var[:, 0:width], in0=var[:, 0:width], scalar1=0.0), reads=varb.r(), writes=varb.r())
            S.op("act", lambda e: e.activation(out=var[:, 0:width], in_=var[:, 0:width], func=AF.Ln, bias=GN_EPS, scale=1.0),
                 reads=varb.r(), writes=varb.r())
            rs, rsb = TF.next()
            S.op("act", lambda e: e.activation(out=rs[:, 0:width], in_=var[:, 0:width], func=AF.Exp, scale=-0.5), reads=varb.r(), writes=rsb.r())
            S.op("dve", lambda e: e.tensor_tensor(out=o[:, 0:width], in0=o[:, 0:width], in1=mean[:, 0:width], op=ALU.subtract),
                 reads=ob.r() + meanb.r(), writes=ob.r())
            S.op("dve", lambda e: e.tensor_tensor(out=o[:, 0:width], in0=o[:, 0:width], in1=rs[:, 0:width], op=ALU.mult),
                 reads=ob.r() + rsb.r(), writes=ob.r())
            dst, sgv, ov = dst_fn(o)
            S.op("dve", lambda e: e.scalar_tensor_tensor(out=dst, in0=ov, scalar=P("gng", h), in1=sgv, op0=ALU.mult, op1=ALU.mult),
                 reads=ob.r() + BSG.r(c_lo, c_hi) + Bpar.r(), writes=Bmix.r(c_lo, c_hi))


        def even_mixer(hi, tail):
            t0, npr, has_s, sc0, nt = _half_geom(hi)
            b0 = _blocks0(hi)
            nch = npr // 128
            UW = 30 + npr + (NSQ * 38 if has_s else 0)
            us0 = 30 + npr
            ps_mode(6)
            with ExitStack() as ph:
                mix = ph.enter_context(nc.sbuf_tensor(f"mix_{hi}", [128, KC, nt], BF16))
                Bmix = Buf(mix, "mix", nt)
                S.op("dve", lambda e: e.memset(mix[:, :, :], 0.0), writes=Bmix.r())
                with ExitStack() as pa:
                    uT = pa.enter_context(nc.sbuf_tensor(f"uT_{hi}", [128, 4, UW], BF16))
                    BuTc = [Buf(uT, f"uT{c_}", UW) for c_ in range(4)]

                    class _AllU:
                        def r(self, c0=None, c1=None):
                            out = []
                            for b_ in BuTc:
                                out += b_.r(c0, c1)
                            return out
                    BuT = _AllU()
                    diag = pa.enter_context(nc.sbuf_tensor(f"diag_{hi}", [128, 4, 31, 128], BF16))
                    Bdiagc = [Buf(diag, f"diag{c_}") for c_ in range(4)]
                    ut = pa.enter_context(nc.sbuf_tensor(f"ut_{hi}", [128, 4, 30], F32))
                    But = Buf(ut, "ut")
                    o_w, _ = PAR["caw"]
                    for c in range(4):
                        for k in range(31):
                            S.op("dve", lambda e: e.tensor_scalar(out=diag[:, c, k, :], in0=identb[:], scalar1=par[:, o_w + 31 * c + k:o_w + 31 * c + k + 1],
                                                                  scalar2=None, op0=ALU.mult),
                                 reads=Bidb.r() + Bpar.r(), writes=Bdiagc[c].r())
                    if hi == 0:
                        S.op("dve", lambda e: e.memset(uT[:, :, 0:30], 0.0), writes=BuT.r(0, 30))
                    else:
                        S.op("dve", lambda e: e.tensor_copy(out=uT[:, :, 0:30], in_=cu[:, :, :]), reads=Bcu.r(), writes=BuT.r(0, 30))
                    if has_s:
                        us = pa.enter_context(nc.sbuf_tensor(f"us_{hi}", [128, 4, 128], F32))
                        Bus = Buf(us, "us")
                        for q in range(4):
                            si, sib = TF.next()
                            S.dma("sp", si[0:120, :], sca_d[120 * q:120 * q + 120, :], writes=sib.r())
                            ps, pb = PS.next()
                            for c in range(4):
                                S.op("pe", lambda e: e.transpose(ps[:, 128 * c:128 * c + 120], si[0:120, 128 * c:128 * c + 128], identf[0:120, 0:120]),
                                     reads=sib.r() + Bid.r(), writes=pb.r())
                            for c in range(4):
                                dst = uT[:, c, us0 + 38 * 4 * q:us0 + 38 * 4 * (q + 1)].rearrange("p (s c) -> p s c", c=38)[:, :, 0:30]
                                src = ps[:, 128 * c:128 * c + 120].rearrange("p (s c) -> p s c", c=30)
                                S.op("act", lambda e: e.activation(out=dst, in_=src, func=AF.Copy), reads=pb.r(), writes=BuTc[c].r(us0, UW))
                        S.dma("sp", o_sca.rearrange("(s r) f -> s r f", r=30)[:, 0:22, :],
                              sca_d.rearrange("(s r) f -> s r f", r=30)[:, 8:30, :])
                    for c in range(4):
                        wli, wlt, wlb = w_next(("alin", c))
                        wgi, wgt, wgb = w_next(("agate", c))
                        for bi, (c0, ln) in enumerate(b0):
                            psl, pbl = PS.next()
                            proj_block(psl, pbl, wli, wlt, wlb, xn, Bxn, c0, ln)
                            psg, pbg = PS.next()
                            proj_block(psg, pbg, wgi, wgt, wgb, xn, Bxn, c0, ln)
                            sg, sgb = TF.next()
                            S.op("act", lambda e: e.activation(out=sg[:, 0:ln], in_=psg[:, 0:ln], func=AF.Sigmoid),
                                 reads=pbg.r(), writes=sgb.r())
                            p1 = min(c0 + ln, sc0)
                            pn = p1 - c0
                            if pn > 0:
                                uc = 30 + (c0 - H)
                                S.op("dve", lambda e: e.tensor_tensor(out=uT[:, c, uc:uc + pn], in0=psl[:, 0:pn], in1=sg[:, 0:pn], op=ALU.mult),
                                     reads=pbl.r() + sgb.r(), writes=BuTc[c].r(uc, uc + pn))
                                if p1 == sc0:
                                    S.op("dve", lambda e: e.tensor_tensor(out=ut[:, c, :], in0=psl[:, pn - 30:pn], in1=sg[:, pn - 30:pn], op=ALU.mult),
                                         reads=pbl.r() + sgb.r(), writes=But.r())
                            if has_s and c0 + ln > sc0:
                                pl = sc0 - c0
                                lv = psl[:, pl:ln].rearrange("p (s c) -> p s c", c=SW)[:, :, H:SW]
                                gv = sg[:, pl:ln].rearrange("p (s c) -> p s c", c=SW)[:, :, H:SW]
                                dv = uT[:, c, us0:UW].rearrange("p (s c) -> p s c", c=38)[:, :, 30:38]
                                S.op("dve", lambda e: e.tensor_tensor(out=dv, in0=lv, in1=gv, op=ALU.mult),
                                     reads=pbl.r() + sgb.r(), writes=BuTc[c].r(us0, UW))
                                S.op("dve", lambda e: e.tensor_tensor(out=us[:, c, :].rearrange("p (s c) -> p s c", c=TS), in0=lv, in1=gv, op=ALU.mult),
                                     reads=pbl.r() + sgb.r(), writes=Bus.r())
                    if hi == 0:
                        S.op("act", lambda e: e.activation(out=cu[:, :, :], in_=uT[:, :, us0 - 30:us0], func=AF.Copy), reads=BuT.r(us0 - 30, us0), writes=Bcu.r())
                    else:
                        transpose_out([(ut[:, c, :], But.r()) for c in range(4)], 30, [(o_pca[:, :], 0, 30)])
                        ps, pb = PS.next()
                        for c in range(4):
                            S.op("pe", lambda e: e.transpose(ps[:, 128 * c:128 * c + 128], us[:, c, :], identf[:]),
                                 reads=Bus.r() + Bid.r(), writes=pb.r())
                        st, stb = TF.next()
                        S.op("act", lambda e: e.activation(out=st[:, :], in_=ps[:, :], func=AF.Copy), reads=pb.r(), writes=stb.r())
                        osv = o_sca.rearrange("(s r) f -> s r f", r=30)
                        for s in range(NSQ):
                            S.dma("sp", osv[s, 22:30, :], st[8 * s:8 * s + 8, :], reads=stb.r())
                    for bi, (c0, ln) in enumerate(b0):
                        segs = []
                        p1 = min(c0 + ln, sc0)
                        if p1 > c0:
                            segs.append(("p", c0, p1 - c0))
                        if has_s and c0 + ln > sc0:
                            segs.append(("s", sc0, NSQ * TS))
                        for kind, s0, sl in segs:
                            cvs = []
                            psm, pbm = PSL.next()
                            psq, pbq = PSL.next()
                            for c in range(4):
                                ps, pb = PS.next()
                                for k in range(31):
                                    if kind == "p":
                                        ub = (s0 - H) + k
                                        rhs = uT[:, c, ub:ub + sl]
                                        urd = BuTc[c].r(ub, ub + sl)
                                    else:
                                        rhs = uT[:, c, us0:UW].rearrange("p (s c) -> p s c", c=38)[:, :, k:k + TS]
                                        urd = BuTc[c].r(us0, UW)
                                    mm(ps[:, 0:sl], diag[:, c, k, :], rhs, k == 0, k == 30, urd + Bdiagc[c].r(), pb)
                                cv, cvb = TF.next()
                                S.op("act", lambda e: e.activation(out=cv[:, 0:sl], in_=ps[:, 0:sl], func=AF.Identity, bias=P("cab", c), scale=1.0),
                                     reads=pb.r() + Bpar.r(), writes=cvb.r())
                                c16, c16b = TB.next()
                                S.op("act", lambda e: e.activation(out=c16[:, 0:sl], in_=cv[:, 0:sl], func=AF.Copy), reads=cvb.r(), writes=c16b.r())
                                q16, q16b = TB.next()
                                S.op("act", lambda e: e.activation(out=q16[:, 0:sl], in_=cv[:, 0:sl], func=AF.Square), reads=cvb.r(), writes=q16b.r())
                                mm(psm[:, 0:sl], ones[:, 1, :], c16[:, 0:sl], c == 0, c == 3, c16b.r() + Bones.r(), pbm)
                                mm(psq[:, 0:sl], ones[:, 1, :], q16[:, 0:sl], c == 0, c == 3, q16b.r() + Bones.r(), pbq)
                                cvs.append((cv, cvb))
                            mean, meanb = TF.next()
                            S.op("act", lambda e: e.activation(out=mean[:, 0:sl], in_=psm[:, 0:sl], func=AF.Copy), reads=pbm.r(), writes=meanb.r())
                            var, varb = TF.next()
                            S.op("dve", lambda e: e.tensor_tensor(out=var[:, 0:sl], in0=mean[:, 0:sl], in1=mean[:, 0:sl], op=ALU.mult),
                                 reads=meanb.r(), writes=varb.r())
                            S.op("dve", lambda e: e.tensor_tensor(out=var[:, 0:sl], in0=psq[:, 0:sl], in1=var[:, 0:sl], op=ALU.subtract),
                                 reads=pbq.r() + varb.r(), writes=varb.r())
                            S.op("dve", lambda e: e.tensor_scalar_max(out=var[:, 0:sl], in0=var[:, 0:sl], scalar1=0.0), reads=varb.r(), writes=varb.r())
                            S.op("act", lambda e: e.activation(out=var[:, 0:sl], in_=var[:, 0:sl], func=AF.Ln, bias=LN_EPS, scale=1.0),
                                 reads=varb.r(), writes=varb.r())
                            rs, rsb = TF.next()
                            S.op("act", lambda e: e.activation(out=rs[:, 0:sl], in_=var[:, 0:sl], func=AF.Exp, scale=-0.5), reads=varb.r(), writes=rsb.r())
                            for c in range(4):
                                cv, cvb = cvs[c]
                                S.op("dve", lambda e: e.tensor_tensor(out=cv[:, 0:sl], in0=cv[:, 0:sl], in1=mean[:, 0:sl], op=ALU.subtract),
                                     reads=cvb.r() + meanb.r(), writes=cvb.r())
                                S.op("dve", lambda e: e.tensor_tensor(out=cv[:, 0:sl], in0=cv[:, 0:sl], in1=rs[:, 0:sl], op=ALU.mult),
                                     reads=cvb.r() + rsb.r(), writes=cvb.r())
                                if kind == "p":
                                    dst = mix[:, c, s0:s0 + sl]
                                    src = cv[:, 0:sl]
                                    wr = Bmix.r(s0, s0 + sl)
                                else:
                                    dst = mix[:, c, sc0:nt].rearrange("p (s c) -> p s c", c=SW)[:, :, H:SW]
                                    src = cv[:, 0:sl].rearrange("p (s c) -> p s c", c=TS)
                                    wr = Bmix.r(sc0, nt)
                                S.op("act", lambda e: e.activation(out=dst, in_=src, func=AF.Silu, scale=P("lng", c), bias=P("lnb", c)),
                                     reads=cvb.r() + Bpar.r(), writes=wr)
                    S.barrier()
                with ExitStack() as pbk:
                    rope = pbk.enter_context(nc.sbuf_tensor(f"rope_{hi}", [128, 2, nt], F32))
                    Brope = Buf(rope, "rope")
                    S.dma("sp", rope[:], rope_d[hi], writes=Brope.r())
                    QT = pbk.enter_context(nc.sbuf_tensor(f"QT_{hi}", [128, nt], BF16))
                    KT = pbk.enter_context(nc.sbuf_tensor(f"KT_{hi}", [128, nt], BF16))
                    VT = pbk.enter_context(nc.sbuf_tensor(f"VT_{hi}", [128, nt], BF16))
                    SG = pbk.enter_context(nc.sbuf_tensor(f"SG_{hi}", [128, nt], F32))
                    BQT, BKT, BVT, BSG = Buf(QT, "QT", nt), Buf(KT, "KT", nt), Buf(VT, "VT", nt), Buf(SG, "SG", nt)
                    ntl = nch + (1 if has_s else 0)
                    Ktok = pbk.enter_context(nc.sbuf_tensor(f"Ktok_{hi}", [128, ntl, 128], BF16))
                    Vdec = pbk.enter_context(nc.sbuf_tensor(f"Vdec_{hi}", [128, ntl, 128], BF16))
                    BKtok, BVdec = Buf(Ktok, "Ktok"), Buf(Vdec, "Vdec")
                    Sbf = pbk.enter_context(nc.sbuf_tensor(f"Sbf_{hi}", [128, nch + 1, 128], BF16))
                    BSbf = [Buf(Sbf, f"Sbf{i}") for i in range(nch + 1)]
                    Sf = pbk.enter_context(nc.sbuf_tensor(f"Sf_{hi}", [128, 128], F32))
                    BSf = Buf(Sf, "Sf")
                    if has_s:
                        S0f = pbk.enter_context(nc.sbuf_tensor(f"S0f_{hi}", [128, NSQ, 128], F32))
                        BS0f = Buf(S0f, "S0f")
                        S0b = pbk.enter_context(nc.sbuf_tensor(f"S0b_{hi}", [128, NSQ, 128], BF16))
                        BS0b = Buf(S0b, "S0b")
                        Vbd = pbk.enter_context(nc.sbuf_tensor(f"Vbd_{hi}", [128, NSQ, 128], BF16))
                        BVbd = Buf(Vbd, "Vbd")
                        Kds = pbk.enter_context(nc.sbuf_tensor(f"Kds_{hi}", [128, 128], BF16))
                        BKds = Buf(Kds, "Kds")
                        Qds = pbk.enter_context(nc.sbuf_tensor(f"Qds_{hi}", [128, 128], BF16))
                        BQds = Buf(Qds, "Qds")
                        cmpS = pbk.enter_context(nc.sbuf_tensor(f"cmpS_{hi}", [128, 3, 128], BF16))
                        BcmpS = Buf(cmpS, "cmpS")
                    dk_scale = 128.0 ** -0.5
                    for h in range(4):
                        ws = {nm: w_next((nm, h)) for nm in ("q", "qs", "k", "ks", "v", "g")}
                        if has_s:
                            S.dma("sp", S0f[:], sret_d[:, h, :, :].rearrange("s d v -> d s v"), writes=BS0f.r())
                            S.dma("pool", S0b[:], sret_d[:, h, :, :].rearrange("s d v -> d s v"), writes=BS0b.r())
                        for (c0, ln) in b0:
                            for (dstT, Bd, n1, n2, sc) in ((QT, BQT, "q", "qs", 1.0), (KT, BKT, "k", "ks", dk_scale)):
                                ps1, pb1 = PS.next()
                                proj_block(ps1, pb1, *ws[n1], xn, Bxn, c0, ln)
                                ps2, pb2 = PS.next()
                                proj_block(ps2, pb2, *ws[n2], xn, Bxn, c0, ln)
                                t1, t1b = TF.next()
                                t2, t2b = TF.next()
                                S.op("dve", lambda e: e.scalar_tensor_tensor(out=t1[:, 0:ln], in0=ps1[:, 0:ln], scalar=sc, in1=rope[:, 0, c0:c0 + ln],
                                                                             op0=ALU.mult, op1=ALU.mult),
                                     reads=pb1.r() + Brope.r(), writes=t1b.r())
                                S.op("dve", lambda e: e.scalar_tensor_tensor(out=t2[:, 0:ln], in0=ps2[:, 0:ln], scalar=sc, in1=rope[:, 1, c0:c0 + ln],
                                                                             op0=ALU.mult, op1=ALU.mult),
                                     reads=pb2.r() + Brope.r(), writes=t2b.r())
                                S.op("dve", lambda e: e.tensor_tensor(out=dstT[:, c0:c0 + ln], in0=t1[:, 0:ln], in1=t2[:, 0:ln], op=ALU.add),
                                     reads=t1b.r() + t2b.r(), writes=Bd.r(c0, c0 + ln))
                            psv, pbv = PS.next()
                            proj_block(psv, pbv, *ws["v"], xn, Bxn, c0, ln)
                            S.op("act", lambda e: e.activation(out=VT[:, c0:c0 + ln], in_=psv[:, 0:ln], func=AF.Copy), reads=pbv.r(), writes=BVT.r(c0, c0 + ln))
                            psg, pbg = PS.next()
                            proj_block(psg, pbg, *ws["g"], xn, Bxn, c0, ln)
                            S.op("act", lambda e: e.activation(out=SG[:, c0:c0 + ln], in_=psg[:, 0:ln], func=AF.Silu), reads=pbg.r(), writes=BSG.r(c0, c0 + ln))

                        if has_s:
                            for (srcT, Bs, j_) in ((QT, BQT, 0), (KT, BKT, 1), (VT, BVT, 2)):
                                S.op("act", lambda e: e.activation(out=cmpS[:, j_, :].rearrange("p (s c) -> p s c", c=TS),
                                                                   in_=srcT[:, sc0:nt].rearrange("p (s c) -> p s c", c=SW)[:, :, H:SW], func=AF.Copy),
                                     reads=Bs.r(sc0, nt), writes=BcmpS.r())
                        cmp_idx = {id(QT): 0, id(KT): 1, id(VT): 2}

                        def tcols(tsr, i):
                            if i < nch:
                                return tsr[:, H + 128 * i:H + 128 * i + 128]
                            return cmpS[:, cmp_idx[id(tsr)], :]

                        def tregs(B, i):
                            if i < nch:
                                return B.r(H + 128 * i, H + 128 * i + 128)
                            return BcmpS.r()
                        for i0 in range(0, ntl, 8):
                            i1 = min(ntl, i0 + 8)
                            for (srcT, Bs, dstk) in ((KT, BKT, "k"), (VT, BVT, "v")):
                                ps, pb = PS.next()
                                psb = ps[:].bitcast(BF16)
                                for i in range(i0, i1):
                                    S.op("pe", lambda e: e.transpose(psb[:, 128 * (i - i0):128 * (i - i0) + 128], tcols(srcT, i), identb[:]),
                                         reads=tregs(Bs, i) + Bidb.r(), writes=pb.r())
                                npz = min(i1, nch) - i0
                                if dstk == "k":
                                    if npz > 0:
                                        S.op("act", lambda e: e.activation(out=Ktok[:, i0:i0 + npz, :].rearrange("p a b -> p (a b)"),
                                                                           in_=psb[:, 0:128 * npz], func=AF.Copy),
                                             reads=pb.r(), writes=BKtok.r())
                                    if has_s and i1 == ntl:
                                        off = 128 * (nch - i0)
                                        S.op("act", lambda e: e.activation(out=Kds[:, :], in_=psb[:, off:off + 128], func=AF.Copy, scale=rcon[:, 4 + h:5 + h]),
                                             reads=pb.r() + Bcon.r(), writes=BKds.r())
                                else:
                                    if npz > 0:
                                        S.op("act", lambda e: e.activation(out=Vdec[:, i0:i0 + npz, :].rearrange("p a b -> p (a b)"),
                                                                           in_=psb[:, 0:128 * npz], func=AF.Copy, scale=rcon[:, h:h + 1]),
                                             reads=pb.r() + Bcon.r(), writes=BVdec.r())
                                    if has_s and i1 == ntl:
                                        off = 128 * (nch - i0)
                                        S.op("act", lambda e: e.activation(out=Vdec[:, nch, :], in_=psb[:, off:off + 128], func=AF.Copy),
                                             reads=pb.r(), writes=BVdec.r())
                        if hi == 0:
                            S.op("dve", lambda e: e.memset(Sf[:, :], 0.0), writes=BSf.r())
                            S.op("dve", lambda e: e.memset(Sbf[:, 0, :], 0.0), writes=BSbf[0].r())
                        else:
                            S.op("dve", lambda e: e.tensor_copy(out=Sf[:, :], in_=cS[:, h, :]), reads=BcS.r(), writes=BSf.r())
                            S.op("act", lambda e: e.activation(out=Sbf[:, 0, :], in_=cS[:, h, :], func=AF.Copy), reads=BcS.r(), writes=BSbf[0].r())

                        for cg in range(0, nch, 4):
                            cn = min(4, nch - cg)
                            ps_o, pb_o = PSL.next()
                            for ci in range(cg, cg + cn):
                                cc0 = H + 128 * ci
                                ps_u, pb_u = PS.next()
                                mm(ps_u[:, 0:128], Ktok[:, ci, :], Vdec[:, ci, :], True, True, BKtok.r() + BVdec.r(), pb_u)
                                S.op("dve", lambda e: e.scalar_tensor_tensor(out=Sf[:, :], in0=Sf[:, :], scalar=gC[h], in1=ps_u[:, 0:128],
                                                                             op0=ALU.mult, op1=ALU.add),
                                     reads=pb_u.r() + BSf.r(), writes=BSf.r())
                                S.op("act", lambda e: e.activation(out=Sbf[:, ci + 1, :], in_=Sf[:, :], func=AF.Copy), reads=BSf.r(), writes=BSbf[ci + 1].r())
                                ps_s, pb_s = PS.next()
                                mm(ps_s[:, 0:128], KT[:, cc0:cc0 + 128], QT[:, cc0:cc0 + 128], True, True,
                                   BKT.r(cc0, cc0 + 128) + BQT.r(cc0, cc0 + 128), pb_s)
                                scm, scmb = TB.next()
                                S.op("dve", lambda e: e.tensor_tensor(out=scm[:, 0:128], in0=ps_s[:, 0:128], in1=maskP[:, h, :], op=ALU.mult),
                                     reads=pb_s.r() + Bcon.r(), writes=scmb.r())
                                oc_ = 128 * (ci - cg)
                                mm(ps_o[:, oc_:oc_ + 128], Vdec[:, ci, :], scm[:, 0:128], True, False, BVdec.r() + scmb.r(), pb_o)
                                mm(ps_o[:, oc_:oc_ + 128], Sbf[:, ci, :], QT[:, cc0:cc0 + 128], False, True,
                                   BSbf[ci].r() + BQT.r(cc0, cc0 + 128), pb_o)
                            w_ = 128 * cn
                            g0 = H + 128 * cg
                            qd_ap = qdP[:, h, :].unsqueeze(1).broadcast_to([128, cn, 128])

                            def dst_fn(o, g0=g0, w_=w_):
                                return mix[:, 4 + h, g0:g0 + w_], SG[:, g0:g0 + w_], o[:, 0:w_]
                            gn_block_3d(ps_o, pb_o, w_, cn, qd_ap, dst_fn, g0, g0 + w_, h, Bmix, BSG)
                        if hi == 0:
                            S.op("dve", lambda e: e.tensor_copy(out=cS[:, h, :], in_=Sf[:, :]), reads=BSf.r(), writes=BcS.r())
                        else:
                            S.dma("sp", o_pret[h, :, :], Sf[:, :], reads=BSf.r())
                        if has_s:
                            qs_v = tcols(QT, nch)
                            ks_v = tcols(KT, nch)
                            S.op("dve", lambda e: e.tensor_tensor(out=Qds[:, :], in0=qs_v, in1=qd8[:, h, :], op=ALU.mult),
                                 reads=BcmpS.r() + Bcon.r(), writes=BQds.r())
                            ps_s, pb_s = PS.next()
                            mm(ps_s[:, 0:128], ks_v, qs_v, True, True, BcmpS.r(), pb_s)
                            scm, scmb = TB.next()
                            S.op("dve", lambda e: e.tensor_tensor(out=scm[:, 0:128], in0=ps_s[:, 0:128], in1=maskS[:, h, :], op=ALU.mult),
                                 reads=pb_s.r() + Bcon.r(), writes=scmb.r())
                            ps_o, pb_o = PSL.next()
                            mm(ps_o[:, 0:128], Vdec[:, nch, :], scm[:, 0:128], True, False, BVdec.r() + scmb.r(), pb_o)
                            for s in range(NSQ):
                                mm(ps_o[:, 8 * s:8 * s + 8], S0b[:, s, :], Qds[:, 8 * s:8 * s + 8], False, s == NSQ - 1,
                                   BS0b.r() + BQds.r(), pb_o)

                            def dst_fn_s(o):
                                return (mix[:, 4 + h, sc0:nt].rearrange("p (s c) -> p s c", c=SW)[:, :, H:SW],
                                        SG[:, sc0:nt].rearrange("p (s c) -> p s c", c=SW)[:, :, H:SW],
                                        o[:, 0:128].rearrange("p (s c) -> p s c", c=TS))
                            gn_block_3d(ps_o, pb_o, 128, None, None, dst_fn_s, sc0, nt, h, Bmix, BSG)
                            S.op("dve", lambda e: e.tensor_tensor(out=Vbd[:, :, :], in0=Vdec[:, nch, :].unsqueeze(1).broadcast_to([128, NSQ, 128]),
                                                                  in1=rcon[:, 8:24].unsqueeze(2).broadcast_to([128, NSQ, 128]), op=ALU.mult),
                                 reads=BVdec.r() + Bcon.r(), writes=BVbd.r())
                            for q in range(4):
                                ps_u, pb_u = PS.next()
                                mm(ps_u[:, :], Kds[:, :], Vbd[:, 4 * q:4 * q + 4, :], True, True, BKds.r() + BVbd.r(), pb_u)
                                sn, snb = TF.next()
                                S.op("dve", lambda e: e.scalar_tensor_tensor(out=sn[:, :], in0=S0f[:, 4 * q:4 * q + 4, :].rearrange("p a b -> p (a b)"),
                                                                             scalar=g8[h], in1=ps_u[:, :], op0=ALU.mult, op1=ALU.add),
                                     reads=pb_u.r() + BS0f.r(), writes=snb.r())
                                S.dma("sp", o_sret[4 * q:4 * q + 4, h, :, :].rearrange("s d v -> d s v"),
                                      sn[:, :].rearrange("p (s v) -> p s v", v=128), reads=snb.r())
                    out_proj(hi, "wo0", mix, Bmix)
                    tail()
                    S.barrier()

        for hi in range(2):
            load_x(hi)
            rmsnorm(hi, "nmix0", None)
            even_mixer(hi, lambda hi=hi: rmsnorm(hi, "nffn0", 0))
            with ExitStack() as shared:
                SH.clear()
                SH["on"] = True
                SH["ph"] = shared
                ffn(hi, 0, lambda hi=hi: rmsnorm(hi, "nmix1", 1))
                lru_mixer(hi, lambda hi=hi: rmsnorm(hi, "nffn1", 2))
                ffn(hi, 1, lambda hi=hi: final_out(hi))
                SH["on"] = False
                S.barrier()
        S.barrier()
        build.marks = S.marks

    return nc


_NC_CACHE = {}


def _core_inputs(inp, params, consts, w_sw, c):
    m = {
        "xp": np.ascontiguousarray(inp["x_prompt"][c]),
        "xs": np.ascontiguousarray(inp["x_sample"][NSQ * c:NSQ * (c + 1)].reshape(NSQ * TS, D)),
        "st_conv_a": np.ascontiguousarray(inp["state_conv_a"][0, NSQ * c:NSQ * (c + 1)].reshape(NSQ * 30, 512)),
        "st_ret": np.ascontiguousarray(inp["state_ret"][0, NSQ * c:NSQ * (c + 1)]),
        "st_lru_conv": np.ascontiguousarray(inp["state_lru_conv"][0, NSQ * c:NSQ * (c + 1)].reshape(NSQ * 3, D)),
        "st_lru_h": np.ascontiguousarray(inp["state_lru_h"][0, NSQ * c:NSQ * (c + 1)]),
        "st_ffn": np.ascontiguousarray(inp["state_ffn_conv"][:, NSQ * c:NSQ * (c + 1)].reshape(2, NSQ * 2, 2 * FF)),
        "w_in_ab": inp["w_in_ab"][0],
        "w_sw": w_sw,
        "w_out_ab": inp["w_out_ab"][0],
        "w_in_c": inp["w_in_c"][0],
        "w_lru_a": inp["w_lru_a"][0],
        "w_lru_x": inp["w_lru_x"][0],
        "w_out_c": inp["w_out_c"][0],
        "w_ffn_up": inp["w_ffn_up"],
        "w_ffn_down": inp["w_ffn_down"],
        "params": params,
    }
    m.update(consts)
    return m


def _prep(inputs):
    inp = {k: np.asarray(v, dtype=np.float32) for k, v in inputs.items()}
    params, npar = _pack_params(inp)
    consts = _host_consts()
    wqk = inp["w_in_ab"][0][:, 1024:2048].reshape(D, 8, 2, 64)
    w_sw = np.ascontiguousarray(wqk[:, :, ::-1, :].reshape(D, 1024))
    return inp, params, npar, consts, w_sw


def _assemble(res, ncores):
    def g(name):
        return [np.asarray(r[name]) for r in res]
    y_p = np.stack(g("y_p"), 0)
    y_s = np.concatenate([a.reshape(NSQ, TS, D) for a in g("y_s")], 0)
    p_ca = np.stack(g("p_conv_a"), 0)[None]
    p_ret = np.stack(g("p_ret"), 0)[None]
    p_lc = np.stack(g("p_lru_conv"), 0)[None]
    p_lh = np.stack([a.reshape(D) for a in g("p_lru_h")], 0)[None]
    p_ffn = np.stack(g("p_ffn"), 1)
    s_ca = np.concatenate([a.reshape(NSQ, 30, 512) for a in g("s_conv_a")], 0)[None]
    s_ret = np.concatenate(g("s_ret"), 0)[None]
    s_lc = np.concatenate([a.reshape(NSQ, 3, D) for a in g("s_lru_conv")], 0)[None]
    s_lh = np.concatenate(g("s_lru_h"), 0)[None]
    s_ffn = np.concatenate([a.reshape(2, NSQ, 2, 2 * FF) for a in g("s_ffn")], 1)
    outs = (y_p, y_s, p_ca, p_ret, p_lc, p_lh, p_ffn, s_ca, s_ret, s_lc, s_lh, s_ffn)
    return tuple(np.ascontiguousarray(o, dtype=np.float32) for o in outs)


def kernel(**inputs):
    inp, params, npar, consts, w_sw = _prep(inputs)
    ncores = 8
    nc = build(npar)
    in_maps = [_core_inputs(inp, params, consts, w_sw, c) for c in range(ncores)]
    res = run_bass_kernel_spmd(nc, in_maps, core_ids=list(range(ncores)))
    return _assemble(res.results, ncores)
```

```python
import math
from contextlib import ExitStack

import numpy as np
import concourse.bass as bass
import concourse.mybir as mybir
from concourse.bass_utils import run_bass_kernel_spmd

F32 = mybir.dt.float32
BF16 = mybir.dt.bfloat16
AF = mybir.ActivationFunctionType
ALU = mybir.AluOpType

D = 1024
KC = 8
SEQ = 2048
NSQ = 16
TS = 8
PAST = 16384
H = 3
SW = H + TS
FF = 2816
NJ = 22
HALVES = [(0, 896, False), (896, 1152, True)]
GROUPS = [(0, 8), (8, 7), (15, 7)]
RMS_EPS = 1e-6
LN_EPS = 1e-5
GN_EPS = 1e-5
NSLOT = 12
PF = 3
NTF = 12
NTB = 8
STAGE = 4
import os as _os
_LAT = float(_os.environ.get('SCHED_LAT', '500'))
_EPS = float(_os.environ.get('SCHED_EPS', '150'))
_TBL_INIT = None


class Reg:
    __slots__ = ("name", "w", "rd", "excl")

    def __init__(self, name, excl=False):
        self.name = name
        self.w = None
        self.rd = {}
        self.excl = excl


class Buf:
    def __init__(self, t, name, ncols=None, G=256, excl=False):
        self.t = t
        self.name = name
        self.G = G
        n = 1 if ncols is None else (ncols + G - 1) // G
        self.regs = [Reg(f"{name}.{i}", excl) for i in range(n)]
        self.ncols = ncols

    def r(self, c0=None, c1=None):
        if self.ncols is None or c0 is None:
            return list(self.regs)
        return self.regs[c0 // self.G:(c1 - 1) // self.G + 1]


class _Rec:
    def __init__(self):
        self.call = None

    def __getattr__(self, name):
        def f(*a, **k):
            self.call = (name, a, k)
            return self
        return f


class _Node:
    __slots__ = ("idx", "eng", "call", "deps", "dur", "lat", "tbl", "tok", "is_dma", "succ", "ndep", "ready")


_ACT_TBL = {}


def _free_elems(ap):
    try:
        sh = ap.shape
        n = 1
        for d in sh[1:]:
            n *= int(d)
        return n
    except Exception:
        return 512


class Sched:
    def __init__(self, nc, ndma=6):
        self.nc = nc
        self.E = {"pe": nc.tensor, "act": nc.scalar, "dve": nc.vector, "pool": nc.gpsimd, "sp": nc.sync}
        self.sem = {}
        self.cnt = {}
        self.waited = {e: {} for e in self.E}
        for e in self.E:
            self.sem[e] = nc.alloc_semaphore(name=f"c_{e}")
            self.cnt[e] = 0
        self.dsem = {}
        for q in ("sp", "pool"):
            self.dsem[q] = [[nc.alloc_semaphore(name=f"d_{q}{i}"), 0] for i in range(ndma)]
        self.drr = {"sp": 0, "pool": 0}
        self.nodes = []
        self.nidx = 0
        self.reorder = True
        self.marks = []

    def _record(self, node, reads, writes):
        deps = {}

        def add(n, raw):
            if n is None:
                return
            if deps.get(n.idx, (None, False))[1] is False:
                deps[n.idx] = (n, raw or deps.get(n.idx, (None, False))[1])

        for r in reads:
            add(r.w, True)
            if r.excl:
                for n in r.rd.values():
                    add(n, False)
        for r in writes:
            add(r.w, r.excl)
            for n in r.rd.values():
                add(n, False)
        node.deps = list(deps.values())
        for r in writes:
            r.w = node
            r.rd = {}
        for r in reads:
            if r.excl:
                r.w = node
                r.rd = {}
            else:
                r.rd[node.idx] = node
        self.nodes.append(node)

    def op(self, e, fn, reads=(), writes=(), dur=None, tbl=None):
        rec = _Rec()
        fn(rec)
        n = _Node()
        n.idx = self.nidx
        self.nidx += 1
        n.eng = e
        n.call = rec.call
        n.is_dma = False
        n.tbl = tbl
        n.tok = None
        if dur is None:
            name, a, k = rec.call
            out = k.get("out", a[0] if a else None)
            fe = _free_elems(out) if out is not None else 512
            if e == "pe":
                dur = 10.0 + fe / 2.4
            elif e == "act":
                dur = 280.0 + fe / 1.1
                if name == "activation":
                    f = k.get("func")
                    tbl = _ACT_TBL.get(f, None)
                    n.tbl = tbl
            elif e == "dve":
                dur = 180.0 + fe / 0.9
                if name == "reciprocal":
                    dur = 120.0 + 4 * fe / 0.96
                elif name in ("memset",):
                    dur = 100.0 + fe / 3.0
            else:
                dur = 300.0
        n.dur = dur
        n.lat = 60.0 if e == "pe" else _LAT
        self._record(n, reads, writes)

    def dma(self, q, out, in_, reads=(), writes=(), nbytes=None):
        n = _Node()
        n.idx = self.nidx
        self.nidx += 1
        n.eng = q
        n.call = (out, in_)
        n.is_dma = True
        n.tbl = None
        n.tok = None
        if nbytes is None:
            try:
                nb = 1
                for d in out.shape:
                    nb *= int(d)
                nbytes = nb * 4
            except Exception:
                nbytes = 65536
        n.dur = 1000.0 if q == "pool" else 150.0
        n.lat = 2000.0 + nbytes / 150.0
        self._record(n, reads, writes)

    def _wait(self, e, key, sem, val):
        if self.waited[e].get(key, 0) >= val:
            return
        self.E[e].wait_ge(sem, val)
        self.waited[e][key] = val

    def _emit(self, n):
        e = n.eng
        toks = {}
        for (d, raw) in n.deps:
            key, sem, val = d.tok
            if key == e:
                if e in ("pe", "pool", "sp"):
                    continue
                if not raw:
                    continue
                if val <= self.cnt[e] - 2:
                    continue
            if toks.get(key, (None, 0))[1] < val:
                toks[key] = (sem, val)
        if n.is_dma:
            i = self.drr[e]
            self.drr[e] = (i + 1) % len(self.dsem[e])
            ent = self.dsem[e][i]
            key = f"d_{e}{i}"
            if ent[1] > 0:
                self._wait(e, key, ent[0], 16 * ent[1])
            for k2, (sem, val) in toks.items():
                self._wait(e, k2, sem, val)
            out, in_ = n.call
            self.E[e].dma_start(out=out, in_=in_).then_inc(ent[0], 16)
            ent[1] += 1
            n.tok = (key, ent[0], 16 * ent[1])
        else:
            for k2, (sem, val) in toks.items():
                self._wait(e, k2, sem, val)
            name, a, k = n.call
            ins = getattr(self.E[e], name)(*a, **k)
            self.cnt[e] += 1
            ins.then_inc(self.sem[e], 1)
            n.tok = (e, self.sem[e], self.cnt[e])
        n.deps = None
        n.call = None

    def flush(self):
        nodes = self.nodes
        self.nodes = []
        if not nodes:
            return
        if not self.reorder:
            for n in nodes:
                self._emit(n)
            return
        inwin = {n.idx: n for n in nodes}
        for n in nodes:
            n.succ = []
            n.ndep = 0
            n.ready = 0.0
        for n in nodes:
            for (d, raw) in n.deps:
                if d.idx in inwin:
                    d.succ.append(n)
                    n.ndep += 1
        bl = {}
        for n in reversed(nodes):
            m = 0.0
            for s_ in n.succ:
                v = n.lat + bl[s_.idx]
                if v > m:
                    m = v
            bl[n.idx] = n.dur + m
        free = {e: 0.0 for e in self.E}
        last_tbl = None
        ready = {e: [] for e in self.E}
        for n in nodes:
            if n.ndep == 0:
                ready[n.eng].append(n)
        left = len(nodes)
        EPS = _EPS
        while left:
            best = None
            for e, lst in ready.items():
                if not lst:
                    continue
                mr = min(x.ready for x in lst)
                t_e = max(free[e], mr)
                if best is None or t_e < best[0]:
                    best = (t_e, e)
            t_e, e = best
            lst = ready[e]
            cands = [x for x in lst if x.ready <= t_e + EPS]
            if e in ("sp", "pool"):
                n = min(cands, key=lambda x: x.idx)
            else:
                if e == "act" and last_tbl is not None:
                    same = [x for x in lst if x.ready <= t_e + 1500.0 and (x.tbl is None or x.tbl == last_tbl)]
                    if same:
                        cands = same
                n = max(cands, key=lambda x: (bl[x.idx], -x.idx))
            lst.remove(n)
            st = max(free[e], n.ready)
            if e == "act" and n.tbl is not None:
                if last_tbl is not None and n.tbl != last_tbl:
                    st += 1300.0
                last_tbl = n.tbl
            end = st + n.dur
            free[e] = end
            fin_t = end + n.lat
            self._emit(n)
            left -= 1
            for s_ in n.succ:
                if fin_t > s_.ready:
                    s_.ready = fin_t
                s_.ndep -= 1
                if s_.ndep == 0:
                    ready[s_.eng].append(s_)
            n.succ = None

    def barrier(self):
        self.flush()
        self.marks.append(dict(self.cnt))
        for e in self.E:
            for e2 in ("pe", "act", "dve", "pool"):
                if e2 != e and self.cnt[e2] > 0:
                    self._wait(e, e2, self.sem[e2], self.cnt[e2])
            for q in ("sp", "pool"):
                for i, ent in enumerate(self.dsem[q]):
                    if ent[1] > 0:
                        self._wait(e, f"d_{q}{i}", ent[0], 16 * ent[1])


class Pool:
    def __init__(self, es, nc, name, n, shape, dt, excl=False, space="sbuf"):
        self.items = []
        for i in range(n):
            if space == "sbuf":
                t = es.enter_context(nc.sbuf_tensor(f"{name}{i}", shape, dt))
            else:
                t = es.enter_context(nc.psum_tensor(f"{name}{i}", shape, dt))
            self.items.append((t, Buf(t, f"{name}{i}", excl=excl)))
        self.i = 0

    def next(self):
        it = self.items[self.i]
        self.i = (self.i + 1) % len(self.items)
        return it


def _init_tbl():
    _ACT_TBL.update({AF.Gelu_apprx_tanh: "gelu", AF.Sigmoid: "sig", AF.Silu: "silu", AF.Exp: "exp", AF.Ln: "exp",
                     AF.Sqrt: "sqrt", AF.Tanh: "exp"})


def _gammas():
    lg = np.log(np.float32(1.0) - np.float32(2.0) ** (-5.0 - np.arange(4, dtype=np.float32))).astype(np.float32)
    return lg


def _half_geom(hi):
    t0, npr, has_s = HALVES[hi]
    sc0 = H + npr
    nt = sc0 + (NSQ * SW if has_s else 0)
    return t0, npr, has_s, sc0, nt


def _blocks0(hi):
    t0, npr, has_s, sc0, nt = _half_geom(hi)
    out = []
    c = H
    while c < nt:
        l = min(512, nt - c)
        out.append((c, l))
        c += l
    return out


def _blocks3(hi):
    t0, npr, has_s, sc0, nt = _half_geom(hi)
    out = []
    c = 0
    while True:
        l = min(512, nt - c)
        out.append((c, l))
        if c + l >= nt:
            break
        c += l - H
    return out


def _host_consts():
    c = {}
    c["ident"] = np.eye(128, dtype=np.float32)
    pm = np.zeros((128, 128), np.float32)
    for m_ in range(128):
        pm[(m_ + 64) % 128, m_] = 1.0
    c["perm"] = pm
    lg = _gammas()
    inv_freq = (np.float32(10000.0) ** (-np.arange(0, 128, 2, dtype=np.float32) / np.float32(128))).astype(np.float32)
    for hi in range(2):
        t0, npr, has_s, sc0, nt = _half_geom(hi)
        pos = np.zeros(nt, np.float32)
        valid = np.zeros(nt, bool)
        pos[H:H + npr] = np.arange(t0, t0 + npr, dtype=np.float32)
        valid[H:H + npr] = True
        if has_s:
            for s in range(NSQ):
                b = sc0 + SW * s + H
                pos[b:b + TS] = np.arange(PAST, PAST + TS, dtype=np.float32)
                valid[b:b + TS] = True
        ang = (pos[:, None] * inv_freq[None, :]).astype(np.float32)
        cs = np.cos(ang).astype(np.float32)
        sn = np.sin(ang).astype(np.float32)
        cosT = np.concatenate([cs.T, cs.T], axis=0)
        sinT = np.concatenate([-sn.T, sn.T], axis=0)
        cosT[:, ~valid] = 0
        sinT[:, ~valid] = 0
        c[f"rope{hi}"] = np.ascontiguousarray(np.stack([cosT, sinT], axis=1).astype(np.float32))
    idx = np.arange(128, dtype=np.float32)
    maskP = np.zeros((128, 4, 128), np.float32)
    qdP = np.zeros((128, 4, 128), np.float32)
    kdecP = np.zeros((128, 4), np.float32)
    maskS = np.zeros((128, 4, 128), np.float32)
    qd8 = np.zeros((128, 4, 128), np.float32)
    kdec8 = np.zeros((128, 4), np.float32)
    causal = (idx[:, None] <= idx[None, :]).astype(np.float32)
    sj = np.arange(128) // 8
    jj = (np.arange(128) % 8).astype(np.float32)
    for h in range(4):
        maskP[:, h, :] = causal * np.exp(np.float32(-128.0) * lg[h]).astype(np.float32)
        qdP[:, h, :] = np.exp((idx + 1.0) * lg[h])[None, :]
        kdecP[:, h] = np.exp((127.0 - idx) * lg[h])
        rel = jj[None, :] - jj[:, None]
        m = np.where((sj[:, None] == sj[None, :]) & (rel >= 0), np.exp(np.maximum(rel, 0) * lg[h]), 0.0)
        maskS[:, h, :] = m
        qd8[:, h, :] = np.exp((jj + 1.0) * lg[h])[None, :]
        kdec8[:, h] = np.exp((7.0 - jj) * lg[h])
    c["maskP"] = maskP
    c["qdP"] = qdP
    c["maskS"] = maskS.astype(np.float32)
    c["qd8"] = qd8
    rc = np.zeros((128, 32), np.float32)
    rc[:, 0:4] = kdecP
    rc[:, 4:8] = kdec8
    rc[:, 8:24] = (sj[:, None] == np.arange(16)[None, :]).astype(np.float32)
    c["rcon"] = rc
    t0, npr, has_s, sc0, nt = _half_geom(1)
    b3 = _blocks3(1)
    lc0, ll = b3[-1]
    sel2 = np.zeros((32, ll), np.float32)
    sel3 = np.zeros((48, ll), np.float32)
    for s in range(NSQ):
        for r in range(2):
            sel2[2 * s + r, sc0 + SW * s + 1 + r - lc0] = 1
        for r in range(3):
            sel3[3 * s + r, sc0 + SW * s + r - lc0] = 1
    c["sel2"] = sel2
    c["sel3"] = sel3
    return c


def _fm(v):
    return np.ascontiguousarray(v.reshape(-1, 128).T)


PAR = {}


def _pack_params(inp):
    cols = []
    off = 0

    def add(name, arr):
        nonlocal off
        arr = np.ascontiguousarray(arr, dtype=np.float32).reshape(128, -1)
        PAR[name] = (off, arr.shape[1])
        cols.append(arr)
        off += arr.shape[1]

    for l in range(2):
        add(f"nmix{l}", _fm(inp["norm_mix"][l]))
        add(f"nffn{l}", _fm(inp["norm_ffn"][l]))
    add("nfin", _fm(inp["norm_final"]))
    add("cab", _fm(inp["conv_a_b"][0]))
    add("lng", _fm(inp["ln_a_g"][0]))
    add("lnb", _fm(inp["ln_a_b"][0]))
    add("gng", _fm(inp["gn_ret_g"][0]))
    add("caw", inp["conv_a_w"][0].reshape(31, 4, 128).transpose(2, 1, 0))
    add("ccw", inp["conv_c_w"][0].reshape(4, 8, 128).transpose(2, 1, 0))
    add("ccb", _fm(inp["conv_c_b"][0]))
    add("ba", _fm(inp["b_lru_a"][0]))
    add("bx", _fm(inp["b_lru_x"][0]))
    add("lam", _fm(inp["lru_lambda"][0]))
    for l in range(2):
        add(f"fcw{l}", inp["ffn_conv_w"][l].reshape(3, 44, 128).transpose(2, 1, 0))
        add(f"fcb{l}", _fm(inp["ffn_conv_b"][l]))
    return np.ascontiguousarray(np.concatenate(cols, axis=1)), off


def build(npar):
    nc = bass.Bass("TRN2", target_bir_lowering=False)

    def din(name, shape):
        return nc.dram_tensor(name, list(shape), F32, kind="ExternalInput").ap()

    def dout(name, shape):
        return nc.dram_tensor(name, list(shape), F32, kind="ExternalOutput").ap()

    xp_d = din("xp", (SEQ, D))
    xs_d = din("xs", (NSQ * TS, D))
    sca_d = din("st_conv_a", (NSQ * 30, 512))
    sret_d = din("st_ret", (NSQ, 4, 128, 128))
    slc_d = din("st_lru_conv", (NSQ * 3, D))
    slh_d = din("st_lru_h", (NSQ, D))
    sffn_d = din("st_ffn", (2, NSQ * 2, 2 * FF))
    w_in_ab = din("w_in_ab", (D, 3072))
    w_out_ab = din("w_out_ab", (D, D))
    w_in_c = din("w_in_c", (D, 2048))
    w_lru_a = din("w_lru_a", (8, 128, 128))
    w_lru_x = din("w_lru_x", (8, 128, 128))
    w_out_c = din("w_out_c", (D, D))
    w_up = din("w_ffn_up", (2, D, 2 * FF))
    w_dn = din("w_ffn_down", (2, FF, D))
    par_d = din("params", (128, npar))
    ident_d = din("ident", (128, 128))
    perm_d = din("perm", (128, 128))
    rope_d = [din(f"rope{hi}", (128, 2, _half_geom(hi)[4])) for hi in range(2)]
    maskP_d = din("maskP", (128, 4, 128))
    qdP_d = din("qdP", (128, 4, 128))
    maskS_d = din("maskS", (128, 4, 128))
    qd8_d = din("qd8", (128, 4, 128))
    rcon_d = din("rcon", (128, 32))
    b3_1 = _blocks3(1)
    sel2_d = din("sel2", (32, b3_1[-1][1]))
    sel3_d = din("sel3", (48, b3_1[-1][1]))

    yp_d = dout("y_p", (SEQ, D))
    ys_d = dout("y_s", (NSQ * TS, D))
    o_pca = dout("p_conv_a", (30, 512))
    o_pret = dout("p_ret", (4, 128, 128))
    o_plc = dout("p_lru_conv", (3, D))
    o_plh = dout("p_lru_h", (1, D))
    o_pffn = dout("p_ffn", (2, 2, 2 * FF))
    o_sca = dout("s_conv_a", (NSQ * 30, 512))
    o_sret = dout("s_ret", (NSQ, 4, 128, 128))
    o_slc = dout("s_lru_conv", (NSQ * 3, D))
    o_slh = dout("s_lru_h", (NSQ, D))
    o_sffn = dout("s_ffn", (2, NSQ * 2, 2 * FF))

    lg = _gammas()
    gC = [float(np.exp(np.float32(128.0) * lg[h])) for h in range(4)]
    g8 = [float(np.exp(np.float32(8.0) * lg[h])) for h in range(4)]

    _init_tbl()
    S = Sched(nc)
    NTMAX = _half_geom(1)[4]

    with ExitStack() as es:
        def sb(name, shape, dt=F32):
            return es.enter_context(nc.sbuf_tensor("sb_" + name, list(shape), dt))

        xT = sb("xT", (128, KC, NTMAX))
        xn = sb("xn", (128, KC, NTMAX), BF16)
        BxT = Buf(xT, "xT", NTMAX)
        Bxn = Buf(xn, "xn", NTMAX)
        par = sb("par", (128, npar))
        Bpar = Buf(par, "par")
        identf = sb("identf", (128, 128))
        identb = sb("identb", (128, 128), BF16)
        Bid = Buf(identf, "identf")
        Bidb = Buf(identb, "identb")
        permb = sb("permb", (128, 128), BF16)
        Bperm = Buf(permb, "permb")
        ones = sb("ones", (128, 3, 128), BF16)
        Bones = Buf(ones, "ones")
        maskP = sb("maskP", (128, 4, 128))
        qdP = sb("qdP", (128, 4, 128))
        maskS = sb("maskS", (128, 4, 128))
        qd8 = sb("qd8", (128, 4, 128))
        rcon = sb("rcon", (128, 32))
        Bcon = Buf(maskP, "retconst")
        sel2 = sb("sel2", (32, b3_1[-1][1]), BF16)
        sel3 = sb("sel3", (48, b3_1[-1][1]), BF16)
        Bsel = Buf(sel2, "sel")
        lruc = sb("lruc", (128, 8, 2))
        hbias = sb("hbias", (128, 2, 8))
        Blruc = Buf(lruc, "lruc")
        cxn = sb("cxn", (128, 3, KC, H), BF16)
        Bcxn = Buf(cxn, "cxn")
        cu = sb("cu", (128, 4, 30), BF16)
        Bcu = Buf(cu, "cu")
        cS = sb("cS", (128, 4, 128))
        BcS = Buf(cS, "cS")
        ch = sb("ch", (128, 8))
        Bch = Buf(ch, "ch")

        slots = Pool(es, nc, "wsl", NSLOT, (128, 1024), BF16)
        TF = Pool(es, nc, "tf", NTF, (128, 512), F32)
        TB = Pool(es, nc, "tb", NTB, (128, 512), BF16)
        PS = Pool(es, nc, "ps", 6, (128, 512), F32, excl=True, space="psum")
        PSL = Pool(es, nc, "psl", 2, (128, 512), F32, excl=True, space="psum")
        _ps6 = list(PS.items)
        _ps8 = list(PS.items) + list(PSL.items)

        def ps_mode(n):
            PS.items = _ps8 if n == 8 else _ps6
            PS.i = 0

        def P(name, k=None):
            o, n = PAR[name]
            if k is None:
                return par[:, o:o + n]
            return par[:, o + k:o + k + 1]

        S.dma("sp", par[:], par_d, writes=Bpar.r())
        S.dma("sp", identf[:], ident_d, writes=Bid.r())
        S.dma("pool", identb[:], ident_d, writes=Bidb.r())
        S.dma("pool", permb[:], perm_d, writes=Bperm.r())
        S.dma("sp", maskP[:], maskP_d, writes=Bcon.r())
        S.dma("sp", qdP[:], qdP_d, writes=Bcon.r())
        S.dma("sp", maskS[:], maskS_d, writes=Bcon.r())
        S.dma("sp", qd8[:], qd8_d, writes=Bcon.r())
        S.dma("sp", rcon[:], rcon_d, writes=Bcon.r())
        S.dma("pool", sel2[:], sel2_d, writes=Bsel.r())
        S.dma("pool", sel3[:], sel3_d, writes=Bsel.r())
        S.op("dve", lambda e: e.memset(ones[:, 0, :], 1.0 / 1024), writes=Bones.r())
        S.op("dve", lambda e: e.memset(ones[:, 1, :], 1.0 / 512), writes=Bones.r())
        S.op("dve", lambda e: e.memset(ones[:, 2, :], 1.0 / 128), writes=Bones.r())
        S.op("act", lambda e: e.activation(out=lruc[:, :, 0], in_=P("lam"), func=AF.Exp, scale=-1.0),
             reads=Bpar.r(), writes=Blruc.r())
        S.op("act", lambda e: e.activation(out=lruc[:, :, 0], in_=lruc[:, :, 0], func=AF.Ln, bias=1.0, scale=1.0),
             reads=Blruc.r(), writes=Blruc.r())
        S.op("dve", lambda e: e.tensor_scalar(out=lruc[:, :, 1], in0=lruc[:, :, 0], scalar1=-8.0, scalar2=None, op0=ALU.mult),
             reads=Blruc.r(), writes=Blruc.r())
        S.op("dve", lambda e: e.tensor_scalar(out=lruc[:, :, 0], in0=lruc[:, :, 0], scalar1=-4.0, scalar2=None, op0=ALU.mult),
             reads=Blruc.r(), writes=Blruc.r())
        S.op("dve", lambda e: e.tensor_scalar(out=hbias[:, 0, :], in0=P("ba"), scalar1=0.5, scalar2=None, op0=ALU.mult),
             reads=Bpar.r(), writes=Blruc.r())
        S.op("dve", lambda e: e.tensor_scalar(out=hbias[:, 1, :], in0=P("bx"), scalar1=0.5, scalar2=None, op0=ALU.mult),
             reads=Bpar.r(), writes=Blruc.r())

        def wap_cols(w2d, c0, ncols=128):
            return w2d.rearrange("(k p) c -> p k c", p=128)[:, :, c0:c0 + ncols]

        def weight_plan():
            for hi in range(2):
                if STAGE < 1:
                    continue
                for c in range(4):
                    yield ("alin", c), wap_cols(w_in_ab, 128 * c)
                    yield ("agate", c), wap_cols(w_in_ab, 512 + 128 * c)
                for h in range(4):
                    yield ("q", h), wap_cols(w_in_ab, 1024 + 128 * h)
                    yield ("k", h), wap_cols(w_in_ab, 1536 + 128 * h)
                    yield ("v", h), wap_cols(w_in_ab, 2048 + 128 * h)
                    yield ("g", h), wap_cols(w_in_ab, 2560 + 128 * h)
                for o in range(8):
                    yield ("wo0", o), wap_cols(w_out_ab, 128 * o)
                for l in range(2):
                    if l == 0 and STAGE < 2:
                        continue
                    if l == 1 and STAGE < 3:
                        continue
                    if l == 1:
                        for n in range(8):
                            yield ("gate", n), wap_cols(w_in_c, 128 * n)
                            yield ("rec", n), wap_cols(w_in_c, 1024 + 128 * n)
                        for o in range(8):
                            yield ("wo1", o), wap_cols(w_out_c, 128 * o)
                        if STAGE < 4:
                            continue
                    for (j0, jn) in GROUPS:
                        for j in range(j0, j0 + jn):
                            yield ("upg", l, j), wap_cols(w_up[l], 128 * j)
                            yield ("upu", l, j), wap_cols(w_up[l], FF + 128 * j)
                        for j in range(j0, j0 + jn):
                            yield ("wd", l, j), w_dn[l][128 * j:128 * j + 128, :]

        plan = list(weight_plan())
        wstate = {"issued": 0, "used": 0}
        wslot_of = {}

        def w_issue(upto):
            while wstate["issued"] < min(upto, len(plan)):
                i = wstate["issued"]
                name, ap = plan[i]
                t, b = slots.items[i % NSLOT]
                if len(ap.shape) == 3:
                    dst = t[:].rearrange("p (k c) -> p k c", c=128)
                else:
                    dst = t[:]
                S.dma("pool", dst, ap, writes=b.r())
                wslot_of[i] = (t, b)
                wstate["issued"] += 1

        def w_next(name):
            i = wstate["used"]
            assert plan[i][0] == name, (plan[i][0], name)
            w_issue(i + 1 + PF)
            wstate["used"] += 1
            t, b = wslot_of[i]
            return i, t, b

        def w_check(i):
            assert wstate["issued"] <= i + NSLOT, ("weight evicted", plan[i][0])

        def mm(ps, lhsT, rhs, start, stop, reads, pb):
            S.op("pe", lambda e: e.matmul(ps, lhsT, rhs, start=start, stop=stop), reads=reads, writes=pb.r())

        def proj_block(ps, pb, wi, wt, wb, rhs_buf, rhs_B, c0, ln, nk=KC, last_stop=True):
            w_check(wi)
            w3 = wt[:].rearrange("p (k c) -> p k c", c=128)
            for k in range(nk):
                mm(ps[:, 0:ln], w3[:, k, :], rhs_buf[:, k, c0:c0 + ln], k == 0, (k == nk - 1) and last_stop,
                   wb.r() + rhs_B.r(c0, c0 + ln), pb)

        def rmsnorm(hi, gname, phase_idx):
            t0, npr, has_s, sc0, nt = _half_geom(hi)
            for (c0, ln) in _blocks0(hi):
                ps, pb = PS.next()
                for k in range(KC):
                    sq, sqb = TB.next()
                    if k % 3 != 2:
                        S.op("act", lambda e: e.activation(out=sq[:, 0:ln], in_=xT[:, k, c0:c0 + ln], func=AF.Square),
                             reads=BxT.r(c0, c0 + ln), writes=sqb.r())
                    else:
                        S.op("dve", lambda e: e.tensor_tensor(out=sq[:, 0:ln], in0=xT[:, k, c0:c0 + ln], in1=xT[:, k, c0:c0 + ln], op=ALU.mult),
                             reads=BxT.r(c0, c0 + ln), writes=sqb.r())
                    mm(ps[:, 0:ln], ones[:, 0, :], sq[:, 0:ln], k == 0, k == KC - 1, sqb.r() + Bones.r(), pb)
                sd, sdb = TF.next()
                S.op("act", lambda e: e.activation(out=sd[:, 0:ln], in_=ps[:, 0:ln], func=AF.Ln, bias=RMS_EPS, scale=1.0),
                     reads=pb.r(), writes=sdb.r())
                rs, rsb = TF.next()
                S.op("act", lambda e: e.activation(out=rs[:, 0:ln], in_=sd[:, 0:ln], func=AF.Exp, scale=-0.5), reads=sdb.r(), writes=rsb.r())
                for k in range(KC):
                    S.op("dve", lambda e: e.scalar_tensor_tensor(out=xn[:, k, c0:c0 + ln], in0=xT[:, k, c0:c0 + ln],
                                                                 scalar=P(gname, k), in1=rs[:, 0:ln],
                                                                 op0=ALU.mult, op1=ALU.mult),
                         reads=BxT.r(c0, c0 + ln) + rsb.r() + Bpar.r(), writes=Bxn.r(c0, c0 + ln))
            if has_s:
                for k in range(KC):
                    v = xn[:, k, sc0:nt].rearrange("p (s c) -> p s c", c=SW)[:, :, 0:H]
                    S.op("dve", lambda e: e.memset(v, 0.0), writes=Bxn.r(sc0, nt))
            if phase_idx is not None:
                if hi == 0:
                    S.op("dve", lambda e: e.tensor_copy(out=cxn[:, phase_idx, :, :], in_=xn[:, :, nt - H:nt]),
                         reads=Bxn.r(nt - H, nt), writes=Bcxn.r())
                else:
                    S.op("dve", lambda e: e.tensor_copy(out=xn[:, :, 0:H], in_=cxn[:, phase_idx, :, :]),
                         reads=Bcxn.r(), writes=Bxn.r(0, H))
            elif hi == 0 or True:
                S.op("dve", lambda e: e.memset(xn[:, :, 0:H], 0.0), writes=Bxn.r(0, H))

        def out_proj(hi, wname, mix, Bmix):
            ws = [w_next((wname, o)) for o in range(8)]
            for (c0, ln) in _blocks0(hi):
                for o in range(8):
                    wi, wt, wb = ws[o]
                    ps, pb = PS.next()
                    proj_block(ps, pb, wi, wt, wb, mix, Bmix, c0, ln)
                    S.op("dve", lambda e: e.tensor_tensor(out=xT[:, o, c0:c0 + ln], in0=xT[:, o, c0:c0 + ln],
                                                          in1=ps[:, 0:ln], op=ALU.add),
                         reads=pb.r() + BxT.r(c0, c0 + ln), writes=BxT.r(c0, c0 + ln))

        def transpose_out(src_aps, nrows, dsts):
            ps, pb = PS.next()
            n = len(src_aps)
            for i, (ap, rr) in enumerate(src_aps):
                S.op("pe", lambda e: e.transpose(ps[0:nrows, 128 * i:128 * i + 128], ap, identf[:]),
                     reads=rr + Bid.r(), writes=pb.r())
            st, stb = TF.next()
            S.op("act", lambda e: e.activation(out=st[0:nrows, 0:128 * n], in_=ps[0:nrows, 0:128 * n], func=AF.Copy),
                 reads=pb.r(), writes=stb.r())
            for (dap, r0, r1) in dsts:
                S.dma("sp", dap, st[r0:r1, 0:128 * n], reads=stb.r())

        def load_x(hi):
            t0, npr, has_s, sc0, nt = _half_geom(hi)
            S.op("dve", lambda e: e.memset(xT[:, :, 0:H], 0.0), writes=BxT.r(0, H))
            if has_s:
                for k in range(KC):
                    v = xT[:, k, sc0:nt].rearrange("p (s c) -> p s c", c=SW)[:, :, 0:H]
                    S.op("dve", lambda e: e.memset(v, 0.0), writes=BxT.r(sc0, nt))
            ntile = npr // 128 + (1 if has_s else 0)
            for ti in range(ntile):
                xi, xib = TF.next()
                xi2, xib2 = TF.next()
                is_s = ti == npr // 128
                src = xs_d if is_s else xp_d[t0 + 128 * ti:t0 + 128 * ti + 128, :]
                S.dma("sp", xi[:], src[:, 0:512], writes=xib.r())
                S.dma("sp", xi2[:], src[:, 512:1024], writes=xib2.r())
                for half, (xt_, xb_) in enumerate(((xi, xib), (xi2, xib2))):
                    ps, pb = PS.next()
                    for q in range(4):
                        S.op("pe", lambda e: e.transpose(ps[:, 128 * q:128 * q + 128], xt_[:, 128 * q:128 * q + 128], identf[:]),
                             reads=xb_.r() + Bid.r(), writes=pb.r())
                    eng = "act" if half == 0 else "dve"
                    if not is_s:
                        c0 = H + 128 * ti
                        dst = xT[:, 4 * half:4 * half + 4, c0:c0 + 128]
                        src_ps = ps[:, :].rearrange("p (k c) -> p k c", c=128)
                        if eng == "act":
                            S.op("act", lambda e: e.activation(out=dst, in_=src_ps, func=AF.Copy), reads=pb.r(), writes=BxT.r(c0, c0 + 128))
                        else:
                            S.op("dve", lambda e: e.tensor_copy(out=dst, in_=src_ps), reads=pb.r(), writes=BxT.r(c0, c0 + 128))
                    else:
                        for q in range(4):
                            k = 4 * half + q
                            dst = xT[:, k, sc0:nt].rearrange("p (s c) -> p s c", c=SW)[:, :, H:SW]
                            src_ps = ps[:, 128 * q:128 * q + 128].rearrange("p (s c) -> p s c", c=TS)
                            if eng == "act":
                                S.op("act", lambda e: e.activation(out=dst, in_=src_ps, func=AF.Copy), reads=pb.r(), writes=BxT.r(sc0, nt))
                            else:
                                S.op("dve", lambda e: e.tensor_copy(out=dst, in_=src_ps), reads=pb.r(), writes=BxT.r(sc0, nt))

        def final_out(hi):
            t0, npr, has_s, sc0, nt = _half_geom(hi)
            ntile = npr // 128 + (1 if has_s else 0)
            for ti in range(ntile):
                is_s = ti == npr // 128
                if not is_s:
                    c0, c1 = H + 128 * ti, H + 128 * ti + 128

                    def cols(t3, k):
                        return t3[:, k, c0:c1]
                else:
                    c0, c1 = sc0, nt

                    def cols(t3, k):
                        return t3[:, k, sc0:nt].rearrange("p (s c) -> p s c", c=SW)[:, :, H:SW]
                ps, pb = PS.next()
                for k in range(KC):
                    sq, sqb = TB.next()
                    sqv = sq[:, 0:128] if not is_s else sq[:, 0:128].rearrange("p (s c) -> p s c", c=TS)
                    S.op("act", lambda e: e.activation(out=sqv, in_=cols(xT, k), func=AF.Square),
                         reads=BxT.r(c0, c1), writes=sqb.r())
                    mm(ps[:, 0:128], ones[:, 0, :], sq[:, 0:128], k == 0, k == KC - 1, sqb.r() + Bones.r(), pb)
                sd, sdb = TF.next()
                S.op("act", lambda e: e.activation(out=sd[:, 0:128], in_=ps[:, 0:128], func=AF.Ln, bias=RMS_EPS, scale=1.0),
                     reads=pb.r(), writes=sdb.r())
                rs, rsb = TF.next()
                S.op("act", lambda e: e.activation(out=rs[:, 0:128], in_=sd[:, 0:128], func=AF.Exp, scale=-0.5), reads=sdb.r(), writes=rsb.r())
                rsv = rs[:, 0:128] if not is_s else rs[:, 0:128].rearrange("p (s c) -> p s c", c=TS)
                ya, yab = TF.next()
                yb_, ybb = TF.next()
                for k in range(KC):
                    yt = ya if k < 4 else yb_
                    ytb = yab if k < 4 else ybb
                    o = yt[:, 128 * (k % 4):128 * (k % 4) + 128]
                    if is_s:
                        o = o.rearrange("p (s c) -> p s c", c=TS)
                    S.op("dve", lambda e: e.scalar_tensor_tensor(out=o, in0=cols(xT, k), scalar=P("nfin", k), in1=rsv,
                                                                 op0=ALU.mult, op1=ALU.mult),
                         reads=BxT.r(c0, c1) + rsb.r() + Bpar.r(), writes=ytb.r())
                dst_rows = ys_d if is_s else yp_d[t0 + 128 * ti:t0 + 128 * ti + 128, :]
                for half, (yt, ytb) in enumerate(((ya, yab), (yb_, ybb))):
                    ps2, pb2 = PS.next()
                    for q in range(4):
                        S.op("pe", lambda e: e.transpose(ps2[:, 128 * q:128 * q + 128], yt[:, 128 * q:128 * q + 128], identf[:]),
                             reads=ytb.r() + Bid.r(), writes=pb2.r())
                    st, stb = TF.next()
                    if half == 0:
                        S.op("act", lambda e: e.activation(out=st[:, :], in_=ps2[:, :], func=AF.Copy), reads=pb2.r(), writes=stb.r())
                    else:
                        S.op("dve", lambda e: e.tensor_copy(out=st[:, :], in_=ps2[:, :]), reads=pb2.r(), writes=stb.r())
                    S.dma("sp", dst_rows[:, 512 * half:512 * half + 512], st[:, :], reads=stb.r())

        SH = {}

        def shget(ph, key, make):
            if SH.get("on"):
                if key not in SH:
                    SH[key] = make(SH["ph"])
                return SH[key]
            return make(ph)

        def mk_buf(p, name, shape, dt, ncols=None):
            t = p.enter_context(nc.sbuf_tensor(name, list(shape), dt))
            return t, Buf(t, name, ncols)

        def ffn(hi, l, tail):
            t0, npr, has_s, sc0, nt = _half_geom(hi)
            b3 = _blocks3(hi)
            ps_mode(8)
            with ExitStack() as ph_:
                ph = SH["ph"] if SH.get("on") else ph_
                jmax = max(g[1] for g in GROUPS)
                assert jmax == KC
                tag = "" if SH.get("on") else f"_{l}"
                act, Bact = shget(ph, ("mixact", hi), lambda p: mk_buf(p, f"mixact_{hi}{tag}", [128, jmax, nt], BF16, nt))
                FLT = shget(ph, ("LT", hi), lambda p: Pool(p, nc, f"lt{hi}_", 15, (128, 512), F32)) if SH.get("on") else None
                if has_s:
                    stt, Bstt = shget(ph, ("stt", hi), lambda p: mk_buf(p, f"stt_{hi}{tag}", [32, 2, jmax * 128], BF16))
                    zt, Bzt = shget(ph, ("zt", hi), lambda p: mk_buf(p, f"zt_{hi}{tag}", [128, 2, jmax, 34], F32))
                for (j0, jn) in GROUPS:
                    if has_s:
                        S.dma("pool", stt[:, 0, 0:jn * 128], sffn_d[l][:, 128 * j0:128 * (j0 + jn)], writes=Bstt.r())
                        S.dma("pool", stt[:, 1, 0:jn * 128], sffn_d[l][:, FF + 128 * j0:FF + 128 * (j0 + jn)], writes=Bstt.r())
                    for j in range(j0, j0 + jn):
                        jl = j - j0
                        zc = {}
                        for which, wname in ((0, "upg"), (1, "upu")):
                            wi, wt, wb = w_next((wname, l, j))
                            fch = j + NJ * which
                            o_w, _ = PAR[f"fcw{l}"]
                            o_b, _ = PAR[f"fcb{l}"]
                            wcol = lambda tap: par[:, o_w + 3 * fch + tap:o_w + 3 * fch + tap + 1]
                            bcol = par[:, o_b + fch:o_b + fch + 1]
                            zc[which] = []
                            for bi, (c0, ln) in enumerate(b3):
                                last = bi == len(b3) - 1
                                ps, pb = PS.next()
                                inj = has_s and last
                                proj_block(ps, pb, wi, wt, wb, xn, Bxn, c0, ln, last_stop=not inj)
                                if inj:
                                    mm(ps[:, 0:ln], stt[:, which, 128 * jl:128 * jl + 128], sel2[:, 0:ln], False, True,
                                       Bstt.r() + Bsel.r(), pb)
                                acc, accb = TF.next()
                                lo = ln - H
                                S.op("act", lambda e: e.activation(out=acc[:, 0:lo], in_=ps[:, H:ln], func=AF.Identity,
                                                                   scale=wcol(2), bias=bcol),
                                     reads=pb.r() + Bpar.r(), writes=accb.r())
                                if which == 0 and FLT is not None:
                                    tb_, tbb_ = FLT.next()
                                    S.op("act", lambda e: e.activation(out=tb_[:, 0:lo], in_=ps[:, H - 1:ln - 1], func=AF.Copy, scale=wcol(1)),
                                         reads=pb.r() + Bpar.r(), writes=tbb_.r())
                                    S.op("dve", lambda e: e.scalar_tensor_tensor(out=acc[:, 0:lo], in0=ps[:, H - 2:ln - 2], scalar=wcol(0),
                                                                                 in1=acc[:, 0:lo], op0=ALU.mult, op1=ALU.add),
                                         reads=pb.r() + Bpar.r() + accb.r(), writes=accb.r())
                                    S.op("pool", lambda e: e.tensor_tensor(out=acc[:, 0:lo], in0=acc[:, 0:lo], in1=tb_[:, 0:lo], op=ALU.add),
                                         reads=accb.r() + tbb_.r(), writes=accb.r(), dur=150.0 + 2.2 * lo)
                                else:
                                    S.op("dve", lambda e: e.scalar_tensor_tensor(out=acc[:, 0:lo], in0=ps[:, H - 1:ln - 1], scalar=wcol(1),
                                                                                 in1=acc[:, 0:lo], op0=ALU.mult, op1=ALU.add),
                                         reads=pb.r() + Bpar.r() + accb.r(), writes=accb.r())
                                    S.op("dve", lambda e: e.scalar_tensor_tensor(out=acc[:, 0:lo], in0=ps[:, H - 2:ln - 2], scalar=wcol(0),
                                                                                 in1=acc[:, 0:lo], op0=ALU.mult, op1=ALU.add),
                                         reads=pb.r() + Bpar.r() + accb.r(), writes=accb.r())
                                if has_s and last:
                                    pl = sc0 - c0
                                    S.op("act", lambda e: e.activation(out=zt[:, which, jl, 0:2], in_=ps[:, pl - 2:pl], func=AF.Copy),
                                         reads=pb.r(), writes=Bzt.r())
                                    sv = ps[:, pl:ln].rearrange("p (s c) -> p s c", c=SW)[:, :, H + 6:H + 8]
                                    dv = zt[:, which, jl, 2:34].rearrange("p (s c) -> p s c", c=2)
                                    S.op("act", lambda e: e.activation(out=dv, in_=sv, func=AF.Copy), reads=pb.r(), writes=Bzt.r())
                                zc[which].append((acc, accb, c0 + H, lo))
                        for (ag, agb, oc, lo), (au, aub, _, _) in zip(zc[0], zc[1]):
                            S.op("act", lambda e: e.activation(out=ag[:, 0:lo], in_=ag[:, 0:lo], func=AF.Gelu_apprx_tanh),
                                 reads=agb.r(), writes=agb.r())
                            S.op("dve", lambda e: e.tensor_tensor(out=act[:, jl, oc:oc + lo], in0=ag[:, 0:lo], in1=au[:, 0:lo], op=ALU.mult),
                                 reads=agb.r() + aub.r(), writes=Bact.r(oc, oc + lo))
                    if has_s:
                        for which in range(2):
                            for q0 in range(0, jn, 4):
                                qn = min(4, jn - q0)
                                f0 = FF * which + 128 * (j0 + q0)
                                srcs = [(zt[:, which, q0 + i, :], Bzt.r()) for i in range(qn)]
                                transpose_out(srcs, 34, [(o_pffn[l][:, f0:f0 + 128 * qn], 0, 2),
                                                         (o_sffn[l][:, f0:f0 + 128 * qn], 2, 34)])
                    wds = [w_next(("wd", l, j)) for j in range(j0, j0 + jn)]
                    for bi0, (c0, ln) in enumerate(_blocks0(hi)):
                        if bi0 == 0:
                            NOPEN = 4
                            pss = [PS.next() for o in range(NOPEN)]
                            for o in range(NOPEN):
                                ps, pb = pss[o]
                                for jl in range(jn - 1):
                                    wi, wt, wb = wds[jl]
                                    w_check(wi)
                                    mm(ps[:, 0:ln], wt[:, 128 * o:128 * o + 128], act[:, jl, c0:c0 + ln], jl == 0, False,
                                       wb.r() + Bact.r(c0, c0 + ln), pb)
                            for o in range(NOPEN):
                                ps, pb = pss[o]
                                wi, wt, wb = wds[jn - 1]
                                mm(ps[:, 0:ln], wt[:, 128 * o:128 * o + 128], act[:, jn - 1, c0:c0 + ln], False, True,
                                   wb.r() + Bact.r(c0, c0 + ln), pb)
                                S.op("dve", lambda e: e.tensor_tensor(out=xT[:, o, c0:c0 + ln], in0=xT[:, o, c0:c0 + ln],
                                                                      in1=ps[:, 0:ln], op=ALU.add),
                                     reads=pb.r() + BxT.r(c0, c0 + ln), writes=BxT.r(c0, c0 + ln))
                        for o in range(NOPEN if bi0 == 0 else 0, 8):
                            ps, pb = PS.next()
                            for jl in range(jn):
                                wi, wt, wb = wds[jl]
                                w_check(wi)
                                mm(ps[:, 0:ln], wt[:, 128 * o:128 * o + 128], act[:, jl, c0:c0 + ln], jl == 0, jl == jn - 1,
                                   wb.r() + Bact.r(c0, c0 + ln), pb)
                            S.op("dve", lambda e: e.tensor_tensor(out=xT[:, o, c0:c0 + ln], in0=xT[:, o, c0:c0 + ln],
                                                                  in1=ps[:, 0:ln], op=ALU.add),
                                 reads=pb.r() + BxT.r(c0, c0 + ln), writes=BxT.r(c0, c0 + ln))
                tail()
                if not SH.get("on"):
                    S.barrier()

        def lru_mixer(hi, tail):
            t0, npr, has_s, sc0, nt = _half_geom(hi)
            b3 = _blocks3(hi)
            ps_mode(8)
            with ExitStack() as ph_:
                ph = SH["ph"] if SH.get("on") else ph_
                LT = shget(ph, ("LT", hi), lambda p: Pool(p, nc, f"lt{hi}_", 15, (128, 512), F32))
                mix, Bmix = shget(ph, ("mixact", hi), lambda p: mk_buf(p, f"mixact_{hi}", [128, KC, nt], BF16, nt))
                wax = ph.enter_context(nc.sbuf_tensor(f"wax_{hi}", [128, 2, 8, 128], BF16))
                Bwax = Buf(wax, "wax")
                S.dma("pool", wax[:, 0, :, :], w_lru_a.rearrange("n c d -> c n d"), writes=Bwax.r())
                S.dma("pool", wax[:, 1, :, :], w_lru_x.rearrange("n c d -> c n d"), writes=Bwax.r())
                if has_s:
                    st3 = ph.enter_context(nc.sbuf_tensor(f"st3_{hi}", [48, D], BF16))
                    Bst3 = Buf(st3, "st3")
                    S.dma("pool", st3[:], slc_d, writes=Bst3.r())
                    h0in = ph.enter_context(nc.sbuf_tensor(f"h0in_{hi}", [16, D], F32))
                    Bh0in = Buf(h0in, "h0in")
                    S.dma("sp", h0in[:], slh_d, writes=Bh0in.r())
                    h0T = ph.enter_context(nc.sbuf_tensor(f"h0T_{hi}", [128, 8, 16], F32))
                    Bh0T = Buf(h0T, "h0T")
                    ps, pb = PS.next()
                    for n in range(8):
                        S.op("pe", lambda e: e.transpose(ps[:, 16 * n:16 * n + 16], h0in[:, 128 * n:128 * n + 128], identf[0:16, 0:16]),
                             reads=Bh0in.r() + Bid.r(), writes=pb.r())
                    S.op("act", lambda e: e.activation(out=h0T[:].rearrange("p n s -> p (n s)"), in_=ps[:, 0:128], func=AF.Copy),
                         reads=pb.r(), writes=Bh0T.r())
                    rt = ph.enter_context(nc.sbuf_tensor(f"rt_{hi}", [128, 8, 51], F32))
                    Brt = Buf(rt, "rt")
                    ht = ph.enter_context(nc.sbuf_tensor(f"ht_{hi}", [128, 8, 17], F32))
                    Bht = Buf(ht, "ht")
                hprev = ph.enter_context(nc.sbuf_tensor(f"hprev_{hi}", [128, 8, 4], F32))
                Bhp = Buf(hprev, "hprev")
                for n in range(8):
                    wgi, wgt, wgb = w_next(("gate", n))
                    wri, wrt, wrb = w_next(("rec", n))
                    o_w, _ = PAR["ccw"]
                    wcol = lambda tap: par[:, o_w + 4 * n + tap:o_w + 4 * n + tap + 1]
                    units = []
                    for bi, (c0, ln) in enumerate(b3):
                        last = bi == len(b3) - 1
                        lo = ln - H
                        oc = c0 + H
                        psg, pbg = PS.next()
                        proj_block(psg, pbg, wgi, wgt, wgb, xn, Bxn, c0, ln)
                        S.op("act", lambda e: e.activation(out=mix[:, n, oc:oc + lo], in_=psg[:, H:ln], func=AF.Gelu_apprx_tanh),
                             reads=pbg.r(), writes=Bmix.r(oc, oc + lo))
                        psr, pbr = PS.next()
                        inj = has_s and last
                        proj_block(psr, pbr, wri, wrt, wrb, xn, Bxn, c0, ln, last_stop=not inj)
                        if inj:
                            mm(psr[:, 0:ln], st3[:, 128 * n:128 * n + 128], sel3[:, 0:ln], False, True, Bst3.r() + Bsel.r(), pbr)
                        xc, xcb = LT.next()
                        S.op("dve", lambda e: e.tensor_scalar(out=xc[:, 0:lo], in0=psr[:, H:ln], scalar1=wcol(3), scalar2=P("ccb", n),
                                                              op0=ALU.mult, op1=ALU.add),
                             reads=pbr.r() + Bpar.r(), writes=xcb.r())
                        for tap in (2, 1, 0):
                            sh = 3 - tap
                            S.op("dve", lambda e: e.scalar_tensor_tensor(out=xc[:, 0:lo], in0=psr[:, H - sh:ln - sh], scalar=wcol(tap),
                                                                         in1=xc[:, 0:lo], op0=ALU.mult, op1=ALU.add),
                                 reads=pbr.r() + Bpar.r() + xcb.r(), writes=xcb.r())
                        if has_s and last:
                            pl = sc0 - c0
                            S.op("act", lambda e: e.activation(out=rt[:, n, 0:3], in_=psr[:, pl - 3:pl], func=AF.Copy),
                                 reads=pbr.r(), writes=Brt.r())
                            sv = psr[:, pl:ln].rearrange("p (s c) -> p s c", c=SW)[:, :, H + 5:H + 8]
                            dv = rt[:, n, 3:51].rearrange("p (s c) -> p s c", c=3)
                            S.op("act", lambda e: e.activation(out=dv, in_=sv, func=AF.Copy), reads=pbr.r(), writes=Brt.r())
                        units.append((bi, c0, ln, last, lo, oc, xc, xcb))
                    for (bi, c0, ln, last, lo, oc, xc, xcb) in units:
                        xb, xbb = TB.next()
                        S.op("dve", lambda e: e.tensor_copy(out=xb[:, 0:lo], in_=xc[:, 0:lo]), reads=xcb.r(), writes=xbb.r())
                        psa, pba = PS.next()
                        mm(psa[:, 0:lo], wax[:, 0, n, :], xb[:, 0:lo], True, True, Bwax.r() + xbb.r(), pba)
                        psx, pbx = PS.next()
                        mm(psx[:, 0:lo], wax[:, 1, n, :], xb[:, 0:lo], True, True, Bwax.r() + xbb.r(), pbx)
                        A, Ab = LT.next()
                        I, Ib = TF.next()
                        S2, S2b = TF.next()
                        S.op("act", lambda e: e.activation(out=A[:, 0:lo], in_=psa[:, 0:lo], func=AF.Tanh, bias=hbias[:, 0, n:n + 1], scale=0.5),
                             reads=pba.r() + Blruc.r(), writes=Ab.r())
                        S.op("act", lambda e: e.activation(out=I[:, 0:lo], in_=psx[:, 0:lo], func=AF.Tanh, bias=hbias[:, 1, n:n + 1], scale=0.5),
                             reads=pbx.r() + Blruc.r(), writes=Ib.r())
                        S.op("act", lambda e: e.activation(out=S2[:, 0:lo], in_=A[:, 0:lo], func=AF.Exp, scale=lruc[:, n, 1:2], bias=lruc[:, n, 1:2]),
                             reads=Ab.r() + Blruc.r(), writes=S2b.r())
                        S.op("act", lambda e: e.activation(out=A[:, 0:lo], in_=A[:, 0:lo], func=AF.Exp, scale=lruc[:, n, 0:1], bias=lruc[:, n, 0:1]),
                             reads=Ab.r() + Blruc.r(), writes=Ab.r())
                        S.op("dve", lambda e: e.scalar_tensor_tensor(out=I[:, 0:lo], in0=I[:, 0:lo], scalar=1.0, in1=xc[:, 0:lo], op0=ALU.add, op1=ALU.mult),
                             reads=Ib.r() + xcb.r(), writes=Ib.r())
                        S.op("act", lambda e: e.activation(out=S2[:, 0:lo], in_=S2[:, 0:lo], func=AF.Sqrt, scale=-0.25, bias=0.25),
                             reads=S2b.r(), writes=S2b.r())
                        S.op("dve", lambda e: e.tensor_tensor(out=I[:, 0:lo], in0=I[:, 0:lo], in1=S2[:, 0:lo], op=ALU.mult),
                             reads=Ib.r() + S2b.r(), writes=Ib.r())
                        if has_s and last:
                            pl = sc0 - oc
                            av = A[:, pl:lo].rearrange("p (s c) -> p s c", c=SW)[:, :, 0:H]
                            bv = I[:, pl:lo].rearrange("p (s c) -> p s c", c=SW)
                            S.op("dve", lambda e: e.memset(av, 0.0), writes=Ab.r())
                            S.op("dve", lambda e: e.memset(bv[:, :, 0:H - 1], 0.0), writes=Ib.r())
                            S.op("dve", lambda e: e.tensor_copy(out=bv[:, :, H - 1:H], in_=h0T[:, n, :].unsqueeze(2)),
                                 reads=Bh0T.r(), writes=Ib.r())
                        if bi == 0:
                            init = 0.0 if hi == 0 else ch[:, n:n + 1]
                            ird = [] if hi == 0 else Bch.r()
                        else:
                            init = hprev[:, n, bi - 1:bi]
                            ird = Bhp.r()
                        S.op("dve", lambda e: e.tensor_tensor_scan(out=xc[:, 0:lo], data0=A[:, 0:lo], data1=I[:, 0:lo], initial=init,
                                                                   op0=ALU.mult, op1=ALU.add),
                             reads=Ab.r() + Ib.r() + ird + xcb.r(), writes=xcb.r(), dur=200.0 + 2.3 * lo)
                        S.op("act", lambda e: e.activation(out=hprev[:, n, bi:bi + 1], in_=xc[:, lo - 1:lo], func=AF.Copy),
                             reads=xcb.r(), writes=Bhp.r())
                        if last:
                            if hi == 0:
                                S.op("act", lambda e: e.activation(out=ch[:, n:n + 1], in_=xc[:, lo - 1:lo], func=AF.Copy),
                                     reads=xcb.r(), writes=Bch.r())
                            else:
                                pl = sc0 - oc
                                S.op("act", lambda e: e.activation(out=ht[:, n, 0:1], in_=xc[:, pl - 1:pl], func=AF.Copy),
                                     reads=xcb.r(), writes=Bht.r())
                                sv = xc[:, pl:lo].rearrange("p (s c) -> p s c", c=SW)[:, :, SW - 1:SW]
                                S.op("act", lambda e: e.activation(out=ht[:, n, 1:17].unsqueeze(2), in_=sv, func=AF.Copy),
                                     reads=xcb.r(), writes=Bht.r())
                        S.op("dve", lambda e: e.tensor_tensor(out=mix[:, n, oc:oc + lo], in0=xc[:, 0:lo], in1=mix[:, n, oc:oc + lo], op=ALU.mult),
                             reads=xcb.r() + Bmix.r(oc, oc + lo), writes=Bmix.r(oc, oc + lo))
                if has_s:
                    for q0 in (0, 4):
                        transpose_out([(rt[:, q0 + i, :], Brt.r()) for i in range(4)], 51,
                                      [(o_plc[:, 512 * (q0 // 4):512 * (q0 // 4) + 512], 0, 3),
                                       (o_slc[:, 512 * (q0 // 4):512 * (q0 // 4) + 512], 3, 51)])
                        transpose_out([(ht[:, q0 + i, :], Bht.r()) for i in range(4)], 17,
                                      [(o_plh[:, 512 * (q0 // 4):512 * (q0 // 4) + 512], 0, 1),
                                       (o_slh[:, 512 * (q0 // 4):512 * (q0 // 4) + 512], 1, 17)])
                out_proj(hi, "wo1", mix, Bmix)
                tail()
                if not SH.get("on"):
                    S.barrier()

        def gn_block_3d(ps_o, pb_o, width, cn, qd_ap, dst_fn, c_lo, c_hi, h, Bmix, BSG):
            o, ob = TF.next()
            if qd_ap is not None:
                S.op("dve", lambda e: e.tensor_tensor(out=o[:, 0:width].rearrange("p (a b) -> p a b", b=128),
                                                      in0=ps_o[:, 0:width].rearrange("p (a b) -> p a b", b=128), in1=qd_ap, op=ALU.mult),
                     reads=pb_o.r() + Bcon.r(), writes=ob.r())
            else:
                S.op("act", lambda e: e.activation(out=o[:, 0:width], in_=ps_o[:, 0:width], func=AF.Copy), reads=pb_o.r(), writes=ob.r())
            o16, o16b = TB.next()
            S.op("act", lambda e: e.activation(out=o16[:, 0:width], in_=o[:, 0:width], func=AF.Copy), reads=ob.r(), writes=o16b.r())
            q16, q16b = TB.next()
            S.op("act", lambda e: e.activation(out=q16[:, 0:width], in_=o[:, 0:width], func=AF.Square), reads=ob.r(), writes=q16b.r())
            psm, pbm = PSL.next()
            mm(psm[:, 0:width], ones[:, 2, :], o16[:, 0:width], True, True, o16b.r() + Bones.r(), pbm)
            psq, pbq = PSL.next()
            mm(psq[:, 0:width], ones[:, 2, :], q16[:, 0:width], True, True, q16b.r() + Bones.r(), pbq)
            mean, meanb = TF.next()
            S.op("act", lambda e: e.activation(out=mean[:, 0:width], in_=psm[:, 0:width], func=AF.Copy), reads=pbm.r(), writes=meanb.r())
            var, varb = TF.next()
            S.op("dve", lambda e: e.tensor_tensor(out=var[:, 0:width], in0=mean[:, 0:width], in1=mean[:, 0:width], op=ALU.mult),
                 reads=meanb.r(), writes=varb.r())
            S.op("dve", lambda e: e.tensor_tensor(out=var[:, 0:width], in0=psq[:, 0:width], in1=var[:, 0:width], op=ALU.subtract),
                 reads=pbq.r() + varb.r(), writes=varb.r())
            S.op("dve", lambda e: e.tensor_scalar_max(out=var[:, 0:width], in0=var[:, 0:width], scalar1=0.0), reads=varb.r(), writes=varb.r())
            S.op("act", lambda e: e.activation(out=var[:, 0:width], in_=var[:, 0:width], func=AF.Ln, bias=GN_EPS, scale=1.0),
                 reads=varb.r(), writes=varb.r())
            rs, rsb = TF.next()
            S.op("act", lambda e: e.activation(out=rs[:, 0:width], in_=var[:, 0:width], func=AF.Exp, scale=-0.5), reads=varb.r(), writes=rsb.r())
            S.op("dve", lambda e: e.tensor_tensor(out=o[:, 0:width], in0=o[:, 0:width], in1=mean[:, 0:width], op=ALU.subtract),
                 reads=ob.r() + meanb.r(), writes=ob.r())
            S.op("dve", lambda e: e.tensor_tensor(out=o[:, 0:width], in0=o[:, 0:width], in1=rs[:, 0:width], op=ALU.mult),
                 reads=ob.r() + rsb.r(), writes=ob.r())
            dst, sgv, ov = dst_fn(o)
            S.op("dve", lambda e: e.scalar_tensor_tensor(out=dst, in0=ov, scalar=P("gng", h), in1=sgv, op0=ALU.mult, op1=ALU.mult),
                 reads=ob.r() + BSG.r(c_lo, c_hi) + Bpar.r(), writes=Bmix.r(c_lo, c_hi))


        def even_mixer(hi, tail):
            t0, npr, has_s, sc0, nt = _half_geom(hi)
            b0 = _blocks0(hi)
            nch = npr // 128
            UW = 30 + npr + (NSQ * 38 if has_s else 0)
            us0 = 30 + npr
            ps_mode(6)
            with ExitStack() as ph:
                mix = ph.enter_context(nc.sbuf_tensor(f"mix_{hi}", [128, KC, nt], BF16))
                Bmix = Buf(mix, "mix", nt)
                S.op("dve", lambda e: e.memset(mix[:, :, :], 0.0), writes=Bmix.r())
                with ExitStack() as pa:
                    uT = pa.enter_context(nc.sbuf_tensor(f"uT_{hi}", [128, 4, UW], BF16))
                    BuTc = [Buf(uT, f"uT{c_}", UW) for c_ in range(4)]

                    class _AllU:
                        def r(self, c0=None, c1=None):
                            out = []
                            for b_ in BuTc:
                                out += b_.r(c0, c1)
                            return out
                    BuT = _AllU()
                    diag = pa.enter_context(nc.sbuf_tensor(f"diag_{hi}", [128, 4, 31, 128], BF16))
                    Bdiagc = [Buf(diag, f"diag{c_}") for c_ in range(4)]
                    ut = pa.enter_context(nc.sbuf_tensor(f"ut_{hi}", [128, 4, 30], F32))
                    But = Buf(ut, "ut")
                    o_w, _ = PAR["caw"]
                    for c in range(4):
                        for k in range(31):
                            S.op("dve", lambda e: e.tensor_scalar(out=diag[:, c, k, :], in0=identb[:], scalar1=par[:, o_w + 31 * c + k:o_w + 31 * c + k + 1],
                                                                  scalar2=None, op0=ALU.mult),
                                 reads=Bidb.r() + Bpar.r(), writes=Bdiagc[c].r())
                    if hi == 0:
                        S.op("dve", lambda e: e.memset(uT[:, :, 0:30], 0.0), writes=BuT.r(0, 30))
                    else:
                        S.op("dve", lambda e: e.tensor_copy(out=uT[:, :, 0:30], in_=cu[:, :, :]), reads=Bcu.r(), writes=BuT.r(0, 30))
                    if has_s:
                        us = pa.enter_context(nc.sbuf_tensor(f"us_{hi}", [128, 4, 128], F32))
                        Bus = Buf(us, "us")
                        for q in range(4):
                            si, sib = TF.next()
                            S.dma("sp", si[0:120, :], sca_d[120 * q:120 * q + 120, :], writes=sib.r())
                            ps, pb = PS.next()
                            for c in range(4):
                                S.op("pe", lambda e: e.transpose(ps[:, 128 * c:128 * c + 120], si[0:120, 128 * c:128 * c + 128], identf[0:120, 0:120]),
                                     reads=sib.r() + Bid.r(), writes=pb.r())
                            for c in range(4):
                                dst = uT[:, c, us0 + 38 * 4 * q:us0 + 38 * 4 * (q + 1)].rearrange("p (s c) -> p s c", c=38)[:, :, 0:30]
                                src = ps[:, 128 * c:128 * c + 120].rearrange("p (s c) -> p s c", c=30)
                                S.op("act", lambda e: e.activation(out=dst, in_=src, func=AF.Copy), reads=pb.r(), writes=BuTc[c].r(us0, UW))
                        S.dma("sp", o_sca.rearrange("(s r) f -> s r f", r=30)[:, 0:22, :],
                              sca_d.rearrange("(s r) f -> s r f", r=30)[:, 8:30, :])
                    for c in range(4):
                        wli, wlt, wlb = w_next(("alin", c))
                        wgi, wgt, wgb = w_next(("agate", c))
                        for bi, (c0, ln) in enumerate(b0):
                            psl, pbl = PS.next()
                            proj_block(psl, pbl, wli, wlt, wlb, xn, Bxn, c0, ln)
                            psg, pbg = PS.next()
                            proj_block(psg, pbg, wgi, wgt, wgb, xn, Bxn, c0, ln)
                            sg, sgb = TF.next()
                            S.op("act", lambda e: e.activation(out=sg[:, 0:ln], in_=psg[:, 0:ln], func=AF.Sigmoid),
                                 reads=pbg.r(), writes=sgb.r())
                            p1 = min(c0 + ln, sc0)
                            pn = p1 - c0
                            if pn > 0:
                                uc = 30 + (c0 - H)
                                S.op("dve", lambda e: e.tensor_tensor(out=uT[:, c, uc:uc + pn], in0=psl[:, 0:pn], in1=sg[:, 0:pn], op=ALU.mult),
                                     reads=pbl.r() + sgb.r(), writes=BuTc[c].r(uc, uc + pn))
                                if p1 == sc0:
                                    S.op("dve", lambda e: e.tensor_tensor(out=ut[:, c, :], in0=psl[:, pn - 30:pn], in1=sg[:, pn - 30:pn], op=ALU.mult),
                                         reads=pbl.r() + sgb.r(), writes=But.r())
                            if has_s and c0 + ln > sc0:
                                pl = sc0 - c0
                                lv = psl[:, pl:ln].rearrange("p (s c) -> p s c", c=SW)[:, :, H:SW]
                                gv = sg[:, pl:ln].rearrange("p (s c) -> p s c", c=SW)[:, :, H:SW]
                                dv = uT[:, c, us0:UW].rearrange("p (s c) -> p s c", c=38)[:, :, 30:38]
                                S.op("dve", lambda e: e.tensor_tensor(out=dv, in0=lv, in1=gv, op=ALU.mult),
                                     reads=pbl.r() + sgb.r(), writes=BuTc[c].r(us0, UW))
                                S.op("dve", lambda e: e.tensor_tensor(out=us[:, c, :].rearrange("p (s c) -> p s c", c=TS), in0=lv, in1=gv, op=ALU.mult),
                                     reads=pbl.r() + sgb.r(), writes=Bus.r())
                    if hi == 0:
                        S.op("act", lambda e: e.activation(out=cu[:, :, :], in_=uT[:, :, us0 - 30:us0], func=AF.Copy), reads=BuT.r(us0 - 30, us0), writes=Bcu.r())
                    else:
                        transpose_out([(ut[:, c, :], But.r()) for c in range(4)], 30, [(o_pca[:, :], 0, 30)])
                        ps, pb = PS.next()
                        for c in range(4):
                            S.op("pe", lambda e: e.transpose(ps[:, 128 * c:128 * c + 128], us[:, c, :], identf[:]),
                                 reads=Bus.r() + Bid.r(), writes=pb.r())
                        st, stb = TF.next()
                        S.op("act", lambda e: e.activation(out=st[:, :], in_=ps[:, :], func=AF.Copy), reads=pb.r(), writes=stb.r())
                        osv = o_sca.rearrange("(s r) f -> s r f", r=30)
                        for s in range(NSQ):
                            S.dma("sp", osv[s, 22:30, :], st[8 * s:8 * s + 8, :], reads=stb.r())
                    for bi, (c0, ln) in enumerate(b0):
                        segs = []
                        p1 = min(c0 + ln, sc0)
                        if p1 > c0:
                            segs.append(("p", c0, p1 - c0))
                        if has_s and c0 + ln > sc0:
                            segs.append(("s", sc0, NSQ * TS))
                        for kind, s0, sl in segs:
                            cvs = []
                            psm, pbm = PSL.next()
                            psq, pbq = PSL.next()
                            for c in range(4):
                                ps, pb = PS.next()
                                for k in range(31):
                                    if kind == "p":
                                        ub = (s0 - H) + k
                                        rhs = uT[:, c, ub:ub + sl]
                                        urd = BuTc[c].r(ub, ub + sl)
                                    else:
                                        rhs = uT[:, c, us0:UW].rearrange("p (s c) -> p s c", c=38)[:, :, k:k + TS]
                                        urd = BuTc[c].r(us0, UW)
                                    mm(ps[:, 0:sl], diag[:, c, k, :], rhs, k == 0, k == 30, urd + Bdiagc[c].r(), pb)
                                cv, cvb = TF.next()
                                S.op("act", lambda e: e.activation(out=cv[:, 0:sl], in_=ps[:, 0:sl], func=AF.Identity, bias=P("cab", c), scale=1.0),
                                     reads=pb.r() + Bpar.r(), writes=cvb.r())
                                c16, c16b = TB.next()
                                S.op("act", lambda e: e.activation(out=c16[:, 0:sl], in_=cv[:, 0:sl], func=AF.Copy), reads=cvb.r(), writes=c16b.r())
                                q16, q16b = TB.next()
                                S.op("act", lambda e: e.activation(out=q16[:, 0:sl], in_=cv[:, 0:sl], func=AF.Square), reads=cvb.r(), writes=q16b.r())
                                mm(psm[:, 0:sl], ones[:, 1, :], c16[:, 0:sl], c == 0, c == 3, c16b.r() + Bones.r(), pbm)
                                mm(psq[:, 0:sl], ones[:, 1, :], q16[:, 0:sl], c == 0, c == 3, q16b.r() + Bones.r(), pbq)
                                cvs.append((cv, cvb))
                            mean, meanb = TF.next()
                            S.op("act", lambda e: e.activation(out=mean[:, 0:sl], in_=psm[:, 0:sl], func=AF.Copy), reads=pbm.r(), writes=meanb.r())
                            var, varb = TF.next()
                            S.op("dve", lambda e: e.tensor_tensor(out=var[:, 0:sl], in0=mean[:, 0:sl], in1=mean[:, 0:sl], op=ALU.mult),
                                 reads=meanb.r(), writes=varb.r())
                            S.op("dve", lambda e: e.tensor_tensor(out=var[:, 0:sl], in0=psq[:, 0:sl], in1=var[:, 0:sl], op=ALU.subtract),
                                 reads=pbq.r() + varb.r(), writes=varb.r())
                            S.op("dve", lambda e: e.tensor_scalar_max(out=var[:, 0:sl], in0=var[:, 0:sl], scalar1=0.0), reads=varb.r(), writes=varb.r())
                            S.op("act", lambda e: e.activation(out=var[:, 0:sl], in_=var[:, 0:sl], func=AF.Ln, bias=LN_EPS, scale=1.0),
                                 reads=varb.r(), writes=varb.r())
                            rs, rsb = TF.next()
                            S.op("act", lambda e: e.activation(out=rs[:, 0:sl], in_=var[:, 0:sl], func=AF.Exp, scale=-0.5), reads=varb.r(), writes=rsb.r())
                            for c in range(4):
                                cv, cvb = cvs[c]
                                S.op("dve", lambda e: e.tensor_tensor(out=cv[:, 0:sl], in0=cv[:, 0:sl], in1=mean[:, 0:sl], op=ALU.subtract),
                                     reads=cvb.r() + meanb.r(), writes=cvb.r())
                                S.op("dve", lambda e: e.tensor_tensor(out=cv[:, 0:sl], in0=cv[:, 0:sl], in1=rs[:, 0:sl], op=ALU.mult),
                                     reads=cvb.r() + rsb.r(), writes=cvb.r())
                                if kind == "p":
                                    dst = mix[:, c, s0:s0 + sl]
                                    src = cv[:, 0:sl]
                                    wr = Bmix.r(s0, s0 + sl)
                                else:
                                    dst = mix[:, c, sc0:nt].rearrange("p (s c) -> p s c", c=SW)[:, :, H:SW]
                                    src = cv[:, 0:sl].rearrange("p (s c) -> p s c", c=TS)
                                    wr = Bmix.r(sc0, nt)
                                S.op("act", lambda e: e.activation(out=dst, in_=src, func=AF.Silu, scale=P("lng", c), bias=P("lnb", c)),
                                     reads=cvb.r() + Bpar.r(), writes=wr)
                    S.barrier()
                with ExitStack() as pbk:
                    rope = pbk.enter_context(nc.sbuf_tensor(f"rope_{hi}", [128, 2, nt], F32))
                    Brope = Buf(rope, "rope")
                    S.dma("sp", rope[:], rope_d[hi], writes=Brope.r())
                    QT = pbk.enter_context(nc.sbuf_tensor(f"QT_{hi}", [128, nt], BF16))
                    KT = pbk.enter_context(nc.sbuf_tensor(f"KT_{hi}", [128, nt], BF16))
                    VT = pbk.enter_context(nc.sbuf_tensor(f"VT_{hi}", [128, nt], BF16))
                    SG = pbk.enter_context(nc.sbuf_tensor(f"SG_{hi}", [128, nt], F32))
                    BQT, BKT, BVT, BSG = Buf(QT, "QT", nt), Buf(KT, "KT", nt), Buf(VT, "VT", nt), Buf(SG, "SG", nt)
                    ntl = nch + (1 if has_s else 0)
                    Ktok = pbk.enter_context(nc.sbuf_tensor(f"Ktok_{hi}", [128, ntl, 128], BF16))
                    Vdec = pbk.enter_context(nc.sbuf_tensor(f"Vdec_{hi}", [128, ntl, 128], BF16))
                    BKtok, BVdec = Buf(Ktok, "Ktok"), Buf(Vdec, "Vdec")
                    Sbf = pbk.enter_context(nc.sbuf_tensor(f"Sbf_{hi}", [128, nch + 1, 128], BF16))
                    BSbf = [Buf(Sbf, f"Sbf{i}") for i in range(nch + 1)]
                    Sf = pbk.enter_context(nc.sbuf_tensor(f"Sf_{hi}", [128, 128], F32))
                    BSf = Buf(Sf, "Sf")
                    if has_s:
                        S0f = pbk.enter_context(nc.sbuf_tensor(f"S0f_{hi}", [128, NSQ, 128], F32))
                        BS0f = Buf(S0f, "S0f")
                        S0b = pbk.enter_context(nc.sbuf_tensor(f"S0b_{hi}", [128, NSQ, 128], BF16))
                        BS0b = Buf(S0b, "S0b")
                        Vbd = pbk.enter_context(nc.sbuf_tensor(f"Vbd_{hi}", [128, NSQ, 128], BF16))
                        BVbd = Buf(Vbd, "Vbd")
                        Kds = pbk.enter_context(nc.sbuf_tensor(f"Kds_{hi}", [128, 128], BF16))
                        BKds = Buf(Kds, "Kds")
                        Qds = pbk.enter_context(nc.sbuf_tensor(f"Qds_{hi}", [128, 128], BF16))
                        BQds = Buf(Qds, "Qds")
                        cmpS = pbk.enter_context(nc.sbuf_tensor(f"cmpS_{hi}", [128, 3, 128], BF16))
                        BcmpS = Buf(cmpS, "cmpS")
                    dk_scale = 128.0 ** -0.5
                    for h in range(4):
                        ws = {nm: w_next((nm, h)) for nm in ("q", "k", "v", "g")}
                        if has_s:
                            S.dma("sp", S0f[:], sret_d[:, h, :, :].rearrange("s d v -> d s v"), writes=BS0f.r())
                            S.dma("pool", S0b[:], sret_d[:, h, :, :].rearrange("s d v -> d s v"), writes=BS0b.r())
                        for (c0, ln) in b0:
                            for (dstT, Bd, n1, sc) in ((QT, BQT, "q", 1.0), (KT, BKT, "k", dk_scale)):
                                ps1, pb1 = PS.next()
                                proj_block(ps1, pb1, *ws[n1], xn, Bxn, c0, ln)
                                qb, qbb = TB.next()
                                S.op("act", lambda e: e.activation(out=qb[:, 0:ln], in_=ps1[:, 0:ln], func=AF.Copy), reads=pb1.r(), writes=qbb.r())
                                ps2, pb2 = PS.next()
                                mm(ps2[:, 0:ln], permb[:], qb[:, 0:ln], True, True, Bperm.r() + qbb.r(), pb2)
                                t1, t1b = TF.next()
                                t2, t2b = TF.next()
                                S.op("dve", lambda e: e.scalar_tensor_tensor(out=t1[:, 0:ln], in0=ps1[:, 0:ln], scalar=sc, in1=rope[:, 0, c0:c0 + ln],
                                                                             op0=ALU.mult, op1=ALU.mult),
                                     reads=pb1.r() + Brope.r(), writes=t1b.r())
                                S.op("dve", lambda e: e.scalar_tensor_tensor(out=t2[:, 0:ln], in0=ps2[:, 0:ln], scalar=sc, in1=rope[:, 1, c0:c0 + ln],
                                                                             op0=ALU.mult, op1=ALU.mult),
                                     reads=pb2.r() + Brope.r(), writes=t2b.r())
                                S.op("dve", lambda e: e.tensor_tensor(out=dstT[:, c0:c0 + ln], in0=t1[:, 0:ln], in1=t2[:, 0:ln], op=ALU.add),
                                     reads=t1b.r() + t2b.r(), writes=Bd.r(c0, c0 + ln))
                            psv, pbv = PS.next()
                            proj_block(psv, pbv, *ws["v"], xn, Bxn, c0, ln)
                            S.op("act", lambda e: e.activation(out=VT[:, c0:c0 + ln], in_=psv[:, 0:ln], func=AF.Copy), reads=pbv.r(), writes=BVT.r(c0, c0 + ln))
                            psg, pbg = PS.next()
                            proj_block(psg, pbg, *ws["g"], xn, Bxn, c0, ln)
                            S.op("act", lambda e: e.activation(out=SG[:, c0:c0 + ln], in_=psg[:, 0:ln], func=AF.Silu), reads=pbg.r(), writes=BSG.r(c0, c0 + ln))

                        if has_s:
                            for (srcT, Bs, j_) in ((QT, BQT, 0), (KT, BKT, 1), (VT, BVT, 2)):
                                S.op("act", lambda e: e.activation(out=cmpS[:, j_, :].rearrange("p (s c) -> p s c", c=TS),
                                                                   in_=srcT[:, sc0:nt].rearrange("p (s c) -> p s c", c=SW)[:, :, H:SW], func=AF.Copy),
                                     reads=Bs.r(sc0, nt), writes=BcmpS.r())
                        cmp_idx = {id(QT): 0, id(KT): 1, id(VT): 2}

                        def tcols(tsr, i):
                            if i < nch:
                                return tsr[:, H + 128 * i:H + 128 * i + 128]
                            return cmpS[:, cmp_idx[id(tsr)], :]

                        def tregs(B, i):
                            if i < nch:
                                return B.r(H + 128 * i, H + 128 * i + 128)
                            return BcmpS.r()
                        for i0 in range(0, ntl, 8):
                            i1 = min(ntl, i0 + 8)
                            for (srcT, Bs, dstk) in ((KT, BKT, "k"), (VT, BVT, "v")):
                                ps, pb = PS.next()
                                psb = ps[:].bitcast(BF16)
                                for i in range(i0, i1):
                                    S.op("pe", lambda e: e.transpose(psb[:, 128 * (i - i0):128 * (i - i0) + 128], tcols(srcT, i), identb[:]),
                                         reads=tregs(Bs, i) + Bidb.r(), writes=pb.r())
                                npz = min(i1, nch) - i0
                                if dstk == "k":
                                    if npz > 0:
                                        S.op("act", lambda e: e.activation(out=Ktok[:, i0:i0 + npz, :].rearrange("p a b -> p (a b)"),
                                                                           in_=psb[:, 0:128 * npz], func=AF.Copy),
                                             reads=pb.r(), writes=BKtok.r())
                                    if has_s and i1 == ntl:
                                        off = 128 * (nch - i0)
                                        S.op("act", lambda e: e.activation(out=Kds[:, :], in_=psb[:, off:off + 128], func=AF.Copy, scale=rcon[:, 4 + h:5 + h]),
                                             reads=pb.r() + Bcon.r(), writes=BKds.r())
                                else:
                                    if npz > 0:
                                        S.op("act", lambda e: e.activation(out=Vdec[:, i0:i0 + npz, :].rearrange("p a b -> p (a b)"),
                                                                           in_=psb[:, 0:128 * npz], func=AF.Copy, scale=rcon[:, h:h + 1]),
                                             reads=pb.r() + Bcon.r(), writes=BVdec.r())
                                    if has_s and i1 == ntl:
                                        off = 128 * (nch - i0)
                                        S.op("act", lambda e: e.activation(out=Vdec[:, nch, :], in_=psb[:, off:off + 128], func=AF.Copy),
                                             reads=pb.r(), writes=BVdec.r())
                        if hi == 0:
                            S.op("dve", lambda e: e.memset(Sf[:, :], 0.0), writes=BSf.r())
                            S.op("dve", lambda e: e.memset(Sbf[:, 0, :], 0.0), writes=BSbf[0].r())
                        else:
                            S.op("dve", lambda e: e.tensor_copy(out=Sf[:, :], in_=cS[:, h, :]), reads=BcS.r(), writes=BSf.r())
                            S.op("act", lambda e: e.activation(out=Sbf[:, 0, :], in_=cS[:, h, :], func=AF.Copy), reads=BcS.r(), writes=BSbf[0].r())

                        for cg in range(0, nch, 4):
                            cn = min(4, nch - cg)
                            ps_o, pb_o = PSL.next()
                            for ci in range(cg, cg + cn):
                                cc0 = H + 128 * ci
                                ps_u, pb_u = PS.next()
                                mm(ps_u[:, 0:128], Ktok[:, ci, :], Vdec[:, ci, :], True, True, BKtok.r() + BVdec.r(), pb_u)
                                S.op("dve", lambda e: e.scalar_tensor_tensor(out=Sf[:, :], in0=Sf[:, :], scalar=gC[h], in1=ps_u[:, 0:128],
                                                                             op0=ALU.mult, op1=ALU.add),
                                     reads=pb_u.r() + BSf.r(), writes=BSf.r())
                                S.op("act", lambda e: e.activation(out=Sbf[:, ci + 1, :], in_=Sf[:, :], func=AF.Copy), reads=BSf.r(), writes=BSbf[ci + 1].r())
                                ps_s, pb_s = PS.next()
                                mm(ps_s[:, 0:128], KT[:, cc0:cc0 + 128], QT[:, cc0:cc0 + 128], True, True,
                                   BKT.r(cc0, cc0 + 128) + BQT.r(cc0, cc0 + 128), pb_s)
                                scm, scmb = TB.next()
                                S.op("dve", lambda e: e.tensor_tensor(out=scm[:, 0:128], in0=ps_s[:, 0:128], in1=maskP[:, h, :], op=ALU.mult),
                                     reads=pb_s.r() + Bcon.r(), writes=scmb.r())
                                oc_ = 128 * (ci - cg)
                                mm(ps_o[:, oc_:oc_ + 128], Vdec[:, ci, :], scm[:, 0:128], True, False, BVdec.r() + scmb.r(), pb_o)
                                mm(ps_o[:, oc_:oc_ + 128], Sbf[:, ci, :], QT[:, cc0:cc0 + 128], False, True,
                                   BSbf[ci].r() + BQT.r(cc0, cc0 + 128), pb_o)
                            w_ = 128 * cn
                            g0 = H + 128 * cg
                            qd_ap = qdP[:, h, :].unsqueeze(1).broadcast_to([128, cn, 128])

                            def dst_fn(o, g0=g0, w_=w_):
                                return mix[:, 4 + h, g0:g0 + w_], SG[:, g0:g0 + w_], o[:, 0:w_]
                            gn_block_3d(ps_o, pb_o, w_, cn, qd_ap, dst_fn, g0, g0 + w_, h, Bmix, BSG)
                        if hi == 0:
                            S.op("dve", lambda e: e.tensor_copy(out=cS[:, h, :], in_=Sf[:, :]), reads=BSf.r(), writes=BcS.r())
                        else:
                            S.dma("sp", o_pret[h, :, :], Sf[:, :], reads=BSf.r())
                        if has_s:
                            qs_v = tcols(QT, nch)
                            ks_v = tcols(KT, nch)
                            S.op("dve", lambda e: e.tensor_tensor(out=Qds[:, :], in0=qs_v, in1=qd8[:, h, :], op=ALU.mult),
                                 reads=BcmpS.r() + Bcon.r(), writes=BQds.r())
                            ps_s, pb_s = PS.next()
                            mm(ps_s[:, 0:128], ks_v, qs_v, True, True, BcmpS.r(), pb_s)
                            scm, scmb = TB.next()
                            S.op("dve", lambda e: e.tensor_tensor(out=scm[:, 0:128], in0=ps_s[:, 0:128], in1=maskS[:, h, :], op=ALU.mult),
                                 reads=pb_s.r() + Bcon.r(), writes=scmb.r())
                            ps_o, pb_o = PSL.next()
                            mm(ps_o[:, 0:128], Vdec[:, nch, :], scm[:, 0:128], True, False, BVdec.r() + scmb.r(), pb_o)
                            for s in range(NSQ):
                                mm(ps_o[:, 8 * s:8 * s + 8], S0b[:, s, :], Qds[:, 8 * s:8 * s + 8], False, s == NSQ - 1,
                                   BS0b.r() + BQds.r(), pb_o)

                            def dst_fn_s(o):
                                return (mix[:, 4 + h, sc0:nt].rearrange("p (s c) -> p s c", c=SW)[:, :, H:SW],
                                        SG[:, sc0:nt].rearrange("p (s c) -> p s c", c=SW)[:, :, H:SW],
                                        o[:, 0:128].rearrange("p (s c) -> p s c", c=TS))
                            gn_block_3d(ps_o, pb_o, 128, None, None, dst_fn_s, sc0, nt, h, Bmix, BSG)
                            S.op("dve", lambda e: e.tensor_tensor(out=Vbd[:, :, :], in0=Vdec[:, nch, :].unsqueeze(1).broadcast_to([128, NSQ, 128]),
                                                                  in1=rcon[:, 8:24].unsqueeze(2).broadcast_to([128, NSQ, 128]), op=ALU.mult),
                                 reads=BVdec.r() + Bcon.r(), writes=BVbd.r())
                            for q in range(4):
                                ps_u, pb_u = PS.next()
                                mm(ps_u[:, :], Kds[:, :], Vbd[:, 4 * q:4 * q + 4, :], True, True, BKds.r() + BVbd.r(), pb_u)
                                sn, snb = TF.next()
                                S.op("dve", lambda e: e.scalar_tensor_tensor(out=sn[:, :], in0=S0f[:, 4 * q:4 * q + 4, :].rearrange("p a b -> p (a b)"),
                                                                             scalar=g8[h], in1=ps_u[:, :], op0=ALU.mult, op1=ALU.add),
                                     reads=pb_u.r() + BS0f.r(), writes=snb.r())
                                S.dma("sp", o_sret[4 * q:4 * q + 4, h, :, :].rearrange("s d v -> d s v"),
                                      sn[:, :].rearrange("p (s v) -> p s v", v=128), reads=snb.r())
                    out_proj(hi, "wo0", mix, Bmix)
                    tail()
                    S.barrier()

        for hi in range(2):
            load_x(hi)
            rmsnorm(hi, "nmix0", None)
            even_mixer(hi, lambda hi=hi: rmsnorm(hi, "nffn0", 0))
            with ExitStack() as shared:
                SH.clear()
                SH["on"] = True
                SH["ph"] = shared
                ffn(hi, 0, lambda hi=hi: rmsnorm(hi, "nmix1", 1))
                lru_mixer(hi, lambda hi=hi: rmsnorm(hi, "nffn1", 2))
                ffn(hi, 1, lambda hi=hi: final_out(hi))
                SH["on"] = False
                S.barrier()
        S.barrier()
        build.marks = S.marks

    return nc


_NC_CACHE = {}


def _core_inputs(inp, params, consts, w_sw, c):
    m = {
        "xp": np.ascontiguousarray(inp["x_prompt"][c]),
        "xs": np.ascontiguousarray(inp["x_sample"][NSQ * c:NSQ * (c + 1)].reshape(NSQ * TS, D)),
        "st_conv_a": np.ascontiguousarray(inp["state_conv_a"][0, NSQ * c:NSQ * (c + 1)].reshape(NSQ * 30, 512)),
        "st_ret": np.ascontiguousarray(inp["state_ret"][0, NSQ * c:NSQ * (c + 1)]),
        "st_lru_conv": np.ascontiguousarray(inp["state_lru_conv"][0, NSQ * c:NSQ * (c + 1)].reshape(NSQ * 3, D)),
        "st_lru_h": np.ascontiguousarray(inp["state_lru_h"][0, NSQ * c:NSQ * (c + 1)]),
        "st_ffn": np.ascontiguousarray(inp["state_ffn_conv"][:, NSQ * c:NSQ * (c + 1)].reshape(2, NSQ * 2, 2 * FF)),
        "w_in_ab": inp["w_in_ab"][0],
        "w_out_ab": inp["w_out_ab"][0],
        "w_in_c": inp["w_in_c"][0],
        "w_lru_a": inp["w_lru_a"][0],
        "w_lru_x": inp["w_lru_x"][0],
        "w_out_c": inp["w_out_c"][0],
        "w_ffn_up": inp["w_ffn_up"],
        "w_ffn_down": inp["w_ffn_down"],
        "params": params,
    }
    m.update(consts)
    return m


def _prep(inputs):
    inp = {k: np.asarray(v, dtype=np.float32) for k, v in inputs.items()}
    params, npar = _pack_params(inp)
    consts = _host_consts()
    wqk = inp["w_in_ab"][0][:, 1024:2048].reshape(D, 8, 2, 64)
    w_sw = np.ascontiguousarray(wqk[:, :, ::-1, :].reshape(D, 1024))
    return inp, params, npar, consts, w_sw


def _assemble(res, ncores):
    def g(name):
        return [np.asarray(r[name]) for r in res]
    y_p = np.stack(g("y_p"), 0)
    y_s = np.concatenate([a.reshape(NSQ, TS, D) for a in g("y_s")], 0)
    p_ca = np.stack(g("p_conv_a"), 0)[None]
    p_ret = np.stack(g("p_ret"), 0)[None]
    p_lc = np.stack(g("p_lru_conv"), 0)[None]
    p_lh = np.stack([a.reshape(D) for a in g("p_lru_h")], 0)[None]
    p_ffn = np.stack(g("p_ffn"), 1)
    s_ca = np.concatenate([a.reshape(NSQ, 30, 512) for a in g("s_conv_a")], 0)[None]
    s_ret = np.concatenate(g("s_ret"), 0)[None]
    s_lc = np.concatenate([a.reshape(NSQ, 3, D) for a in g("s_lru_conv")], 0)[None]
    s_lh = np.concatenate(g("s_lru_h"), 0)[None]
    s_ffn = np.concatenate([a.reshape(2, NSQ, 2, 2 * FF) for a in g("s_ffn")], 1)
    outs = (y_p, y_s, p_ca, p_ret, p_lc, p_lh, p_ffn, s_ca, s_ret, s_lc, s_lh, s_ffn)
    return tuple(np.ascontiguousarray(o, dtype=np.float32) for o in outs)


def kernel(**inputs):
    inp, params, npar, consts, w_sw = _prep(inputs)
    ncores = 8
    nc = build(npar)
    in_maps = [_core_inputs(inp, params, consts, w_sw, c) for c in range(ncores)]
    res = run_bass_kernel_spmd(nc, in_maps, core_ids=list(range(ncores)))
    return _assemble(res.results, ncores)
```

```python
import math
from contextlib import ExitStack

import numpy as np
import concourse.bass as bass
import concourse.mybir as mybir
from concourse.bass_utils import run_bass_kernel_spmd

F32 = mybir.dt.float32
BF16 = mybir.dt.bfloat16
AF = mybir.ActivationFunctionType
ALU = mybir.AluOpType

D = 1024
KC = 8
SEQ = 2048
NSQ = 16
TS = 8
PAST = 16384
H = 3
SW = H + TS
FF = 2816
NJ = 22
HALVES = [(0, 896, False), (896, 1152, True)]
GROUPS = [(0, 8), (8, 7), (15, 7)]
RMS_EPS = 1e-6
LN_EPS = 1e-5
GN_EPS = 1e-5
NSLOT = 12
PF = 3
NTF = 12
NTB = 8
STAGE = 4
import os as _os
_LAT = float(_os.environ.get('SCHED_LAT', '500'))
_EPS = float(_os.environ.get('SCHED_EPS', '150'))
_FLUSH_HALF = bool(int(_os.environ.get('FLUSH_HALF', '1')))
_TBL_INIT = None


class Reg:
    __slots__ = ("name", "w", "rd", "excl")

    def __init__(self, name, excl=False):
        self.name = name
        self.w = None
        self.rd = {}
        self.excl = excl


class Buf:
    scoped = False
    live = []
    pending = None

    def __init__(self, t, name, ncols=None, G=256, excl=False):
        self.t = t
        self.name = name
        self.G = G
        n = 1 if ncols is None else (ncols + G - 1) // G
        self.regs = [Reg(f"{name}.{i}", excl) for i in range(n)]
        self.ncols = ncols
        if Buf.scoped:
            for r_ in self.regs:
                r_.w = Buf.pending
            Buf.live.append(self)

    def r(self, c0=None, c1=None):
        if self.ncols is None or c0 is None:
            return list(self.regs)
        return self.regs[c0 // self.G:(c1 - 1) // self.G + 1]


class _Rec:
    def __init__(self):
        self.call = None

    def __getattr__(self, name):
        def f(*a, **k):
            self.call = (name, a, k)
            return self
        return f


class _Node:
    __slots__ = ("idx", "eng", "call", "deps", "dur", "lat", "tbl", "tok", "is_dma", "succ", "ndep", "ready")


_ACT_TBL = {}


def _free_elems(ap):
    try:
        sh = ap.shape
        n = 1
        for d in sh[1:]:
            n *= int(d)
        return n
    except Exception:
        return 512


class Sched:
    def __init__(self, nc, ndma=6):
        self.nc = nc
        self.E = {"pe": nc.tensor, "act": nc.scalar, "dve": nc.vector, "pool": nc.gpsimd, "sp": nc.sync}
        self.sem = {}
        self.cnt = {}
        self.waited = {e: {} for e in self.E}
        for e in self.E:
            self.sem[e] = nc.alloc_semaphore(name=f"c_{e}")
            self.cnt[e] = 0
        self.dsem = {}
        for q in ("sp", "pool"):
            self.dsem[q] = [[nc.alloc_semaphore(name=f"d_{q}{i}"), 0] for i in range(ndma)]
        self.drr = {"sp": 0, "pool": 0}
        self.nodes = []
        self.nidx = 0
        self.reorder = True
        self.marks = []

    def _record(self, node, reads, writes):
        deps = {}

        def add(n, raw):
            if n is None:
                return
            if deps.get(n.idx, (None, False))[1] is False:
                deps[n.idx] = (n, raw or deps.get(n.idx, (None, False))[1])

        for r in reads:
            add(r.w, True)
            if r.excl:
                for n in r.rd.values():
                    add(n, False)
        for r in writes:
            add(r.w, r.excl)
            for n in r.rd.values():
                add(n, False)
        node.deps = list(deps.values())
        for r in writes:
            r.w = node
            r.rd = {}
        for r in reads:
            if r.excl:
                r.w = node
                r.rd = {}
            else:
                r.rd[node.idx] = node
        self.nodes.append(node)

    def op(self, e, fn, reads=(), writes=(), dur=None, tbl=None):
        rec = _Rec()
        fn(rec)
        n = _Node()
        n.idx = self.nidx
        self.nidx += 1
        n.eng = e
        n.call = rec.call
        n.is_dma = False
        n.tbl = tbl
        n.tok = None
        if dur is None:
            name, a, k = rec.call
            out = k.get("out", a[0] if a else None)
            fe = _free_elems(out) if out is not None else 512
            if e == "pe":
                dur = 10.0 + fe / 2.4
            elif e == "act":
                dur = 280.0 + fe / 1.1
                if name == "activation":
                    f = k.get("func")
                    tbl = _ACT_TBL.get(f, None)
                    n.tbl = tbl
            elif e == "dve":
                dur = 180.0 + fe / 0.9
                if name == "reciprocal":
                    dur = 120.0 + 4 * fe / 0.96
                elif name in ("memset",):
                    dur = 100.0 + fe / 3.0
            else:
                dur = 300.0
        n.dur = dur
        n.lat = 60.0 if e == "pe" else _LAT
        self._record(n, reads, writes)
        return n

    def fence(self, scratch, scratch_buf, keep=()):
        keep_ids = {id(b) for b in keep}
        regs = []
        seen = set()
        for b in Buf.live:
            if id(b) in keep_ids:
                continue
            for r_ in b.regs:
                if id(r_) not in seen:
                    seen.add(id(r_))
                    regs.append(r_)
        node = self.op("dve", lambda e: e.memset(scratch, 0.0), writes=regs + scratch_buf.r(), dur=80.0)
        Buf.pending = node
        Buf.live = [b for b in Buf.live if id(b) in keep_ids]

    def dma(self, q, out, in_, reads=(), writes=(), nbytes=None):
        n = _Node()
        n.idx = self.nidx
        self.nidx += 1
        n.eng = q
        n.call = (out, in_)
        n.is_dma = True
        n.tbl = None
        n.tok = None
        if nbytes is None:
            try:
                nb = 1
                for d in out.shape:
                    nb *= int(d)
                nbytes = nb * 4
            except Exception:
                nbytes = 65536
        n.dur = 1000.0 if q == "pool" else 150.0
        n.lat = 2000.0 + nbytes / 150.0
        self._record(n, reads, writes)

    def _wait(self, e, key, sem, val):
        if self.waited[e].get(key, 0) >= val:
            return
        self.E[e].wait_ge(sem, val)
        self.waited[e][key] = val

    def _emit(self, n):
        e = n.eng
        toks = {}
        for (d, raw) in n.deps:
            key, sem, val = d.tok
            if key == e:
                if e in ("pe", "pool", "sp"):
                    continue
                if not raw:
                    continue
                if val <= self.cnt[e] - 2:
                    continue
            if toks.get(key, (None, 0))[1] < val:
                toks[key] = (sem, val)
        if n.is_dma:
            i = self.drr[e]
            self.drr[e] = (i + 1) % len(self.dsem[e])
            ent = self.dsem[e][i]
            key = f"d_{e}{i}"
            if ent[1] > 0:
                self._wait(e, key, ent[0], 16 * ent[1])
            for k2, (sem, val) in toks.items():
                self._wait(e, k2, sem, val)
            out, in_ = n.call
            self.E[e].dma_start(out=out, in_=in_).then_inc(ent[0], 16)
            ent[1] += 1
            n.tok = (key, ent[0], 16 * ent[1])
        else:
            for k2, (sem, val) in toks.items():
                self._wait(e, k2, sem, val)
            name, a, k = n.call
            ins = getattr(self.E[e], name)(*a, **k)
            self.cnt[e] += 1
            ins.then_inc(self.sem[e], 1)
            n.tok = (e, self.sem[e], self.cnt[e])
        n.deps = None
        n.call = None

    def flush(self):
        nodes = self.nodes
        self.nodes = []
        if not nodes:
            return
        if not self.reorder:
            for n in nodes:
                self._emit(n)
            return
        inwin = {n.idx: n for n in nodes}
        for n in nodes:
            n.succ = []
            n.ndep = 0
            n.ready = 0.0
        for n in nodes:
            for (d, raw) in n.deps:
                if d.idx in inwin:
                    d.succ.append(n)
                    n.ndep += 1
        bl = {}
        for n in reversed(nodes):
            m = 0.0
            for s_ in n.succ:
                v = n.lat + bl[s_.idx]
                if v > m:
                    m = v
            bl[n.idx] = n.dur + m
        free = {e: 0.0 for e in self.E}
        last_tbl = None
        ready = {e: [] for e in self.E}
        for n in nodes:
            if n.ndep == 0:
                ready[n.eng].append(n)
        left = len(nodes)
        EPS = _EPS
        while left:
            best = None
            for e, lst in ready.items():
                if not lst:
                    continue
                mr = min(x.ready for x in lst)
                t_e = max(free[e], mr)
                if best is None or t_e < best[0]:
                    best = (t_e, e)
            t_e, e = best
            lst = ready[e]
            cands = [x for x in lst if x.ready <= t_e + EPS]
            if e in ("sp", "pool"):
                n = min(cands, key=lambda x: x.idx)
            else:
                if e == "act" and last_tbl is not None:
                    same = [x for x in lst if x.ready <= t_e + 1500.0 and (x.tbl is None or x.tbl == last_tbl)]
                    if same:
                        cands = same
                n = max(cands, key=lambda x: (bl[x.idx], -x.idx))
            lst.remove(n)
            st = max(free[e], n.ready)
            if e == "act" and n.tbl is not None:
                if last_tbl is not None and n.tbl != last_tbl:
                    st += 1300.0
                last_tbl = n.tbl
            end = st + n.dur
            free[e] = end
            fin_t = end + n.lat
            self._emit(n)
            left -= 1
            for s_ in n.succ:
                if fin_t > s_.ready:
                    s_.ready = fin_t
                s_.ndep -= 1
                if s_.ndep == 0:
                    ready[s_.eng].append(s_)
            n.succ = None

    def barrier(self):
        self.flush()
        self.marks.append(dict(self.cnt))
        for e in self.E:
            for e2 in ("pe", "act", "dve", "pool"):
                if e2 != e and self.cnt[e2] > 0:
                    self._wait(e, e2, self.sem[e2], self.cnt[e2])
            for q in ("sp", "pool"):
                for i, ent in enumerate(self.dsem[q]):
                    if ent[1] > 0:
                        self._wait(e, f"d_{q}{i}", ent[0], 16 * ent[1])


class Pool:
    def __init__(self, es, nc, name, n, shape, dt, excl=False, space="sbuf"):
        self.items = []
        for i in range(n):
            if space == "sbuf":
                t = es.enter_context(nc.sbuf_tensor(f"{name}{i}", shape, dt))
            else:
                t = es.enter_context(nc.psum_tensor(f"{name}{i}", shape, dt))
            self.items.append((t, Buf(t, f"{name}{i}", excl=excl)))
        self.i = 0

    def next(self):
        it = self.items[self.i]
        self.i = (self.i + 1) % len(self.items)
        return it


def _init_tbl():
    _ACT_TBL.update({AF.Gelu_apprx_tanh: "gelu", AF.Sigmoid: "sig", AF.Silu: "silu", AF.Exp: "exp", AF.Ln: "exp",
                     AF.Sqrt: "sqrt", AF.Tanh: "exp"})


def _gammas():
    lg = np.log(np.float32(1.0) - np.float32(2.0) ** (-5.0 - np.arange(4, dtype=np.float32))).astype(np.float32)
    return lg


def _half_geom(hi):
    t0, npr, has_s = HALVES[hi]
    sc0 = H + npr
    nt = sc0 + (NSQ * SW if has_s else 0)
    return t0, npr, has_s, sc0, nt


def _blocks0(hi):
    t0, npr, has_s, sc0, nt = _half_geom(hi)
    out = []
    c = H
    while c < nt:
        l = min(512, nt - c)
        out.append((c, l))
        c += l
    return out


def _blocks3(hi):
    t0, npr, has_s, sc0, nt = _half_geom(hi)
    out = []
    c = 0
    while True:
        l = min(512, nt - c)
        out.append((c, l))
        if c + l >= nt:
            break
        c += l - H
    return out


def _host_consts():
    c = {}
    c["ident"] = np.eye(128, dtype=np.float32)
    pm = np.zeros((128, 128), np.float32)
    for m_ in range(128):
        pm[(m_ + 64) % 128, m_] = 1.0
    c["perm"] = pm
    lg = _gammas()
    inv_freq = (np.float32(10000.0) ** (-np.arange(0, 128, 2, dtype=np.float32) / np.float32(128))).astype(np.float32)
    for hi in range(2):
        t0, npr, has_s, sc0, nt = _half_geom(hi)
        pos = np.zeros(nt, np.float32)
        valid = np.zeros(nt, bool)
        pos[H:H + npr] = np.arange(t0, t0 + npr, dtype=np.float32)
        valid[H:H + npr] = True
        if has_s:
            for s in range(NSQ):
                b = sc0 + SW * s + H
                pos[b:b + TS] = np.arange(PAST, PAST + TS, dtype=np.float32)
                valid[b:b + TS] = True
        ang = (pos[:, None] * inv_freq[None, :]).astype(np.float32)
        cs = np.cos(ang).astype(np.float32)
        sn = np.sin(ang).astype(np.float32)
        cosT = np.concatenate([cs.T, cs.T], axis=0)
        sinT = np.concatenate([-sn.T, sn.T], axis=0)
        cosT[:, ~valid] = 0
        sinT[:, ~valid] = 0
        c[f"rope{hi}"] = np.ascontiguousarray(np.stack([cosT, sinT], axis=1).astype(np.float32))
    idx = np.arange(128, dtype=np.float32)
    maskP = np.zeros((128, 4, 128), np.float32)
    qdP = np.zeros((128, 4, 128), np.float32)
    kdecP = np.zeros((128, 4), np.float32)
    maskS = np.zeros((128, 4, 128), np.float32)
    qd8 = np.zeros((128, 4, 128), np.float32)
    kdec8 = np.zeros((128, 4), np.float32)
    causal = (idx[:, None] <= idx[None, :]).astype(np.float32)
    sj = np.arange(128) // 8
    jj = (np.arange(128) % 8).astype(np.float32)
    for h in range(4):
        maskP[:, h, :] = causal * np.exp(np.float32(-128.0) * lg[h]).astype(np.float32)
        qdP[:, h, :] = np.exp((idx + 1.0) * lg[h])[None, :]
        kdecP[:, h] = np.exp((127.0 - idx) * lg[h])
        rel = jj[None, :] - jj[:, None]
        m = np.where((sj[:, None] == sj[None, :]) & (rel >= 0), np.exp(np.maximum(rel, 0) * lg[h]), 0.0)
        maskS[:, h, :] = m
        qd8[:, h, :] = np.exp((jj + 1.0) * lg[h])[None, :]
        kdec8[:, h] = np.exp((7.0 - jj) * lg[h])
    c["maskP"] = maskP
    c["qdP"] = qdP
    c["maskS"] = maskS.astype(np.float32)
    c["qd8"] = qd8
    rc = np.zeros((128, 32), np.float32)
    rc[:, 0:4] = kdecP
    rc[:, 4:8] = kdec8
    rc[:, 8:24] = (sj[:, None] == np.arange(16)[None, :]).astype(np.float32)
    c["rcon"] = rc
    t0, npr, has_s, sc0, nt = _half_geom(1)
    b3 = _blocks3(1)
    lc0, ll = b3[-1]
    sel2 = np.zeros((32, ll), np.float32)
    sel3 = np.zeros((48, ll), np.float32)
    for s in range(NSQ):
        for r in range(2):
            sel2[2 * s + r, sc0 + SW * s + 1 + r - lc0] = 1
        for r in range(3):
            sel3[3 * s + r, sc0 + SW * s + r - lc0] = 1
    c["sel2"] = sel2
    c["sel3"] = sel3
    return c


def _fm(v):
    return np.ascontiguousarray(v.reshape(-1, 128).T)


PAR = {}


def _pack_params(inp):
    cols = []
    off = 0

    def add(name, arr):
        nonlocal off
        arr = np.ascontiguousarray(arr, dtype=np.float32).reshape(128, -1)
        PAR[name] = (off, arr.shape[1])
        cols.append(arr)
        off += arr.shape[1]

    for l in range(2):
        add(f"nmix{l}", _fm(inp["norm_mix"][l]))
        add(f"nffn{l}", _fm(inp["norm_ffn"][l]))
    add("nfin", _fm(inp["norm_final"]))
    add("cab", _fm(inp["conv_a_b"][0]))
    add("lng", _fm(inp["ln_a_g"][0]))
    add("lnb", _fm(inp["ln_a_b"][0]))
    add("gng", _fm(inp["gn_ret_g"][0]))
    add("caw", inp["conv_a_w"][0].reshape(31, 4, 128).transpose(2, 1, 0))
    add("ccw", inp["conv_c_w"][0].reshape(4, 8, 128).transpose(2, 1, 0))
    add("ccb", _fm(inp["conv_c_b"][0]))
    add("ba", _fm(inp["b_lru_a"][0]))
    add("bx", _fm(inp["b_lru_x"][0]))
    add("lam", _fm(inp["lru_lambda"][0]))
    for l in range(2):
        add(f"fcw{l}", inp["ffn_conv_w"][l].reshape(3, 44, 128).transpose(2, 1, 0))
        add(f"fcb{l}", _fm(inp["ffn_conv_b"][l]))
    return np.ascontiguousarray(np.concatenate(cols, axis=1)), off


def build(npar):
    nc = bass.Bass("TRN2", target_bir_lowering=False)

    def din(name, shape):
        return nc.dram_tensor(name, list(shape), F32, kind="ExternalInput").ap()

    def dout(name, shape):
        return nc.dram_tensor(name, list(shape), F32, kind="ExternalOutput").ap()

    xp_d = din("xp", (SEQ, D))
    xs_d = din("xs", (NSQ * TS, D))
    sca_d = din("st_conv_a", (NSQ * 30, 512))
    sret_d = din("st_ret", (NSQ, 4, 128, 128))
    slc_d = din("st_lru_conv", (NSQ * 3, D))
    slh_d = din("st_lru_h", (NSQ, D))
    sffn_d = din("st_ffn", (2, NSQ * 2, 2 * FF))
    w_in_ab = din("w_in_ab", (D, 3072))
    w_out_ab = din("w_out_ab", (D, D))
    w_in_c = din("w_in_c", (D, 2048))
    w_lru_a = din("w_lru_a", (8, 128, 128))
    w_lru_x = din("w_lru_x", (8, 128, 128))
    w_out_c = din("w_out_c", (D, D))
    w_up = din("w_ffn_up", (2, D, 2 * FF))
    w_dn = din("w_ffn_down", (2, FF, D))
    par_d = din("params", (128, npar))
    ident_d = din("ident", (128, 128))
    perm_d = din("perm", (128, 128))
    rope_d = [din(f"rope{hi}", (128, 2, _half_geom(hi)[4])) for hi in range(2)]
    maskP_d = din("maskP", (128, 4, 128))
    qdP_d = din("qdP", (128, 4, 128))
    maskS_d = din("maskS", (128, 4, 128))
    qd8_d = din("qd8", (128, 4, 128))
    rcon_d = din("rcon", (128, 32))
    b3_1 = _blocks3(1)
    sel2_d = din("sel2", (32, b3_1[-1][1]))
    sel3_d = din("sel3", (48, b3_1[-1][1]))

    yp_d = dout("y_p", (SEQ, D))
    ys_d = dout("y_s", (NSQ * TS, D))
    o_pca = dout("p_conv_a", (30, 512))
    o_pret = dout("p_ret", (4, 128, 128))
    o_plc = dout("p_lru_conv", (3, D))
    o_plh = dout("p_lru_h", (1, D))
    o_pffn = dout("p_ffn", (2, 2, 2 * FF))
    o_sca = dout("s_conv_a", (NSQ * 30, 512))
    o_sret = dout("s_ret", (NSQ, 4, 128, 128))
    o_slc = dout("s_lru_conv", (NSQ * 3, D))
    o_slh = dout("s_lru_h", (NSQ, D))
    o_sffn = dout("s_ffn", (2, NSQ * 2, 2 * FF))

    lg = _gammas()
    gC = [float(np.exp(np.float32(128.0) * lg[h])) for h in range(4)]
    g8 = [float(np.exp(np.float32(8.0) * lg[h])) for h in range(4)]

    _init_tbl()
    S = Sched(nc)
    NTMAX = _half_geom(1)[4]

    with ExitStack() as es:
        def sb(name, shape, dt=F32):
            return es.enter_context(nc.sbuf_tensor("sb_" + name, list(shape), dt))

        xT = sb("xT", (128, KC, NTMAX))
        xn = sb("xn", (128, KC, NTMAX), BF16)
        BxT = Buf(xT, "xT", NTMAX)
        Bxn = Buf(xn, "xn", NTMAX)
        par = sb("par", (128, npar))
        Bpar = Buf(par, "par")
        identf = sb("identf", (128, 128))
        identb = sb("identb", (128, 128), BF16)
        Bid = Buf(identf, "identf")
        Bidb = Buf(identb, "identb")
        permb = sb("permb", (128, 128), BF16)
        Bperm = Buf(permb, "permb")
        ones = sb("ones", (128, 3, 128), BF16)
        Bones = Buf(ones, "ones")
        maskP = sb("maskP", (128, 4, 128))
        qdP = sb("qdP", (128, 4, 128))
        maskS = sb("maskS", (128, 4, 128))
        qd8 = sb("qd8", (128, 4, 128))
        rcon = sb("rcon", (128, 32))
        Bcon = Buf(maskP, "retconst")
        sel2 = sb("sel2", (32, b3_1[-1][1]), BF16)
        sel3 = sb("sel3", (48, b3_1[-1][1]), BF16)
        Bsel = Buf(sel2, "sel")
        lruc = sb("lruc", (128, 8, 2))
        hbias = sb("hbias", (128, 2, 8))
        Blruc = Buf(lruc, "lruc")
        cxn = sb("cxn", (128, 3, KC, H), BF16)
        Bcxn = Buf(cxn, "cxn")
        cu = sb("cu", (128, 4, 30), BF16)
        Bcu = Buf(cu, "cu")
        cS = sb("cS", (128, 4, 128))
        BcS = Buf(cS, "cS")
        ch = sb("ch", (128, 8))
        Bch = Buf(ch, "ch")

        fsc = sb("fsc", (128, 2))
        Bfsc = Buf(fsc, "fsc")
        slots = Pool(es, nc, "wsl", NSLOT, (128, 1024), BF16)
        TF = Pool(es, nc, "tf", NTF, (128, 512), F32)
        TB = Pool(es, nc, "tb", NTB, (128, 512), BF16)
        PS = Pool(es, nc, "ps", 6, (128, 512), F32, excl=True, space="psum")
        PSL = Pool(es, nc, "psl", 2, (128, 512), F32, excl=True, space="psum")
        _ps6 = list(PS.items)
        _ps8 = list(PS.items) + list(PSL.items)

        def ps_mode(n):
            PS.items = _ps8 if n == 8 else _ps6
            PS.i = 0

        def P(name, k=None):
            o, n = PAR[name]
            if k is None:
                return par[:, o:o + n]
            return par[:, o + k:o + k + 1]

        S.dma("sp", par[:], par_d, writes=Bpar.r())
        S.dma("sp", identf[:], ident_d, writes=Bid.r())
        S.dma("pool", identb[:], ident_d, writes=Bidb.r())
        S.dma("pool", permb[:], perm_d, writes=Bperm.r())
        S.dma("sp", maskP[:], maskP_d, writes=Bcon.r())
        S.dma("sp", qdP[:], qdP_d, writes=Bcon.r())
        S.dma("sp", maskS[:], maskS_d, writes=Bcon.r())
        S.dma("sp", qd8[:], qd8_d, writes=Bcon.r())
        S.dma("sp", rcon[:], rcon_d, writes=Bcon.r())
        S.dma("pool", sel2[:], sel2_d, writes=Bsel.r())
        S.dma("pool", sel3[:], sel3_d, writes=Bsel.r())
        S.op("dve", lambda e: e.memset(ones[:, 0, :], 1.0 / 1024), writes=Bones.r())
        S.op("dve", lambda e: e.memset(ones[:, 1, :], 1.0 / 512), writes=Bones.r())
        S.op("dve", lambda e: e.memset(ones[:, 2, :], 1.0 / 128), writes=Bones.r())
        S.op("act", lambda e: e.activation(out=lruc[:, :, 0], in_=P("lam"), func=AF.Exp, scale=-1.0),
             reads=Bpar.r(), writes=Blruc.r())
        S.op("act", lambda e: e.activation(out=lruc[:, :, 0], in_=lruc[:, :, 0], func=AF.Ln, bias=1.0, scale=1.0),
             reads=Blruc.r(), writes=Blruc.r())
        S.op("dve", lambda e: e.tensor_scalar(out=lruc[:, :, 1], in0=lruc[:, :, 0], scalar1=-8.0, scalar2=None, op0=ALU.mult),
             reads=Blruc.r(), writes=Blruc.r())
        S.op("dve", lambda e: e.tensor_scalar(out=lruc[:, :, 0], in0=lruc[:, :, 0], scalar1=-4.0, scalar2=None, op0=ALU.mult),
             reads=Blruc.r(), writes=Blruc.r())
        S.op("dve", lambda e: e.tensor_scalar(out=hbias[:, 0, :], in0=P("ba"), scalar1=0.5, scalar2=None, op0=ALU.mult),
             reads=Bpar.r(), writes=Blruc.r())
        S.op("dve", lambda e: e.tensor_scalar(out=hbias[:, 1, :], in0=P("bx"), scalar1=0.5, scalar2=None, op0=ALU.mult),
             reads=Bpar.r(), writes=Blruc.r())

        def wap_cols(w2d, c0, ncols=128):
            return w2d.rearrange("(k p) c -> p k c", p=128)[:, :, c0:c0 + ncols]

        def weight_plan():
            for hi in range(2):
                if STAGE < 1:
                    continue
                for c in range(4):
                    yield ("alin", c), wap_cols(w_in_ab, 128 * c)
                    yield ("agate", c), wap_cols(w_in_ab, 512 + 128 * c)
                for h in range(4):
                    yield ("q", h), wap_cols(w_in_ab, 1024 + 128 * h)
                    yield ("k", h), wap_cols(w_in_ab, 1536 + 128 * h)
                    yield ("v", h), wap_cols(w_in_ab, 2048 + 128 * h)
                    yield ("g", h), wap_cols(w_in_ab, 2560 + 128 * h)
                for o in range(8):
                    yield ("wo0", o), wap_cols(w_out_ab, 128 * o)
                for l in range(2):
                    if l == 0 and STAGE < 2:
                        continue
                    if l == 1 and STAGE < 3:
                        continue
                    if l == 1:
                        for n in range(8):
                            yield ("gate", n), wap_cols(w_in_c, 128 * n)
                            yield ("rec", n), wap_cols(w_in_c, 1024 + 128 * n)
                        for o in range(8):
                            yield ("wo1", o), wap_cols(w_out_c, 128 * o)
                        if STAGE < 4:
                            continue
                    for (j0, jn) in GROUPS:
                        for j in range(j0, j0 + jn):
                            yield ("upg", l, j), wap_cols(w_up[l], 128 * j)
                            yield ("upu", l, j), wap_cols(w_up[l], FF + 128 * j)
                        for j in range(j0, j0 + jn):
                            yield ("wd", l, j), w_dn[l][128 * j:128 * j + 128, :]

        plan = list(weight_plan())
        wstate = {"issued": 0, "used": 0}
        wslot_of = {}

        def w_issue(upto):
            while wstate["issued"] < min(upto, len(plan)):
                i = wstate["issued"]
                name, ap = plan[i]
                t, b = slots.items[i % NSLOT]
                if len(ap.shape) == 3:
                    dst = t[:].rearrange("p (k c) -> p k c", c=128)
                else:
                    dst = t[:]
                S.dma("pool", dst, ap, writes=b.r())
                wslot_of[i] = (t, b)
                wstate["issued"] += 1

        def w_next(name):
            i = wstate["used"]
            assert plan[i][0] == name, (plan[i][0], name)
            w_issue(i + 1 + PF)
            wstate["used"] += 1
            t, b = wslot_of[i]
            return i, t, b

        def w_check(i):
            assert wstate["issued"] <= i + NSLOT, ("weight evicted", plan[i][0])

        def mm(ps, lhsT, rhs, start, stop, reads, pb):
            S.op("pe", lambda e: e.matmul(ps, lhsT, rhs, start=start, stop=stop), reads=reads, writes=pb.r())

        def proj_block(ps, pb, wi, wt, wb, rhs_buf, rhs_B, c0, ln, nk=KC, last_stop=True):
            w_check(wi)
            w3 = wt[:].rearrange("p (k c) -> p k c", c=128)
            for k in range(nk):
                mm(ps[:, 0:ln], w3[:, k, :], rhs_buf[:, k, c0:c0 + ln], k == 0, (k == nk - 1) and last_stop,
                   wb.r() + rhs_B.r(c0, c0 + ln), pb)

        def rmsnorm(hi, gname, phase_idx):
            t0, npr, has_s, sc0, nt = _half_geom(hi)
            for (c0, ln) in _blocks0(hi):
                ps, pb = PS.next()
                for k in range(KC):
                    sq, sqb = TB.next()
                    if k % 3 != 2:
                        S.op("act", lambda e: e.activation(out=sq[:, 0:ln], in_=xT[:, k, c0:c0 + ln], func=AF.Square),
                             reads=BxT.r(c0, c0 + ln), writes=sqb.r())
                    else:
                        S.op("dve", lambda e: e.tensor_tensor(out=sq[:, 0:ln], in0=xT[:, k, c0:c0 + ln], in1=xT[:, k, c0:c0 + ln], op=ALU.mult),
                             reads=BxT.r(c0, c0 + ln), writes=sqb.r())
                    mm(ps[:, 0:ln], ones[:, 0, :], sq[:, 0:ln], k == 0, k == KC - 1, sqb.r() + Bones.r(), pb)
                sd, sdb = TF.next()
                S.op("act", lambda e: e.activation(out=sd[:, 0:ln], in_=ps[:, 0:ln], func=AF.Ln, bias=RMS_EPS, scale=1.0),
                     reads=pb.r(), writes=sdb.r())
                rs, rsb = TF.next()
                S.op("act", lambda e: e.activation(out=rs[:, 0:ln], in_=sd[:, 0:ln], func=AF.Exp, scale=-0.5), reads=sdb.r(), writes=rsb.r())
                for k in range(KC):
                    S.op("dve", lambda e: e.scalar_tensor_tensor(out=xn[:, k, c0:c0 + ln], in0=xT[:, k, c0:c0 + ln],
                                                                 scalar=P(gname, k), in1=rs[:, 0:ln],
                                                                 op0=ALU.mult, op1=ALU.mult),
                         reads=BxT.r(c0, c0 + ln) + rsb.r() + Bpar.r(), writes=Bxn.r(c0, c0 + ln))
            if has_s:
                for k in range(KC):
                    v = xn[:, k, sc0:nt].rearrange("p (s c) -> p s c", c=SW)[:, :, 0:H]
                    S.op("dve", lambda e: e.memset(v, 0.0), writes=Bxn.r(sc0, nt))
            if phase_idx is not None:
                if hi == 0:
                    S.op("dve", lambda e: e.tensor_copy(out=cxn[:, phase_idx, :, :], in_=xn[:, :, nt - H:nt]),
                         reads=Bxn.r(nt - H, nt), writes=Bcxn.r())
                else:
                    S.op("dve", lambda e: e.tensor_copy(out=xn[:, :, 0:H], in_=cxn[:, phase_idx, :, :]),
                         reads=Bcxn.r(), writes=Bxn.r(0, H))
            elif hi == 0 or True:
                S.op("dve", lambda e: e.memset(xn[:, :, 0:H], 0.0), writes=Bxn.r(0, H))

        def out_proj(hi, wname, mix, Bmix):
            ws = [w_next((wname, o)) for o in range(8)]
            for (c0, ln) in _blocks0(hi):
                for o in range(8):
                    wi, wt, wb = ws[o]
                    ps, pb = PS.next()
                    proj_block(ps, pb, wi, wt, wb, mix, Bmix, c0, ln)
                    S.op("dve", lambda e: e.tensor_tensor(out=xT[:, o, c0:c0 + ln], in0=xT[:, o, c0:c0 + ln],
                                                          in1=ps[:, 0:ln], op=ALU.add),
                         reads=pb.r() + BxT.r(c0, c0 + ln), writes=BxT.r(c0, c0 + ln))

        def transpose_out(src_aps, nrows, dsts):
            ps, pb = PS.next()
            n = len(src_aps)
            for i, (ap, rr) in enumerate(src_aps):
                S.op("pe", lambda e: e.transpose(ps[0:nrows, 128 * i:128 * i + 128], ap, identf[:]),
                     reads=rr + Bid.r(), writes=pb.r())
            st, stb = TF.next()
            S.op("act", lambda e: e.activation(out=st[0:nrows, 0:128 * n], in_=ps[0:nrows, 0:128 * n], func=AF.Copy),
                 reads=pb.r(), writes=stb.r())
            for (dap, r0, r1) in dsts:
                S.dma("sp", dap, st[r0:r1, 0:128 * n], reads=stb.r())

        def load_x(hi):
            t0, npr, has_s, sc0, nt = _half_geom(hi)
            S.op("dve", lambda e: e.memset(xT[:, :, 0:H], 0.0), writes=BxT.r(0, H))
            if has_s:
                for k in range(KC):
                    v = xT[:, k, sc0:nt].rearrange("p (s c) -> p s c", c=SW)[:, :, 0:H]
                    S.op("dve", lambda e: e.memset(v, 0.0), writes=BxT.r(sc0, nt))
            ntile = npr // 128 + (1 if has_s else 0)
            for ti in range(ntile):
                xi, xib = TF.next()
                xi2, xib2 = TF.next()
                is_s = ti == npr // 128
                src = xs_d if is_s else xp_d[t0 + 128 * ti:t0 + 128 * ti + 128, :]
                S.dma("sp", xi[:], src[:, 0:512], writes=xib.r())
                S.dma("sp", xi2[:], src[:, 512:1024], writes=xib2.r())
                for half, (xt_, xb_) in enumerate(((xi, xib), (xi2, xib2))):
                    ps, pb = PS.next()
                    for q in range(4):
                        S.op("pe", lambda e: e.transpose(ps[:, 128 * q:128 * q + 128], xt_[:, 128 * q:128 * q + 128], identf[:]),
                             reads=xb_.r() + Bid.r(), writes=pb.r())
                    eng = "act" if half == 0 else "dve"
                    if not is_s:
                        c0 = H + 128 * ti
                        dst = xT[:, 4 * half:4 * half + 4, c0:c0 + 128]
                        src_ps = ps[:, :].rearrange("p (k c) -> p k c", c=128)
                        if eng == "act":
                            S.op("act", lambda e: e.activation(out=dst, in_=src_ps, func=AF.Copy), reads=pb.r(), writes=BxT.r(c0, c0 + 128))
                        else:
                            S.op("dve", lambda e: e.tensor_copy(out=dst, in_=src_ps), reads=pb.r(), writes=BxT.r(c0, c0 + 128))
                    else:
                        for q in range(4):
                            k = 4 * half + q
                            dst = xT[:, k, sc0:nt].rearrange("p (s c) -> p s c", c=SW)[:, :, H:SW]
                            src_ps = ps[:, 128 * q:128 * q + 128].rearrange("p (s c) -> p s c", c=TS)
                            if eng == "act":
                                S.op("act", lambda e: e.activation(out=dst, in_=src_ps, func=AF.Copy), reads=pb.r(), writes=BxT.r(sc0, nt))
                            else:
                                S.op("dve", lambda e: e.tensor_copy(out=dst, in_=src_ps), reads=pb.r(), writes=BxT.r(sc0, nt))

        def final_out(hi):
            t0, npr, has_s, sc0, nt = _half_geom(hi)
            ntile = npr // 128 + (1 if has_s else 0)
            for ti in range(ntile):
                is_s = ti == npr // 128
                if not is_s:
                    c0, c1 = H + 128 * ti, H + 128 * ti + 128

                    def cols(t3, k):
                        return t3[:, k, c0:c1]
                else:
                    c0, c1 = sc0, nt

                    def cols(t3, k):
                        return t3[:, k, sc0:nt].rearrange("p (s c) -> p s c", c=SW)[:, :, H:SW]
                ps, pb = PS.next()
                for k in range(KC):
                    sq, sqb = TB.next()
                    sqv = sq[:, 0:128] if not is_s else sq[:, 0:128].rearrange("p (s c) -> p s c", c=TS)
                    S.op("act", lambda e: e.activation(out=sqv, in_=cols(xT, k), func=AF.Square),
                         reads=BxT.r(c0, c1), writes=sqb.r())
                    mm(ps[:, 0:128], ones[:, 0, :], sq[:, 0:128], k == 0, k == KC - 1, sqb.r() + Bones.r(), pb)
                sd, sdb = TF.next()
                S.op("act", lambda e: e.activation(out=sd[:, 0:128], in_=ps[:, 0:128], func=AF.Ln, bias=RMS_EPS, scale=1.0),
                     reads=pb.r(), writes=sdb.r())
                rs, rsb = TF.next()
                S.op("act", lambda e: e.activation(out=rs[:, 0:128], in_=sd[:, 0:128], func=AF.Exp, scale=-0.5), reads=sdb.r(), writes=rsb.r())
                rsv = rs[:, 0:128] if not is_s else rs[:, 0:128].rearrange("p (s c) -> p s c", c=TS)
                ya, yab = TF.next()
                yb_, ybb = TF.next()
                for k in range(KC):
                    yt = ya if k < 4 else yb_
                    ytb = yab if k < 4 else ybb
                    o = yt[:, 128 * (k % 4):128 * (k % 4) + 128]
                    if is_s:
                        o = o.rearrange("p (s c) -> p s c", c=TS)
                    S.op("dve", lambda e: e.scalar_tensor_tensor(out=o, in0=cols(xT, k), scalar=P("nfin", k), in1=rsv,
                                                                 op0=ALU.mult, op1=ALU.mult),
                         reads=BxT.r(c0, c1) + rsb.r() + Bpar.r(), writes=ytb.r())
                dst_rows = ys_d if is_s else yp_d[t0 + 128 * ti:t0 + 128 * ti + 128, :]
                for half, (yt, ytb) in enumerate(((ya, yab), (yb_, ybb))):
                    ps2, pb2 = PS.next()
                    for q in range(4):
                        S.op("pe", lambda e: e.transpose(ps2[:, 128 * q:128 * q + 128], yt[:, 128 * q:128 * q + 128], identf[:]),
                             reads=ytb.r() + Bid.r(), writes=pb2.r())
                    st, stb = TF.next()
                    if half == 0:
                        S.op("act", lambda e: e.activation(out=st[:, :], in_=ps2[:, :], func=AF.Copy), reads=pb2.r(), writes=stb.r())
                    else:
                        S.op("dve", lambda e: e.tensor_copy(out=st[:, :], in_=ps2[:, :]), reads=pb2.r(), writes=stb.r())
                    S.dma("sp", dst_rows[:, 512 * half:512 * half + 512], st[:, :], reads=stb.r())

        SH = {}

        def shget(ph, key, make):
            if SH.get("on"):
                if key not in SH:
                    SH[key] = make(SH["ph"])
                return SH[key]
            return make(ph)

        def mk_buf(p, name, shape, dt, ncols=None):
            t = p.enter_context(nc.sbuf_tensor(name, list(shape), dt))
            return t, Buf(t, name, ncols)

        def ffn(hi, l, tail):
            t0, npr, has_s, sc0, nt = _half_geom(hi)
            b3 = _blocks3(hi)
            ps_mode(8)
            with ExitStack() as ph_:
                ph = SH["ph"] if SH.get("on") else ph_
                jmax = max(g[1] for g in GROUPS)
                assert jmax == KC
                tag = "" if SH.get("on") else f"_{l}"
                act, Bact = shget(ph, ("mixact", hi), lambda p: mk_buf(p, f"mixact_{hi}{tag}", [128, jmax, nt], BF16, nt))
                FLT = shget(ph, ("LT", hi), lambda p: Pool(p, nc, f"lt{hi}_", 15, (128, 512), F32)) if SH.get("on") else None
                if has_s:
                    stt, Bstt = shget(ph, ("stt", hi), lambda p: mk_buf(p, f"stt_{hi}{tag}", [32, 2, jmax * 128], BF16))
                    zt, Bzt = shget(ph, ("zt", hi), lambda p: mk_buf(p, f"zt_{hi}{tag}", [128, 2, jmax, 34], F32))
                for (j0, jn) in GROUPS:
                    if has_s:
                        S.dma("pool", stt[:, 0, 0:jn * 128], sffn_d[l][:, 128 * j0:128 * (j0 + jn)], writes=Bstt.r())
                        S.dma("pool", stt[:, 1, 0:jn * 128], sffn_d[l][:, FF + 128 * j0:FF + 128 * (j0 + jn)], writes=Bstt.r())
                    for j in range(j0, j0 + jn):
                        jl = j - j0
                        zc = {}
                        for which, wname in ((0, "upg"), (1, "upu")):
                            wi, wt, wb = w_next((wname, l, j))
                            fch = j + NJ * which
                            o_w, _ = PAR[f"fcw{l}"]
                            o_b, _ = PAR[f"fcb{l}"]
                            wcol = lambda tap: par[:, o_w + 3 * fch + tap:o_w + 3 * fch + tap + 1]
                            bcol = par[:, o_b + fch:o_b + fch + 1]
                            zc[which] = []
                            for bi, (c0, ln) in enumerate(b3):
                                last = bi == len(b3) - 1
                                ps, pb = PS.next()
                                inj = has_s and last
                                proj_block(ps, pb, wi, wt, wb, xn, Bxn, c0, ln, last_stop=not inj)
                                if inj:
                                    mm(ps[:, 0:ln], stt[:, which, 128 * jl:128 * jl + 128], sel2[:, 0:ln], False, True,
                                       Bstt.r() + Bsel.r(), pb)
                                acc, accb = TF.next()
                                lo = ln - H
                                S.op("act", lambda e: e.activation(out=acc[:, 0:lo], in_=ps[:, H:ln], func=AF.Identity,
                                                                   scale=wcol(2), bias=bcol),
                                     reads=pb.r() + Bpar.r(), writes=accb.r())
                                if which == 0 and FLT is not None:
                                    tb_, tbb_ = FLT.next()
                                    S.op("act", lambda e: e.activation(out=tb_[:, 0:lo], in_=ps[:, H - 1:ln - 1], func=AF.Copy, scale=wcol(1)),
                                         reads=pb.r() + Bpar.r(), writes=tbb_.r())
                                    S.op("dve", lambda e: e.scalar_tensor_tensor(out=acc[:, 0:lo], in0=ps[:, H - 2:ln - 2], scalar=wcol(0),
                                                                                 in1=acc[:, 0:lo], op0=ALU.mult, op1=ALU.add),
                                         reads=pb.r() + Bpar.r() + accb.r(), writes=accb.r())
                                    S.op("pool", lambda e: e.tensor_tensor(out=acc[:, 0:lo], in0=acc[:, 0:lo], in1=tb_[:, 0:lo], op=ALU.add),
                                         reads=accb.r() + tbb_.r(), writes=accb.r(), dur=150.0 + 2.2 * lo)
                                else:
                                    S.op("dve", lambda e: e.scalar_tensor_tensor(out=acc[:, 0:lo], in0=ps[:, H - 1:ln - 1], scalar=wcol(1),
                                                                                 in1=acc[:, 0:lo], op0=ALU.mult, op1=ALU.add),
                                         reads=pb.r() + Bpar.r() + accb.r(), writes=accb.r())
                                    S.op("dve", lambda e: e.scalar_tensor_tensor(out=acc[:, 0:lo], in0=ps[:, H - 2:ln - 2], scalar=wcol(0),
                                                                                 in1=acc[:, 0:lo], op0=ALU.mult, op1=ALU.add),
                                         reads=pb.r() + Bpar.r() + accb.r(), writes=accb.r())
                                if has_s and last:
                                    pl = sc0 - c0
                                    S.op("act", lambda e: e.activation(out=zt[:, which, jl, 0:2], in_=ps[:, pl - 2:pl], func=AF.Copy),
                                         reads=pb.r(), writes=Bzt.r())
                                    sv = ps[:, pl:ln].rearrange("p (s c) -> p s c", c=SW)[:, :, H + 6:H + 8]
                                    dv = zt[:, which, jl, 2:34].rearrange("p (s c) -> p s c", c=2)
                                    S.op("act", lambda e: e.activation(out=dv, in_=sv, func=AF.Copy), reads=pb.r(), writes=Bzt.r())
                                zc[which].append((acc, accb, c0 + H, lo))
                        for (ag, agb, oc, lo), (au, aub, _, _) in zip(zc[0], zc[1]):
                            S.op("act", lambda e: e.activation(out=ag[:, 0:lo], in_=ag[:, 0:lo], func=AF.Gelu_apprx_tanh),
                                 reads=agb.r(), writes=agb.r())
                            S.op("dve", lambda e: e.tensor_tensor(out=act[:, jl, oc:oc + lo], in0=ag[:, 0:lo], in1=au[:, 0:lo], op=ALU.mult),
                                 reads=agb.r() + aub.r(), writes=Bact.r(oc, oc + lo))
                    if has_s:
                        for which in range(2):
                            for q0 in range(0, jn, 4):
                                qn = min(4, jn - q0)
                                f0 = FF * which + 128 * (j0 + q0)
                                srcs = [(zt[:, which, q0 + i, :], Bzt.r()) for i in range(qn)]
                                transpose_out(srcs, 34, [(o_pffn[l][:, f0:f0 + 128 * qn], 0, 2),
                                                         (o_sffn[l][:, f0:f0 + 128 * qn], 2, 34)])
                    wds = [w_next(("wd", l, j)) for j in range(j0, j0 + jn)]
                    for bi0, (c0, ln) in enumerate(_blocks0(hi)):
                        if bi0 == 0:
                            NOPEN = 4
                            pss = [PS.next() for o in range(NOPEN)]
                            for o in range(NOPEN):
                                ps, pb = pss[o]
                                for jl in range(jn - 1):
                                    wi, wt, wb = wds[jl]
                                    w_check(wi)
                                    mm(ps[:, 0:ln], wt[:, 128 * o:128 * o + 128], act[:, jl, c0:c0 + ln], jl == 0, False,
                                       wb.r() + Bact.r(c0, c0 + ln), pb)
                            for o in range(NOPEN):
                                ps, pb = pss[o]
                                wi, wt, wb = wds[jn - 1]
                                mm(ps[:, 0:ln], wt[:, 128 * o:128 * o + 128], act[:, jn - 1, c0:c0 + ln], False, True,
                                   wb.r() + Bact.r(c0, c0 + ln), pb)
                                S.op("dve", lambda e: e.tensor_tensor(out=xT[:, o, c0:c0 + ln], in0=xT[:, o, c0:c0 + ln],
                                                                      in1=ps[:, 0:ln], op=ALU.add),
                                     reads=pb.r() + BxT.r(c0, c0 + ln), writes=BxT.r(c0, c0 + ln))
                        for o in range(NOPEN if bi0 == 0 else 0, 8):
                            ps, pb = PS.next()
                            for jl in range(jn):
                                wi, wt, wb = wds[jl]
                                w_check(wi)
                                mm(ps[:, 0:ln], wt[:, 128 * o:128 * o + 128], act[:, jl, c0:c0 + ln], jl == 0, jl == jn - 1,
                                   wb.r() + Bact.r(c0, c0 + ln), pb)
                            S.op("dve", lambda e: e.tensor_tensor(out=xT[:, o, c0:c0 + ln], in0=xT[:, o, c0:c0 + ln],
                                                                  in1=ps[:, 0:ln], op=ALU.add),
                                 reads=pb.r() + BxT.r(c0, c0 + ln), writes=BxT.r(c0, c0 + ln))
                tail()
                if not SH.get("on"):
                    S.barrier()

        def lru_mixer(hi, tail):
            t0, npr, has_s, sc0, nt = _half_geom(hi)
            b3 = _blocks3(hi)
            ps_mode(8)
            with ExitStack() as ph_:
                ph = SH["ph"] if SH.get("on") else ph_
                LT = shget(ph, ("LT", hi), lambda p: Pool(p, nc, f"lt{hi}_", 15, (128, 512), F32))
                mix, Bmix = shget(ph, ("mixact", hi), lambda p: mk_buf(p, f"mixact_{hi}", [128, KC, nt], BF16, nt))
                wax = ph.enter_context(nc.sbuf_tensor(f"wax_{hi}", [128, 2, 8, 128], BF16))
                Bwax = Buf(wax, "wax")
                S.dma("pool", wax[:, 0, :, :], w_lru_a.rearrange("n c d -> c n d"), writes=Bwax.r())
                S.dma("pool", wax[:, 1, :, :], w_lru_x.rearrange("n c d -> c n d"), writes=Bwax.r())
                if has_s:
                    st3 = ph.enter_context(nc.sbuf_tensor(f"st3_{hi}", [48, D], BF16))
                    Bst3 = Buf(st3, "st3")
                    S.dma("pool", st3[:], slc_d, writes=Bst3.r())
                    h0in = ph.enter_context(nc.sbuf_tensor(f"h0in_{hi}", [16, D], F32))
                    Bh0in = Buf(h0in, "h0in")
                    S.dma("sp", h0in[:], slh_d, writes=Bh0in.r())
                    h0T = ph.enter_context(nc.sbuf_tensor(f"h0T_{hi}", [128, 8, 16], F32))
                    Bh0T = Buf(h0T, "h0T")
                    ps, pb = PS.next()
                    for n in range(8):
                        S.op("pe", lambda e: e.transpose(ps[:, 16 * n:16 * n + 16], h0in[:, 128 * n:128 * n + 128], identf[0:16, 0:16]),
                             reads=Bh0in.r() + Bid.r(), writes=pb.r())
                    S.op("act", lambda e: e.activation(out=h0T[:].rearrange("p n s -> p (n s)"), in_=ps[:, 0:128], func=AF.Copy),
                         reads=pb.r(), writes=Bh0T.r())
                    rt = ph.enter_context(nc.sbuf_tensor(f"rt_{hi}", [128, 8, 51], F32))
                    Brt = Buf(rt, "rt")
                    ht = ph.enter_context(nc.sbuf_tensor(f"ht_{hi}", [128, 8, 17], F32))
                    Bht = Buf(ht, "ht")
                hprev = ph.enter_context(nc.sbuf_tensor(f"hprev_{hi}", [128, 8, 4], F32))
                Bhp = Buf(hprev, "hprev")
                for n in range(8):
                    wgi, wgt, wgb = w_next(("gate", n))
                    wri, wrt, wrb = w_next(("rec", n))
                    o_w, _ = PAR["ccw"]
                    wcol = lambda tap: par[:, o_w + 4 * n + tap:o_w + 4 * n + tap + 1]
                    units = []
                    for bi, (c0, ln) in enumerate(b3):
                        last = bi == len(b3) - 1
                        lo = ln - H
                        oc = c0 + H
                        psg, pbg = PS.next()
                        proj_block(psg, pbg, wgi, wgt, wgb, xn, Bxn, c0, ln)
                        S.op("act", lambda e: e.activation(out=mix[:, n, oc:oc + lo], in_=psg[:, H:ln], func=AF.Gelu_apprx_tanh),
                             reads=pbg.r(), writes=Bmix.r(oc, oc + lo))
                        psr, pbr = PS.next()
                        inj = has_s and last
                        proj_block(psr, pbr, wri, wrt, wrb, xn, Bxn, c0, ln, last_stop=not inj)
                        if inj:
                            mm(psr[:, 0:ln], st3[:, 128 * n:128 * n + 128], sel3[:, 0:ln], False, True, Bst3.r() + Bsel.r(), pbr)
                        xc, xcb = LT.next()
                        S.op("dve", lambda e: e.tensor_scalar(out=xc[:, 0:lo], in0=psr[:, H:ln], scalar1=wcol(3), scalar2=P("ccb", n),
                                                              op0=ALU.mult, op1=ALU.add),
                             reads=pbr.r() + Bpar.r(), writes=xcb.r())
                        for tap in (2, 1, 0):
                            sh = 3 - tap
                            S.op("dve", lambda e: e.scalar_tensor_tensor(out=xc[:, 0:lo], in0=psr[:, H - sh:ln - sh], scalar=wcol(tap),
                                                                         in1=xc[:, 0:lo], op0=ALU.mult, op1=ALU.add),
                                 reads=pbr.r() + Bpar.r() + xcb.r(), writes=xcb.r())
                        if has_s and last:
                            pl = sc0 - c0
                            S.op("act", lambda e: e.activation(out=rt[:, n, 0:3], in_=psr[:, pl - 3:pl], func=AF.Copy),
                                 reads=pbr.r(), writes=Brt.r())
                            sv = psr[:, pl:ln].rearrange("p (s c) -> p s c", c=SW)[:, :, H + 5:H + 8]
                            dv = rt[:, n, 3:51].rearrange("p (s c) -> p s c", c=3)
                            S.op("act", lambda e: e.activation(out=dv, in_=sv, func=AF.Copy), reads=pbr.r(), writes=Brt.r())
                        units.append((bi, c0, ln, last, lo, oc, xc, xcb))
                    for (bi, c0, ln, last, lo, oc, xc, xcb) in units:
                        xb, xbb = TB.next()
                        S.op("dve", lambda e: e.tensor_copy(out=xb[:, 0:lo], in_=xc[:, 0:lo]), reads=xcb.r(), writes=xbb.r())
                        psa, pba = PS.next()
                        mm(psa[:, 0:lo], wax[:, 0, n, :], xb[:, 0:lo], True, True, Bwax.r() + xbb.r(), pba)
                        psx, pbx = PS.next()
                        mm(psx[:, 0:lo], wax[:, 1, n, :], xb[:, 0:lo], True, True, Bwax.r() + xbb.r(), pbx)
                        A, Ab = LT.next()
                        I, Ib = TF.next()
                        S2, S2b = TF.next()
                        S.op("act", lambda e: e.activation(out=A[:, 0:lo], in_=psa[:, 0:lo], func=AF.Tanh, bias=hbias[:, 0, n:n + 1], scale=0.5),
                             reads=pba.r() + Blruc.r(), writes=Ab.r())
                        S.op("act", lambda e: e.activation(out=I[:, 0:lo], in_=psx[:, 0:lo], func=AF.Tanh, bias=hbias[:, 1, n:n + 1], scale=0.5),
                             reads=pbx.r() + Blruc.r(), writes=Ib.r())
                        S.op("act", lambda e: e.activation(out=S2[:, 0:lo], in_=A[:, 0:lo], func=AF.Exp, scale=lruc[:, n, 1:2], bias=lruc[:, n, 1:2]),
                             reads=Ab.r() + Blruc.r(), writes=S2b.r())
                        S.op("act", lambda e: e.activation(out=A[:, 0:lo], in_=A[:, 0:lo], func=AF.Exp, scale=lruc[:, n, 0:1], bias=lruc[:, n, 0:1]),
                             reads=Ab.r() + Blruc.r(), writes=Ab.r())
                        S.op("dve", lambda e: e.scalar_tensor_tensor(out=I[:, 0:lo], in0=I[:, 0:lo], scalar=1.0, in1=xc[:, 0:lo], op0=ALU.add, op1=ALU.mult),
                             reads=Ib.r() + xcb.r(), writes=Ib.r())
                        S.op("act", lambda e: e.activation(out=S2[:, 0:lo], in_=S2[:, 0:lo], func=AF.Sqrt, scale=-0.25, bias=0.25),
                             reads=S2b.r(), writes=S2b.r())
                        S.op("dve", lambda e: e.tensor_tensor(out=I[:, 0:lo], in0=I[:, 0:lo], in1=S2[:, 0:lo], op=ALU.mult),
                             reads=Ib.r() + S2b.r(), writes=Ib.r())
                        if has_s and last:
                            pl = sc0 - oc
                            av = A[:, pl:lo].rearrange("p (s c) -> p s c", c=SW)[:, :, 0:H]
                            bv = I[:, pl:lo].rearrange("p (s c) -> p s c", c=SW)
                            S.op("dve", lambda e: e.memset(av, 0.0), writes=Ab.r())
                            S.op("dve", lambda e: e.memset(bv[:, :, 0:H - 1], 0.0), writes=Ib.r())
                            S.op("dve", lambda e: e.tensor_copy(out=bv[:, :, H - 1:H], in_=h0T[:, n, :].unsqueeze(2)),
                                 reads=Bh0T.r(), writes=Ib.r())
                        if bi == 0:
                            init = 0.0 if hi == 0 else ch[:, n:n + 1]
                            ird = [] if hi == 0 else Bch.r()
                        else:
                            init = hprev[:, n, bi - 1:bi]
                            ird = Bhp.r()
                        S.op("dve", lambda e: e.tensor_tensor_scan(out=xc[:, 0:lo], data0=A[:, 0:lo], data1=I[:, 0:lo], initial=init,
                                                                   op0=ALU.mult, op1=ALU.add),
                             reads=Ab.r() + Ib.r() + ird + xcb.r(), writes=xcb.r(), dur=200.0 + 2.3 * lo)
                        S.op("act", lambda e: e.activation(out=hprev[:, n, bi:bi + 1], in_=xc[:, lo - 1:lo], func=AF.Copy),
                             reads=xcb.r(), writes=Bhp.r())
                        if last:
                            if hi == 0:
                                S.op("act", lambda e: e.activation(out=ch[:, n:n + 1], in_=xc[:, lo - 1:lo], func=AF.Copy),
                                     reads=xcb.r(), writes=Bch.r())
                            else:
                                pl = sc0 - oc
                                S.op("act", lambda e: e.activation(out=ht[:, n, 0:1], in_=xc[:, pl - 1:pl], func=AF.Copy),
                                     reads=xcb.r(), writes=Bht.r())
                                sv = xc[:, pl:lo].rearrange("p (s c) -> p s c", c=SW)[:, :, SW - 1:SW]
                                S.op("act", lambda e: e.activation(out=ht[:, n, 1:17].unsqueeze(2), in_=sv, func=AF.Copy),
                                     reads=xcb.r(), writes=Bht.r())
                        S.op("dve", lambda e: e.tensor_tensor(out=mix[:, n, oc:oc + lo], in0=xc[:, 0:lo], in1=mix[:, n, oc:oc + lo], op=ALU.mult),
                             reads=xcb.r() + Bmix.r(oc, oc + lo), writes=Bmix.r(oc, oc + lo))
                if has_s:
                    for q0 in (0, 4):
                        transpose_out([(rt[:, q0 + i, :], Brt.r()) for i in range(4)], 51,
                                      [(o_plc[:, 512 * (q0 // 4):512 * (q0 // 4) + 512], 0, 3),
                                       (o_slc[:, 512 * (q0 // 4):512 * (q0 // 4) + 512], 3, 51)])
                        transpose_out([(ht[:, q0 + i, :], Bht.r()) for i in range(4)], 17,
                                      [(o_plh[:, 512 * (q0 // 4):512 * (q0 // 4) + 512], 0, 1),
                                       (o_slh[:, 512 * (q0 // 4):512 * (q0 // 4) + 512], 1, 17)])
                out_proj(hi, "wo1", mix, Bmix)
                tail()
                if not SH.get("on"):
                    S.barrier()

        def gn_block_3d(ps_o, pb_o, width, cn, qd_ap, dst_fn, c_lo, c_hi, h, Bmix, BSG):
            o, ob = TF.next()
            if qd_ap is not None:
                S.op("dve", lambda e: e.tensor_tensor(out=o[:, 0:width].rearrange("p (a b) -> p a b", b=128),
                                                      in0=ps_o[:, 0:width].rearrange("p (a b) -> p a b", b=128), in1=qd_ap, op=ALU.mult),
                     reads=pb_o.r() + Bcon.r(), writes=ob.r())
            else:
                S.op("act", lambda e: e.activation(out=o[:, 0:width], in_=ps_o[:, 0:width], func=AF.Copy), reads=pb_o.r(), writes=ob.r())
            o16, o16b = TB.next()
            S.op("act", lambda e: e.activation(out=o16[:, 0:width], in_=o[:, 0:width], func=AF.Copy), reads=ob.r(), writes=o16b.r())
            q16, q16b = TB.next()
            S.op("act", lambda e: e.activation(out=q16[:, 0:width], in_=o[:, 0:width], func=AF.Square), reads=ob.r(), writes=q16b.r())
            psm, pbm = PSL.next()
            mm(psm[:, 0:width], ones[:, 2, :], o16[:, 0:width], True, True, o16b.r() + Bones.r(), pbm)
            psq, pbq = PSL.next()
            mm(psq[:, 0:width], ones[:, 2, :], q16[:, 0:width], True, True, q16b.r() + Bones.r(), pbq)
            mean, meanb = TF.next()
            S.op("act", lambda e: e.activation(out=mean[:, 0:width], in_=psm[:, 0:width], func=AF.Copy), reads=pbm.r(), writes=meanb.r())
            var, varb = TF.next()
            S.op("dve", lambda e: e.tensor_tensor(out=var[:, 0:width], in0=mean[:, 0:width], in1=mean[:, 0:width], op=ALU.mult),
                 reads=meanb.r(), writes=varb.r())
            S.op("dve", lambda e: e.tensor_tensor(out=var[:, 0:width], in0=psq[:, 0:width], in1=var[:, 0:width], op=ALU.subtract),
                 reads=pbq.r() + varb.r(), writes=varb.r())
            S.op("dve", lambda e: e.tensor_scalar_max(out=var[:, 0:width], in0=var[:, 0:width], scalar1=0.0), reads=varb.r(), writes=varb.r())
            S.op("act", lambda e: e.activation(out=var[:, 0:width], in_=var[:, 0:width], func=AF.Ln, bias=GN_EPS, scale=1.0),
                 reads=varb.r(), writes=varb.r())
            rs, rsb = TF.next()
            S.op("act", lambda e: e.activation(out=rs[:, 0:width], in_=var[:, 0:width], func=AF.Exp, scale=-0.5), reads=varb.r(), writes=rsb.r())
            S.op("dve", lambda e: e.tensor_tensor(out=o[:, 0:width], in0=o[:, 0:width], in1=mean[:, 0:width], op=ALU.subtract),
                 reads=ob.r() + meanb.r(), writes=ob.r())
            S.op("dve", lambda e: e.tensor_tensor(out=o[:, 0:width], in0=o[:, 0:width], in1=rs[:, 0:width], op=ALU.mult),
                 reads=ob.r() + rsb.r(), writes=ob.r())
            dst, sgv, ov = dst_fn(o)
            S.op("dve", lambda e: e.scalar_tensor_tensor(out=dst, in0=ov, scalar=P("gng", h), in1=sgv, op0=ALU.mult, op1=ALU.mult),
                 reads=ob.r() + BSG.r(c_lo, c_hi) + Bpar.r(), writes=Bmix.r(c_lo, c_hi))


        def even_mixer(hi, tail):
            t0, npr, has_s, sc0, nt = _half_geom(hi)
            b0 = _blocks0(hi)
            nch = npr // 128
            UW = 30 + npr + (NSQ * 38 if has_s else 0)
            us0 = 30 + npr
            ps_mode(6)
            with ExitStack() as ph:
                mix = ph.enter_context(nc.sbuf_tensor(f"mix_{hi}", [128, KC, nt], BF16))
                Bmix = Buf(mix, "mix", nt)
                S.op("dve", lambda e: e.memset(mix[:, :, :], 0.0), writes=Bmix.r())
                with ExitStack() as pa:
                    uT = pa.enter_context(nc.sbuf_tensor(f"uT_{hi}", [128, 4, UW], BF16))
                    BuTc = [Buf(uT, f"uT{c_}", UW) for c_ in range(4)]

                    class _AllU:
                        def r(self, c0=None, c1=None):
                            out = []
                            for b_ in BuTc:
                                out += b_.r(c0, c1)
                            return out
                    BuT = _AllU()
                    diag = pa.enter_context(nc.sbuf_tensor(f"diag_{hi}", [128, 4, 31, 128], BF16))
                    Bdiagc = [Buf(diag, f"diag{c_}") for c_ in range(4)]
                    ut = pa.enter_context(nc.sbuf_tensor(f"ut_{hi}", [128, 4, 30], F32))
                    But = Buf(ut, "ut")
                    o_w, _ = PAR["caw"]
                    for c in range(4):
                        for k in range(31):
                            S.op("dve", lambda e: e.tensor_scalar(out=diag[:, c, k, :], in0=identb[:], scalar1=par[:, o_w + 31 * c + k:o_w + 31 * c + k + 1],
                                                                  scalar2=None, op0=ALU.mult),
                                 reads=Bidb.r() + Bpar.r(), writes=Bdiagc[c].r())
                    if hi == 0:
                        S.op("dve", lambda e: e.memset(uT[:, :, 0:30], 0.0), writes=BuT.r(0, 30))
                    else:
                        S.op("dve", lambda e: e.tensor_copy(out=uT[:, :, 0:30], in_=cu[:, :, :]), reads=Bcu.r(), writes=BuT.r(0, 30))
                    if has_s:
                        us = pa.enter_context(nc.sbuf_tensor(f"us_{hi}", [128, 4, 128], F32))
                        Bus = Buf(us, "us")
                        for q in range(4):
                            si, sib = TF.next()
                            S.dma("sp", si[0:120, :], sca_d[120 * q:120 * q + 120, :], writes=sib.r())
                            ps, pb = PS.next()
                            for c in range(4):
                                S.op("pe", lambda e: e.transpose(ps[:, 128 * c:128 * c + 120], si[0:120, 128 * c:128 * c + 128], identf[0:120, 0:120]),
                                     reads=sib.r() + Bid.r(), writes=pb.r())
                            for c in range(4):
                                dst = uT[:, c, us0 + 38 * 4 * q:us0 + 38 * 4 * (q + 1)].rearrange("p (s c) -> p s c", c=38)[:, :, 0:30]
                                src = ps[:, 128 * c:128 * c + 120].rearrange("p (s c) -> p s c", c=30)
                                S.op("act", lambda e: e.activation(out=dst, in_=src, func=AF.Copy), reads=pb.r(), writes=BuTc[c].r(us0, UW))
                        S.dma("sp", o_sca.rearrange("(s r) f -> s r f", r=30)[:, 0:22, :],
                              sca_d.rearrange("(s r) f -> s r f", r=30)[:, 8:30, :])
                    for c in range(4):
                        wli, wlt, wlb = w_next(("alin", c))
                        wgi, wgt, wgb = w_next(("agate", c))
                        for bi, (c0, ln) in enumerate(b0):
                            psl, pbl = PS.next()
                            proj_block(psl, pbl, wli, wlt, wlb, xn, Bxn, c0, ln)
                            psg, pbg = PS.next()
                            proj_block(psg, pbg, wgi, wgt, wgb, xn, Bxn, c0, ln)
                            sg, sgb = TF.next()
                            S.op("act", lambda e: e.activation(out=sg[:, 0:ln], in_=psg[:, 0:ln], func=AF.Sigmoid),
                                 reads=pbg.r(), writes=sgb.r())
                            p1 = min(c0 + ln, sc0)
                            pn = p1 - c0
                            if pn > 0:
                                uc = 30 + (c0 - H)
                                S.op("dve", lambda e: e.tensor_tensor(out=uT[:, c, uc:uc + pn], in0=psl[:, 0:pn], in1=sg[:, 0:pn], op=ALU.mult),
                                     reads=pbl.r() + sgb.r(), writes=BuTc[c].r(uc, uc + pn))
                                if p1 == sc0:
                                    S.op("dve", lambda e: e.tensor_tensor(out=ut[:, c, :], in0=psl[:, pn - 30:pn], in1=sg[:, pn - 30:pn], op=ALU.mult),
                                         reads=pbl.r() + sgb.r(), writes=But.r())
                            if has_s and c0 + ln > sc0:
                                pl = sc0 - c0
                                lv = psl[:, pl:ln].rearrange("p (s c) -> p s c", c=SW)[:, :, H:SW]
                                gv = sg[:, pl:ln].rearrange("p (s c) -> p s c", c=SW)[:, :, H:SW]
                                dv = uT[:, c, us0:UW].rearrange("p (s c) -> p s c", c=38)[:, :, 30:38]
                                S.op("dve", lambda e: e.tensor_tensor(out=dv, in0=lv, in1=gv, op=ALU.mult),
                                     reads=pbl.r() + sgb.r(), writes=BuTc[c].r(us0, UW))
                                S.op("dve", lambda e: e.tensor_tensor(out=us[:, c, :].rearrange("p (s c) -> p s c", c=TS), in0=lv, in1=gv, op=ALU.mult),
                                     reads=pbl.r() + sgb.r(), writes=Bus.r())
                    if hi == 0:
                        S.op("act", lambda e: e.activation(out=cu[:, :, :], in_=uT[:, :, us0 - 30:us0], func=AF.Copy), reads=BuT.r(us0 - 30, us0), writes=Bcu.r())
                    else:
                        transpose_out([(ut[:, c, :], But.r()) for c in range(4)], 30, [(o_pca[:, :], 0, 30)])
                        ps, pb = PS.next()
                        for c in range(4):
                            S.op("pe", lambda e: e.transpose(ps[:, 128 * c:128 * c + 128], us[:, c, :], identf[:]),
                                 reads=Bus.r() + Bid.r(), writes=pb.r())
                        st, stb = TF.next()
                        S.op("act", lambda e: e.activation(out=st[:, :], in_=ps[:, :], func=AF.Copy), reads=pb.r(), writes=stb.r())
                        osv = o_sca.rearrange("(s r) f -> s r f", r=30)
                        for s in range(NSQ):
                            S.dma("sp", osv[s, 22:30, :], st[8 * s:8 * s + 8, :], reads=stb.r())
                    for bi, (c0, ln) in enumerate(b0):
                        segs = []
                        p1 = min(c0 + ln, sc0)
                        if p1 > c0:
                            segs.append(("p", c0, p1 - c0))
                        if has_s and c0 + ln > sc0:
                            segs.append(("s", sc0, NSQ * TS))
                        for kind, s0, sl in segs:
                            cvs = []
                            psm, pbm = PSL.next()
                            psq, pbq = PSL.next()
                            for c in range(4):
                                ps, pb = PS.next()
                                for k in range(31):
                                    if kind == "p":
                                        ub = (s0 - H) + k
                                        rhs = uT[:, c, ub:ub + sl]
                                        urd = BuTc[c].r(ub, ub + sl)
                                    else:
                                        rhs = uT[:, c, us0:UW].rearrange("p (s c) -> p s c", c=38)[:, :, k:k + TS]
                                        urd = BuTc[c].r(us0, UW)
                                    mm(ps[:, 0:sl], diag[:, c, k, :], rhs, k == 0, k == 30, urd + Bdiagc[c].r(), pb)
                                cv, cvb = TF.next()
                                S.op("act", lambda e: e.activation(out=cv[:, 0:sl], in_=ps[:, 0:sl], func=AF.Identity, bias=P("cab", c), scale=1.0),
                                     reads=pb.r() + Bpar.r(), writes=cvb.r())
                                c16, c16b = TB.next()
                                S.op("act", lambda e: e.activation(out=c16[:, 0:sl], in_=cv[:, 0:sl], func=AF.Copy), reads=cvb.r(), writes=c16b.r())
                                q16, q16b = TB.next()
                                S.op("act", lambda e: e.activation(out=q16[:, 0:sl], in_=cv[:, 0:sl], func=AF.Square), reads=cvb.r(), writes=q16b.r())
                                mm(psm[:, 0:sl], ones[:, 1, :], c16[:, 0:sl], c == 0, c == 3, c16b.r() + Bones.r(), pbm)
                                mm(psq[:, 0:sl], ones[:, 1, :], q16[:, 0:sl], c == 0, c == 3, q16b.r() + Bones.r(), pbq)
                                cvs.append((cv, cvb))
                            mean, meanb = TF.next()
                            S.op("act", lambda e: e.activation(out=mean[:, 0:sl], in_=psm[:, 0:sl], func=AF.Copy), reads=pbm.r(), writes=meanb.r())
                            var, varb = TF.next()
                            S.op("dve", lambda e: e.tensor_tensor(out=var[:, 0:sl], in0=mean[:, 0:sl], in1=mean[:, 0:sl], op=ALU.mult),
                                 reads=meanb.r(), writes=varb.r())
                            S.op("dve", lambda e: e.tensor_tensor(out=var[:, 0:sl], in0=psq[:, 0:sl], in1=var[:, 0:sl], op=ALU.subtract),
                                 reads=pbq.r() + varb.r(), writes=varb.r())
                            S.op("dve", lambda e: e.tensor_scalar_max(out=var[:, 0:sl], in0=var[:, 0:sl], scalar1=0.0), reads=varb.r(), writes=varb.r())
                            S.op("act", lambda e: e.activation(out=var[:, 0:sl], in_=var[:, 0:sl], func=AF.Ln, bias=LN_EPS, scale=1.0),
                                 reads=varb.r(), writes=varb.r())
                            rs, rsb = TF.next()
                            S.op("act", lambda e: e.activation(out=rs[:, 0:sl], in_=var[:, 0:sl], func=AF.Exp, scale=-0.5), reads=varb.r(), writes=rsb.r())
                            for c in range(4):
                                cv, cvb = cvs[c]
                                S.op("dve", lambda e: e.tensor_tensor(out=cv[:, 0:sl], in0=cv[:, 0:sl], in1=mean[:, 0:sl], op=ALU.subtract),
                                     reads=cvb.r() + meanb.r(), writes=cvb.r())
                                S.op("dve", lambda e: e.tensor_tensor(out=cv[:, 0:sl], in0=cv[:, 0:sl], in1=rs[:, 0:sl], op=ALU.mult),
                                     reads=cvb.r() + rsb.r(), writes=cvb.r())
                                if kind == "p":
                                    dst = mix[:, c, s0:s0 + sl]
                                    src = cv[:, 0:sl]
                                    wr = Bmix.r(s0, s0 + sl)
                                else:
                                    dst = mix[:, c, sc0:nt].rearrange("p (s c) -> p s c", c=SW)[:, :, H:SW]
                                    src = cv[:, 0:sl].rearrange("p (s c) -> p s c", c=TS)
                                    wr = Bmix.r(sc0, nt)
                                S.op("act", lambda e: e.activation(out=dst, in_=src, func=AF.Silu, scale=P("lng", c), bias=P("lnb", c)),
                                     reads=cvb.r() + Bpar.r(), writes=wr)
                    S.fence(fsc[:, 0:1], Bfsc, keep=[Bmix])
                with ExitStack() as pbk:
                    rope = pbk.enter_context(nc.sbuf_tensor(f"rope_{hi}", [128, 2, nt], F32))
                    Brope = Buf(rope, "rope")
                    S.dma("sp", rope[:], rope_d[hi], writes=Brope.r())
                    QT = pbk.enter_context(nc.sbuf_tensor(f"QT_{hi}", [128, nt], BF16))
                    KT = pbk.enter_context(nc.sbuf_tensor(f"KT_{hi}", [128, nt], BF16))
                    VT = pbk.enter_context(nc.sbuf_tensor(f"VT_{hi}", [128, nt], BF16))
                    SG = pbk.enter_context(nc.sbuf_tensor(f"SG_{hi}", [128, nt], F32))
                    BQT, BKT, BVT, BSG = Buf(QT, "QT", nt), Buf(KT, "KT", nt), Buf(VT, "VT", nt), Buf(SG, "SG", nt)
                    ntl = nch + (1 if has_s else 0)
                    Ktok = pbk.enter_context(nc.sbuf_tensor(f"Ktok_{hi}", [128, ntl, 128], BF16))
                    Vdec = pbk.enter_context(nc.sbuf_tensor(f"Vdec_{hi}", [128, ntl, 128], BF16))
                    BKtok, BVdec = Buf(Ktok, "Ktok"), Buf(Vdec, "Vdec")
                    Sbf = pbk.enter_context(nc.sbuf_tensor(f"Sbf_{hi}", [128, nch + 1, 128], BF16))
                    BSbf = [Buf(Sbf, f"Sbf{i}") for i in range(nch + 1)]
                    Sf = pbk.enter_context(nc.sbuf_tensor(f"Sf_{hi}", [128, 128], F32))
                    BSf = Buf(Sf, "Sf")
                    if has_s:
                        S0f = pbk.enter_context(nc.sbuf_tensor(f"S0f_{hi}", [128, NSQ, 128], F32))
                        BS0f = Buf(S0f, "S0f")
                        S0b = pbk.enter_context(nc.sbuf_tensor(f"S0b_{hi}", [128, NSQ, 128], BF16))
                        BS0b = Buf(S0b, "S0b")
                        Vbd = pbk.enter_context(nc.sbuf_tensor(f"Vbd_{hi}", [128, NSQ, 128], BF16))
                        BVbd = Buf(Vbd, "Vbd")
                        Kds = pbk.enter_context(nc.sbuf_tensor(f"Kds_{hi}", [128, 128], BF16))
                        BKds = Buf(Kds, "Kds")
                        Qds = pbk.enter_context(nc.sbuf_tensor(f"Qds_{hi}", [128, 128], BF16))
                        BQds = Buf(Qds, "Qds")
                        cmpS = pbk.enter_context(nc.sbuf_tensor(f"cmpS_{hi}", [128, 3, 128], BF16))
                        BcmpS = Buf(cmpS, "cmpS")
                    dk_scale = 128.0 ** -0.5
                    for h in range(4):
                        ws = {nm: w_next((nm, h)) for nm in ("q", "k", "v", "g")}
                        if has_s:
                            S.dma("sp", S0f[:], sret_d[:, h, :, :].rearrange("s d v -> d s v"), writes=BS0f.r())
                            S.dma("pool", S0b[:], sret_d[:, h, :, :].rearrange("s d v -> d s v"), writes=BS0b.r())
                        for (c0, ln) in b0:
                            for (dstT, Bd, n1, sc) in ((QT, BQT, "q", 1.0), (KT, BKT, "k", dk_scale)):
                                ps1, pb1 = PS.next()
                                proj_block(ps1, pb1, *ws[n1], xn, Bxn, c0, ln)
                                qb, qbb = TB.next()
                                S.op("act", lambda e: e.activation(out=qb[:, 0:ln], in_=ps1[:, 0:ln], func=AF.Copy), reads=pb1.r(), writes=qbb.r())
                                ps2, pb2 = PS.next()
                                mm(ps2[:, 0:ln], permb[:], qb[:, 0:ln], True, True, Bperm.r() + qbb.r(), pb2)
                                t1, t1b = TF.next()
                                t2, t2b = TF.next()
                                S.op("dve", lambda e: e.scalar_tensor_tensor(out=t1[:, 0:ln], in0=ps1[:, 0:ln], scalar=sc, in1=rope[:, 0, c0:c0 + ln],
                                                                             op0=ALU.mult, op1=ALU.mult),
                                     reads=pb1.r() + Brope.r(), writes=t1b.r())
                                S.op("dve", lambda e: e.scalar_tensor_tensor(out=t2[:, 0:ln], in0=ps2[:, 0:ln], scalar=sc, in1=rope[:, 1, c0:c0 + ln],
                                                                             op0=ALU.mult, op1=ALU.mult),
                                     reads=pb2.r() + Brope.r(), writes=t2b.r())
                                S.op("dve", lambda e: e.tensor_tensor(out=dstT[:, c0:c0 + ln], in0=t1[:, 0:ln], in1=t2[:, 0:ln], op=ALU.add),
                                     reads=t1b.r() + t2b.r(), writes=Bd.r(c0, c0 + ln))
                            psv, pbv = PS.next()
                            proj_block(psv, pbv, *ws["v"], xn, Bxn, c0, ln)
                            S.op("act", lambda e: e.activation(out=VT[:, c0:c0 + ln], in_=psv[:, 0:ln], func=AF.Copy), reads=pbv.r(), writes=BVT.r(c0, c0 + ln))
                            psg, pbg = PS.next()
                            proj_block(psg, pbg, *ws["g"], xn, Bxn, c0, ln)
                            S.op("act", lambda e: e.activation(out=SG[:, c0:c0 + ln], in_=psg[:, 0:ln], func=AF.Silu), reads=pbg.r(), writes=BSG.r(c0, c0 + ln))

                        if has_s:
                            for (srcT, Bs, j_) in ((QT, BQT, 0), (KT, BKT, 1), (VT, BVT, 2)):
                                S.op("act", lambda e: e.activation(out=cmpS[:, j_, :].rearrange("p (s c) -> p s c", c=TS),
                                                                   in_=srcT[:, sc0:nt].rearrange("p (s c) -> p s c", c=SW)[:, :, H:SW], func=AF.Copy),
                                     reads=Bs.r(sc0, nt), writes=BcmpS.r())
                        cmp_idx = {id(QT): 0, id(KT): 1, id(VT): 2}

                        def tcols(tsr, i):
                            if i < nch:
                                return tsr[:, H + 128 * i:H + 128 * i + 128]
                            return cmpS[:, cmp_idx[id(tsr)], :]

                        def tregs(B, i):
                            if i < nch:
                                return B.r(H + 128 * i, H + 128 * i + 128)
                            return BcmpS.r()
                        for i0 in range(0, ntl, 8):
                            i1 = min(ntl, i0 + 8)
                            for (srcT, Bs, dstk) in ((KT, BKT, "k"), (VT, BVT, "v")):
                                ps, pb = PS.next()
                                psb = ps[:].bitcast(BF16)
                                for i in range(i0, i1):
                                    S.op("pe", lambda e: e.transpose(psb[:, 128 * (i - i0):128 * (i - i0) + 128], tcols(srcT, i), identb[:]),
                                         reads=tregs(Bs, i) + Bidb.r(), writes=pb.r())
                                npz = min(i1, nch) - i0
                                if dstk == "k":
                                    if npz > 0:
                                        S.op("act", lambda e: e.activation(out=Ktok[:, i0:i0 + npz, :].rearrange("p a b -> p (a b)"),
                                                                           in_=psb[:, 0:128 * npz], func=AF.Copy),
                                             reads=pb.r(), writes=BKtok.r())
                                    if has_s and i1 == ntl:
                                        off = 128 * (nch - i0)
                                        S.op("act", lambda e: e.activation(out=Kds[:, :], in_=psb[:, off:off + 128], func=AF.Copy, scale=rcon[:, 4 + h:5 + h]),
                                             reads=pb.r() + Bcon.r(), writes=BKds.r())
                                else:
                                    if npz > 0:
                                        S.op("act", lambda e: e.activation(out=Vdec[:, i0:i0 + npz, :].rearrange("p a b -> p (a b)"),
                                                                           in_=psb[:, 0:128 * npz], func=AF.Copy, scale=rcon[:, h:h + 1]),
                                             reads=pb.r() + Bcon.r(), writes=BVdec.r())
                                    if has_s and i1 == ntl:
                                        off = 128 * (nch - i0)
                                        S.op("act", lambda e: e.activation(out=Vdec[:, nch, :], in_=psb[:, off:off + 128], func=AF.Copy),
                                             reads=pb.r(), writes=BVdec.r())
                        if hi == 0:
                            S.op("dve", lambda e: e.memset(Sf[:, :], 0.0), writes=BSf.r())
                            S.op("dve", lambda e: e.memset(Sbf[:, 0, :], 0.0), writes=BSbf[0].r())
                        else:
                            S.op("dve", lambda e: e.tensor_copy(out=Sf[:, :], in_=cS[:, h, :]), reads=BcS.r(), writes=BSf.r())
                            S.op("act", lambda e: e.activation(out=Sbf[:, 0, :], in_=cS[:, h, :], func=AF.Copy), reads=BcS.r(), writes=BSbf[0].r())

                        for cg in range(0, nch, 4):
                            cn = min(4, nch - cg)
                            ps_o, pb_o = PSL.next()
                            for ci in range(cg, cg + cn):
                                cc0 = H + 128 * ci
                                ps_u, pb_u = PS.next()
                                mm(ps_u[:, 0:128], Ktok[:, ci, :], Vdec[:, ci, :], True, True, BKtok.r() + BVdec.r(), pb_u)
                                S.op("dve", lambda e: e.scalar_tensor_tensor(out=Sf[:, :], in0=Sf[:, :], scalar=gC[h], in1=ps_u[:, 0:128],
                                                                             op0=ALU.mult, op1=ALU.add),
                                     reads=pb_u.r() + BSf.r(), writes=BSf.r())
                                S.op("act", lambda e: e.activation(out=Sbf[:, ci + 1, :], in_=Sf[:, :], func=AF.Copy), reads=BSf.r(), writes=BSbf[ci + 1].r())
                                ps_s, pb_s = PS.next()
                                mm(ps_s[:, 0:128], KT[:, cc0:cc0 + 128], QT[:, cc0:cc0 + 128], True, True,
                                   BKT.r(cc0, cc0 + 128) + BQT.r(cc0, cc0 + 128), pb_s)
                                scm, scmb = TB.next()
                                S.op("dve", lambda e: e.tensor_tensor(out=scm[:, 0:128], in0=ps_s[:, 0:128], in1=maskP[:, h, :], op=ALU.mult),
                                     reads=pb_s.r() + Bcon.r(), writes=scmb.r())
                                oc_ = 128 * (ci - cg)
                                mm(ps_o[:, oc_:oc_ + 128], Vdec[:, ci, :], scm[:, 0:128], True, False, BVdec.r() + scmb.r(), pb_o)
                                mm(ps_o[:, oc_:oc_ + 128], Sbf[:, ci, :], QT[:, cc0:cc0 + 128], False, True,
                                   BSbf[ci].r() + BQT.r(cc0, cc0 + 128), pb_o)
                            w_ = 128 * cn
                            g0 = H + 128 * cg
                            qd_ap = qdP[:, h, :].unsqueeze(1).broadcast_to([128, cn, 128])

                            def dst_fn(o, g0=g0, w_=w_):
                                return mix[:, 4 + h, g0:g0 + w_], SG[:, g0:g0 + w_], o[:, 0:w_]
                            gn_block_3d(ps_o, pb_o, w_, cn, qd_ap, dst_fn, g0, g0 + w_, h, Bmix, BSG)
                        if hi == 0:
                            S.op("dve", lambda e: e.tensor_copy(out=cS[:, h, :], in_=Sf[:, :]), reads=BSf.r(), writes=BcS.r())
                        else:
                            S.dma("sp", o_pret[h, :, :], Sf[:, :], reads=BSf.r())
                        if has_s:
                            qs_v = tcols(QT, nch)
                            ks_v = tcols(KT, nch)
                            S.op("dve", lambda e: e.tensor_tensor(out=Qds[:, :], in0=qs_v, in1=qd8[:, h, :], op=ALU.mult),
                                 reads=BcmpS.r() + Bcon.r(), writes=BQds.r())
                            ps_s, pb_s = PS.next()
                            mm(ps_s[:, 0:128], ks_v, qs_v, True, True, BcmpS.r(), pb_s)
                            scm, scmb = TB.next()
                            S.op("dve", lambda e: e.tensor_tensor(out=scm[:, 0:128], in0=ps_s[:, 0:128], in1=maskS[:, h, :], op=ALU.mult),
                                 reads=pb_s.r() + Bcon.r(), writes=scmb.r())
                            ps_o, pb_o = PSL.next()
                            mm(ps_o[:, 0:128], Vdec[:, nch, :], scm[:, 0:128], True, False, BVdec.r() + scmb.r(), pb_o)
                            for s in range(NSQ):
                                mm(ps_o[:, 8 * s:8 * s + 8], S0b[:, s, :], Qds[:, 8 * s:8 * s + 8], False, s == NSQ - 1,
                                   BS0b.r() + BQds.r(), pb_o)

                            def dst_fn_s(o):
                                return (mix[:, 4 + h, sc0:nt].rearrange("p (s c) -> p s c", c=SW)[:, :, H:SW],
                                        SG[:, sc0:nt].rearrange("p (s c) -> p s c", c=SW)[:, :, H:SW],
                                        o[:, 0:128].rearrange("p (s c) -> p s c", c=TS))
                            gn_block_3d(ps_o, pb_o, 128, None, None, dst_fn_s, sc0, nt, h, Bmix, BSG)
                            S.op("dve", lambda e: e.tensor_tensor(out=Vbd[:, :, :], in0=Vdec[:, nch, :].unsqueeze(1).broadcast_to([128, NSQ, 128]),
                                                                  in1=rcon[:, 8:24].unsqueeze(2).broadcast_to([128, NSQ, 128]), op=ALU.mult),
                                 reads=BVdec.r() + Bcon.r(), writes=BVbd.r())
                            for q in range(4):
                                ps_u, pb_u = PS.next()
                                mm(ps_u[:, :], Kds[:, :], Vbd[:, 4 * q:4 * q + 4, :], True, True, BKds.r() + BVbd.r(), pb_u)
                                sn, snb = TF.next()
                                S.op("dve", lambda e: e.scalar_tensor_tensor(out=sn[:, :], in0=S0f[:, 4 * q:4 * q + 4, :].rearrange("p a b -> p (a b)"),
                                                                             scalar=g8[h], in1=ps_u[:, :], op0=ALU.mult, op1=ALU.add),
                                     reads=pb_u.r() + BS0f.r(), writes=snb.r())
                                S.dma("sp", o_sret[4 * q:4 * q + 4, h, :, :].rearrange("s d v -> d s v"),
                                      sn[:, :].rearrange("p (s v) -> p s v", v=128), reads=snb.r())
                    out_proj(hi, "wo0", mix, Bmix)
                    tail()
                    S.fence(fsc[:, 0:1], Bfsc)

        Buf.scoped = True
        Buf.live = []
        Buf.pending = None
        for hi in range(2):
            load_x(hi)
            rmsnorm(hi, "nmix0", None)
            even_mixer(hi, lambda hi=hi: rmsnorm(hi, "nffn0", 0))
            with ExitStack() as shared:
                SH.clear()
                SH["on"] = True
                SH["ph"] = shared
                ffn(hi, 0, lambda hi=hi: rmsnorm(hi, "nmix1", 1))
                lru_mixer(hi, lambda hi=hi: rmsnorm(hi, "nffn1", 2))
                ffn(hi, 1, lambda hi=hi: final_out(hi))
                SH["on"] = False
                S.fence(fsc[:, 0:1], Bfsc)
            if _FLUSH_HALF:
                S.flush()
        S.barrier()
        Buf.scoped = False
        build.marks = S.marks

    return nc


_NC_CACHE = {}


def _core_inputs(inp, params, consts, w_sw, c):
    m = {
        "xp": np.ascontiguousarray(inp["x_prompt"][c]),
        "xs": np.ascontiguousarray(inp["x_sample"][NSQ * c:NSQ * (c + 1)].reshape(NSQ * TS, D)),
        "st_conv_a": np.ascontiguousarray(inp["state_conv_a"][0, NSQ * c:NSQ * (c + 1)].reshape(NSQ * 30, 512)),
        "st_ret": np.ascontiguousarray(inp["state_ret"][0, NSQ * c:NSQ * (c + 1)]),
        "st_lru_conv": np.ascontiguousarray(inp["state_lru_conv"][0, NSQ * c:NSQ * (c + 1)].reshape(NSQ * 3, D)),
        "st_lru_h": np.ascontiguousarray(inp["state_lru_h"][0, NSQ * c:NSQ * (c + 1)]),
        "st_ffn": np.ascontiguousarray(inp["state_ffn_conv"][:, NSQ * c:NSQ * (c + 1)].reshape(2, NSQ * 2, 2 * FF)),
        "w_in_ab": inp["w_in_ab"][0],
        "w_out_ab": inp["w_out_ab"][0],
        "w_in_c": inp["w_in_c"][0],
        "w_lru_a": inp["w_lru_a"][0],
        "w_lru_x": inp["w_lru_x"][0],
        "w_out_c": inp["w_out_c"][0],
        "w_ffn_up": inp["w_ffn_up"],
        "w_ffn_down": inp["w_ffn_down"],
        "params": params,
    }
    m.update(consts)
    return m


def _prep(inputs):
    inp = {k: np.asarray(v, dtype=np.float32) for k, v in inputs.items()}
    params, npar = _pack_params(inp)
    consts = _host_consts()
    wqk = inp["w_in_ab"][0][:, 1024:2048].reshape(D, 8, 2, 64)
    w_sw = np.ascontiguousarray(wqk[:, :, ::-1, :].reshape(D, 1024))
    return inp, params, npar, consts, w_sw


def _assemble(res, ncores):
    def g(name):
        return [np.asarray(r[name]) for r in res]
    y_p = np.stack(g("y_p"), 0)
    y_s = np.concatenate([a.reshape(NSQ, TS, D) for a in g("y_s")], 0)
    p_ca = np.stack(g("p_conv_a"), 0)[None]
    p_ret = np.stack(g("p_ret"), 0)[None]
    p_lc = np.stack(g("p_lru_conv"), 0)[None]
    p_lh = np.stack([a.reshape(D) for a in g("p_lru_h")], 0)[None]
    p_ffn = np.stack(g("p_ffn"), 1)
    s_ca = np.concatenate([a.reshape(NSQ, 30, 512) for a in g("s_conv_a")], 0)[None]
    s_ret = np.concatenate(g("s_ret"), 0)[None]
    s_lc = np.concatenate([a.reshape(NSQ, 3, D) for a in g("s_lru_conv")], 0)[None]
    s_lh = np.concatenate(g("s_lru_h"), 0)[None]
    s_ffn = np.concatenate([a.reshape(2, NSQ, 2, 2 * FF) for a in g("s_ffn")], 1)
    outs = (y_p, y_s, p_ca, p_ret, p_lc, p_lh, p_ffn, s_ca, s_ret, s_lc, s_lh, s_ffn)
    return tuple(np.ascontiguousarray(o, dtype=np.float32) for o in outs)


def kernel(**inputs):
    inp, params, npar, consts, w_sw = _prep(inputs)
    ncores = 8
    nc = build(npar)
    in_maps = [_core_inputs(inp, params, consts, w_sw, c) for c in range(ncores)]
    res = run_bass_kernel_spmd(nc, in_maps, core_ids=list(range(ncores)))
    return _assemble(res.results, ncores)
```

```python
import math
from contextlib import ExitStack

import numpy as np
import concourse.bass as bass
import concourse.mybir as mybir
from concourse.bass_utils import run_bass_kernel_spmd

F32 = mybir.dt.float32
BF16 = mybir.dt.bfloat16
AF = mybir.ActivationFunctionType
ALU = mybir.AluOpType

D = 1024
KC = 8
SEQ = 2048
NSQ = 16
TS = 8
PAST = 16384
H = 3
SW = H + TS
FF = 2816
NJ = 22
HALVES = [(0, 896, False), (896, 1152, True)]
GROUPS = [(0, 8), (8, 7), (15, 7)]
RMS_EPS = 1e-6
LN_EPS = 1e-5
GN_EPS = 1e-5
NSLOT = 12
PF = 3
NTF = 12
NTB = 8
STAGE = 4
import os as _os
_LAT = float(_os.environ.get('SCHED_LAT', '500'))
_EPS = float(_os.environ.get('SCHED_EPS', '150'))
_FLUSH_HALF = bool(int(_os.environ.get('FLUSH_HALF', '1')))
_TBL_INIT = None


class Reg:
    __slots__ = ("name", "w", "rd", "excl")

    def __init__(self, name, excl=False):
        self.name = name
        self.w = None
        self.rd = {}
        self.excl = excl


class Buf:
    scoped = False
    live = []
    pending = None

    def __init__(self, t, name, ncols=None, G=256, excl=False):
        self.t = t
        self.name = name
        self.G = G
        n = 1 if ncols is None else (ncols + G - 1) // G
        self.regs = [Reg(f"{name}.{i}", excl) for i in range(n)]
        self.ncols = ncols
        if Buf.scoped:
            for r_ in self.regs:
                r_.w = Buf.pending
            Buf.live.append(self)

    def r(self, c0=None, c1=None):
        if self.ncols is None or c0 is None:
            return list(self.regs)
        return self.regs[c0 // self.G:(c1 - 1) // self.G + 1]


class _Rec:
    def __init__(self):
        self.call = None

    def __getattr__(self, name):
        def f(*a, **k):
            self.call = (name, a, k)
            return self
        return f


class _Node:
    __slots__ = ("idx", "eng", "call", "deps", "dur", "lat", "tbl", "tok", "is_dma", "succ", "ndep", "ready")


_ACT_TBL = {}


def _free_elems(ap):
    try:
        sh = ap.shape
        n = 1
        for d in sh[1:]:
            n *= int(d)
        return n
    except Exception:
        return 512


class Sched:
    def __init__(self, nc, ndma=6):
        self.nc = nc
        self.E = {"pe": nc.tensor, "act": nc.scalar, "dve": nc.vector, "pool": nc.gpsimd, "sp": nc.sync}
        self.sem = {}
        self.cnt = {}
        self.waited = {e: {} for e in self.E}
        for e in self.E:
            self.sem[e] = nc.alloc_semaphore(name=f"c_{e}")
            self.cnt[e] = 0
        self.dsem = {}
        for q in ("sp", "pool"):
            self.dsem[q] = [[nc.alloc_semaphore(name=f"d_{q}{i}"), 0] for i in range(ndma)]
        self.drr = {"sp": 0, "pool": 0}
        self.nodes = []
        self.nidx = 0
        self.reorder = True
        self.marks = []

    def _record(self, node, reads, writes):
        deps = {}

        def add(n, raw):
            if n is None:
                return
            if deps.get(n.idx, (None, False))[1] is False:
                deps[n.idx] = (n, raw or deps.get(n.idx, (None, False))[1])

        for r in reads:
            add(r.w, True)
            if r.excl:
                for n in r.rd.values():
                    add(n, False)
        for r in writes:
            add(r.w, r.excl)
            for n in r.rd.values():
                add(n, False)
        node.deps = list(deps.values())
        for r in writes:
            r.w = node
            r.rd = {}
        for r in reads:
            if r.excl:
                r.w = node
                r.rd = {}
            else:
                r.rd[node.idx] = node
        self.nodes.append(node)

    def op(self, e, fn, reads=(), writes=(), dur=None, tbl=None):
        rec = _Rec()
        fn(rec)
        n = _Node()
        n.idx = self.nidx
        self.nidx += 1
        n.eng = e
        n.call = rec.call
        n.is_dma = False
        n.tbl = tbl
        n.tok = None
        if dur is None:
            name, a, k = rec.call
            out = k.get("out", a[0] if a else None)
            fe = _free_elems(out) if out is not None else 512
            if e == "pe":
                dur = 10.0 + fe / 2.4
            elif e == "act":
                dur = 280.0 + fe / 1.1
                if name == "activation":
                    f = k.get("func")
                    tbl = _ACT_TBL.get(f, None)
                    n.tbl = tbl
            elif e == "dve":
                dur = 180.0 + fe / 0.9
                if name == "reciprocal":
                    dur = 120.0 + 4 * fe / 0.96
                elif name in ("memset",):
                    dur = 100.0 + fe / 3.0
            else:
                dur = 300.0
        n.dur = dur
        n.lat = 60.0 if e == "pe" else _LAT
        self._record(n, reads, writes)
        return n

    def fence(self, scratch, scratch_buf, keep=()):
        keep_ids = {id(b) for b in keep}
        regs = []
        seen = set()
        for b in Buf.live:
            if id(b) in keep_ids:
                continue
            for r_ in b.regs:
                if id(r_) not in seen:
                    seen.add(id(r_))
                    regs.append(r_)
        node = self.op("dve", lambda e: e.memset(scratch, 0.0), writes=regs + scratch_buf.r(), dur=80.0)
        Buf.pending = node
        Buf.live = [b for b in Buf.live if id(b) in keep_ids]

    def dma(self, q, out, in_, reads=(), writes=(), nbytes=None):
        n = _Node()
        n.idx = self.nidx
        self.nidx += 1
        n.eng = q
        n.call = (out, in_)
        n.is_dma = True
        n.tbl = None
        n.tok = None
        if nbytes is None:
            try:
                nb = 1
                for d in out.shape:
                    nb *= int(d)
                nbytes = nb * 4
            except Exception:
                nbytes = 65536
        n.dur = 1000.0 if q == "pool" else 150.0
        n.lat = 2000.0 + nbytes / 150.0
        self._record(n, reads, writes)

    def _wait(self, e, key, sem, val):
        if self.waited[e].get(key, 0) >= val:
            return
        self.E[e].wait_ge(sem, val)
        self.waited[e][key] = val

    def _emit(self, n):
        e = n.eng
        toks = {}
        for (d, raw) in n.deps:
            key, sem, val = d.tok
            if key == e:
                if e in ("pe", "pool", "sp"):
                    continue
                if not raw:
                    continue
                if val <= self.cnt[e] - 2:
                    continue
            if toks.get(key, (None, 0))[1] < val:
                toks[key] = (sem, val)
        if n.is_dma:
            i = self.drr[e]
            self.drr[e] = (i + 1) % len(self.dsem[e])
            ent = self.dsem[e][i]
            key = f"d_{e}{i}"
            if ent[1] > 0:
                self._wait(e, key, ent[0], 16 * ent[1])
            for k2, (sem, val) in toks.items():
                self._wait(e, k2, sem, val)
            out, in_ = n.call
            self.E[e].dma_start(out=out, in_=in_).then_inc(ent[0], 16)
            ent[1] += 1
            n.tok = (key, ent[0], 16 * ent[1])
        else:
            for k2, (sem, val) in toks.items():
                self._wait(e, k2, sem, val)
            name, a, k = n.call
            ins = getattr(self.E[e], name)(*a, **k)
            self.cnt[e] += 1
            ins.then_inc(self.sem[e], 1)
            n.tok = (e, self.sem[e], self.cnt[e])
        n.deps = None
        n.call = None

    def flush(self):
        nodes = self.nodes
        self.nodes = []
        if not nodes:
            return
        if not self.reorder:
            for n in nodes:
                self._emit(n)
            return
        inwin = {n.idx: n for n in nodes}
        for n in nodes:
            n.succ = []
            n.ndep = 0
            n.ready = 0.0
        for n in nodes:
            for (d, raw) in n.deps:
                if d.idx in inwin:
                    d.succ.append(n)
                    n.ndep += 1
        bl = {}
        for n in reversed(nodes):
            m = 0.0
            for s_ in n.succ:
                v = n.lat + bl[s_.idx]
                if v > m:
                    m = v
            bl[n.idx] = n.dur + m
        free = {e: 0.0 for e in self.E}
        last_tbl = None
        ready = {e: [] for e in self.E}
        for n in nodes:
            if n.ndep == 0:
                ready[n.eng].append(n)
        left = len(nodes)
        EPS = _EPS
        while left:
            best = None
            for e, lst in ready.items():
                if not lst:
                    continue
                mr = min(x.ready for x in lst)
                t_e = max(free[e], mr)
                if best is None or t_e < best[0]:
                    best = (t_e, e)
            t_e, e = best
            lst = ready[e]
            cands = [x for x in lst if x.ready <= t_e + EPS]
            if e in ("sp", "pool"):
                n = min(cands, key=lambda x: x.idx)
            else:
                if e == "act" and last_tbl is not None:
                    same = [x for x in lst if x.ready <= t_e + 1500.0 and (x.tbl is None or x.tbl == last_tbl)]
                    if same:
                        cands = same
                n = max(cands, key=lambda x: (bl[x.idx], -x.idx))
            lst.remove(n)
            st = max(free[e], n.ready)
            if e == "act" and n.tbl is not None:
                if last_tbl is not None and n.tbl != last_tbl:
                    st += 1300.0
                last_tbl = n.tbl
            end = st + n.dur
            free[e] = end
            fin_t = end + n.lat
            self._emit(n)
            left -= 1
            for s_ in n.succ:
                if fin_t > s_.ready:
                    s_.ready = fin_t
                s_.ndep -= 1
                if s_.ndep == 0:
                    ready[s_.eng].append(s_)
            n.succ = None

    def barrier(self):
        self.flush()
        self.marks.append(dict(self.cnt))
        for e in self.E:
            for e2 in ("pe", "act", "dve", "pool"):
                if e2 != e and self.cnt[e2] > 0:
                    self._wait(e, e2, self.sem[e2], self.cnt[e2])
            for q in ("sp", "pool"):
                for i, ent in enumerate(self.dsem[q]):
                    if ent[1] > 0:
                        self._wait(e, f"d_{q}{i}", ent[0], 16 * ent[1])


class Pool:
    def __init__(self, es, nc, name, n, shape, dt, excl=False, space="sbuf"):
        self.items = []
        for i in range(n):
            if space == "sbuf":
                t = es.enter_context(nc.sbuf_tensor(f"{name}{i}", shape, dt))
            else:
                t = es.enter_context(nc.psum_tensor(f"{name}{i}", shape, dt))
            self.items.append((t, Buf(t, f"{name}{i}", excl=excl)))
        self.i = 0

    def next(self):
        it = self.items[self.i]
        self.i = (self.i + 1) % len(self.items)
        return it


def _init_tbl():
    _ACT_TBL.update({AF.Gelu_apprx_tanh: "gelu", AF.Sigmoid: "sig", AF.Silu: "silu", AF.Exp: "exp", AF.Ln: "exp",
                     AF.Sqrt: "sqrt", AF.Tanh: "exp"})


def _gammas():
    lg = np.log(np.float32(1.0) - np.float32(2.0) ** (-5.0 - np.arange(4, dtype=np.float32))).astype(np.float32)
    return lg


def _half_geom(hi):
    t0, npr, has_s = HALVES[hi]
    sc0 = H + npr
    nt = sc0 + (NSQ * SW if has_s else 0)
    return t0, npr, has_s, sc0, nt


def _blocks0(hi):
    t0, npr, has_s, sc0, nt = _half_geom(hi)
    out = []
    c = H
    while c < nt:
        l = min(512, nt - c)
        out.append((c, l))
        c += l
    return out


def _blocks3(hi):
    t0, npr, has_s, sc0, nt = _half_geom(hi)
    out = []
    c = 0
    while True:
        l = min(512, nt - c)
        out.append((c, l))
        if c + l >= nt:
            break
        c += l - H
    return out


def _host_consts():
    c = {}
    c["ident"] = np.eye(128, dtype=np.float32)
    pm = np.zeros((128, 128), np.float32)
    for m_ in range(128):
        pm[(m_ + 64) % 128, m_] = 1.0
    c["perm"] = pm
    lg = _gammas()
    inv_freq = (np.float32(10000.0) ** (-np.arange(0, 128, 2, dtype=np.float32) / np.float32(128))).astype(np.float32)
    for hi in range(2):
        t0, npr, has_s, sc0, nt = _half_geom(hi)
        pos = np.zeros(nt, np.float32)
        valid = np.zeros(nt, bool)
        pos[H:H + npr] = np.arange(t0, t0 + npr, dtype=np.float32)
        valid[H:H + npr] = True
        if has_s:
            for s in range(NSQ):
                b = sc0 + SW * s + H
                pos[b:b + TS] = np.arange(PAST, PAST + TS, dtype=np.float32)
                valid[b:b + TS] = True
        ang = (pos[:, None] * inv_freq[None, :]).astype(np.float32)
        cs = np.cos(ang).astype(np.float32)
        sn = np.sin(ang).astype(np.float32)
        cosT = np.concatenate([cs.T, cs.T], axis=0)
        sinT = np.concatenate([-sn.T, sn.T], axis=0)
        cosT[:, ~valid] = 0
        sinT[:, ~valid] = 0
        c[f"rope{hi}"] = np.ascontiguousarray(np.stack([cosT, sinT], axis=1).astype(np.float32))
    idx = np.arange(128, dtype=np.float32)
    maskP = np.zeros((128, 4, 128), np.float32)
    qdP = np.zeros((128, 4, 128), np.float32)
    kdecP = np.zeros((128, 4), np.float32)
    maskS = np.zeros((128, 4, 128), np.float32)
    qd8 = np.zeros((128, 4, 128), np.float32)
    kdec8 = np.zeros((128, 4), np.float32)
    causal = (idx[:, None] <= idx[None, :]).astype(np.float32)
    sj = np.arange(128) // 8
    jj = (np.arange(128) % 8).astype(np.float32)
    for h in range(4):
        maskP[:, h, :] = causal * np.exp(np.float32(-128.0) * lg[h]).astype(np.float32)
        qdP[:, h, :] = np.exp((idx + 1.0) * lg[h])[None, :]
        kdecP[:, h] = np.exp((127.0 - idx) * lg[h])
        rel = jj[None, :] - jj[:, None]
        m = np.where((sj[:, None] == sj[None, :]) & (rel >= 0), np.exp(np.maximum(rel, 0) * lg[h]), 0.0)
        maskS[:, h, :] = m
        qd8[:, h, :] = np.exp((jj + 1.0) * lg[h])[None, :]
        kdec8[:, h] = np.exp((7.0 - jj) * lg[h])
    c["maskP"] = maskP
    c["qdP"] = qdP
    c["maskS"] = maskS.astype(np.float32)
    c["qd8"] = qd8
    rc = np.zeros((128, 32), np.float32)
    rc[:, 0:4] = kdecP
    rc[:, 4:8] = kdec8
    rc[:, 8:24] = (sj[:, None] == np.arange(16)[None, :]).astype(np.float32)
    c["rcon"] = rc
    t0, npr, has_s, sc0, nt = _half_geom(1)
    b3 = _blocks3(1)
    lc0, ll = b3[-1]
    sel2 = np.zeros((32, ll), np.float32)
    sel3 = np.zeros((48, ll), np.float32)
    for s in range(NSQ):
        for r in range(2):
            sel2[2 * s + r, sc0 + SW * s + 1 + r - lc0] = 1
        for r in range(3):
            sel3[3 * s + r, sc0 + SW * s + r - lc0] = 1
    c["sel2"] = sel2
    c["sel3"] = sel3
    return c


def _fm(v):
    return np.ascontiguousarray(v.reshape(-1, 128).T)


PAR = {}


def _pack_params(inp):
    cols = []
    off = 0

    def add(name, arr):
        nonlocal off
        arr = np.ascontiguousarray(arr, dtype=np.float32).reshape(128, -1)
        PAR[name] = (off, arr.shape[1])
        cols.append(arr)
        off += arr.shape[1]

    for l in range(2):
        add(f"nmix{l}", _fm(inp["norm_mix"][l]))
        add(f"nffn{l}", _fm(inp["norm_ffn"][l]))
    add("nfin", _fm(inp["norm_final"]))
    add("cab", _fm(inp["conv_a_b"][0]))
    add("lng", _fm(inp["ln_a_g"][0]))
    add("lnb", _fm(inp["ln_a_b"][0]))
    add("gng", _fm(inp["gn_ret_g"][0]))
    add("caw", inp["conv_a_w"][0].reshape(31, 4, 128).transpose(2, 1, 0))
    add("ccw", inp["conv_c_w"][0].reshape(4, 8, 128).transpose(2, 1, 0))
    add("ccb", _fm(inp["conv_c_b"][0]))
    add("ba", _fm(inp["b_lru_a"][0]))
    add("bx", _fm(inp["b_lru_x"][0]))
    add("lam", _fm(inp["lru_lambda"][0]))
    for l in range(2):
        add(f"fcw{l}", inp["ffn_conv_w"][l].reshape(3, 44, 128).transpose(2, 1, 0))
        add(f"fcb{l}", _fm(inp["ffn_conv_b"][l]))
    return np.ascontiguousarray(np.concatenate(cols, axis=1)), off


def build(npar):
    nc = bass.Bass("TRN2", target_bir_lowering=False)

    def din(name, shape):
        return nc.dram_tensor(name, list(shape), F32, kind="ExternalInput").ap()

    def dout(name, shape):
        return nc.dram_tensor(name, list(shape), F32, kind="ExternalOutput").ap()

    xp_d = din("xp", (SEQ, D))
    xs_d = din("xs", (NSQ * TS, D))
    sca_d = din("st_conv_a", (NSQ * 30, 512))
    sret_d = din("st_ret", (NSQ, 4, 128, 128))
    slc_d = din("st_lru_conv", (NSQ * 3, D))
    slh_d = din("st_lru_h", (NSQ, D))
    sffn_d = din("st_ffn", (2, NSQ * 2, 2 * FF))
    w_in_ab = din("w_in_ab", (D, 3072))
    w_out_ab = din("w_out_ab", (D, D))
    w_in_c = din("w_in_c", (D, 2048))
    w_lru_a = din("w_lru_a", (8, 128, 128))
    w_lru_x = din("w_lru_x", (8, 128, 128))
    w_out_c = din("w_out_c", (D, D))
    w_up = din("w_ffn_up", (2, D, 2 * FF))
    w_dn = din("w_ffn_down", (2, FF, D))
    par_d = din("params", (128, npar))
    ident_d = din("ident", (128, 128))
    perm_d = din("perm", (128, 128))
    rope_d = [din(f"rope{hi}", (128, 2, _half_geom(hi)[4])) for hi in range(2)]
    maskP_d = din("maskP", (128, 4, 128))
    qdP_d = din("qdP", (128, 4, 128))
    maskS_d = din("maskS", (128, 4, 128))
    qd8_d = din("qd8", (128, 4, 128))
    rcon_d = din("rcon", (128, 32))
    b3_1 = _blocks3(1)
    sel2_d = din("sel2", (32, b3_1[-1][1]))
    sel3_d = din("sel3", (48, b3_1[-1][1]))

    yp_d = dout("y_p", (SEQ, D))
    ys_d = dout("y_s", (NSQ * TS, D))
    o_pca = dout("p_conv_a", (30, 512))
    o_pret = dout("p_ret", (4, 128, 128))
    o_plc = dout("p_lru_conv", (3, D))
    o_plh = dout("p_lru_h", (1, D))
    o_pffn = dout("p_ffn", (2, 2, 2 * FF))
    o_sca = dout("s_conv_a", (NSQ * 30, 512))
    o_sret = dout("s_ret", (NSQ, 4, 128, 128))
    o_slc = dout("s_lru_conv", (NSQ * 3, D))
    o_slh = dout("s_lru_h", (NSQ, D))
    o_sffn = dout("s_ffn", (2, NSQ * 2, 2 * FF))

    lg = _gammas()
    gC = [float(np.exp(np.float32(128.0) * lg[h])) for h in range(4)]
    g8 = [float(np.exp(np.float32(8.0) * lg[h])) for h in range(4)]

    _init_tbl()
    S = Sched(nc)
    NTMAX = _half_geom(1)[4]

    with ExitStack() as es:
        def sb(name, shape, dt=F32):
            return es.enter_context(nc.sbuf_tensor("sb_" + name, list(shape), dt))

        xT = sb("xT", (128, KC, NTMAX))
        xn = sb("xn", (128, KC, NTMAX), BF16)
        BxT = Buf(xT, "xT", NTMAX)
        Bxn = Buf(xn, "xn", NTMAX)
        par = sb("par", (128, npar))
        Bpar = Buf(par, "par")
        identf = sb("identf", (128, 128))
        identb = sb("identb", (128, 128), BF16)
        Bid = Buf(identf, "identf")
        Bidb = Buf(identb, "identb")
        permb = sb("permb", (128, 128), BF16)
        Bperm = Buf(permb, "permb")
        ones = sb("ones", (128, 3, 128), BF16)
        Bones = Buf(ones, "ones")
        maskP = sb("maskP", (128, 4, 128))
        qdP = sb("qdP", (128, 4, 128))
        maskS = sb("maskS", (128, 4, 128))
        qd8 = sb("qd8", (128, 4, 128))
        rcon = sb("rcon", (128, 32))
        Bcon = Buf(maskP, "retconst")
        sel2 = sb("sel2", (32, b3_1[-1][1]), BF16)
        sel3 = sb("sel3", (48, b3_1[-1][1]), BF16)
        Bsel = Buf(sel2, "sel")
        lruc = sb("lruc", (128, 8, 2))
        hbias = sb("hbias", (128, 2, 8))
        Blruc = Buf(lruc, "lruc")
        cxn = sb("cxn", (128, 3, KC, H), BF16)
        Bcxn = Buf(cxn, "cxn")
        cu = sb("cu", (128, 4, 30), BF16)
        Bcu = Buf(cu, "cu")
        cS = sb("cS", (128, 4, 128))
        BcS = Buf(cS, "cS")
        ch = sb("ch", (128, 8))
        Bch = Buf(ch, "ch")

        fsc = sb("fsc", (128, 2))
        Bfsc = Buf(fsc, "fsc")
        slots = Pool(es, nc, "wsl", NSLOT, (128, 1024), BF16)
        TF = Pool(es, nc, "tf", NTF, (128, 512), F32)
        TB = Pool(es, nc, "tb", NTB, (128, 512), BF16)
        PS = Pool(es, nc, "ps", 6, (128, 512), F32, excl=True, space="psum")
        PSL = Pool(es, nc, "psl", 2, (128, 512), F32, excl=True, space="psum")
        _ps6 = list(PS.items)
        _ps8 = list(PS.items) + list(PSL.items)

        _psl_all = list(PSL.items)

        def ps_mode(n):
            if n == 8:
                PS.items = _ps8
                PSL.items = _psl_all
            elif n == 7:
                PS.items = _ps6 + [_psl_all[0]]
                PSL.items = [_psl_all[1]]
            else:
                PS.items = _ps6
                PSL.items = _psl_all
            PS.i = 0
            PSL.i = 0

        def P(name, k=None):
            o, n = PAR[name]
            if k is None:
                return par[:, o:o + n]
            return par[:, o + k:o + k + 1]

        S.dma("sp", par[:], par_d, writes=Bpar.r())
        S.dma("sp", identf[:], ident_d, writes=Bid.r())
        S.dma("pool", identb[:], ident_d, writes=Bidb.r())
        S.dma("pool", permb[:], perm_d, writes=Bperm.r())
        S.dma("sp", maskP[:], maskP_d, writes=Bcon.r())
        S.dma("sp", qdP[:], qdP_d, writes=Bcon.r())
        S.dma("sp", maskS[:], maskS_d, writes=Bcon.r())
        S.dma("sp", qd8[:], qd8_d, writes=Bcon.r())
        S.dma("sp", rcon[:], rcon_d, writes=Bcon.r())
        S.dma("pool", sel2[:], sel2_d, writes=Bsel.r())
        S.dma("pool", sel3[:], sel3_d, writes=Bsel.r())
        S.op("dve", lambda e: e.memset(ones[:, 0, :], 1.0 / 1024), writes=Bones.r())
        S.op("dve", lambda e: e.memset(ones[:, 1, :], 1.0 / 512), writes=Bones.r())
        S.op("dve", lambda e: e.memset(ones[:, 2, :], 1.0 / 128), writes=Bones.r())
        S.op("act", lambda e: e.activation(out=lruc[:, :, 0], in_=P("lam"), func=AF.Exp, scale=-1.0),
             reads=Bpar.r(), writes=Blruc.r())
        S.op("act", lambda e: e.activation(out=lruc[:, :, 0], in_=lruc[:, :, 0], func=AF.Ln, bias=1.0, scale=1.0),
             reads=Blruc.r(), writes=Blruc.r())
        S.op("dve", lambda e: e.tensor_scalar(out=lruc[:, :, 1], in0=lruc[:, :, 0], scalar1=-8.0, scalar2=None, op0=ALU.mult),
             reads=Blruc.r(), writes=Blruc.r())
        S.op("dve", lambda e: e.tensor_scalar(out=lruc[:, :, 0], in0=lruc[:, :, 0], scalar1=-4.0, scalar2=None, op0=ALU.mult),
             reads=Blruc.r(), writes=Blruc.r())
        S.op("dve", lambda e: e.tensor_scalar(out=hbias[:, 0, :], in0=P("ba"), scalar1=0.5, scalar2=None, op0=ALU.mult),
             reads=Bpar.r(), writes=Blruc.r())
        S.op("dve", lambda e: e.tensor_scalar(out=hbias[:, 1, :], in0=P("bx"), scalar1=0.5, scalar2=None, op0=ALU.mult),
             reads=Bpar.r(), writes=Blruc.r())

        def wap_cols(w2d, c0, ncols=128):
            return w2d.rearrange("(k p) c -> p k c", p=128)[:, :, c0:c0 + ncols]

        def weight_plan():
            for hi in range(2):
                if STAGE < 1:
                    continue
                for c in range(4):
                    yield ("alin", c), wap_cols(w_in_ab, 128 * c)
                    yield ("agate", c), wap_cols(w_in_ab, 512 + 128 * c)
                for h in range(4):
                    yield ("q", h), wap_cols(w_in_ab, 1024 + 128 * h)
                    yield ("k", h), wap_cols(w_in_ab, 1536 + 128 * h)
                    yield ("v", h), wap_cols(w_in_ab, 2048 + 128 * h)
                    yield ("g", h), wap_cols(w_in_ab, 2560 + 128 * h)
                for o in range(8):
                    yield ("wo0", o), wap_cols(w_out_ab, 128 * o)
                for l in range(2):
                    if l == 0 and STAGE < 2:
                        continue
                    if l == 1 and STAGE < 3:
                        continue
                    if l == 1:
                        for n in range(8):
                            yield ("gate", n), wap_cols(w_in_c, 128 * n)
                            yield ("rec", n), wap_cols(w_in_c, 1024 + 128 * n)
                        for o in range(8):
                            yield ("wo1", o), wap_cols(w_out_c, 128 * o)
                        if STAGE < 4:
                            continue
                    for (j0, jn) in GROUPS:
                        for j in range(j0, j0 + jn):
                            yield ("upg", l, j), wap_cols(w_up[l], 128 * j)
                            yield ("upu", l, j), wap_cols(w_up[l], FF + 128 * j)
                        for j in range(j0, j0 + jn):
                            yield ("wd", l, j), w_dn[l][128 * j:128 * j + 128, :]

        plan = list(weight_plan())
        wstate = {"issued": 0, "used": 0}
        wslot_of = {}

        def w_issue(upto):
            while wstate["issued"] < min(upto, len(plan)):
                i = wstate["issued"]
                name, ap = plan[i]
                t, b = slots.items[i % NSLOT]
                if len(ap.shape) == 3:
                    dst = t[:].rearrange("p (k c) -> p k c", c=128)
                else:
                    dst = t[:]
                S.dma("pool", dst, ap, writes=b.r())
                wslot_of[i] = (t, b)
                wstate["issued"] += 1

        def w_next(name):
            i = wstate["used"]
            assert plan[i][0] == name, (plan[i][0], name)
            w_issue(i + 1 + PF)
            wstate["used"] += 1
            t, b = wslot_of[i]
            return i, t, b

        def w_check(i):
            assert wstate["issued"] <= i + NSLOT, ("weight evicted", plan[i][0])

        def mm(ps, lhsT, rhs, start, stop, reads, pb):
            S.op("pe", lambda e: e.matmul(ps, lhsT, rhs, start=start, stop=stop), reads=reads, writes=pb.r())

        def proj_block(ps, pb, wi, wt, wb, rhs_buf, rhs_B, c0, ln, nk=KC, last_stop=True):
            w_check(wi)
            w3 = wt[:].rearrange("p (k c) -> p k c", c=128)
            for k in range(nk):
                mm(ps[:, 0:ln], w3[:, k, :], rhs_buf[:, k, c0:c0 + ln], k == 0, (k == nk - 1) and last_stop,
                   wb.r() + rhs_B.r(c0, c0 + ln), pb)

        def rmsnorm(hi, gname, phase_idx):
            t0, npr, has_s, sc0, nt = _half_geom(hi)
            for (c0, ln) in _blocks0(hi):
                ps, pb = PS.next()
                for k in range(KC):
                    sq, sqb = TB.next()
                    if k % 3 != 2:
                        S.op("act", lambda e: e.activation(out=sq[:, 0:ln], in_=xT[:, k, c0:c0 + ln], func=AF.Square),
                             reads=BxT.r(c0, c0 + ln), writes=sqb.r())
                    else:
                        S.op("dve", lambda e: e.tensor_tensor(out=sq[:, 0:ln], in0=xT[:, k, c0:c0 + ln], in1=xT[:, k, c0:c0 + ln], op=ALU.mult),
                             reads=BxT.r(c0, c0 + ln), writes=sqb.r())
                    mm(ps[:, 0:ln], ones[:, 0, :], sq[:, 0:ln], k == 0, k == KC - 1, sqb.r() + Bones.r(), pb)
                sd, sdb = TF.next()
                S.op("act", lambda e: e.activation(out=sd[:, 0:ln], in_=ps[:, 0:ln], func=AF.Ln, bias=RMS_EPS, scale=1.0),
                     reads=pb.r(), writes=sdb.r())
                rs, rsb = TF.next()
                S.op("act", lambda e: e.activation(out=rs[:, 0:ln], in_=sd[:, 0:ln], func=AF.Exp, scale=-0.5), reads=sdb.r(), writes=rsb.r())
                for k in range(KC):
                    S.op("dve", lambda e: e.scalar_tensor_tensor(out=xn[:, k, c0:c0 + ln], in0=xT[:, k, c0:c0 + ln],
                                                                 scalar=P(gname, k), in1=rs[:, 0:ln],
                                                                 op0=ALU.mult, op1=ALU.mult),
                         reads=BxT.r(c0, c0 + ln) + rsb.r() + Bpar.r(), writes=Bxn.r(c0, c0 + ln))
            if has_s:
                for k in range(KC):
                    v = xn[:, k, sc0:nt].rearrange("p (s c) -> p s c", c=SW)[:, :, 0:H]
                    S.op("dve", lambda e: e.memset(v, 0.0), writes=Bxn.r(sc0, nt))
            if phase_idx is not None:
                if hi == 0:
                    S.op("dve", lambda e: e.tensor_copy(out=cxn[:, phase_idx, :, :], in_=xn[:, :, nt - H:nt]),
                         reads=Bxn.r(nt - H, nt), writes=Bcxn.r())
                else:
                    S.op("dve", lambda e: e.tensor_copy(out=xn[:, :, 0:H], in_=cxn[:, phase_idx, :, :]),
                         reads=Bcxn.r(), writes=Bxn.r(0, H))
            elif hi == 0 or True:
                S.op("dve", lambda e: e.memset(xn[:, :, 0:H], 0.0), writes=Bxn.r(0, H))

        def out_proj(hi, wname, mix, Bmix):
            ws = [w_next((wname, o)) for o in range(8)]
            for (c0, ln) in _blocks0(hi):
                for o in range(8):
                    wi, wt, wb = ws[o]
                    ps, pb = PS.next()
                    proj_block(ps, pb, wi, wt, wb, mix, Bmix, c0, ln)
                    S.op("dve", lambda e: e.tensor_tensor(out=xT[:, o, c0:c0 + ln], in0=xT[:, o, c0:c0 + ln],
                                                          in1=ps[:, 0:ln], op=ALU.add),
                         reads=pb.r() + BxT.r(c0, c0 + ln), writes=BxT.r(c0, c0 + ln))

        def transpose_out(src_aps, nrows, dsts):
            ps, pb = PS.next()
            n = len(src_aps)
            for i, (ap, rr) in enumerate(src_aps):
                S.op("pe", lambda e: e.transpose(ps[0:nrows, 128 * i:128 * i + 128], ap, identf[:]),
                     reads=rr + Bid.r(), writes=pb.r())
            st, stb = TF.next()
            S.op("act", lambda e: e.activation(out=st[0:nrows, 0:128 * n], in_=ps[0:nrows, 0:128 * n], func=AF.Copy),
                 reads=pb.r(), writes=stb.r())
            for (dap, r0, r1) in dsts:
                S.dma("sp", dap, st[r0:r1, 0:128 * n], reads=stb.r())

        def load_x(hi):
            t0, npr, has_s, sc0, nt = _half_geom(hi)
            S.op("dve", lambda e: e.memset(xT[:, :, 0:H], 0.0), writes=BxT.r(0, H))
            if has_s:
                for k in range(KC):
                    v = xT[:, k, sc0:nt].rearrange("p (s c) -> p s c", c=SW)[:, :, 0:H]
                    S.op("dve", lambda e: e.memset(v, 0.0), writes=BxT.r(sc0, nt))
            ntile = npr // 128 + (1 if has_s else 0)
            for ti in range(ntile):
                xi, xib = TF.next()
                xi2, xib2 = TF.next()
                is_s = ti == npr // 128
                src = xs_d if is_s else xp_d[t0 + 128 * ti:t0 + 128 * ti + 128, :]
                S.dma("sp", xi[:], src[:, 0:512], writes=xib.r())
                S.dma("sp", xi2[:], src[:, 512:1024], writes=xib2.r())
                for half, (xt_, xb_) in enumerate(((xi, xib), (xi2, xib2))):
                    ps, pb = PS.next()
                    for q in range(4):
                        S.op("pe", lambda e: e.transpose(ps[:, 128 * q:128 * q + 128], xt_[:, 128 * q:128 * q + 128], identf[:]),
                             reads=xb_.r() + Bid.r(), writes=pb.r())
                    eng = "act" if half == 0 else "dve"
                    if not is_s:
                        c0 = H + 128 * ti
                        dst = xT[:, 4 * half:4 * half + 4, c0:c0 + 128]
                        src_ps = ps[:, :].rearrange("p (k c) -> p k c", c=128)
                        if eng == "act":
                            S.op("act", lambda e: e.activation(out=dst, in_=src_ps, func=AF.Copy), reads=pb.r(), writes=BxT.r(c0, c0 + 128))
                        else:
                            S.op("dve", lambda e: e.tensor_copy(out=dst, in_=src_ps), reads=pb.r(), writes=BxT.r(c0, c0 + 128))
                    else:
                        for q in range(4):
                            k = 4 * half + q
                            dst = xT[:, k, sc0:nt].rearrange("p (s c) -> p s c", c=SW)[:, :, H:SW]
                            src_ps = ps[:, 128 * q:128 * q + 128].rearrange("p (s c) -> p s c", c=TS)
                            if eng == "act":
                                S.op("act", lambda e: e.activation(out=dst, in_=src_ps, func=AF.Copy), reads=pb.r(), writes=BxT.r(sc0, nt))
                            else:
                                S.op("dve", lambda e: e.tensor_copy(out=dst, in_=src_ps), reads=pb.r(), writes=BxT.r(sc0, nt))

        def final_out(hi):
            t0, npr, has_s, sc0, nt = _half_geom(hi)
            ntile = npr // 128 + (1 if has_s else 0)
            for ti in range(ntile):
                is_s = ti == npr // 128
                if not is_s:
                    c0, c1 = H + 128 * ti, H + 128 * ti + 128

                    def cols(t3, k):
                        return t3[:, k, c0:c1]
                else:
                    c0, c1 = sc0, nt

                    def cols(t3, k):
                        return t3[:, k, sc0:nt].rearrange("p (s c) -> p s c", c=SW)[:, :, H:SW]
                ps, pb = PS.next()
                for k in range(KC):
                    sq, sqb = TB.next()
                    sqv = sq[:, 0:128] if not is_s else sq[:, 0:128].rearrange("p (s c) -> p s c", c=TS)
                    S.op("act", lambda e: e.activation(out=sqv, in_=cols(xT, k), func=AF.Square),
                         reads=BxT.r(c0, c1), writes=sqb.r())
                    mm(ps[:, 0:128], ones[:, 0, :], sq[:, 0:128], k == 0, k == KC - 1, sqb.r() + Bones.r(), pb)
                sd, sdb = TF.next()
                S.op("act", lambda e: e.activation(out=sd[:, 0:128], in_=ps[:, 0:128], func=AF.Ln, bias=RMS_EPS, scale=1.0),
                     reads=pb.r(), writes=sdb.r())
                rs, rsb = TF.next()
                S.op("act", lambda e: e.activation(out=rs[:, 0:128], in_=sd[:, 0:128], func=AF.Exp, scale=-0.5), reads=sdb.r(), writes=rsb.r())
                rsv = rs[:, 0:128] if not is_s else rs[:, 0:128].rearrange("p (s c) -> p s c", c=TS)
                ya, yab = TF.next()
                yb_, ybb = TF.next()
                for k in range(KC):
                    yt = ya if k < 4 else yb_
                    ytb = yab if k < 4 else ybb
                    o = yt[:, 128 * (k % 4):128 * (k % 4) + 128]
                    if is_s:
                        o = o.rearrange("p (s c) -> p s c", c=TS)
                    S.op("dve", lambda e: e.scalar_tensor_tensor(out=o, in0=cols(xT, k), scalar=P("nfin", k), in1=rsv,
                                                                 op0=ALU.mult, op1=ALU.mult),
                         reads=BxT.r(c0, c1) + rsb.r() + Bpar.r(), writes=ytb.r())
                dst_rows = ys_d if is_s else yp_d[t0 + 128 * ti:t0 + 128 * ti + 128, :]
                for half, (yt, ytb) in enumerate(((ya, yab), (yb_, ybb))):
                    ps2, pb2 = PS.next()
                    for q in range(4):
                        S.op("pe", lambda e: e.transpose(ps2[:, 128 * q:128 * q + 128], yt[:, 128 * q:128 * q + 128], identf[:]),
                             reads=ytb.r() + Bid.r(), writes=pb2.r())
                    st, stb = TF.next()
                    if half == 0:
                        S.op("act", lambda e: e.activation(out=st[:, :], in_=ps2[:, :], func=AF.Copy), reads=pb2.r(), writes=stb.r())
                    else:
                        S.op("dve", lambda e: e.tensor_copy(out=st[:, :], in_=ps2[:, :]), reads=pb2.r(), writes=stb.r())
                    S.dma("sp", dst_rows[:, 512 * half:512 * half + 512], st[:, :], reads=stb.r())

        SH = {}

        def shget(ph, key, make):
            if SH.get("on"):
                if key not in SH:
                    SH[key] = make(SH["ph"])
                return SH[key]
            return make(ph)

        def mk_buf(p, name, shape, dt, ncols=None):
            t = p.enter_context(nc.sbuf_tensor(name, list(shape), dt))
            return t, Buf(t, name, ncols)

        def ffn(hi, l, tail):
            t0, npr, has_s, sc0, nt = _half_geom(hi)
            b3 = _blocks3(hi)
            ps_mode(8)
            with ExitStack() as ph_:
                ph = SH["ph"] if SH.get("on") else ph_
                jmax = max(g[1] for g in GROUPS)
                assert jmax == KC
                tag = "" if SH.get("on") else f"_{l}"
                act, Bact = shget(ph, ("mixact", hi), lambda p: mk_buf(p, f"mixact_{hi}{tag}", [128, jmax, nt], BF16, nt))
                FLT = shget(ph, ("LT", hi), lambda p: Pool(p, nc, f"lt{hi}_", 15, (128, 512), F32)) if SH.get("on") else None
                if has_s:
                    stt, Bstt = shget(ph, ("stt", hi), lambda p: mk_buf(p, f"stt_{hi}{tag}", [32, 2, jmax * 128], BF16))
                    zt, Bzt = shget(ph, ("zt", hi), lambda p: mk_buf(p, f"zt_{hi}{tag}", [128, 2, jmax, 34], F32))
                for (j0, jn) in GROUPS:
                    if has_s:
                        S.dma("pool", stt[:, 0, 0:jn * 128], sffn_d[l][:, 128 * j0:128 * (j0 + jn)], writes=Bstt.r())
                        S.dma("pool", stt[:, 1, 0:jn * 128], sffn_d[l][:, FF + 128 * j0:FF + 128 * (j0 + jn)], writes=Bstt.r())
                    for j in range(j0, j0 + jn):
                        jl = j - j0
                        zc = {}
                        for which, wname in ((0, "upg"), (1, "upu")):
                            wi, wt, wb = w_next((wname, l, j))
                            fch = j + NJ * which
                            o_w, _ = PAR[f"fcw{l}"]
                            o_b, _ = PAR[f"fcb{l}"]
                            wcol = lambda tap: par[:, o_w + 3 * fch + tap:o_w + 3 * fch + tap + 1]
                            bcol = par[:, o_b + fch:o_b + fch + 1]
                            zc[which] = []
                            for bi, (c0, ln) in enumerate(b3):
                                last = bi == len(b3) - 1
                                ps, pb = PS.next()
                                inj = has_s and last
                                proj_block(ps, pb, wi, wt, wb, xn, Bxn, c0, ln, last_stop=not inj)
                                if inj:
                                    mm(ps[:, 0:ln], stt[:, which, 128 * jl:128 * jl + 128], sel2[:, 0:ln], False, True,
                                       Bstt.r() + Bsel.r(), pb)
                                acc, accb = TF.next()
                                lo = ln - H
                                S.op("act", lambda e: e.activation(out=acc[:, 0:lo], in_=ps[:, H:ln], func=AF.Identity,
                                                                   scale=wcol(2), bias=bcol),
                                     reads=pb.r() + Bpar.r(), writes=accb.r())
                                if which == 0 and FLT is not None:
                                    tb_, tbb_ = FLT.next()
                                    S.op("act", lambda e: e.activation(out=tb_[:, 0:lo], in_=ps[:, H - 1:ln - 1], func=AF.Copy, scale=wcol(1)),
                                         reads=pb.r() + Bpar.r(), writes=tbb_.r())
                                    S.op("dve", lambda e: e.scalar_tensor_tensor(out=acc[:, 0:lo], in0=ps[:, H - 2:ln - 2], scalar=wcol(0),
                                                                                 in1=acc[:, 0:lo], op0=ALU.mult, op1=ALU.add),
                                         reads=pb.r() + Bpar.r() + accb.r(), writes=accb.r())
                                    S.op("pool", lambda e: e.tensor_tensor(out=acc[:, 0:lo], in0=acc[:, 0:lo], in1=tb_[:, 0:lo], op=ALU.add),
                                         reads=accb.r() + tbb_.r(), writes=accb.r(), dur=150.0 + 2.2 * lo)
                                else:
                                    S.op("dve", lambda e: e.scalar_tensor_tensor(out=acc[:, 0:lo], in0=ps[:, H - 1:ln - 1], scalar=wcol(1),
                                                                                 in1=acc[:, 0:lo], op0=ALU.mult, op1=ALU.add),
                                         reads=pb.r() + Bpar.r() + accb.r(), writes=accb.r())
                                    S.op("dve", lambda e: e.scalar_tensor_tensor(out=acc[:, 0:lo], in0=ps[:, H - 2:ln - 2], scalar=wcol(0),
                                                                                 in1=acc[:, 0:lo], op0=ALU.mult, op1=ALU.add),
                                         reads=pb.r() + Bpar.r() + accb.r(), writes=accb.r())
                                if has_s and last:
                                    pl = sc0 - c0
                                    S.op("act", lambda e: e.activation(out=zt[:, which, jl, 0:2], in_=ps[:, pl - 2:pl], func=AF.Copy),
                                         reads=pb.r(), writes=Bzt.r())
                                    sv = ps[:, pl:ln].rearrange("p (s c) -> p s c", c=SW)[:, :, H + 6:H + 8]
                                    dv = zt[:, which, jl, 2:34].rearrange("p (s c) -> p s c", c=2)
                                    S.op("act", lambda e: e.activation(out=dv, in_=sv, func=AF.Copy), reads=pb.r(), writes=Bzt.r())
                                zc[which].append((acc, accb, c0 + H, lo))
                        for (ag, agb, oc, lo), (au, aub, _, _) in zip(zc[0], zc[1]):
                            S.op("act", lambda e: e.activation(out=ag[:, 0:lo], in_=ag[:, 0:lo], func=AF.Gelu_apprx_tanh),
                                 reads=agb.r(), writes=agb.r())
                            S.op("dve", lambda e: e.tensor_tensor(out=act[:, jl, oc:oc + lo], in0=ag[:, 0:lo], in1=au[:, 0:lo], op=ALU.mult),
                                 reads=agb.r() + aub.r(), writes=Bact.r(oc, oc + lo))
                    if has_s:
                        for which in range(2):
                            for q0 in range(0, jn, 4):
                                qn = min(4, jn - q0)
                                f0 = FF * which + 128 * (j0 + q0)
                                srcs = [(zt[:, which, q0 + i, :], Bzt.r()) for i in range(qn)]
                                transpose_out(srcs, 34, [(o_pffn[l][:, f0:f0 + 128 * qn], 0, 2),
                                                         (o_sffn[l][:, f0:f0 + 128 * qn], 2, 34)])
                    wds = [w_next(("wd", l, j)) for j in range(j0, j0 + jn)]
                    for bi0, (c0, ln) in enumerate(_blocks0(hi)):
                        if bi0 == 0:
                            NOPEN = 4
                            pss = [PS.next() for o in range(NOPEN)]
                            for o in range(NOPEN):
                                ps, pb = pss[o]
                                for jl in range(jn - 1):
                                    wi, wt, wb = wds[jl]
                                    w_check(wi)
                                    mm(ps[:, 0:ln], wt[:, 128 * o:128 * o + 128], act[:, jl, c0:c0 + ln], jl == 0, False,
                                       wb.r() + Bact.r(c0, c0 + ln), pb)
                            for o in range(NOPEN):
                                ps, pb = pss[o]
                                wi, wt, wb = wds[jn - 1]
                                mm(ps[:, 0:ln], wt[:, 128 * o:128 * o + 128], act[:, jn - 1, c0:c0 + ln], False, True,
                                   wb.r() + Bact.r(c0, c0 + ln), pb)
                                S.op("dve", lambda e: e.tensor_tensor(out=xT[:, o, c0:c0 + ln], in0=xT[:, o, c0:c0 + ln],
                                                                      in1=ps[:, 0:ln], op=ALU.add),
                                     reads=pb.r() + BxT.r(c0, c0 + ln), writes=BxT.r(c0, c0 + ln))
                        for o in range(NOPEN if bi0 == 0 else 0, 8):
                            ps, pb = PS.next()
                            for jl in range(jn):
                                wi, wt, wb = wds[jl]
                                w_check(wi)
                                mm(ps[:, 0:ln], wt[:, 128 * o:128 * o + 128], act[:, jl, c0:c0 + ln], jl == 0, jl == jn - 1,
                                   wb.r() + Bact.r(c0, c0 + ln), pb)
                            S.op("dve", lambda e: e.tensor_tensor(out=xT[:, o, c0:c0 + ln], in0=xT[:, o, c0:c0 + ln],
                                                                  in1=ps[:, 0:ln], op=ALU.add),
                                 reads=pb.r() + BxT.r(c0, c0 + ln), writes=BxT.r(c0, c0 + ln))
                tail()
                if not SH.get("on"):
                    S.barrier()

        def lru_mixer(hi, tail):
            t0, npr, has_s, sc0, nt = _half_geom(hi)
            b3 = _blocks3(hi)
            ps_mode(8)
            with ExitStack() as ph_:
                ph = SH["ph"] if SH.get("on") else ph_
                LT = shget(ph, ("LT", hi), lambda p: Pool(p, nc, f"lt{hi}_", 15, (128, 512), F32))
                mix, Bmix = shget(ph, ("mixact", hi), lambda p: mk_buf(p, f"mixact_{hi}", [128, KC, nt], BF16, nt))
                wax = ph.enter_context(nc.sbuf_tensor(f"wax_{hi}", [128, 2, 8, 128], BF16))
                Bwax = Buf(wax, "wax")
                S.dma("pool", wax[:, 0, :, :], w_lru_a.rearrange("n c d -> c n d"), writes=Bwax.r())
                S.dma("pool", wax[:, 1, :, :], w_lru_x.rearrange("n c d -> c n d"), writes=Bwax.r())
                if has_s:
                    st3 = ph.enter_context(nc.sbuf_tensor(f"st3_{hi}", [48, D], BF16))
                    Bst3 = Buf(st3, "st3")
                    S.dma("pool", st3[:], slc_d, writes=Bst3.r())
                    h0in = ph.enter_context(nc.sbuf_tensor(f"h0in_{hi}", [16, D], F32))
                    Bh0in = Buf(h0in, "h0in")
                    S.dma("sp", h0in[:], slh_d, writes=Bh0in.r())
                    h0T = ph.enter_context(nc.sbuf_tensor(f"h0T_{hi}", [128, 8, 16], F32))
                    Bh0T = Buf(h0T, "h0T")
                    ps, pb = PS.next()
                    for n in range(8):
                        S.op("pe", lambda e: e.transpose(ps[:, 16 * n:16 * n + 16], h0in[:, 128 * n:128 * n + 128], identf[0:16, 0:16]),
                             reads=Bh0in.r() + Bid.r(), writes=pb.r())
                    S.op("act", lambda e: e.activation(out=h0T[:].rearrange("p n s -> p (n s)"), in_=ps[:, 0:128], func=AF.Copy),
                         reads=pb.r(), writes=Bh0T.r())
                    rt = ph.enter_context(nc.sbuf_tensor(f"rt_{hi}", [128, 8, 51], F32))
                    Brt = Buf(rt, "rt")
                    ht = ph.enter_context(nc.sbuf_tensor(f"ht_{hi}", [128, 8, 17], F32))
                    Bht = Buf(ht, "ht")
                hprev = ph.enter_context(nc.sbuf_tensor(f"hprev_{hi}", [128, 8, 4], F32))
                Bhp = Buf(hprev, "hprev")
                for n in range(8):
                    wgi, wgt, wgb = w_next(("gate", n))
                    wri, wrt, wrb = w_next(("rec", n))
                    o_w, _ = PAR["ccw"]
                    wcol = lambda tap: par[:, o_w + 4 * n + tap:o_w + 4 * n + tap + 1]
                    units = []
                    for bi, (c0, ln) in enumerate(b3):
                        last = bi == len(b3) - 1
                        lo = ln - H
                        oc = c0 + H
                        psg, pbg = PS.next()
                        proj_block(psg, pbg, wgi, wgt, wgb, xn, Bxn, c0, ln)
                        S.op("act", lambda e: e.activation(out=mix[:, n, oc:oc + lo], in_=psg[:, H:ln], func=AF.Gelu_apprx_tanh),
                             reads=pbg.r(), writes=Bmix.r(oc, oc + lo))
                        psr, pbr = PS.next()
                        inj = has_s and last
                        proj_block(psr, pbr, wri, wrt, wrb, xn, Bxn, c0, ln, last_stop=not inj)
                        if inj:
                            mm(psr[:, 0:ln], st3[:, 128 * n:128 * n + 128], sel3[:, 0:ln], False, True, Bst3.r() + Bsel.r(), pbr)
                        xc, xcb = LT.next()
                        S.op("dve", lambda e: e.tensor_scalar(out=xc[:, 0:lo], in0=psr[:, H:ln], scalar1=wcol(3), scalar2=P("ccb", n),
                                                              op0=ALU.mult, op1=ALU.add),
                             reads=pbr.r() + Bpar.r(), writes=xcb.r())
                        for tap in (2, 1, 0):
                            sh = 3 - tap
                            S.op("dve", lambda e: e.scalar_tensor_tensor(out=xc[:, 0:lo], in0=psr[:, H - sh:ln - sh], scalar=wcol(tap),
                                                                         in1=xc[:, 0:lo], op0=ALU.mult, op1=ALU.add),
                                 reads=pbr.r() + Bpar.r() + xcb.r(), writes=xcb.r())
                        if has_s and last:
                            pl = sc0 - c0
                            S.op("act", lambda e: e.activation(out=rt[:, n, 0:3], in_=psr[:, pl - 3:pl], func=AF.Copy),
                                 reads=pbr.r(), writes=Brt.r())
                            sv = psr[:, pl:ln].rearrange("p (s c) -> p s c", c=SW)[:, :, H + 5:H + 8]
                            dv = rt[:, n, 3:51].rearrange("p (s c) -> p s c", c=3)
                            S.op("act", lambda e: e.activation(out=dv, in_=sv, func=AF.Copy), reads=pbr.r(), writes=Brt.r())
                        units.append((bi, c0, ln, last, lo, oc, xc, xcb))
                    for (bi, c0, ln, last, lo, oc, xc, xcb) in units:
                        xb, xbb = TB.next()
                        S.op("dve", lambda e: e.tensor_copy(out=xb[:, 0:lo], in_=xc[:, 0:lo]), reads=xcb.r(), writes=xbb.r())
                        psa, pba = PS.next()
                        mm(psa[:, 0:lo], wax[:, 0, n, :], xb[:, 0:lo], True, True, Bwax.r() + xbb.r(), pba)
                        psx, pbx = PS.next()
                        mm(psx[:, 0:lo], wax[:, 1, n, :], xb[:, 0:lo], True, True, Bwax.r() + xbb.r(), pbx)
                        A, Ab = LT.next()
                        I, Ib = TF.next()
                        S2, S2b = TF.next()
                        S.op("act", lambda e: e.activation(out=A[:, 0:lo], in_=psa[:, 0:lo], func=AF.Tanh, bias=hbias[:, 0, n:n + 1], scale=0.5),
                             reads=pba.r() + Blruc.r(), writes=Ab.r())
                        S.op("act", lambda e: e.activation(out=I[:, 0:lo], in_=psx[:, 0:lo], func=AF.Tanh, bias=hbias[:, 1, n:n + 1], scale=0.5),
                             reads=pbx.r() + Blruc.r(), writes=Ib.r())
                        S.op("act", lambda e: e.activation(out=S2[:, 0:lo], in_=A[:, 0:lo], func=AF.Exp, scale=lruc[:, n, 1:2], bias=lruc[:, n, 1:2]),
                             reads=Ab.r() + Blruc.r(), writes=S2b.r())
                        S.op("act", lambda e: e.activation(out=A[:, 0:lo], in_=A[:, 0:lo], func=AF.Exp, scale=lruc[:, n, 0:1], bias=lruc[:, n, 0:1]),
                             reads=Ab.r() + Blruc.r(), writes=Ab.r())
                        S.op("dve", lambda e: e.scalar_tensor_tensor(out=I[:, 0:lo], in0=I[:, 0:lo], scalar=1.0, in1=xc[:, 0:lo], op0=ALU.add, op1=ALU.mult),
                             reads=Ib.r() + xcb.r(), writes=Ib.r())
                        S.op("act", lambda e: e.activation(out=S2[:, 0:lo], in_=S2[:, 0:lo], func=AF.Sqrt, scale=-0.25, bias=0.25),
                             reads=S2b.r(), writes=S2b.r())
                        S.op("dve", lambda e: e.tensor_tensor(out=I[:, 0:lo], in0=I[:, 0:lo], in1=S2[:, 0:lo], op=ALU.mult),
                             reads=Ib.r() + S2b.r(), writes=Ib.r())
                        if has_s and last:
                            pl = sc0 - oc
                            av = A[:, pl:lo].rearrange("p (s c) -> p s c", c=SW)[:, :, 0:H]
                            bv = I[:, pl:lo].rearrange("p (s c) -> p s c", c=SW)
                            S.op("dve", lambda e: e.memset(av, 0.0), writes=Ab.r())
                            S.op("dve", lambda e: e.memset(bv[:, :, 0:H - 1], 0.0), writes=Ib.r())
                            S.op("dve", lambda e: e.tensor_copy(out=bv[:, :, H - 1:H], in_=h0T[:, n, :].unsqueeze(2)),
                                 reads=Bh0T.r(), writes=Ib.r())
                        if bi == 0:
                            init = 0.0 if hi == 0 else ch[:, n:n + 1]
                            ird = [] if hi == 0 else Bch.r()
                        else:
                            init = hprev[:, n, bi - 1:bi]
                            ird = Bhp.r()
                        S.op("dve", lambda e: e.tensor_tensor_scan(out=xc[:, 0:lo], data0=A[:, 0:lo], data1=I[:, 0:lo], initial=init,
                                                                   op0=ALU.mult, op1=ALU.add),
                             reads=Ab.r() + Ib.r() + ird + xcb.r(), writes=xcb.r(), dur=200.0 + 2.3 * lo)
                        S.op("act", lambda e: e.activation(out=hprev[:, n, bi:bi + 1], in_=xc[:, lo - 1:lo], func=AF.Copy),
                             reads=xcb.r(), writes=Bhp.r())
                        if last:
                            if hi == 0:
                                S.op("act", lambda e: e.activation(out=ch[:, n:n + 1], in_=xc[:, lo - 1:lo], func=AF.Copy),
                                     reads=xcb.r(), writes=Bch.r())
                            else:
                                pl = sc0 - oc
                                S.op("act", lambda e: e.activation(out=ht[:, n, 0:1], in_=xc[:, pl - 1:pl], func=AF.Copy),
                                     reads=xcb.r(), writes=Bht.r())
                                sv = xc[:, pl:lo].rearrange("p (s c) -> p s c", c=SW)[:, :, SW - 1:SW]
                                S.op("act", lambda e: e.activation(out=ht[:, n, 1:17].unsqueeze(2), in_=sv, func=AF.Copy),
                                     reads=xcb.r(), writes=Bht.r())
                        S.op("dve", lambda e: e.tensor_tensor(out=mix[:, n, oc:oc + lo], in0=xc[:, 0:lo], in1=mix[:, n, oc:oc + lo], op=ALU.mult),
                             reads=xcb.r() + Bmix.r(oc, oc + lo), writes=Bmix.r(oc, oc + lo))
                if has_s:
                    for q0 in (0, 4):
                        transpose_out([(rt[:, q0 + i, :], Brt.r()) for i in range(4)], 51,
                                      [(o_plc[:, 512 * (q0 // 4):512 * (q0 // 4) + 512], 0, 3),
                                       (o_slc[:, 512 * (q0 // 4):512 * (q0 // 4) + 512], 3, 51)])
                        transpose_out([(ht[:, q0 + i, :], Bht.r()) for i in range(4)], 17,
                                      [(o_plh[:, 512 * (q0 // 4):512 * (q0 // 4) + 512], 0, 1),
                                       (o_slh[:, 512 * (q0 // 4):512 * (q0 // 4) + 512], 1, 17)])
                out_proj(hi, "wo1", mix, Bmix)
                tail()
                if not SH.get("on"):
                    S.barrier()

        def gn_block_3d(ps_o, pb_o, width, cn, qd_ap, dst_fn, c_lo, c_hi, h, Bmix, BSG):
            o, ob = TF.next()
            if qd_ap is not None:
                S.op("dve", lambda e: e.tensor_tensor(out=o[:, 0:width].rearrange("p (a b) -> p a b", b=128),
                                                      in0=ps_o[:, 0:width].rearrange("p (a b) -> p a b", b=128), in1=qd_ap, op=ALU.mult),
                     reads=pb_o.r() + Bcon.r(), writes=ob.r())
            else:
                S.op("act", lambda e: e.activation(out=o[:, 0:width], in_=ps_o[:, 0:width], func=AF.Copy), reads=pb_o.r(), writes=ob.r())
            o16, o16b = TB.next()
            S.op("act", lambda e: e.activation(out=o16[:, 0:width], in_=o[:, 0:width], func=AF.Copy), reads=ob.r(), writes=o16b.r())
            q16, q16b = TB.next()
            S.op("act", lambda e: e.activation(out=q16[:, 0:width], in_=o[:, 0:width], func=AF.Square), reads=ob.r(), writes=q16b.r())
            psm, pbm = PS.next()
            mm(psm[:, 0:width], ones[:, 2, :], o16[:, 0:width], True, True, o16b.r() + Bones.r(), pbm)
            psq, pbq = PS.next()
            mm(psq[:, 0:width], ones[:, 2, :], q16[:, 0:width], True, True, q16b.r() + Bones.r(), pbq)
            mean, meanb = TF.next()
            S.op("act", lambda e: e.activation(out=mean[:, 0:width], in_=psm[:, 0:width], func=AF.Copy), reads=pbm.r(), writes=meanb.r())
            var, varb = TF.next()
            S.op("dve", lambda e: e.tensor_tensor(out=var[:, 0:width], in0=mean[:, 0:width], in1=mean[:, 0:width], op=ALU.mult),
                 reads=meanb.r(), writes=varb.r())
            S.op("dve", lambda e: e.tensor_tensor(out=var[:, 0:width], in0=psq[:, 0:width], in1=var[:, 0:width], op=ALU.subtract),
                 reads=pbq.r() + varb.r(), writes=varb.r())
            S.op("dve", lambda e: e.tensor_scalar_max(out=var[:, 0:width], in0=var[:, 0:width], scalar1=0.0), reads=varb.r(), writes=varb.r())
            S.op("act", lambda e: e.activation(out=var[:, 0:width], in_=var[:, 0:width], func=AF.Ln, bias=GN_EPS, scale=1.0),
                 reads=varb.r(), writes=varb.r())
            rs, rsb = TF.next()
            S.op("act", lambda e: e.activation(out=rs[:, 0:width], in_=var[:, 0:width], func=AF.Exp, scale=-0.5), reads=varb.r(), writes=rsb.r())
            S.op("dve", lambda e: e.tensor_tensor(out=o[:, 0:width], in0=o[:, 0:width], in1=mean[:, 0:width], op=ALU.subtract),
                 reads=ob.r() + meanb.r(), writes=ob.r())
            S.op("dve", lambda e: e.tensor_tensor(out=o[:, 0:width], in0=o[:, 0:width], in1=rs[:, 0:width], op=ALU.mult),
                 reads=ob.r() + rsb.r(), writes=ob.r())
            dst, sgv, ov = dst_fn(o)
            S.op("dve", lambda e: e.scalar_tensor_tensor(out=dst, in0=ov, scalar=P("gng", h), in1=sgv, op0=ALU.mult, op1=ALU.mult),
                 reads=ob.r() + BSG.r(c_lo, c_hi) + Bpar.r(), writes=Bmix.r(c_lo, c_hi))


        def even_mixer(hi, tail):
            t0, npr, has_s, sc0, nt = _half_geom(hi)
            b0 = _blocks0(hi)
            nch = npr // 128
            UW = 30 + npr + (NSQ * 38 if has_s else 0)
            us0 = 30 + npr
            ps_mode(8)
            with ExitStack() as ph:
                mix = ph.enter_context(nc.sbuf_tensor(f"mix_{hi}", [128, KC, nt], BF16))
                Bmix = Buf(mix, "mix", nt)
                S.op("dve", lambda e: e.memset(mix[:, :, :], 0.0), writes=Bmix.r())
                with ExitStack() as pa:
                    uT = pa.enter_context(nc.sbuf_tensor(f"uT_{hi}", [128, 4, UW], BF16))
                    BuTc = [Buf(uT, f"uT{c_}", UW) for c_ in range(4)]

                    class _AllU:
                        def r(self, c0=None, c1=None):
                            out = []
                            for b_ in BuTc:
                                out += b_.r(c0, c1)
                            return out
                    BuT = _AllU()
                    diag = pa.enter_context(nc.sbuf_tensor(f"diag_{hi}", [128, 4, 31, 128], BF16))
                    Bdiagc = [Buf(diag, f"diag{c_}") for c_ in range(4)]
                    ut = pa.enter_context(nc.sbuf_tensor(f"ut_{hi}", [128, 4, 30], F32))
                    But = Buf(ut, "ut")
                    o_w, _ = PAR["caw"]
                    for c in range(4):
                        for k in range(31):
                            S.op("dve", lambda e: e.tensor_scalar(out=diag[:, c, k, :], in0=identb[:], scalar1=par[:, o_w + 31 * c + k:o_w + 31 * c + k + 1],
                                                                  scalar2=None, op0=ALU.mult),
                                 reads=Bidb.r() + Bpar.r(), writes=Bdiagc[c].r())
                    if hi == 0:
                        S.op("dve", lambda e: e.memset(uT[:, :, 0:30], 0.0), writes=BuT.r(0, 30))
                    else:
                        S.op("dve", lambda e: e.tensor_copy(out=uT[:, :, 0:30], in_=cu[:, :, :]), reads=Bcu.r(), writes=BuT.r(0, 30))
                    if has_s:
                        us = pa.enter_context(nc.sbuf_tensor(f"us_{hi}", [128, 4, 128], F32))
                        Bus = Buf(us, "us")
                        for q in range(4):
                            si, sib = TF.next()
                            S.dma("sp", si[0:120, :], sca_d[120 * q:120 * q + 120, :], writes=sib.r())
                            ps, pb = PS.next()
                            for c in range(4):
                                S.op("pe", lambda e: e.transpose(ps[:, 128 * c:128 * c + 120], si[0:120, 128 * c:128 * c + 128], identf[0:120, 0:120]),
                                     reads=sib.r() + Bid.r(), writes=pb.r())
                            for c in range(4):
                                dst = uT[:, c, us0 + 38 * 4 * q:us0 + 38 * 4 * (q + 1)].rearrange("p (s c) -> p s c", c=38)[:, :, 0:30]
                                src = ps[:, 128 * c:128 * c + 120].rearrange("p (s c) -> p s c", c=30)
                                S.op("act", lambda e: e.activation(out=dst, in_=src, func=AF.Copy), reads=pb.r(), writes=BuTc[c].r(us0, UW))
                        S.dma("sp", o_sca.rearrange("(s r) f -> s r f", r=30)[:, 0:22, :],
                              sca_d.rearrange("(s r) f -> s r f", r=30)[:, 8:30, :])
                    for c in range(4):
                        wli, wlt, wlb = w_next(("alin", c))
                        wgi, wgt, wgb = w_next(("agate", c))
                        for bi, (c0, ln) in enumerate(b0):
                            psl, pbl = PS.next()
                            proj_block(psl, pbl, wli, wlt, wlb, xn, Bxn, c0, ln)
                            psg, pbg = PS.next()
                            proj_block(psg, pbg, wgi, wgt, wgb, xn, Bxn, c0, ln)
                            sg, sgb = TF.next()
                            S.op("act", lambda e: e.activation(out=sg[:, 0:ln], in_=psg[:, 0:ln], func=AF.Sigmoid),
                                 reads=pbg.r(), writes=sgb.r())
                            p1 = min(c0 + ln, sc0)
                            pn = p1 - c0
                            if pn > 0:
                                uc = 30 + (c0 - H)
                                S.op("dve", lambda e: e.tensor_tensor(out=uT[:, c, uc:uc + pn], in0=psl[:, 0:pn], in1=sg[:, 0:pn], op=ALU.mult),
                                     reads=pbl.r() + sgb.r(), writes=BuTc[c].r(uc, uc + pn))
                                if p1 == sc0:
                                    S.op("dve", lambda e: e.tensor_tensor(out=ut[:, c, :], in0=psl[:, pn - 30:pn], in1=sg[:, pn - 30:pn], op=ALU.mult),
                                         reads=pbl.r() + sgb.r(), writes=But.r())
                            if has_s and c0 + ln > sc0:
                                pl = sc0 - c0
                                lv = psl[:, pl:ln].rearrange("p (s c) -> p s c", c=SW)[:, :, H:SW]
                                gv = sg[:, pl:ln].rearrange("p (s c) -> p s c", c=SW)[:, :, H:SW]
                                dv = uT[:, c, us0:UW].rearrange("p (s c) -> p s c", c=38)[:, :, 30:38]
                                S.op("dve", lambda e: e.tensor_tensor(out=dv, in0=lv, in1=gv, op=ALU.mult),
                                     reads=pbl.r() + sgb.r(), writes=BuTc[c].r(us0, UW))
                                S.op("dve", lambda e: e.tensor_tensor(out=us[:, c, :].rearrange("p (s c) -> p s c", c=TS), in0=lv, in1=gv, op=ALU.mult),
                                     reads=pbl.r() + sgb.r(), writes=Bus.r())
                    if hi == 0:
                        S.op("act", lambda e: e.activation(out=cu[:, :, :], in_=uT[:, :, us0 - 30:us0], func=AF.Copy), reads=BuT.r(us0 - 30, us0), writes=Bcu.r())
                    else:
                        transpose_out([(ut[:, c, :], But.r()) for c in range(4)], 30, [(o_pca[:, :], 0, 30)])
                        ps, pb = PS.next()
                        for c in range(4):
                            S.op("pe", lambda e: e.transpose(ps[:, 128 * c:128 * c + 128], us[:, c, :], identf[:]),
                                 reads=Bus.r() + Bid.r(), writes=pb.r())
                        st, stb = TF.next()
                        S.op("act", lambda e: e.activation(out=st[:, :], in_=ps[:, :], func=AF.Copy), reads=pb.r(), writes=stb.r())
                        osv = o_sca.rearrange("(s r) f -> s r f", r=30)
                        for s in range(NSQ):
                            S.dma("sp", osv[s, 22:30, :], st[8 * s:8 * s + 8, :], reads=stb.r())
                    for bi, (c0, ln) in enumerate(b0):
                        segs = []
                        p1 = min(c0 + ln, sc0)
                        if p1 > c0:
                            segs.append(("p", c0, p1 - c0))
                        if has_s and c0 + ln > sc0:
                            segs.append(("s", sc0, NSQ * TS))
                        for kind, s0, sl in segs:
                            cvs = []
                            stat_in = []
                            for c in range(4):
                                ps, pb = PS.next()
                                for k in range(31):
                                    if kind == "p":
                                        ub = (s0 - H) + k
                                        rhs = uT[:, c, ub:ub + sl]
                                        urd = BuTc[c].r(ub, ub + sl)
                                    else:
                                        rhs = uT[:, c, us0:UW].rearrange("p (s c) -> p s c", c=38)[:, :, k:k + TS]
                                        urd = BuTc[c].r(us0, UW)
                                    mm(ps[:, 0:sl], diag[:, c, k, :], rhs, k == 0, k == 30, urd + Bdiagc[c].r(), pb)
                                cv, cvb = TF.next()
                                S.op("act", lambda e: e.activation(out=cv[:, 0:sl], in_=ps[:, 0:sl], func=AF.Identity, bias=P("cab", c), scale=1.0),
                                     reads=pb.r() + Bpar.r(), writes=cvb.r())
                                c16, c16b = TB.next()
                                S.op("act", lambda e: e.activation(out=c16[:, 0:sl], in_=cv[:, 0:sl], func=AF.Copy), reads=cvb.r(), writes=c16b.r())
                                q16, q16b = TB.next()
                                S.op("act", lambda e: e.activation(out=q16[:, 0:sl], in_=cv[:, 0:sl], func=AF.Square), reads=cvb.r(), writes=q16b.r())
                                stat_in.append((c16, c16b, q16, q16b))
                                cvs.append((cv, cvb))
                            psm, pbm = PS.next()
                            for c, (c16, c16b, q16, q16b) in enumerate(stat_in):
                                mm(psm[:, 0:sl], ones[:, 1, :], c16[:, 0:sl], c == 0, c == 3, c16b.r() + Bones.r(), pbm)
                            psq, pbq = PS.next()
                            for c, (c16, c16b, q16, q16b) in enumerate(stat_in):
                                mm(psq[:, 0:sl], ones[:, 1, :], q16[:, 0:sl], c == 0, c == 3, q16b.r() + Bones.r(), pbq)
                            mean, meanb = TF.next()
                            S.op("act", lambda e: e.activation(out=mean[:, 0:sl], in_=psm[:, 0:sl], func=AF.Copy), reads=pbm.r(), writes=meanb.r())
                            var, varb = TF.next()
                            S.op("dve", lambda e: e.tensor_tensor(out=var[:, 0:sl], in0=mean[:, 0:sl], in1=mean[:, 0:sl], op=ALU.mult),
                                 reads=meanb.r(), writes=varb.r())
                            S.op("dve", lambda e: e.tensor_tensor(out=var[:, 0:sl], in0=psq[:, 0:sl], in1=var[:, 0:sl], op=ALU.subtract),
                                 reads=pbq.r() + varb.r(), writes=varb.r())
                            S.op("dve", lambda e: e.tensor_scalar_max(out=var[:, 0:sl], in0=var[:, 0:sl], scalar1=0.0), reads=varb.r(), writes=varb.r())
                            S.op("act", lambda e: e.activation(out=var[:, 0:sl], in_=var[:, 0:sl], func=AF.Ln, bias=LN_EPS, scale=1.0),
                                 reads=varb.r(), writes=varb.r())
                            rs, rsb = TF.next()
                            S.op("act", lambda e: e.activation(out=rs[:, 0:sl], in_=var[:, 0:sl], func=AF.Exp, scale=-0.5), reads=varb.r(), writes=rsb.r())
                            for c in range(4):
                                cv, cvb = cvs[c]
                                S.op("dve", lambda e: e.tensor_tensor(out=cv[:, 0:sl], in0=cv[:, 0:sl], in1=mean[:, 0:sl], op=ALU.subtract),
                                     reads=cvb.r() + meanb.r(), writes=cvb.r())
                                S.op("dve", lambda e: e.tensor_tensor(out=cv[:, 0:sl], in0=cv[:, 0:sl], in1=rs[:, 0:sl], op=ALU.mult),
                                     reads=cvb.r() + rsb.r(), writes=cvb.r())
                                if kind == "p":
                                    dst = mix[:, c, s0:s0 + sl]
                                    src = cv[:, 0:sl]
                                    wr = Bmix.r(s0, s0 + sl)
                                else:
                                    dst = mix[:, c, sc0:nt].rearrange("p (s c) -> p s c", c=SW)[:, :, H:SW]
                                    src = cv[:, 0:sl].rearrange("p (s c) -> p s c", c=TS)
                                    wr = Bmix.r(sc0, nt)
                                S.op("act", lambda e: e.activation(out=dst, in_=src, func=AF.Silu, scale=P("lng", c), bias=P("lnb", c)),
                                     reads=cvb.r() + Bpar.r(), writes=wr)
                    S.fence(fsc[:, 0:1], Bfsc, keep=[Bmix])
                ps_mode(7)
                with ExitStack() as pbk:
                    rope = pbk.enter_context(nc.sbuf_tensor(f"rope_{hi}", [128, 2, nt], F32))
                    Brope = Buf(rope, "rope")
                    S.dma("sp", rope[:], rope_d[hi], writes=Brope.r())
                    QT = pbk.enter_context(nc.sbuf_tensor(f"QT_{hi}", [128, nt], BF16))
                    KT = pbk.enter_context(nc.sbuf_tensor(f"KT_{hi}", [128, nt], BF16))
                    VT = pbk.enter_context(nc.sbuf_tensor(f"VT_{hi}", [128, nt], BF16))
                    SG = pbk.enter_context(nc.sbuf_tensor(f"SG_{hi}", [128, nt], F32))
                    BQT, BKT, BVT, BSG = Buf(QT, "QT", nt), Buf(KT, "KT", nt), Buf(VT, "VT", nt), Buf(SG, "SG", nt)
                    ntl = nch + (1 if has_s else 0)
                    Ktok = pbk.enter_context(nc.sbuf_tensor(f"Ktok_{hi}", [128, ntl, 128], BF16))
                    Vdec = pbk.enter_context(nc.sbuf_tensor(f"Vdec_{hi}", [128, ntl, 128], BF16))
                    BKtok, BVdec = Buf(Ktok, "Ktok"), Buf(Vdec, "Vdec")
                    Sbf = pbk.enter_context(nc.sbuf_tensor(f"Sbf_{hi}", [128, nch + 1, 128], BF16))
                    BSbf = [Buf(Sbf, f"Sbf{i}") for i in range(nch + 1)]
                    Sf = pbk.enter_context(nc.sbuf_tensor(f"Sf_{hi}", [128, 128], F32))
                    BSf = Buf(Sf, "Sf")
                    if has_s:
                        S0f = pbk.enter_context(nc.sbuf_tensor(f"S0f_{hi}", [128, NSQ, 128], F32))
                        BS0f = Buf(S0f, "S0f")
                        S0b = pbk.enter_context(nc.sbuf_tensor(f"S0b_{hi}", [128, NSQ, 128], BF16))
                        BS0b = Buf(S0b, "S0b")
                        Vbd = pbk.enter_context(nc.sbuf_tensor(f"Vbd_{hi}", [128, NSQ, 128], BF16))
                        BVbd = Buf(Vbd, "Vbd")
                        Kds = pbk.enter_context(nc.sbuf_tensor(f"Kds_{hi}", [128, 128], BF16))
                        BKds = Buf(Kds, "Kds")
                        Qds = pbk.enter_context(nc.sbuf_tensor(f"Qds_{hi}", [128, 128], BF16))
                        BQds = Buf(Qds, "Qds")
                        cmpS = pbk.enter_context(nc.sbuf_tensor(f"cmpS_{hi}", [128, 3, 128], BF16))
                        BcmpS = Buf(cmpS, "cmpS")
                    dk_scale = 128.0 ** -0.5
                    for h in range(4):
                        ws = {nm: w_next((nm, h)) for nm in ("q", "k", "v", "g")}
                        if has_s:
                            S.dma("sp", S0f[:], sret_d[:, h, :, :].rearrange("s d v -> d s v"), writes=BS0f.r())
                            S.dma("pool", S0b[:], sret_d[:, h, :, :].rearrange("s d v -> d s v"), writes=BS0b.r())
                        for (c0, ln) in b0:
                            for (dstT, Bd, n1, sc) in ((QT, BQT, "q", 1.0), (KT, BKT, "k", dk_scale)):
                                ps1, pb1 = PS.next()
                                proj_block(ps1, pb1, *ws[n1], xn, Bxn, c0, ln)
                                qb, qbb = TB.next()
                                S.op("act", lambda e: e.activation(out=qb[:, 0:ln], in_=ps1[:, 0:ln], func=AF.Copy), reads=pb1.r(), writes=qbb.r())
                                ps2, pb2 = PS.next()
                                mm(ps2[:, 0:ln], permb[:], qb[:, 0:ln], True, True, Bperm.r() + qbb.r(), pb2)
                                t1, t1b = TF.next()
                                t2, t2b = TF.next()
                                S.op("dve", lambda e: e.scalar_tensor_tensor(out=t1[:, 0:ln], in0=ps1[:, 0:ln], scalar=sc, in1=rope[:, 0, c0:c0 + ln],
                                                                             op0=ALU.mult, op1=ALU.mult),
                                     reads=pb1.r() + Brope.r(), writes=t1b.r())
                                S.op("dve", lambda e: e.scalar_tensor_tensor(out=t2[:, 0:ln], in0=ps2[:, 0:ln], scalar=sc, in1=rope[:, 1, c0:c0 + ln],
                                                                             op0=ALU.mult, op1=ALU.mult),
                                     reads=pb2.r() + Brope.r(), writes=t2b.r())
                                S.op("dve", lambda e: e.tensor_tensor(out=dstT[:, c0:c0 + ln], in0=t1[:, 0:ln], in1=t2[:, 0:ln], op=ALU.add),
                                     reads=t1b.r() + t2b.r(), writes=Bd.r(c0, c0 + ln))
                            psv, pbv = PS.next()
                            proj_block(psv, pbv, *ws["v"], xn, Bxn, c0, ln)
                            S.op("act", lambda e: e.activation(out=VT[:, c0:c0 + ln], in_=psv[:, 0:ln], func=AF.Copy), reads=pbv.r(), writes=BVT.r(c0, c0 + ln))
                            psg, pbg = PS.next()
                            proj_block(psg, pbg, *ws["g"], xn, Bxn, c0, ln)
                            S.op("act", lambda e: e.activation(out=SG[:, c0:c0 + ln], in_=psg[:, 0:ln], func=AF.Silu), reads=pbg.r(), writes=BSG.r(c0, c0 + ln))

                        if has_s:
                            for (srcT, Bs, j_) in ((QT, BQT, 0), (KT, BKT, 1), (VT, BVT, 2)):
                                S.op("act", lambda e: e.activation(out=cmpS[:, j_, :].rearrange("p (s c) -> p s c", c=TS),
                                                                   in_=srcT[:, sc0:nt].rearrange("p (s c) -> p s c", c=SW)[:, :, H:SW], func=AF.Copy),
                                     reads=Bs.r(sc0, nt), writes=BcmpS.r())
                        cmp_idx = {id(QT): 0, id(KT): 1, id(VT): 2}

                        def tcols(tsr, i):
                            if i < nch:
                                return tsr[:, H + 128 * i:H + 128 * i + 128]
                            return cmpS[:, cmp_idx[id(tsr)], :]

                        def tregs(B, i):
                            if i < nch:
                                return B.r(H + 128 * i, H + 128 * i + 128)
                            return BcmpS.r()
                        for i0 in range(0, ntl, 8):
                            i1 = min(ntl, i0 + 8)
                            for (srcT, Bs, dstk) in ((KT, BKT, "k"), (VT, BVT, "v")):
                                ps, pb = PS.next()
                                psb = ps[:].bitcast(BF16)
                                for i in range(i0, i1):
                                    S.op("pe", lambda e: e.transpose(psb[:, 128 * (i - i0):128 * (i - i0) + 128], tcols(srcT, i), identb[:]),
                                         reads=tregs(Bs, i) + Bidb.r(), writes=pb.r())
                                npz = min(i1, nch) - i0
                                if dstk == "k":
                                    if npz > 0:
                                        S.op("act", lambda e: e.activation(out=Ktok[:, i0:i0 + npz, :].rearrange("p a b -> p (a b)"),
                                                                           in_=psb[:, 0:128 * npz], func=AF.Copy),
                                             reads=pb.r(), writes=BKtok.r())
                                    if has_s and i1 == ntl:
                                        off = 128 * (nch - i0)
                                        S.op("act", lambda e: e.activation(out=Kds[:, :], in_=psb[:, off:off + 128], func=AF.Copy, scale=rcon[:, 4 + h:5 + h]),
                                             reads=pb.r() + Bcon.r(), writes=BKds.r())
                                else:
                                    if npz > 0:
                                        S.op("act", lambda e: e.activation(out=Vdec[:, i0:i0 + npz, :].rearrange("p a b -> p (a b)"),
                                                                           in_=psb[:, 0:128 * npz], func=AF.Copy, scale=rcon[:, h:h + 1]),
                                             reads=pb.r() + Bcon.r(), writes=BVdec.r())
                                    if has_s and i1 == ntl:
                                        off = 128 * (nch - i0)
                                        S.op("act", lambda e: e.activation(out=Vdec[:, nch, :], in_=psb[:, off:off + 128], func=AF.Copy),
                                             reads=pb.r(), writes=BVdec.r())
                        if hi == 0:
                            S.op("dve", lambda e: e.memset(Sf[:, :], 0.0), writes=BSf.r())
                            S.op("dve", lambda e: e.memset(Sbf[:, 0, :], 0.0), writes=BSbf[0].r())
                        else:
                            S.op("dve", lambda e: e.tensor_copy(out=Sf[:, :], in_=cS[:, h, :]), reads=BcS.r(), writes=BSf.r())
                            S.op("act", lambda e: e.activation(out=Sbf[:, 0, :], in_=cS[:, h, :], func=AF.Copy), reads=BcS.r(), writes=BSbf[0].r())

                        for cg in range(0, nch, 4):
                            cn = min(4, nch - cg)
                            ps_o, pb_o = PSL.next()
                            for ci in range(cg, cg + cn):
                                cc0 = H + 128 * ci
                                ps_u, pb_u = PS.next()
                                mm(ps_u[:, 0:128], Ktok[:, ci, :], Vdec[:, ci, :], True, True, BKtok.r() + BVdec.r(), pb_u)
                                S.op("dve", lambda e: e.scalar_tensor_tensor(out=Sf[:, :], in0=Sf[:, :], scalar=gC[h], in1=ps_u[:, 0:128],
                                                                             op0=ALU.mult, op1=ALU.add),
                                     reads=pb_u.r() + BSf.r(), writes=BSf.r())
                                S.op("act", lambda e: e.activation(out=Sbf[:, ci + 1, :], in_=Sf[:, :], func=AF.Copy), reads=BSf.r(), writes=BSbf[ci + 1].r())
                                ps_s, pb_s = PS.next()
                                mm(ps_s[:, 0:128], KT[:, cc0:cc0 + 128], QT[:, cc0:cc0 + 128], True, True,
                                   BKT.r(cc0, cc0 + 128) + BQT.r(cc0, cc0 + 128), pb_s)
                                scm, scmb = TB.next()
                                S.op("dve", lambda e: e.tensor_tensor(out=scm[:, 0:128], in0=ps_s[:, 0:128], in1=maskP[:, h, :], op=ALU.mult),
                                     reads=pb_s.r() + Bcon.r(), writes=scmb.r())
                                oc_ = 128 * (ci - cg)
                                mm(ps_o[:, oc_:oc_ + 128], Vdec[:, ci, :], scm[:, 0:128], True, False, BVdec.r() + scmb.r(), pb_o)
                                mm(ps_o[:, oc_:oc_ + 128], Sbf[:, ci, :], QT[:, cc0:cc0 + 128], False, True,
                                   BSbf[ci].r() + BQT.r(cc0, cc0 + 128), pb_o)
                            w_ = 128 * cn
                            g0 = H + 128 * cg
                            qd_ap = qdP[:, h, :].unsqueeze(1).broadcast_to([128, cn, 128])

                            def dst_fn(o, g0=g0, w_=w_):
                                return mix[:, 4 + h, g0:g0 + w_], SG[:, g0:g0 + w_], o[:, 0:w_]
                            gn_block_3d(ps_o, pb_o, w_, cn, qd_ap, dst_fn, g0, g0 + w_, h, Bmix, BSG)
                        if hi == 0:
                            S.op("dve", lambda e: e.tensor_copy(out=cS[:, h, :], in_=Sf[:, :]), reads=BSf.r(), writes=BcS.r())
                        else:
                            S.dma("sp", o_pret[h, :, :], Sf[:, :], reads=BSf.r())
                        if has_s:
                            qs_v = tcols(QT, nch)
                            ks_v = tcols(KT, nch)
                            S.op("dve", lambda e: e.tensor_tensor(out=Qds[:, :], in0=qs_v, in1=qd8[:, h, :], op=ALU.mult),
                                 reads=BcmpS.r() + Bcon.r(), writes=BQds.r())
                            ps_s, pb_s = PS.next()
                            mm(ps_s[:, 0:128], ks_v, qs_v, True, True, BcmpS.r(), pb_s)
                            scm, scmb = TB.next()
                            S.op("dve", lambda e: e.tensor_tensor(out=scm[:, 0:128], in0=ps_s[:, 0:128], in1=maskS[:, h, :], op=ALU.mult),
                                 reads=pb_s.r() + Bcon.r(), writes=scmb.r())
                            ps_o, pb_o = PSL.next()
                            mm(ps_o[:, 0:128], Vdec[:, nch, :], scm[:, 0:128], True, False, BVdec.r() + scmb.r(), pb_o)
                            for s in range(NSQ):
                                mm(ps_o[:, 8 * s:8 * s + 8], S0b[:, s, :], Qds[:, 8 * s:8 * s + 8], False, s == NSQ - 1,
                                   BS0b.r() + BQds.r(), pb_o)

                            def dst_fn_s(o):
                                return (mix[:, 4 + h, sc0:nt].rearrange("p (s c) -> p s c", c=SW)[:, :, H:SW],
                                        SG[:, sc0:nt].rearrange("p (s c) -> p s c", c=SW)[:, :, H:SW],
                                        o[:, 0:128].rearrange("p (s c) -> p s c", c=TS))
                            gn_block_3d(ps_o, pb_o, 128, None, None, dst_fn_s, sc0, nt, h, Bmix, BSG)
                            S.op("dve", lambda e: e.tensor_tensor(out=Vbd[:, :, :], in0=Vdec[:, nch, :].unsqueeze(1).broadcast_to([128, NSQ, 128]),
                                                                  in1=rcon[:, 8:24].unsqueeze(2).broadcast_to([128, NSQ, 128]), op=ALU.mult),
                                 reads=BVdec.r() + Bcon.r(), writes=BVbd.r())
                            for q in range(4):
                                ps_u, pb_u = PS.next()
                                mm(ps_u[:, :], Kds[:, :], Vbd[:, 4 * q:4 * q + 4, :], True, True, BKds.r() + BVbd.r(), pb_u)
                                sn, snb = TF.next()
                                S.op("dve", lambda e: e.scalar_tensor_tensor(out=sn[:, :], in0=S0f[:, 4 * q:4 * q + 4, :].rearrange("p a b -> p (a b)"),
                                                                             scalar=g8[h], in1=ps_u[:, :], op0=ALU.mult, op1=ALU.add),
                                     reads=pb_u.r() + BS0f.r(), writes=snb.r())
                                S.dma("sp", o_sret[4 * q:4 * q + 4, h, :, :].rearrange("s d v -> d s v"),
                                      sn[:, :].rearrange("p (s v) -> p s v", v=128), reads=snb.r())
                    out_proj(hi, "wo0", mix, Bmix)
                    tail()
                    S.fence(fsc[:, 0:1], Bfsc)

        Buf.scoped = True
        Buf.live = []
        Buf.pending = None
        for hi in range(2):
            load_x(hi)
            rmsnorm(hi, "nmix0", None)
            even_mixer(hi, lambda hi=hi: rmsnorm(hi, "nffn0", 0))
            with ExitStack() as shared:
                SH.clear()
                SH["on"] = True
                SH["ph"] = shared
                ffn(hi, 0, lambda hi=hi: rmsnorm(hi, "nmix1", 1))
                lru_mixer(hi, lambda hi=hi: rmsnorm(hi, "nffn1", 2))
                ffn(hi, 1, lambda hi=hi: final_out(hi))
                SH["on"] = False
                S.fence(fsc[:, 0:1], Bfsc)
            if _FLUSH_HALF:
                S.flush()
        S.barrier()
        Buf.scoped = False
        build.marks = S.marks

    return nc


_NC_CACHE = {}


def _core_inputs(inp, params, consts, w_sw, c):
    m = {
        "xp": np.ascontiguousarray(inp["x_prompt"][c]),
        "xs": np.ascontiguousarray(inp["x_sample"][NSQ * c:NSQ * (c + 1)].reshape(NSQ * TS, D)),
        "st_conv_a": np.ascontiguousarray(inp["state_conv_a"][0, NSQ * c:NSQ * (c + 1)].reshape(NSQ * 30, 512)),
        "st_ret": np.ascontiguousarray(inp["state_ret"][0, NSQ * c:NSQ * (c + 1)]),
        "st_lru_conv": np.ascontiguousarray(inp["state_lru_conv"][0, NSQ * c:NSQ * (c + 1)].reshape(NSQ * 3, D)),
        "st_lru_h": np.ascontiguousarray(inp["state_lru_h"][0, NSQ * c:NSQ * (c + 1)]),
        "st_ffn": np.ascontiguousarray(inp["state_ffn_conv"][:, NSQ * c:NSQ * (c + 1)].reshape(2, NSQ * 2, 2 * FF)),
        "w_in_ab": inp["w_in_ab"][0],
        "w_out_ab": inp["w_out_ab"][0],
        "w_in_c": inp["w_in_c"][0],
        "w_lru_a": inp["w_lru_a"][0],
        "w_lru_x": inp["w_lru_x"][0],
        "w_out_c": inp["w_out_c"][0],
        "w_ffn_up": inp["w_ffn_up"],
        "w_ffn_down": inp["w_ffn_down"],
        "params": params,
    }
    m.update(consts)
    return m


def _prep(inputs):
    inp = {k: np.asarray(v, dtype=np.float32) for k, v in inputs.items()}
    params, npar = _pack_params(inp)
    consts = _host_consts()
    wqk = inp["w_in_ab"][0][:, 1024:2048].reshape(D, 8, 2, 64)
    w_sw = np.ascontiguousarray(wqk[:, :, ::-1, :].reshape(D, 1024))
    return inp, params, npar, consts, w_sw


def _assemble(res, ncores):
    def g(name):
        return [np.asarray(r[name]) for r in res]
    y_p = np.stack(g("y_p"), 0)
    y_s = np.concatenate([a.reshape(NSQ, TS, D) for a in g("y_s")], 0)
    p_ca = np.stack(g("p_conv_a"), 0)[None]
    p_ret = np.stack(g("p_ret"), 0)[None]
    p_lc = np.stack(g("p_lru_conv"), 0)[None]
    p_lh = np.stack([a.reshape(D) for a in g("p_lru_h")], 0)[None]
    p_ffn = np.stack(g("p_ffn"), 1)
    s_ca = np.concatenate([a.reshape(NSQ, 30, 512) for a in g("s_conv_a")], 0)[None]
    s_ret = np.concatenate(g("s_ret"), 0)[None]
    s_lc = np.concatenate([a.reshape(NSQ, 3, D) for a in g("s_lru_conv")], 0)[None]
    s_lh = np.concatenate(g("s_lru_h"), 0)[None]
    s_ffn = np.concatenate([a.reshape(2, NSQ, 2, 2 * FF) for a in g("s_ffn")], 1)
    outs = (y_p, y_s, p_ca, p_ret, p_lc, p_lh, p_ffn, s_ca, s_ret, s_lc, s_lh, s_ffn)
    return tuple(np.ascontiguousarray(o, dtype=np.float32) for o in outs)


def kernel(**inputs):
    inp, params, npar, consts, w_sw = _prep(inputs)
    ncores = 8
    nc = build(npar)
    in_maps = [_core_inputs(inp, params, consts, w_sw, c) for c in range(ncores)]
    res = run_bass_kernel_spmd(nc, in_maps, core_ids=list(range(ncores)))
    return _assemble(res.results, ncores)
```

```python
import math
from contextlib import ExitStack

import numpy as np
import concourse.bass as bass
import concourse.mybir as mybir
from concourse.bass_utils import run_bass_kernel_spmd

F32 = mybir.dt.float32
BF16 = mybir.dt.bfloat16
AF = mybir.ActivationFunctionType
ALU = mybir.AluOpType

D = 1024
KC = 8
SEQ = 2048
NSQ = 16
TS = 8
PAST = 16384
H = 3
SW = H + TS
FF = 2816
NJ = 22
HALVES = [(0, 896, False), (896, 1152, True)]
GROUPS = [(0, 8), (8, 7), (15, 7)]
RMS_EPS = 1e-6
LN_EPS = 1e-5
GN_EPS = 1e-5
NSLOT = 12
PF = 3
NTF = 12
NTB = 8
STAGE = 4
import os as _os
_LAT = float(_os.environ.get('SCHED_LAT', '500'))
_EPS = float(_os.environ.get('SCHED_EPS', '600'))
_FLUSH_HALF = bool(int(_os.environ.get('FLUSH_HALF', '1')))
_TBL_INIT = None


class Reg:
    __slots__ = ("name", "w", "rd", "excl")

    def __init__(self, name, excl=False):
        self.name = name
        self.w = None
        self.rd = {}
        self.excl = excl


class Buf:
    scoped = False
    live = []
    pending = None

    def __init__(self, t, name, ncols=None, G=256, excl=False):
        self.t = t
        self.name = name
        self.G = G
        n = 1 if ncols is None else (ncols + G - 1) // G
        self.regs = [Reg(f"{name}.{i}", excl) for i in range(n)]
        self.ncols = ncols
        if Buf.scoped:
            for r_ in self.regs:
                r_.w = Buf.pending
            Buf.live.append(self)

    def r(self, c0=None, c1=None):
        if self.ncols is None or c0 is None:
            return list(self.regs)
        return self.regs[c0 // self.G:(c1 - 1) // self.G + 1]


class _Rec:
    def __init__(self):
        self.call = None

    def __getattr__(self, name):
        def f(*a, **k):
            self.call = (name, a, k)
            return self
        return f


class _Node:
    __slots__ = ("idx", "eng", "call", "deps", "dur", "lat", "tbl", "tok", "is_dma", "succ", "ndep", "ready")


_ACT_TBL = {}


def _free_elems(ap):
    try:
        sh = ap.shape
        n = 1
        for d in sh[1:]:
            n *= int(d)
        return n
    except Exception:
        return 512


class Sched:
    def __init__(self, nc, ndma=6):
        self.nc = nc
        self.E = {"pe": nc.tensor, "act": nc.scalar, "dve": nc.vector, "pool": nc.gpsimd, "sp": nc.sync}
        self.sem = {}
        self.cnt = {}
        self.waited = {e: {} for e in self.E}
        for e in self.E:
            self.sem[e] = nc.alloc_semaphore(name=f"c_{e}")
            self.cnt[e] = 0
        self.dsem = {}
        for q in ("sp", "pool"):
            self.dsem[q] = [[nc.alloc_semaphore(name=f"d_{q}{i}"), 0] for i in range(ndma)]
        self.drr = {"sp": 0, "pool": 0}
        self.nodes = []
        self.nidx = 0
        self.reorder = True
        self.marks = []

    def _record(self, node, reads, writes):
        deps = {}

        def add(n, raw):
            if n is None:
                return
            if deps.get(n.idx, (None, False))[1] is False:
                deps[n.idx] = (n, raw or deps.get(n.idx, (None, False))[1])

        for r in reads:
            add(r.w, True)
            if r.excl:
                for n in r.rd.values():
                    add(n, False)
        for r in writes:
            add(r.w, r.excl)
            for n in r.rd.values():
                add(n, False)
        node.deps = list(deps.values())
        for r in writes:
            r.w = node
            r.rd = {}
        for r in reads:
            if r.excl:
                r.w = node
                r.rd = {}
            else:
                r.rd[node.idx] = node
        self.nodes.append(node)

    def op(self, e, fn, reads=(), writes=(), dur=None, tbl=None):
        rec = _Rec()
        fn(rec)
        n = _Node()
        n.idx = self.nidx
        self.nidx += 1
        n.eng = e
        n.call = rec.call
        n.is_dma = False
        n.tbl = tbl
        n.tok = None
        if dur is None:
            name, a, k = rec.call
            out = k.get("out", a[0] if a else None)
            fe = _free_elems(out) if out is not None else 512
            if e == "pe":
                dur = 10.0 + fe / 2.4
            elif e == "act":
                dur = 280.0 + fe / 1.1
                if name == "activation":
                    f = k.get("func")
                    tbl = _ACT_TBL.get(f, None)
                    n.tbl = tbl
            elif e == "dve":
                dur = 180.0 + fe / 0.9
                if name == "reciprocal":
                    dur = 120.0 + 4 * fe / 0.96
                elif name in ("memset",):
                    dur = 100.0 + fe / 3.0
            else:
                dur = 300.0
        n.dur = dur
        n.lat = 60.0 if e == "pe" else _LAT
        self._record(n, reads, writes)
        return n

    def fence(self, scratch, scratch_buf, keep=()):
        keep_ids = {id(b) for b in keep}
        regs = []
        seen = set()
        for b in Buf.live:
            if id(b) in keep_ids:
                continue
            for r_ in b.regs:
                if id(r_) not in seen:
                    seen.add(id(r_))
                    regs.append(r_)
        node = self.op("dve", lambda e: e.memset(scratch, 0.0), writes=regs + scratch_buf.r(), dur=80.0)
        Buf.pending = node
        Buf.live = [b for b in Buf.live if id(b) in keep_ids]

    def dma(self, q, out, in_, reads=(), writes=(), nbytes=None):
        n = _Node()
        n.idx = self.nidx
        self.nidx += 1
        n.eng = q
        n.call = (out, in_)
        n.is_dma = True
        n.tbl = None
        n.tok = None
        if nbytes is None:
            try:
                nb = 1
                for d in out.shape:
                    nb *= int(d)
                nbytes = nb * 4
            except Exception:
                nbytes = 65536
        n.dur = 1000.0 if q == "pool" else 150.0
        n.lat = 2000.0 + nbytes / 150.0
        self._record(n, reads, writes)

    def _wait(self, e, key, sem, val):
        if self.waited[e].get(key, 0) >= val:
            return
        self.E[e].wait_ge(sem, val)
        self.waited[e][key] = val

    def _emit(self, n):
        e = n.eng
        toks = {}
        for (d, raw) in n.deps:
            key, sem, val = d.tok
            if key == e:
                if e in ("pe", "pool", "sp"):
                    continue
                if not raw:
                    continue
                if val <= self.cnt[e] - 2:
                    continue
            if toks.get(key, (None, 0))[1] < val:
                toks[key] = (sem, val)
        if n.is_dma:
            i = self.drr[e]
            self.drr[e] = (i + 1) % len(self.dsem[e])
            ent = self.dsem[e][i]
            key = f"d_{e}{i}"
            if ent[1] > 0:
                self._wait(e, key, ent[0], 16 * ent[1])
            for k2, (sem, val) in toks.items():
                self._wait(e, k2, sem, val)
            out, in_ = n.call
            self.E[e].dma_start(out=out, in_=in_).then_inc(ent[0], 16)
            ent[1] += 1
            n.tok = (key, ent[0], 16 * ent[1])
        else:
            for k2, (sem, val) in toks.items():
                self._wait(e, k2, sem, val)
            name, a, k = n.call
            ins = getattr(self.E[e], name)(*a, **k)
            self.cnt[e] += 1
            ins.then_inc(self.sem[e], 1)
            n.tok = (e, self.sem[e], self.cnt[e])
        n.deps = None
        n.call = None

    def flush(self):
        nodes = self.nodes
        self.nodes = []
        if not nodes:
            return
        if not self.reorder:
            for n in nodes:
                self._emit(n)
            return
        inwin = {n.idx: n for n in nodes}
        for n in nodes:
            n.succ = []
            n.ndep = 0
            n.ready = 0.0
        for n in nodes:
            for (d, raw) in n.deps:
                if d.idx in inwin:
                    d.succ.append(n)
                    n.ndep += 1
        bl = {}
        for n in reversed(nodes):
            m = 0.0
            for s_ in n.succ:
                v = n.lat + bl[s_.idx]
                if v > m:
                    m = v
            bl[n.idx] = n.dur + m
        free = {e: 0.0 for e in self.E}
        last_tbl = None
        ready = {e: [] for e in self.E}
        for n in nodes:
            if n.ndep == 0:
                ready[n.eng].append(n)
        left = len(nodes)
        EPS = _EPS
        while left:
            best = None
            for e, lst in ready.items():
                if not lst:
                    continue
                mr = min(x.ready for x in lst)
                t_e = max(free[e], mr)
                if best is None or t_e < best[0]:
                    best = (t_e, e)
            t_e, e = best
            lst = ready[e]
            cands = [x for x in lst if x.ready <= t_e + EPS]
            if e in ("sp", "pool"):
                n = min(cands, key=lambda x: x.idx)
            else:
                if e == "act" and last_tbl is not None:
                    same = [x for x in lst if x.ready <= t_e + 1500.0 and (x.tbl is None or x.tbl == last_tbl)]
                    if same:
                        cands = same
                n = max(cands, key=lambda x: (bl[x.idx], -x.idx))
            lst.remove(n)
            st = max(free[e], n.ready)
            if e == "act" and n.tbl is not None:
                if last_tbl is not None and n.tbl != last_tbl:
                    st += 1300.0
                last_tbl = n.tbl
            end = st + n.dur
            free[e] = end
            fin_t = end + n.lat
            self._emit(n)
            left -= 1
            for s_ in n.succ:
                if fin_t > s_.ready:
                    s_.ready = fin_t
                s_.ndep -= 1
                if s_.ndep == 0:
                    ready[s_.eng].append(s_)
            n.succ = None

    def barrier(self):
        self.flush()
        self.marks.append(dict(self.cnt))
        for e in self.E:
            for e2 in ("pe", "act", "dve", "pool"):
                if e2 != e and self.cnt[e2] > 0:
                    self._wait(e, e2, self.sem[e2], self.cnt[e2])
            for q in ("sp", "pool"):
                for i, ent in enumerate(self.dsem[q]):
                    if ent[1] > 0:
                        self._wait(e, f"d_{q}{i}", ent[0], 16 * ent[1])


class Pool:
    def __init__(self, es, nc, name, n, shape, dt, excl=False, space="sbuf"):
        self.items = []
        for i in range(n):
            if space == "sbuf":
                t = es.enter_context(nc.sbuf_tensor(f"{name}{i}", shape, dt))
            else:
                t = es.enter_context(nc.psum_tensor(f"{name}{i}", shape, dt))
            self.items.append((t, Buf(t, f"{name}{i}", excl=excl)))
        self.i = 0

    def next(self):
        it = self.items[self.i]
        self.i = (self.i + 1) % len(self.items)
        return it


def _init_tbl():
    _ACT_TBL.update({AF.Gelu_apprx_tanh: "gelu", AF.Sigmoid: "sig", AF.Silu: "silu", AF.Exp: "exp", AF.Ln: "exp",
                     AF.Sqrt: "sqrt", AF.Tanh: "exp"})


def _gammas():
    lg = np.log(np.float32(1.0) - np.float32(2.0) ** (-5.0 - np.arange(4, dtype=np.float32))).astype(np.float32)
    return lg


def _half_geom(hi):
    t0, npr, has_s = HALVES[hi]
    sc0 = H + npr
    nt = sc0 + (NSQ * SW if has_s else 0)
    return t0, npr, has_s, sc0, nt


def _blocks0(hi):
    t0, npr, has_s, sc0, nt = _half_geom(hi)
    out = []
    c = H
    while c < nt:
        l = min(512, nt - c)
        out.append((c, l))
        c += l
    return out


def _blocks3(hi):
    t0, npr, has_s, sc0, nt = _half_geom(hi)
    out = []
    c = 0
    while True:
        l = min(512, nt - c)
        out.append((c, l))
        if c + l >= nt:
            break
        c += l - H
    return out


def _host_consts():
    c = {}
    c["ident"] = np.eye(128, dtype=np.float32)
    pm = np.zeros((128, 128), np.float32)
    for m_ in range(128):
        pm[(m_ + 64) % 128, m_] = 1.0
    c["perm"] = pm
    lg = _gammas()
    inv_freq = (np.float32(10000.0) ** (-np.arange(0, 128, 2, dtype=np.float32) / np.float32(128))).astype(np.float32)
    for hi in range(2):
        t0, npr, has_s, sc0, nt = _half_geom(hi)
        pos = np.zeros(nt, np.float32)
        valid = np.zeros(nt, bool)
        pos[H:H + npr] = np.arange(t0, t0 + npr, dtype=np.float32)
        valid[H:H + npr] = True
        if has_s:
            for s in range(NSQ):
                b = sc0 + SW * s + H
                pos[b:b + TS] = np.arange(PAST, PAST + TS, dtype=np.float32)
                valid[b:b + TS] = True
        ang = (pos[:, None] * inv_freq[None, :]).astype(np.float32)
        cs = np.cos(ang).astype(np.float32)
        sn = np.sin(ang).astype(np.float32)
        cosT = np.concatenate([cs.T, cs.T], axis=0)
        sinT = np.concatenate([-sn.T, sn.T], axis=0)
        cosT[:, ~valid] = 0
        sinT[:, ~valid] = 0
        c[f"rope{hi}"] = np.ascontiguousarray(np.stack([cosT, sinT], axis=1).astype(np.float32))
    idx = np.arange(128, dtype=np.float32)
    maskP = np.zeros((128, 4, 128), np.float32)
    qdP = np.zeros((128, 4, 128), np.float32)
    kdecP = np.zeros((128, 4), np.float32)
    maskS = np.zeros((128, 4, 128), np.float32)
    qd8 = np.zeros((128, 4, 128), np.float32)
    kdec8 = np.zeros((128, 4), np.float32)
    causal = (idx[:, None] <= idx[None, :]).astype(np.float32)
    sj = np.arange(128) // 8
    jj = (np.arange(128) % 8).astype(np.float32)
    for h in range(4):
        maskP[:, h, :] = causal * np.exp(np.float32(-128.0) * lg[h]).astype(np.float32)
        qdP[:, h, :] = np.exp((idx + 1.0) * lg[h])[None, :]
        kdecP[:, h] = np.exp((127.0 - idx) * lg[h])
        rel = jj[None, :] - jj[:, None]
        m = np.where((sj[:, None] == sj[None, :]) & (rel >= 0), np.exp(np.maximum(rel, 0) * lg[h]), 0.0)
        maskS[:, h, :] = m
        qd8[:, h, :] = np.exp((jj + 1.0) * lg[h])[None, :]
        kdec8[:, h] = np.exp((7.0 - jj) * lg[h])
    c["maskP"] = maskP
    c["qdP"] = qdP
    c["maskS"] = maskS.astype(np.float32)
    c["qd8"] = qd8
    rc = np.zeros((128, 32), np.float32)
    rc[:, 0:4] = kdecP
    rc[:, 4:8] = kdec8
    rc[:, 8:24] = (sj[:, None] == np.arange(16)[None, :]).astype(np.float32)
    c["rcon"] = rc
    t0, npr, has_s, sc0, nt = _half_geom(1)
    b3 = _blocks3(1)
    lc0, ll = b3[-1]
    sel2 = np.zeros((32, ll), np.float32)
    sel3 = np.zeros((48, ll), np.float32)
    for s in range(NSQ):
        for r in range(2):
            sel2[2 * s + r, sc0 + SW * s + 1 + r - lc0] = 1
        for r in range(3):
            sel3[3 * s + r, sc0 + SW * s + r - lc0] = 1
    c["sel2"] = sel2
    c["sel3"] = sel3
    return c


def _fm(v):
    return np.ascontiguousarray(v.reshape(-1, 128).T)


PAR = {}


def _pack_params(inp):
    cols = []
    off = 0

    def add(name, arr):
        nonlocal off
        arr = np.ascontiguousarray(arr, dtype=np.float32).reshape(128, -1)
        PAR[name] = (off, arr.shape[1])
        cols.append(arr)
        off += arr.shape[1]

    for l in range(2):
        add(f"nmix{l}", _fm(inp["norm_mix"][l]))
        add(f"nffn{l}", _fm(inp["norm_ffn"][l]))
    add("nfin", _fm(inp["norm_final"]))
    add("cab", _fm(inp["conv_a_b"][0]))
    add("lng", _fm(inp["ln_a_g"][0]))
    add("lnb", _fm(inp["ln_a_b"][0]))
    add("gng", _fm(inp["gn_ret_g"][0]))
    add("caw", inp["conv_a_w"][0].reshape(31, 4, 128).transpose(2, 1, 0))
    add("ccw", inp["conv_c_w"][0].reshape(4, 8, 128).transpose(2, 1, 0))
    add("ccb", _fm(inp["conv_c_b"][0]))
    add("ba", _fm(inp["b_lru_a"][0]))
    add("bx", _fm(inp["b_lru_x"][0]))
    add("lam", _fm(inp["lru_lambda"][0]))
    for l in range(2):
        add(f"fcw{l}", inp["ffn_conv_w"][l].reshape(3, 44, 128).transpose(2, 1, 0))
        add(f"fcb{l}", _fm(inp["ffn_conv_b"][l]))
    return np.ascontiguousarray(np.concatenate(cols, axis=1)), off


def build(npar):
    nc = bass.Bass("TRN2", target_bir_lowering=False)

    def din(name, shape):
        return nc.dram_tensor(name, list(shape), F32, kind="ExternalInput").ap()

    def dout(name, shape):
        return nc.dram_tensor(name, list(shape), F32, kind="ExternalOutput").ap()

    xp_d = din("xp", (SEQ, D))
    xs_d = din("xs", (NSQ * TS, D))
    sca_d = din("st_conv_a", (NSQ * 30, 512))
    sret_d = din("st_ret", (NSQ, 4, 128, 128))
    slc_d = din("st_lru_conv", (NSQ * 3, D))
    slh_d = din("st_lru_h", (NSQ, D))
    sffn_d = din("st_ffn", (2, NSQ * 2, 2 * FF))
    w_in_ab = din("w_in_ab", (D, 3072))
    w_out_ab = din("w_out_ab", (D, D))
    w_in_c = din("w_in_c", (D, 2048))
    w_lru_a = din("w_lru_a", (8, 128, 128))
    w_lru_x = din("w_lru_x", (8, 128, 128))
    w_out_c = din("w_out_c", (D, D))
    w_up = din("w_ffn_up", (2, D, 2 * FF))
    w_dn = din("w_ffn_down", (2, FF, D))
    par_d = din("params", (128, npar))
    ident_d = din("ident", (128, 128))
    perm_d = din("perm", (128, 128))
    rope_d = [din(f"rope{hi}", (128, 2, _half_geom(hi)[4])) for hi in range(2)]
    maskP_d = din("maskP", (128, 4, 128))
    qdP_d = din("qdP", (128, 4, 128))
    maskS_d = din("maskS", (128, 4, 128))
    qd8_d = din("qd8", (128, 4, 128))
    rcon_d = din("rcon", (128, 32))
    b3_1 = _blocks3(1)
    sel2_d = din("sel2", (32, b3_1[-1][1]))
    sel3_d = din("sel3", (48, b3_1[-1][1]))

    yp_d = dout("y_p", (SEQ, D))
    ys_d = dout("y_s", (NSQ * TS, D))
    o_pca = dout("p_conv_a", (30, 512))
    o_pret = dout("p_ret", (4, 128, 128))
    o_plc = dout("p_lru_conv", (3, D))
    o_plh = dout("p_lru_h", (1, D))
    o_pffn = dout("p_ffn", (2, 2, 2 * FF))
    o_sca = dout("s_conv_a", (NSQ * 30, 512))
    o_sret = dout("s_ret", (NSQ, 4, 128, 128))
    o_slc = dout("s_lru_conv", (NSQ * 3, D))
    o_slh = dout("s_lru_h", (NSQ, D))
    o_sffn = dout("s_ffn", (2, NSQ * 2, 2 * FF))

    lg = _gammas()
    gC = [float(np.exp(np.float32(128.0) * lg[h])) for h in range(4)]
    g8 = [float(np.exp(np.float32(8.0) * lg[h])) for h in range(4)]

    _init_tbl()
    S = Sched(nc)
    NTMAX = _half_geom(1)[4]

    with ExitStack() as es:
        def sb(name, shape, dt=F32):
            return es.enter_context(nc.sbuf_tensor("sb_" + name, list(shape), dt))

        xT = sb("xT", (128, KC, NTMAX))
        xn = sb("xn", (128, KC, NTMAX), BF16)
        BxT = Buf(xT, "xT", NTMAX)
        Bxn = Buf(xn, "xn", NTMAX)
        par = sb("par", (128, npar))
        Bpar = Buf(par, "par")
        identf = sb("identf", (128, 128))
        identb = sb("identb", (128, 128), BF16)
        Bid = Buf(identf, "identf")
        Bidb = Buf(identb, "identb")
        permb = sb("permb", (128, 128), BF16)
        Bperm = Buf(permb, "permb")
        ones = sb("ones", (128, 3, 128), BF16)
        Bones = Buf(ones, "ones")
        maskP = sb("maskP", (128, 4, 128))
        qdP = sb("qdP", (128, 4, 128))
        maskS = sb("maskS", (128, 4, 128))
        qd8 = sb("qd8", (128, 4, 128))
        rcon = sb("rcon", (128, 32))
        Bcon = Buf(maskP, "retconst")
        sel2 = sb("sel2", (32, b3_1[-1][1]), BF16)
        sel3 = sb("sel3", (48, b3_1[-1][1]), BF16)
        Bsel = Buf(sel2, "sel")
        lruc = sb("lruc", (128, 8, 2))
        hbias = sb("hbias", (128, 2, 8))
        Blruc = Buf(lruc, "lruc")
        cxn = sb("cxn", (128, 3, KC, H), BF16)
        Bcxn = Buf(cxn, "cxn")
        cu = sb("cu", (128, 4, 30), BF16)
        Bcu = Buf(cu, "cu")
        cS = sb("cS", (128, 4, 128))
        BcS = Buf(cS, "cS")
        ch = sb("ch", (128, 8))
        Bch = Buf(ch, "ch")

        fsc = sb("fsc", (128, 2))
        Bfsc = Buf(fsc, "fsc")
        slots = Pool(es, nc, "wsl", NSLOT, (128, 1024), BF16)
        TF = Pool(es, nc, "tf", NTF, (128, 512), F32)
        TB = Pool(es, nc, "tb", NTB, (128, 512), BF16)
        PS = Pool(es, nc, "ps", 6, (128, 512), F32, excl=True, space="psum")
        PSL = Pool(es, nc, "psl", 2, (128, 512), F32, excl=True, space="psum")
        _ps6 = list(PS.items)
        _ps8 = list(PS.items) + list(PSL.items)

        _psl_all = list(PSL.items)

        def ps_mode(n):
            if n == 8:
                PS.items = _ps8
                PSL.items = _psl_all
            elif n == 7:
                PS.items = _ps6 + [_psl_all[0]]
                PSL.items = [_psl_all[1]]
            else:
                PS.items = _ps6
                PSL.items = _psl_all
            PS.i = 0
            PSL.i = 0

        def P(name, k=None):
            o, n = PAR[name]
            if k is None:
                return par[:, o:o + n]
            return par[:, o + k:o + k + 1]

        S.dma("sp", par[:], par_d, writes=Bpar.r())
        S.dma("sp", identf[:], ident_d, writes=Bid.r())
        S.dma("pool", identb[:], ident_d, writes=Bidb.r())
        S.dma("pool", permb[:], perm_d, writes=Bperm.r())
        S.dma("sp", maskP[:], maskP_d, writes=Bcon.r())
        S.dma("sp", qdP[:], qdP_d, writes=Bcon.r())
        S.dma("sp", maskS[:], maskS_d, writes=Bcon.r())
        S.dma("sp", qd8[:], qd8_d, writes=Bcon.r())
        S.dma("sp", rcon[:], rcon_d, writes=Bcon.r())
        S.dma("pool", sel2[:], sel2_d, writes=Bsel.r())
        S.dma("pool", sel3[:], sel3_d, writes=Bsel.r())
        S.op("dve", lambda e: e.memset(ones[:, 0, :], 1.0 / 1024), writes=Bones.r())
        S.op("dve", lambda e: e.memset(ones[:, 1, :], 1.0 / 512), writes=Bones.r())
        S.op("dve", lambda e: e.memset(ones[:, 2, :], 1.0 / 128), writes=Bones.r())
        S.op("act", lambda e: e.activation(out=lruc[:, :, 0], in_=P("lam"), func=AF.Exp, scale=-1.0),
             reads=Bpar.r(), writes=Blruc.r())
        S.op("act", lambda e: e.activation(out=lruc[:, :, 0], in_=lruc[:, :, 0], func=AF.Ln, bias=1.0, scale=1.0),
             reads=Blruc.r(), writes=Blruc.r())
        S.op("dve", lambda e: e.tensor_scalar(out=lruc[:, :, 1], in0=lruc[:, :, 0], scalar1=-8.0, scalar2=None, op0=ALU.mult),
             reads=Blruc.r(), writes=Blruc.r())
        S.op("dve", lambda e: e.tensor_scalar(out=lruc[:, :, 0], in0=lruc[:, :, 0], scalar1=-4.0, scalar2=None, op0=ALU.mult),
             reads=Blruc.r(), writes=Blruc.r())
        S.op("dve", lambda e: e.tensor_scalar(out=hbias[:, 0, :], in0=P("ba"), scalar1=0.5, scalar2=None, op0=ALU.mult),
             reads=Bpar.r(), writes=Blruc.r())
        S.op("dve", lambda e: e.tensor_scalar(out=hbias[:, 1, :], in0=P("bx"), scalar1=0.5, scalar2=None, op0=ALU.mult),
             reads=Bpar.r(), writes=Blruc.r())

        def wap_cols(w2d, c0, ncols=128):
            return w2d.rearrange("(k p) c -> p k c", p=128)[:, :, c0:c0 + ncols]

        def weight_plan():
            for hi in range(2):
                if STAGE < 1:
                    continue
                for c in range(4):
                    yield ("alin", c), wap_cols(w_in_ab, 128 * c)
                    yield ("agate", c), wap_cols(w_in_ab, 512 + 128 * c)
                for h in range(4):
                    yield ("q", h), wap_cols(w_in_ab, 1024 + 128 * h)
                    yield ("k", h), wap_cols(w_in_ab, 1536 + 128 * h)
                    yield ("v", h), wap_cols(w_in_ab, 2048 + 128 * h)
                    yield ("g", h), wap_cols(w_in_ab, 2560 + 128 * h)
                for o in range(8):
                    yield ("wo0", o), wap_cols(w_out_ab, 128 * o)
                for l in range(2):
                    if l == 0 and STAGE < 2:
                        continue
                    if l == 1 and STAGE < 3:
                        continue
                    if l == 1:
                        for n in range(8):
                            yield ("gate", n), wap_cols(w_in_c, 128 * n)
                            yield ("rec", n), wap_cols(w_in_c, 1024 + 128 * n)
                        for o in range(8):
                            yield ("wo1", o), wap_cols(w_out_c, 128 * o)
                        if STAGE < 4:
                            continue
                    for (j0, jn) in GROUPS:
                        for j in range(j0, j0 + jn):
                            yield ("upg", l, j), wap_cols(w_up[l], 128 * j)
                            yield ("upu", l, j), wap_cols(w_up[l], FF + 128 * j)
                        for j in range(j0, j0 + jn):
                            yield ("wd", l, j), w_dn[l][128 * j:128 * j + 128, :]

        plan = list(weight_plan())
        wstate = {"issued": 0, "used": 0}
        wslot_of = {}

        def w_issue(upto):
            while wstate["issued"] < min(upto, len(plan)):
                i = wstate["issued"]
                name, ap = plan[i]
                t, b = slots.items[i % NSLOT]
                if len(ap.shape) == 3:
                    dst = t[:].rearrange("p (k c) -> p k c", c=128)
                else:
                    dst = t[:]
                S.dma("pool", dst, ap, writes=b.r())
                wslot_of[i] = (t, b)
                wstate["issued"] += 1

        def w_next(name):
            i = wstate["used"]
            assert plan[i][0] == name, (plan[i][0], name)
            w_issue(i + 1 + PF)
            wstate["used"] += 1
            t, b = wslot_of[i]
            return i, t, b

        def w_check(i):
            assert wstate["issued"] <= i + NSLOT, ("weight evicted", plan[i][0])

        def mm(ps, lhsT, rhs, start, stop, reads, pb):
            S.op("pe", lambda e: e.matmul(ps, lhsT, rhs, start=start, stop=stop), reads=reads, writes=pb.r())

        def proj_block(ps, pb, wi, wt, wb, rhs_buf, rhs_B, c0, ln, nk=KC, last_stop=True):
            w_check(wi)
            w3 = wt[:].rearrange("p (k c) -> p k c", c=128)
            for k in range(nk):
                mm(ps[:, 0:ln], w3[:, k, :], rhs_buf[:, k, c0:c0 + ln], k == 0, (k == nk - 1) and last_stop,
                   wb.r() + rhs_B.r(c0, c0 + ln), pb)

        def rmsnorm(hi, gname, phase_idx):
            t0, npr, has_s, sc0, nt = _half_geom(hi)
            for (c0, ln) in _blocks0(hi):
                ps, pb = PS.next()
                for k in range(KC):
                    sq, sqb = TB.next()
                    if k % 3 != 2:
                        S.op("act", lambda e: e.activation(out=sq[:, 0:ln], in_=xT[:, k, c0:c0 + ln], func=AF.Square),
                             reads=BxT.r(c0, c0 + ln), writes=sqb.r())
                    else:
                        S.op("dve", lambda e: e.tensor_tensor(out=sq[:, 0:ln], in0=xT[:, k, c0:c0 + ln], in1=xT[:, k, c0:c0 + ln], op=ALU.mult),
                             reads=BxT.r(c0, c0 + ln), writes=sqb.r())
                    mm(ps[:, 0:ln], ones[:, 0, :], sq[:, 0:ln], k == 0, k == KC - 1, sqb.r() + Bones.r(), pb)
                sd, sdb = TF.next()
                S.op("act", lambda e: e.activation(out=sd[:, 0:ln], in_=ps[:, 0:ln], func=AF.Ln, bias=RMS_EPS, scale=1.0),
                     reads=pb.r(), writes=sdb.r())
                rs, rsb = TF.next()
                S.op("act", lambda e: e.activation(out=rs[:, 0:ln], in_=sd[:, 0:ln], func=AF.Exp, scale=-0.5), reads=sdb.r(), writes=rsb.r())
                for k in range(KC):
                    S.op("dve", lambda e: e.scalar_tensor_tensor(out=xn[:, k, c0:c0 + ln], in0=xT[:, k, c0:c0 + ln],
                                                                 scalar=P(gname, k), in1=rs[:, 0:ln],
                                                                 op0=ALU.mult, op1=ALU.mult),
                         reads=BxT.r(c0, c0 + ln) + rsb.r() + Bpar.r(), writes=Bxn.r(c0, c0 + ln))
            if has_s:
                for k in range(KC):
                    v = xn[:, k, sc0:nt].rearrange("p (s c) -> p s c", c=SW)[:, :, 0:H]
                    S.op("dve", lambda e: e.memset(v, 0.0), writes=Bxn.r(sc0, nt))
            if phase_idx is not None:
                if hi == 0:
                    S.op("dve", lambda e: e.tensor_copy(out=cxn[:, phase_idx, :, :], in_=xn[:, :, nt - H:nt]),
                         reads=Bxn.r(nt - H, nt), writes=Bcxn.r())
                else:
                    S.op("dve", lambda e: e.tensor_copy(out=xn[:, :, 0:H], in_=cxn[:, phase_idx, :, :]),
                         reads=Bcxn.r(), writes=Bxn.r(0, H))
            elif hi == 0 or True:
                S.op("dve", lambda e: e.memset(xn[:, :, 0:H], 0.0), writes=Bxn.r(0, H))

        def out_proj(hi, wname, mix, Bmix):
            ws = [w_next((wname, o)) for o in range(8)]
            for (c0, ln) in _blocks0(hi):
                for o in range(8):
                    wi, wt, wb = ws[o]
                    ps, pb = PS.next()
                    proj_block(ps, pb, wi, wt, wb, mix, Bmix, c0, ln)
                    S.op("dve", lambda e: e.tensor_tensor(out=xT[:, o, c0:c0 + ln], in0=xT[:, o, c0:c0 + ln],
                                                          in1=ps[:, 0:ln], op=ALU.add),
                         reads=pb.r() + BxT.r(c0, c0 + ln), writes=BxT.r(c0, c0 + ln))

        def transpose_out(src_aps, nrows, dsts):
            ps, pb = PS.next()
            n = len(src_aps)
            for i, (ap, rr) in enumerate(src_aps):
                S.op("pe", lambda e: e.transpose(ps[0:nrows, 128 * i:128 * i + 128], ap, identf[:]),
                     reads=rr + Bid.r(), writes=pb.r())
            st, stb = TF.next()
            S.op("act", lambda e: e.activation(out=st[0:nrows, 0:128 * n], in_=ps[0:nrows, 0:128 * n], func=AF.Copy),
                 reads=pb.r(), writes=stb.r())
            for (dap, r0, r1) in dsts:
                S.dma("sp", dap, st[r0:r1, 0:128 * n], reads=stb.r())

        def load_x(hi):
            t0, npr, has_s, sc0, nt = _half_geom(hi)
            S.op("dve", lambda e: e.memset(xT[:, :, 0:H], 0.0), writes=BxT.r(0, H))
            if has_s:
                for k in range(KC):
                    v = xT[:, k, sc0:nt].rearrange("p (s c) -> p s c", c=SW)[:, :, 0:H]
                    S.op("dve", lambda e: e.memset(v, 0.0), writes=BxT.r(sc0, nt))
            ntile = npr // 128 + (1 if has_s else 0)
            for ti in range(ntile):
                xi, xib = TF.next()
                xi2, xib2 = TF.next()
                is_s = ti == npr // 128
                src = xs_d if is_s else xp_d[t0 + 128 * ti:t0 + 128 * ti + 128, :]
                S.dma("sp", xi[:], src[:, 0:512], writes=xib.r())
                S.dma("sp", xi2[:], src[:, 512:1024], writes=xib2.r())
                for half, (xt_, xb_) in enumerate(((xi, xib), (xi2, xib2))):
                    ps, pb = PS.next()
                    for q in range(4):
                        S.op("pe", lambda e: e.transpose(ps[:, 128 * q:128 * q + 128], xt_[:, 128 * q:128 * q + 128], identf[:]),
                             reads=xb_.r() + Bid.r(), writes=pb.r())
                    eng = "act" if half == 0 else "dve"
                    if not is_s:
                        c0 = H + 128 * ti
                        dst = xT[:, 4 * half:4 * half + 4, c0:c0 + 128]
                        src_ps = ps[:, :].rearrange("p (k c) -> p k c", c=128)
                        if eng == "act":
                            S.op("act", lambda e: e.activation(out=dst, in_=src_ps, func=AF.Copy), reads=pb.r(), writes=BxT.r(c0, c0 + 128))
                        else:
                            S.op("dve", lambda e: e.tensor_copy(out=dst, in_=src_ps), reads=pb.r(), writes=BxT.r(c0, c0 + 128))
                    else:
                        for q in range(4):
                            k = 4 * half + q
                            dst = xT[:, k, sc0:nt].rearrange("p (s c) -> p s c", c=SW)[:, :, H:SW]
                            src_ps = ps[:, 128 * q:128 * q + 128].rearrange("p (s c) -> p s c", c=TS)
                            if eng == "act":
                                S.op("act", lambda e: e.activation(out=dst, in_=src_ps, func=AF.Copy), reads=pb.r(), writes=BxT.r(sc0, nt))
                            else:
                                S.op("dve", lambda e: e.tensor_copy(out=dst, in_=src_ps), reads=pb.r(), writes=BxT.r(sc0, nt))

        def final_out(hi):
            t0, npr, has_s, sc0, nt = _half_geom(hi)
            ntile = npr // 128 + (1 if has_s else 0)
            for ti in range(ntile):
                is_s = ti == npr // 128
                if not is_s:
                    c0, c1 = H + 128 * ti, H + 128 * ti + 128

                    def cols(t3, k):
                        return t3[:, k, c0:c1]
                else:
                    c0, c1 = sc0, nt

                    def cols(t3, k):
                        return t3[:, k, sc0:nt].rearrange("p (s c) -> p s c", c=SW)[:, :, H:SW]
                ps, pb = PS.next()
                for k in range(KC):
                    sq, sqb = TB.next()
                    sqv = sq[:, 0:128] if not is_s else sq[:, 0:128].rearrange("p (s c) -> p s c", c=TS)
                    S.op("act", lambda e: e.activation(out=sqv, in_=cols(xT, k), func=AF.Square),
                         reads=BxT.r(c0, c1), writes=sqb.r())
                    mm(ps[:, 0:128], ones[:, 0, :], sq[:, 0:128], k == 0, k == KC - 1, sqb.r() + Bones.r(), pb)
                sd, sdb = TF.next()
                S.op("act", lambda e: e.activation(out=sd[:, 0:128], in_=ps[:, 0:128], func=AF.Ln, bias=RMS_EPS, scale=1.0),
                     reads=pb.r(), writes=sdb.r())
                rs, rsb = TF.next()
                S.op("act", lambda e: e.activation(out=rs[:, 0:128], in_=sd[:, 0:128], func=AF.Exp, scale=-0.5), reads=sdb.r(), writes=rsb.r())
                rsv = rs[:, 0:128] if not is_s else rs[:, 0:128].rearrange("p (s c) -> p s c", c=TS)
                ya, yab = TF.next()
                yb_, ybb = TF.next()
                for k in range(KC):
                    yt = ya if k < 4 else yb_
                    ytb = yab if k < 4 else ybb
                    o = yt[:, 128 * (k % 4):128 * (k % 4) + 128]
                    if is_s:
                        o = o.rearrange("p (s c) -> p s c", c=TS)
                    S.op("dve", lambda e: e.scalar_tensor_tensor(out=o, in0=cols(xT, k), scalar=P("nfin", k), in1=rsv,
                                                                 op0=ALU.mult, op1=ALU.mult),
                         reads=BxT.r(c0, c1) + rsb.r() + Bpar.r(), writes=ytb.r())
                dst_rows = ys_d if is_s else yp_d[t0 + 128 * ti:t0 + 128 * ti + 128, :]
                for half, (yt, ytb) in enumerate(((ya, yab), (yb_, ybb))):
                    ps2, pb2 = PS.next()
                    for q in range(4):
                        S.op("pe", lambda e: e.transpose(ps2[:, 128 * q:128 * q + 128], yt[:, 128 * q:128 * q + 128], identf[:]),
                             reads=ytb.r() + Bid.r(), writes=pb2.r())
                    st, stb = TF.next()
                    if half == 0:
                        S.op("act", lambda e: e.activation(out=st[:, :], in_=ps2[:, :], func=AF.Copy), reads=pb2.r(), writes=stb.r())
                    else:
                        S.op("dve", lambda e: e.tensor_copy(out=st[:, :], in_=ps2[:, :]), reads=pb2.r(), writes=stb.r())
                    S.dma("sp", dst_rows[:, 512 * half:512 * half + 512], st[:, :], reads=stb.r())

        SH = {}

        def shget(ph, key, make):
            if SH.get("on"):
                if key not in SH:
                    SH[key] = make(SH["ph"])
                return SH[key]
            return make(ph)

        def mk_buf(p, name, shape, dt, ncols=None):
            t = p.enter_context(nc.sbuf_tensor(name, list(shape), dt))
            return t, Buf(t, name, ncols)

        def ffn(hi, l, tail):
            t0, npr, has_s, sc0, nt = _half_geom(hi)
            b3 = _blocks3(hi)
            ps_mode(8)
            with ExitStack() as ph_:
                ph = SH["ph"] if SH.get("on") else ph_
                jmax = max(g[1] for g in GROUPS)
                assert jmax == KC
                tag = "" if SH.get("on") else f"_{l}"
                act, Bact = shget(ph, ("mixact", hi), lambda p: mk_buf(p, f"mixact_{hi}{tag}", [128, jmax, nt], BF16, nt))
                FLT = shget(ph, ("LT", hi), lambda p: Pool(p, nc, f"lt{hi}_", 15, (128, 512), F32)) if SH.get("on") else None
                if has_s:
                    stt, Bstt = shget(ph, ("stt", hi), lambda p: mk_buf(p, f"stt_{hi}{tag}", [32, 2, jmax * 128], BF16))
                    zt, Bzt = shget(ph, ("zt", hi), lambda p: mk_buf(p, f"zt_{hi}{tag}", [128, 2, jmax, 34], F32))
                for (j0, jn) in GROUPS:
                    if has_s:
                        S.dma("pool", stt[:, 0, 0:jn * 128], sffn_d[l][:, 128 * j0:128 * (j0 + jn)], writes=Bstt.r())
                        S.dma("pool", stt[:, 1, 0:jn * 128], sffn_d[l][:, FF + 128 * j0:FF + 128 * (j0 + jn)], writes=Bstt.r())
                    for j in range(j0, j0 + jn):
                        jl = j - j0
                        zc = {}
                        for which, wname in ((0, "upg"), (1, "upu")):
                            wi, wt, wb = w_next((wname, l, j))
                            fch = j + NJ * which
                            o_w, _ = PAR[f"fcw{l}"]
                            o_b, _ = PAR[f"fcb{l}"]
                            wcol = lambda tap: par[:, o_w + 3 * fch + tap:o_w + 3 * fch + tap + 1]
                            bcol = par[:, o_b + fch:o_b + fch + 1]
                            zc[which] = []
                            for bi, (c0, ln) in enumerate(b3):
                                last = bi == len(b3) - 1
                                ps, pb = PS.next()
                                inj = has_s and last
                                proj_block(ps, pb, wi, wt, wb, xn, Bxn, c0, ln, last_stop=not inj)
                                if inj:
                                    mm(ps[:, 0:ln], stt[:, which, 128 * jl:128 * jl + 128], sel2[:, 0:ln], False, True,
                                       Bstt.r() + Bsel.r(), pb)
                                acc, accb = TF.next()
                                lo = ln - H
                                S.op("act", lambda e: e.activation(out=acc[:, 0:lo], in_=ps[:, H:ln], func=AF.Identity,
                                                                   scale=wcol(2), bias=bcol),
                                     reads=pb.r() + Bpar.r(), writes=accb.r())
                                if which == 0 and FLT is not None:
                                    tb_, tbb_ = FLT.next()
                                    S.op("act", lambda e: e.activation(out=tb_[:, 0:lo], in_=ps[:, H - 1:ln - 1], func=AF.Copy, scale=wcol(1)),
                                         reads=pb.r() + Bpar.r(), writes=tbb_.r())
                                    S.op("dve", lambda e: e.scalar_tensor_tensor(out=acc[:, 0:lo], in0=ps[:, H - 2:ln - 2], scalar=wcol(0),
                                                                                 in1=acc[:, 0:lo], op0=ALU.mult, op1=ALU.add),
                                         reads=pb.r() + Bpar.r() + accb.r(), writes=accb.r())
                                    S.op("pool", lambda e: e.tensor_tensor(out=acc[:, 0:lo], in0=acc[:, 0:lo], in1=tb_[:, 0:lo], op=ALU.add),
                                         reads=accb.r() + tbb_.r(), writes=accb.r(), dur=150.0 + 2.2 * lo)
                                else:
                                    S.op("dve", lambda e: e.scalar_tensor_tensor(out=acc[:, 0:lo], in0=ps[:, H - 1:ln - 1], scalar=wcol(1),
                                                                                 in1=acc[:, 0:lo], op0=ALU.mult, op1=ALU.add),
                                         reads=pb.r() + Bpar.r() + accb.r(), writes=accb.r())
                                    S.op("dve", lambda e: e.scalar_tensor_tensor(out=acc[:, 0:lo], in0=ps[:, H - 2:ln - 2], scalar=wcol(0),
                                                                                 in1=acc[:, 0:lo], op0=ALU.mult, op1=ALU.add),
                                         reads=pb.r() + Bpar.r() + accb.r(), writes=accb.r())
                                if has_s and last:
                                    pl = sc0 - c0
                                    S.op("act", lambda e: e.activation(out=zt[:, which, jl, 0:2], in_=ps[:, pl - 2:pl], func=AF.Copy),
                                         reads=pb.r(), writes=Bzt.r())
                                    sv = ps[:, pl:ln].rearrange("p (s c) -> p s c", c=SW)[:, :, H + 6:H + 8]
                                    dv = zt[:, which, jl, 2:34].rearrange("p (s c) -> p s c", c=2)
                                    S.op("act", lambda e: e.activation(out=dv, in_=sv, func=AF.Copy), reads=pb.r(), writes=Bzt.r())
                                zc[which].append((acc, accb, c0 + H, lo))
                        for (ag, agb, oc, lo), (au, aub, _, _) in zip(zc[0], zc[1]):
                            S.op("act", lambda e: e.activation(out=ag[:, 0:lo], in_=ag[:, 0:lo], func=AF.Gelu_apprx_tanh),
                                 reads=agb.r(), writes=agb.r())
                            S.op("dve", lambda e: e.tensor_tensor(out=act[:, jl, oc:oc + lo], in0=ag[:, 0:lo], in1=au[:, 0:lo], op=ALU.mult),
                                 reads=agb.r() + aub.r(), writes=Bact.r(oc, oc + lo))
                    if has_s:
                        for which in range(2):
                            for q0 in range(0, jn, 4):
                                qn = min(4, jn - q0)
                                f0 = FF * which + 128 * (j0 + q0)
                                srcs = [(zt[:, which, q0 + i, :], Bzt.r()) for i in range(qn)]
                                transpose_out(srcs, 34, [(o_pffn[l][:, f0:f0 + 128 * qn], 0, 2),
                                                         (o_sffn[l][:, f0:f0 + 128 * qn], 2, 34)])
                    wds = [w_next(("wd", l, j)) for j in range(j0, j0 + jn)]
                    for bi0, (c0, ln) in enumerate(_blocks0(hi)):
                        if bi0 == 0:
                            NOPEN = 4
                            pss = [PS.next() for o in range(NOPEN)]
                            for o in range(NOPEN):
                                ps, pb = pss[o]
                                for jl in range(jn - 1):
                                    wi, wt, wb = wds[jl]
                                    w_check(wi)
                                    mm(ps[:, 0:ln], wt[:, 128 * o:128 * o + 128], act[:, jl, c0:c0 + ln], jl == 0, False,
                                       wb.r() + Bact.r(c0, c0 + ln), pb)
                            for o in range(NOPEN):
                                ps, pb = pss[o]
                                wi, wt, wb = wds[jn - 1]
                                mm(ps[:, 0:ln], wt[:, 128 * o:128 * o + 128], act[:, jn - 1, c0:c0 + ln], False, True,
                                   wb.r() + Bact.r(c0, c0 + ln), pb)
                                S.op("dve", lambda e: e.tensor_tensor(out=xT[:, o, c0:c0 + ln], in0=xT[:, o, c0:c0 + ln],
                                                                      in1=ps[:, 0:ln], op=ALU.add),
                                     reads=pb.r() + BxT.r(c0, c0 + ln), writes=BxT.r(c0, c0 + ln))
                        for o in range(NOPEN if bi0 == 0 else 0, 8):
                            ps, pb = PS.next()
                            for jl in range(jn):
                                wi, wt, wb = wds[jl]
                                w_check(wi)
                                mm(ps[:, 0:ln], wt[:, 128 * o:128 * o + 128], act[:, jl, c0:c0 + ln], jl == 0, jl == jn - 1,
                                   wb.r() + Bact.r(c0, c0 + ln), pb)
                            S.op("dve", lambda e: e.tensor_tensor(out=xT[:, o, c0:c0 + ln], in0=xT[:, o, c0:c0 + ln],
                                                                  in1=ps[:, 0:ln], op=ALU.add),
                                 reads=pb.r() + BxT.r(c0, c0 + ln), writes=BxT.r(c0, c0 + ln))
                tail()
                if not SH.get("on"):
                    S.barrier()

        def lru_mixer(hi, tail):
            t0, npr, has_s, sc0, nt = _half_geom(hi)
            b3 = _blocks3(hi)
            ps_mode(8)
            with ExitStack() as ph_:
                ph = SH["ph"] if SH.get("on") else ph_
                LT = shget(ph, ("LT", hi), lambda p: Pool(p, nc, f"lt{hi}_", 15, (128, 512), F32))
                mix, Bmix = shget(ph, ("mixact", hi), lambda p: mk_buf(p, f"mixact_{hi}", [128, KC, nt], BF16, nt))
                wax = ph.enter_context(nc.sbuf_tensor(f"wax_{hi}", [128, 2, 8, 128], BF16))
                Bwax = Buf(wax, "wax")
                S.dma("pool", wax[:, 0, :, :], w_lru_a.rearrange("n c d -> c n d"), writes=Bwax.r())
                S.dma("pool", wax[:, 1, :, :], w_lru_x.rearrange("n c d -> c n d"), writes=Bwax.r())
                if has_s:
                    st3 = ph.enter_context(nc.sbuf_tensor(f"st3_{hi}", [48, D], BF16))
                    Bst3 = Buf(st3, "st3")
                    S.dma("pool", st3[:], slc_d, writes=Bst3.r())
                    h0in = ph.enter_context(nc.sbuf_tensor(f"h0in_{hi}", [16, D], F32))
                    Bh0in = Buf(h0in, "h0in")
                    S.dma("sp", h0in[:], slh_d, writes=Bh0in.r())
                    h0T = ph.enter_context(nc.sbuf_tensor(f"h0T_{hi}", [128, 8, 16], F32))
                    Bh0T = Buf(h0T, "h0T")
                    ps, pb = PS.next()
                    for n in range(8):
                        S.op("pe", lambda e: e.transpose(ps[:, 16 * n:16 * n + 16], h0in[:, 128 * n:128 * n + 128], identf[0:16, 0:16]),
                             reads=Bh0in.r() + Bid.r(), writes=pb.r())
                    S.op("act", lambda e: e.activation(out=h0T[:].rearrange("p n s -> p (n s)"), in_=ps[:, 0:128], func=AF.Copy),
                         reads=pb.r(), writes=Bh0T.r())
                    rt = ph.enter_context(nc.sbuf_tensor(f"rt_{hi}", [128, 8, 51], F32))
                    Brt = Buf(rt, "rt")
                    ht = ph.enter_context(nc.sbuf_tensor(f"ht_{hi}", [128, 8, 17], F32))
                    Bht = Buf(ht, "ht")
                hprev = ph.enter_context(nc.sbuf_tensor(f"hprev_{hi}", [128, 8, 4], F32))
                Bhp = Buf(hprev, "hprev")
                for n in range(8):
                    wgi, wgt, wgb = w_next(("gate", n))
                    wri, wrt, wrb = w_next(("rec", n))
                    o_w, _ = PAR["ccw"]
                    wcol = lambda tap: par[:, o_w + 4 * n + tap:o_w + 4 * n + tap + 1]
                    units = []
                    for bi, (c0, ln) in enumerate(b3):
                        last = bi == len(b3) - 1
                        lo = ln - H
                        oc = c0 + H
                        psg, pbg = PS.next()
                        proj_block(psg, pbg, wgi, wgt, wgb, xn, Bxn, c0, ln)
                        S.op("act", lambda e: e.activation(out=mix[:, n, oc:oc + lo], in_=psg[:, H:ln], func=AF.Gelu_apprx_tanh),
                             reads=pbg.r(), writes=Bmix.r(oc, oc + lo))
                        psr, pbr = PS.next()
                        inj = has_s and last
                        proj_block(psr, pbr, wri, wrt, wrb, xn, Bxn, c0, ln, last_stop=not inj)
                        if inj:
                            mm(psr[:, 0:ln], st3[:, 128 * n:128 * n + 128], sel3[:, 0:ln], False, True, Bst3.r() + Bsel.r(), pbr)
                        xc, xcb = LT.next()
                        S.op("dve", lambda e: e.tensor_scalar(out=xc[:, 0:lo], in0=psr[:, H:ln], scalar1=wcol(3), scalar2=P("ccb", n),
                                                              op0=ALU.mult, op1=ALU.add),
                             reads=pbr.r() + Bpar.r(), writes=xcb.r())
                        for tap in (2, 1, 0):
                            sh = 3 - tap
                            S.op("dve", lambda e: e.scalar_tensor_tensor(out=xc[:, 0:lo], in0=psr[:, H - sh:ln - sh], scalar=wcol(tap),
                                                                         in1=xc[:, 0:lo], op0=ALU.mult, op1=ALU.add),
                                 reads=pbr.r() + Bpar.r() + xcb.r(), writes=xcb.r())
                        if has_s and last:
                            pl = sc0 - c0
                            S.op("act", lambda e: e.activation(out=rt[:, n, 0:3], in_=psr[:, pl - 3:pl], func=AF.Copy),
                                 reads=pbr.r(), writes=Brt.r())
                            sv = psr[:, pl:ln].rearrange("p (s c) -> p s c", c=SW)[:, :, H + 5:H + 8]
                            dv = rt[:, n, 3:51].rearrange("p (s c) -> p s c", c=3)
                            S.op("act", lambda e: e.activation(out=dv, in_=sv, func=AF.Copy), reads=pbr.r(), writes=Brt.r())
                        units.append((bi, c0, ln, last, lo, oc, xc, xcb))
                    for (bi, c0, ln, last, lo, oc, xc, xcb) in units:
                        xb, xbb = TB.next()
                        S.op("dve", lambda e: e.tensor_copy(out=xb[:, 0:lo], in_=xc[:, 0:lo]), reads=xcb.r(), writes=xbb.r())
                        psa, pba = PS.next()
                        mm(psa[:, 0:lo], wax[:, 0, n, :], xb[:, 0:lo], True, True, Bwax.r() + xbb.r(), pba)
                        psx, pbx = PS.next()
                        mm(psx[:, 0:lo], wax[:, 1, n, :], xb[:, 0:lo], True, True, Bwax.r() + xbb.r(), pbx)
                        A, Ab = LT.next()
                        I, Ib = TF.next()
                        S2, S2b = TF.next()
                        S.op("act", lambda e: e.activation(out=A[:, 0:lo], in_=psa[:, 0:lo], func=AF.Tanh, bias=hbias[:, 0, n:n + 1], scale=0.5),
                             reads=pba.r() + Blruc.r(), writes=Ab.r())
                        S.op("act", lambda e: e.activation(out=I[:, 0:lo], in_=psx[:, 0:lo], func=AF.Tanh, bias=hbias[:, 1, n:n + 1], scale=0.5),
                             reads=pbx.r() + Blruc.r(), writes=Ib.r())
                        S.op("act", lambda e: e.activation(out=S2[:, 0:lo], in_=A[:, 0:lo], func=AF.Exp, scale=lruc[:, n, 1:2], bias=lruc[:, n, 1:2]),
                             reads=Ab.r() + Blruc.r(), writes=S2b.r())
                        S.op("act", lambda e: e.activation(out=A[:, 0:lo], in_=A[:, 0:lo], func=AF.Exp, scale=lruc[:, n, 0:1], bias=lruc[:, n, 0:1]),
                             reads=Ab.r() + Blruc.r(), writes=Ab.r())
                        S.op("dve", lambda e: e.scalar_tensor_tensor(out=I[:, 0:lo], in0=I[:, 0:lo], scalar=1.0, in1=xc[:, 0:lo], op0=ALU.add, op1=ALU.mult),
                             reads=Ib.r() + xcb.r(), writes=Ib.r())
                        S.op("act", lambda e: e.activation(out=S2[:, 0:lo], in_=S2[:, 0:lo], func=AF.Sqrt, scale=-0.25, bias=0.25),
                             reads=S2b.r(), writes=S2b.r())
                        S.op("dve", lambda e: e.tensor_tensor(out=I[:, 0:lo], in0=I[:, 0:lo], in1=S2[:, 0:lo], op=ALU.mult),
                             reads=Ib.r() + S2b.r(), writes=Ib.r())
                        if has_s and last:
                            pl = sc0 - oc
                            av = A[:, pl:lo].rearrange("p (s c) -> p s c", c=SW)[:, :, 0:H]
                            bv = I[:, pl:lo].rearrange("p (s c) -> p s c", c=SW)
                            S.op("dve", lambda e: e.memset(av, 0.0), writes=Ab.r())
                            S.op("dve", lambda e: e.memset(bv[:, :, 0:H - 1], 0.0), writes=Ib.r())
                            S.op("dve", lambda e: e.tensor_copy(out=bv[:, :, H - 1:H], in_=h0T[:, n, :].unsqueeze(2)),
                                 reads=Bh0T.r(), writes=Ib.r())
                        if bi == 0:
                            init = 0.0 if hi == 0 else ch[:, n:n + 1]
                            ird = [] if hi == 0 else Bch.r()
                        else:
                            init = hprev[:, n, bi - 1:bi]
                            ird = Bhp.r()
                        S.op("dve", lambda e: e.tensor_tensor_scan(out=xc[:, 0:lo], data0=A[:, 0:lo], data1=I[:, 0:lo], initial=init,
                                                                   op0=ALU.mult, op1=ALU.add),
                             reads=Ab.r() + Ib.r() + ird + xcb.r(), writes=xcb.r(), dur=200.0 + 2.3 * lo)
                        S.op("act", lambda e: e.activation(out=hprev[:, n, bi:bi + 1], in_=xc[:, lo - 1:lo], func=AF.Copy),
                             reads=xcb.r(), writes=Bhp.r())
                        if last:
                            if hi == 0:
                                S.op("act", lambda e: e.activation(out=ch[:, n:n + 1], in_=xc[:, lo - 1:lo], func=AF.Copy),
                                     reads=xcb.r(), writes=Bch.r())
                            else:
                                pl = sc0 - oc
                                S.op("act", lambda e: e.activation(out=ht[:, n, 0:1], in_=xc[:, pl - 1:pl], func=AF.Copy),
                                     reads=xcb.r(), writes=Bht.r())
                                sv = xc[:, pl:lo].rearrange("p (s c) -> p s c", c=SW)[:, :, SW - 1:SW]
                                S.op("act", lambda e: e.activation(out=ht[:, n, 1:17].unsqueeze(2), in_=sv, func=AF.Copy),
                                     reads=xcb.r(), writes=Bht.r())
                        S.op("dve", lambda e: e.tensor_tensor(out=mix[:, n, oc:oc + lo], in0=xc[:, 0:lo], in1=mix[:, n, oc:oc + lo], op=ALU.mult),
                             reads=xcb.r() + Bmix.r(oc, oc + lo), writes=Bmix.r(oc, oc + lo))
                if has_s:
                    for q0 in (0, 4):
                        transpose_out([(rt[:, q0 + i, :], Brt.r()) for i in range(4)], 51,
                                      [(o_plc[:, 512 * (q0 // 4):512 * (q0 // 4) + 512], 0, 3),
                                       (o_slc[:, 512 * (q0 // 4):512 * (q0 // 4) + 512], 3, 51)])
                        transpose_out([(ht[:, q0 + i, :], Bht.r()) for i in range(4)], 17,
                                      [(o_plh[:, 512 * (q0 // 4):512 * (q0 // 4) + 512], 0, 1),
                                       (o_slh[:, 512 * (q0 // 4):512 * (q0 // 4) + 512], 1, 17)])
                out_proj(hi, "wo1", mix, Bmix)
                tail()
                if not SH.get("on"):
                    S.barrier()

        def gn_block_3d(ps_o, pb_o, width, cn, qd_ap, dst_fn, c_lo, c_hi, h, Bmix, BSG):
            o, ob = TF.next()
            if qd_ap is not None:
                S.op("dve", lambda e: e.tensor_tensor(out=o[:, 0:width].rearrange("p (a b) -> p a b", b=128),
                                                      in0=ps_o[:, 0:width].rearrange("p (a b) -> p a b", b=128), in1=qd_ap, op=ALU.mult),
                     reads=pb_o.r() + Bcon.r(), writes=ob.r())
            else:
                S.op("act", lambda e: e.activation(out=o[:, 0:width], in_=ps_o[:, 0:width], func=AF.Copy), reads=pb_o.r(), writes=ob.r())
            o16, o16b = TB.next()
            S.op("act", lambda e: e.activation(out=o16[:, 0:width], in_=o[:, 0:width], func=AF.Copy), reads=ob.r(), writes=o16b.r())
            q16, q16b = TB.next()
            S.op("act", lambda e: e.activation(out=q16[:, 0:width], in_=o[:, 0:width], func=AF.Square), reads=ob.r(), writes=q16b.r())
            psm, pbm = PS.next()
            mm(psm[:, 0:width], ones[:, 2, :], o16[:, 0:width], True, True, o16b.r() + Bones.r(), pbm)
            psq, pbq = PS.next()
            mm(psq[:, 0:width], ones[:, 2, :], q16[:, 0:width], True, True, q16b.r() + Bones.r(), pbq)
            mean, meanb = TF.next()
            S.op("act", lambda e: e.activation(out=mean[:, 0:width], in_=psm[:, 0:width], func=AF.Copy), reads=pbm.r(), writes=meanb.r())
            var, varb = TF.next()
            S.op("dve", lambda e: e.tensor_tensor(out=var[:, 0:width], in0=mean[:, 0:width], in1=mean[:, 0:width], op=ALU.mult),
                 reads=meanb.r(), writes=varb.r())
            S.op("dve", lambda e: e.tensor_tensor(out=var[:, 0:width], in0=psq[:, 0:width], in1=var[:, 0:width], op=ALU.subtract),
                 reads=pbq.r() + varb.r(), writes=varb.r())
            S.op("dve", lambda e: e.tensor_scalar_max(out=var[:, 0:width], in0=var[:, 0:width], scalar1=0.0), reads=varb.r(), writes=varb.r())
            S.op("act", lambda e: e.activation(out=var[:, 0:width], in_=var[:, 0:width], func=AF.Ln, bias=GN_EPS, scale=1.0),
                 reads=varb.r(), writes=varb.r())
            rs, rsb = TF.next()
            S.op("act", lambda e: e.activation(out=rs[:, 0:width], in_=var[:, 0:width], func=AF.Exp, scale=-0.5), reads=varb.r(), writes=rsb.r())
            S.op("dve", lambda e: e.tensor_tensor(out=o[:, 0:width], in0=o[:, 0:width], in1=mean[:, 0:width], op=ALU.subtract),
                 reads=ob.r() + meanb.r(), writes=ob.r())
            S.op("dve", lambda e: e.tensor_tensor(out=o[:, 0:width], in0=o[:, 0:width], in1=rs[:, 0:width], op=ALU.mult),
                 reads=ob.r() + rsb.r(), writes=ob.r())
            dst, sgv, ov = dst_fn(o)
            S.op("dve", lambda e: e.scalar_tensor_tensor(out=dst, in0=ov, scalar=P("gng", h), in1=sgv, op0=ALU.mult, op1=ALU.mult),
                 reads=ob.r() + BSG.r(c_lo, c_hi) + Bpar.r(), writes=Bmix.r(c_lo, c_hi))


        def even_mixer(hi, tail):
            t0, npr, has_s, sc0, nt = _half_geom(hi)
            b0 = _blocks0(hi)
            nch = npr // 128
            UW = 30 + npr + (NSQ * 38 if has_s else 0)
            us0 = 30 + npr
            ps_mode(8)
            with ExitStack() as ph:
                mix = ph.enter_context(nc.sbuf_tensor(f"mix_{hi}", [128, KC, nt], BF16))
                Bmix = Buf(mix, "mix", nt)
                S.op("dve", lambda e: e.memset(mix[:, :, :], 0.0), writes=Bmix.r())
                with ExitStack() as pa:
                    uT = pa.enter_context(nc.sbuf_tensor(f"uT_{hi}", [128, 4, UW], BF16))
                    BuTc = [Buf(uT, f"uT{c_}", UW) for c_ in range(4)]

                    class _AllU:
                        def r(self, c0=None, c1=None):
                            out = []
                            for b_ in BuTc:
                                out += b_.r(c0, c1)
                            return out
                    BuT = _AllU()
                    diag = pa.enter_context(nc.sbuf_tensor(f"diag_{hi}", [128, 4, 31, 128], BF16))
                    Bdiagc = [Buf(diag, f"diag{c_}") for c_ in range(4)]
                    ut = pa.enter_context(nc.sbuf_tensor(f"ut_{hi}", [128, 4, 30], F32))
                    But = Buf(ut, "ut")
                    o_w, _ = PAR["caw"]
                    for c in range(4):
                        for k in range(31):
                            S.op("dve", lambda e: e.tensor_scalar(out=diag[:, c, k, :], in0=identb[:], scalar1=par[:, o_w + 31 * c + k:o_w + 31 * c + k + 1],
                                                                  scalar2=None, op0=ALU.mult),
                                 reads=Bidb.r() + Bpar.r(), writes=Bdiagc[c].r())
                    if hi == 0:
                        S.op("dve", lambda e: e.memset(uT[:, :, 0:30], 0.0), writes=BuT.r(0, 30))
                    else:
                        S.op("dve", lambda e: e.tensor_copy(out=uT[:, :, 0:30], in_=cu[:, :, :]), reads=Bcu.r(), writes=BuT.r(0, 30))
                    if has_s:
                        us = pa.enter_context(nc.sbuf_tensor(f"us_{hi}", [128, 4, 128], F32))
                        Bus = Buf(us, "us")
                        for q in range(4):
                            si, sib = TF.next()
                            S.dma("sp", si[0:120, :], sca_d[120 * q:120 * q + 120, :], writes=sib.r())
                            ps, pb = PS.next()
                            for c in range(4):
                                S.op("pe", lambda e: e.transpose(ps[:, 128 * c:128 * c + 120], si[0:120, 128 * c:128 * c + 128], identf[0:120, 0:120]),
                                     reads=sib.r() + Bid.r(), writes=pb.r())
                            for c in range(4):
                                dst = uT[:, c, us0 + 38 * 4 * q:us0 + 38 * 4 * (q + 1)].rearrange("p (s c) -> p s c", c=38)[:, :, 0:30]
                                src = ps[:, 128 * c:128 * c + 120].rearrange("p (s c) -> p s c", c=30)
                                S.op("act", lambda e: e.activation(out=dst, in_=src, func=AF.Copy), reads=pb.r(), writes=BuTc[c].r(us0, UW))
                        S.dma("sp", o_sca.rearrange("(s r) f -> s r f", r=30)[:, 0:22, :],
                              sca_d.rearrange("(s r) f -> s r f", r=30)[:, 8:30, :])
                    for c in range(4):
                        wli, wlt, wlb = w_next(("alin", c))
                        wgi, wgt, wgb = w_next(("agate", c))
                        for bi, (c0, ln) in enumerate(b0):
                            psl, pbl = PS.next()
                            proj_block(psl, pbl, wli, wlt, wlb, xn, Bxn, c0, ln)
                            psg, pbg = PS.next()
                            proj_block(psg, pbg, wgi, wgt, wgb, xn, Bxn, c0, ln)
                            sg, sgb = TF.next()
                            S.op("act", lambda e: e.activation(out=sg[:, 0:ln], in_=psg[:, 0:ln], func=AF.Sigmoid),
                                 reads=pbg.r(), writes=sgb.r())
                            p1 = min(c0 + ln, sc0)
                            pn = p1 - c0
                            if pn > 0:
                                uc = 30 + (c0 - H)
                                S.op("dve", lambda e: e.tensor_tensor(out=uT[:, c, uc:uc + pn], in0=psl[:, 0:pn], in1=sg[:, 0:pn], op=ALU.mult),
                                     reads=pbl.r() + sgb.r(), writes=BuTc[c].r(uc, uc + pn))
                                if p1 == sc0:
                                    S.op("dve", lambda e: e.tensor_tensor(out=ut[:, c, :], in0=psl[:, pn - 30:pn], in1=sg[:, pn - 30:pn], op=ALU.mult),
                                         reads=pbl.r() + sgb.r(), writes=But.r())
                            if has_s and c0 + ln > sc0:
                                pl = sc0 - c0
                                lv = psl[:, pl:ln].rearrange("p (s c) -> p s c", c=SW)[:, :, H:SW]
                                gv = sg[:, pl:ln].rearrange("p (s c) -> p s c", c=SW)[:, :, H:SW]
                                dv = uT[:, c, us0:UW].rearrange("p (s c) -> p s c", c=38)[:, :, 30:38]
                                S.op("dve", lambda e: e.tensor_tensor(out=dv, in0=lv, in1=gv, op=ALU.mult),
                                     reads=pbl.r() + sgb.r(), writes=BuTc[c].r(us0, UW))
                                S.op("dve", lambda e: e.tensor_tensor(out=us[:, c, :].rearrange("p (s c) -> p s c", c=TS), in0=lv, in1=gv, op=ALU.mult),
                                     reads=pbl.r() + sgb.r(), writes=Bus.r())
                    if hi == 0:
                        S.op("act", lambda e: e.activation(out=cu[:, :, :], in_=uT[:, :, us0 - 30:us0], func=AF.Copy), reads=BuT.r(us0 - 30, us0), writes=Bcu.r())
                    else:
                        transpose_out([(ut[:, c, :], But.r()) for c in range(4)], 30, [(o_pca[:, :], 0, 30)])
                        ps, pb = PS.next()
                        for c in range(4):
                            S.op("pe", lambda e: e.transpose(ps[:, 128 * c:128 * c + 128], us[:, c, :], identf[:]),
                                 reads=Bus.r() + Bid.r(), writes=pb.r())
                        st, stb = TF.next()
                        S.op("act", lambda e: e.activation(out=st[:, :], in_=ps[:, :], func=AF.Copy), reads=pb.r(), writes=stb.r())
                        osv = o_sca.rearrange("(s r) f -> s r f", r=30)
                        for s in range(NSQ):
                            S.dma("sp", osv[s, 22:30, :], st[8 * s:8 * s + 8, :], reads=stb.r())
                    for bi, (c0, ln) in enumerate(b0):
                        segs = []
                        p1 = min(c0 + ln, sc0)
                        if p1 > c0:
                            segs.append(("p", c0, p1 - c0))
                        if has_s and c0 + ln > sc0:
                            segs.append(("s", sc0, NSQ * TS))
                        for kind, s0, sl in segs:
                            cvs = []
                            stat_in = []
                            for c in range(4):
                                ps, pb = PS.next()
                                for k in range(31):
                                    if kind == "p":
                                        ub = (s0 - H) + k
                                        rhs = uT[:, c, ub:ub + sl]
                                        urd = BuTc[c].r(ub, ub + sl)
                                    else:
                                        rhs = uT[:, c, us0:UW].rearrange("p (s c) -> p s c", c=38)[:, :, k:k + TS]
                                        urd = BuTc[c].r(us0, UW)
                                    mm(ps[:, 0:sl], diag[:, c, k, :], rhs, k == 0, k == 30, urd + Bdiagc[c].r(), pb)
                                cv, cvb = TF.next()
                                S.op("act", lambda e: e.activation(out=cv[:, 0:sl], in_=ps[:, 0:sl], func=AF.Identity, bias=P("cab", c), scale=1.0),
                                     reads=pb.r() + Bpar.r(), writes=cvb.r())
                                c16, c16b = TB.next()
                                S.op("act", lambda e: e.activation(out=c16[:, 0:sl], in_=cv[:, 0:sl], func=AF.Copy), reads=cvb.r(), writes=c16b.r())
                                q16, q16b = TB.next()
                                S.op("act", lambda e: e.activation(out=q16[:, 0:sl], in_=cv[:, 0:sl], func=AF.Square), reads=cvb.r(), writes=q16b.r())
                                stat_in.append((c16, c16b, q16, q16b))
                                cvs.append((cv, cvb))
                            psm, pbm = PS.next()
                            for c, (c16, c16b, q16, q16b) in enumerate(stat_in):
                                mm(psm[:, 0:sl], ones[:, 1, :], c16[:, 0:sl], c == 0, c == 3, c16b.r() + Bones.r(), pbm)
                            psq, pbq = PS.next()
                            for c, (c16, c16b, q16, q16b) in enumerate(stat_in):
                                mm(psq[:, 0:sl], ones[:, 1, :], q16[:, 0:sl], c == 0, c == 3, q16b.r() + Bones.r(), pbq)
                            mean, meanb = TF.next()
                            S.op("act", lambda e: e.activation(out=mean[:, 0:sl], in_=psm[:, 0:sl], func=AF.Copy), reads=pbm.r(), writes=meanb.r())
                            var, varb = TF.next()
                            S.op("dve", lambda e: e.tensor_tensor(out=var[:, 0:sl], in0=mean[:, 0:sl], in1=mean[:, 0:sl], op=ALU.mult),
                                 reads=meanb.r(), writes=varb.r())
                            S.op("dve", lambda e: e.tensor_tensor(out=var[:, 0:sl], in0=psq[:, 0:sl], in1=var[:, 0:sl], op=ALU.subtract),
                                 reads=pbq.r() + varb.r(), writes=varb.r())
                            S.op("dve", lambda e: e.tensor_scalar_max(out=var[:, 0:sl], in0=var[:, 0:sl], scalar1=0.0), reads=varb.r(), writes=varb.r())
                            S.op("act", lambda e: e.activation(out=var[:, 0:sl], in_=var[:, 0:sl], func=AF.Ln, bias=LN_EPS, scale=1.0),
                                 reads=varb.r(), writes=varb.r())
                            rs, rsb = TF.next()
                            S.op("act", lambda e: e.activation(out=rs[:, 0:sl], in_=var[:, 0:sl], func=AF.Exp, scale=-0.5), reads=varb.r(), writes=rsb.r())
                            for c in range(4):
                                cv, cvb = cvs[c]
                                S.op("dve", lambda e: e.tensor_tensor(out=cv[:, 0:sl], in0=cv[:, 0:sl], in1=mean[:, 0:sl], op=ALU.subtract),
                                     reads=cvb.r() + meanb.r(), writes=cvb.r())
                                S.op("dve", lambda e: e.tensor_tensor(out=cv[:, 0:sl], in0=cv[:, 0:sl], in1=rs[:, 0:sl], op=ALU.mult),
                                     reads=cvb.r() + rsb.r(), writes=cvb.r())
                                if kind == "p":
                                    dst = mix[:, c, s0:s0 + sl]
                                    src = cv[:, 0:sl]
                                    wr = Bmix.r(s0, s0 + sl)
                                else:
                                    dst = mix[:, c, sc0:nt].rearrange("p (s c) -> p s c", c=SW)[:, :, H:SW]
                                    src = cv[:, 0:sl].rearrange("p (s c) -> p s c", c=TS)
                                    wr = Bmix.r(sc0, nt)
                                S.op("act", lambda e: e.activation(out=dst, in_=src, func=AF.Silu, scale=P("lng", c), bias=P("lnb", c)),
                                     reads=cvb.r() + Bpar.r(), writes=wr)
                    S.fence(fsc[:, 0:1], Bfsc, keep=[Bmix])
                ps_mode(7)
                with ExitStack() as pbk:
                    rope = pbk.enter_context(nc.sbuf_tensor(f"rope_{hi}", [128, 2, nt], F32))
                    Brope = Buf(rope, "rope")
                    S.dma("sp", rope[:], rope_d[hi], writes=Brope.r())
                    QT = pbk.enter_context(nc.sbuf_tensor(f"QT_{hi}", [128, nt], BF16))
                    KT = pbk.enter_context(nc.sbuf_tensor(f"KT_{hi}", [128, nt], BF16))
                    VT = pbk.enter_context(nc.sbuf_tensor(f"VT_{hi}", [128, nt], BF16))
                    SG = pbk.enter_context(nc.sbuf_tensor(f"SG_{hi}", [128, nt], F32))
                    BQT, BKT, BVT, BSG = Buf(QT, "QT", nt), Buf(KT, "KT", nt), Buf(VT, "VT", nt), Buf(SG, "SG", nt)
                    ntl = nch + (1 if has_s else 0)
                    Ktok = pbk.enter_context(nc.sbuf_tensor(f"Ktok_{hi}", [128, ntl, 128], BF16))
                    Vdec = pbk.enter_context(nc.sbuf_tensor(f"Vdec_{hi}", [128, ntl, 128], BF16))
                    BKtok, BVdec = Buf(Ktok, "Ktok"), Buf(Vdec, "Vdec")
                    Sbf = pbk.enter_context(nc.sbuf_tensor(f"Sbf_{hi}", [128, nch + 1, 128], BF16))
                    BSbf = [Buf(Sbf, f"Sbf{i}") for i in range(nch + 1)]
                    Sf = pbk.enter_context(nc.sbuf_tensor(f"Sf_{hi}", [128, 128], F32))
                    BSf = Buf(Sf, "Sf")
                    if has_s:
                        S0f = pbk.enter_context(nc.sbuf_tensor(f"S0f_{hi}", [128, NSQ, 128], F32))
                        BS0f = Buf(S0f, "S0f")
                        S0b = pbk.enter_context(nc.sbuf_tensor(f"S0b_{hi}", [128, NSQ, 128], BF16))
                        BS0b = Buf(S0b, "S0b")
                        Vbd = pbk.enter_context(nc.sbuf_tensor(f"Vbd_{hi}", [128, NSQ, 128], BF16))
                        BVbd = Buf(Vbd, "Vbd")
                        Kds = pbk.enter_context(nc.sbuf_tensor(f"Kds_{hi}", [128, 128], BF16))
                        BKds = Buf(Kds, "Kds")
                        Qds = pbk.enter_context(nc.sbuf_tensor(f"Qds_{hi}", [128, 128], BF16))
                        BQds = Buf(Qds, "Qds")
                        cmpS = pbk.enter_context(nc.sbuf_tensor(f"cmpS_{hi}", [128, 3, 128], BF16))
                        BcmpS = Buf(cmpS, "cmpS")
                    dk_scale = 128.0 ** -0.5
                    for h in range(4):
                        ws = {nm: w_next((nm, h)) for nm in ("q", "k", "v", "g")}
                        if has_s:
                            S.dma("sp", S0f[:], sret_d[:, h, :, :].rearrange("s d v -> d s v"), writes=BS0f.r())
                            S.dma("pool", S0b[:], sret_d[:, h, :, :].rearrange("s d v -> d s v"), writes=BS0b.r())
                        for (c0, ln) in b0:
                            for (dstT, Bd, n1, sc) in ((QT, BQT, "q", 1.0), (KT, BKT, "k", dk_scale)):
                                ps1, pb1 = PS.next()
                                proj_block(ps1, pb1, *ws[n1], xn, Bxn, c0, ln)
                                qb, qbb = TB.next()
                                S.op("act", lambda e: e.activation(out=qb[:, 0:ln], in_=ps1[:, 0:ln], func=AF.Copy), reads=pb1.r(), writes=qbb.r())
                                ps2, pb2 = PS.next()
                                mm(ps2[:, 0:ln], permb[:], qb[:, 0:ln], True, True, Bperm.r() + qbb.r(), pb2)
                                t1, t1b = TF.next()
                                t2, t2b = TF.next()
                                S.op("dve", lambda e: e.scalar_tensor_tensor(out=t1[:, 0:ln], in0=ps1[:, 0:ln], scalar=sc, in1=rope[:, 0, c0:c0 + ln],
                                                                             op0=ALU.mult, op1=ALU.mult),
                                     reads=pb1.r() + Brope.r(), writes=t1b.r())
                                S.op("dve", lambda e: e.scalar_tensor_tensor(out=t2[:, 0:ln], in0=ps2[:, 0:ln], scalar=sc, in1=rope[:, 1, c0:c0 + ln],
                                                                             op0=ALU.mult, op1=ALU.mult),
                                     reads=pb2.r() + Brope.r(), writes=t2b.r())
                                S.op("dve", lambda e: e.tensor_tensor(out=dstT[:, c0:c0 + ln], in0=t1[:, 0:ln], in1=t2[:, 0:ln], op=ALU.add),
                                     reads=t1b.r() + t2b.r(), writes=Bd.r(c0, c0 + ln))
                            psv, pbv = PS.next()
                            proj_block(psv, pbv, *ws["v"], xn, Bxn, c0, ln)
                            S.op("act", lambda e: e.activation(out=VT[:, c0:c0 + ln], in_=psv[:, 0:ln], func=AF.Copy), reads=pbv.r(), writes=BVT.r(c0, c0 + ln))
                            psg, pbg = PS.next()
                            proj_block(psg, pbg, *ws["g"], xn, Bxn, c0, ln)
                            S.op("act", lambda e: e.activation(out=SG[:, c0:c0 + ln], in_=psg[:, 0:ln], func=AF.Silu), reads=pbg.r(), writes=BSG.r(c0, c0 + ln))

                        if has_s:
                            for (srcT, Bs, j_) in ((QT, BQT, 0), (KT, BKT, 1), (VT, BVT, 2)):
                                S.op("act", lambda e: e.activation(out=cmpS[:, j_, :].rearrange("p (s c) -> p s c", c=TS),
                                                                   in_=srcT[:, sc0:nt].rearrange("p (s c) -> p s c", c=SW)[:, :, H:SW], func=AF.Copy),
                                     reads=Bs.r(sc0, nt), writes=BcmpS.r())
                        cmp_idx = {id(QT): 0, id(KT): 1, id(VT): 2}

                        def tcols(tsr, i):
                            if i < nch:
                                return tsr[:, H + 128 * i:H + 128 * i + 128]
                            return cmpS[:, cmp_idx[id(tsr)], :]

                        def tregs(B, i):
                            if i < nch:
                                return B.r(H + 128 * i, H + 128 * i + 128)
                            return BcmpS.r()
                        for i0 in range(0, ntl, 8):
                            i1 = min(ntl, i0 + 8)
                            for (srcT, Bs, dstk) in ((KT, BKT, "k"), (VT, BVT, "v")):
                                ps, pb = PS.next()
                                psb = ps[:].bitcast(BF16)
                                for i in range(i0, i1):
                                    S.op("pe", lambda e: e.transpose(psb[:, 128 * (i - i0):128 * (i - i0) + 128], tcols(srcT, i), identb[:]),
                                         reads=tregs(Bs, i) + Bidb.r(), writes=pb.r())
                                npz = min(i1, nch) - i0
                                if dstk == "k":
                                    if npz > 0:
                                        S.op("act", lambda e: e.activation(out=Ktok[:, i0:i0 + npz, :].rearrange("p a b -> p (a b)"),
                                                                           in_=psb[:, 0:128 * npz], func=AF.Copy),
                                             reads=pb.r(), writes=BKtok.r())
                                    if has_s and i1 == ntl:
                                        off = 128 * (nch - i0)
                                        S.op("act", lambda e: e.activation(out=Kds[:, :], in_=psb[:, off:off + 128], func=AF.Copy, scale=rcon[:, 4 + h:5 + h]),
                                             reads=pb.r() + Bcon.r(), writes=BKds.r())
                                else:
                                    if npz > 0:
                                        S.op("act", lambda e: e.activation(out=Vdec[:, i0:i0 + npz, :].rearrange("p a b -> p (a b)"),
                                                                           in_=psb[:, 0:128 * npz], func=AF.Copy, scale=rcon[:, h:h + 1]),
                                             reads=pb.r() + Bcon.r(), writes=BVdec.r())
                                    if has_s and i1 == ntl:
                                        off = 128 * (nch - i0)
                                        S.op("act", lambda e: e.activation(out=Vdec[:, nch, :], in_=psb[:, off:off + 128], func=AF.Copy),
                                             reads=pb.r(), writes=BVdec.r())
                        if hi == 0:
                            S.op("dve", lambda e: e.memset(Sf[:, :], 0.0), writes=BSf.r())
                            S.op("dve", lambda e: e.memset(Sbf[:, 0, :], 0.0), writes=BSbf[0].r())
                        else:
                            S.op("dve", lambda e: e.tensor_copy(out=Sf[:, :], in_=cS[:, h, :]), reads=BcS.r(), writes=BSf.r())
                            S.op("act", lambda e: e.activation(out=Sbf[:, 0, :], in_=cS[:, h, :], func=AF.Copy), reads=BcS.r(), writes=BSbf[0].r())

                        for cg in range(0, nch, 4):
                            cn = min(4, nch - cg)
                            ps_o, pb_o = PSL.next()
                            for ci in range(cg, cg + cn):
                                cc0 = H + 128 * ci
                                ps_u, pb_u = PS.next()
                                mm(ps_u[:, 0:128], Ktok[:, ci, :], Vdec[:, ci, :], True, True, BKtok.r() + BVdec.r(), pb_u)
                                S.op("dve", lambda e: e.scalar_tensor_tensor(out=Sf[:, :], in0=Sf[:, :], scalar=gC[h], in1=ps_u[:, 0:128],
                                                                             op0=ALU.mult, op1=ALU.add),
                                     reads=pb_u.r() + BSf.r(), writes=BSf.r())
                                S.op("act", lambda e: e.activation(out=Sbf[:, ci + 1, :], in_=Sf[:, :], func=AF.Copy), reads=BSf.r(), writes=BSbf[ci + 1].r())
                                ps_s, pb_s = PS.next()
                                mm(ps_s[:, 0:128], KT[:, cc0:cc0 + 128], QT[:, cc0:cc0 + 128], True, True,
                                   BKT.r(cc0, cc0 + 128) + BQT.r(cc0, cc0 + 128), pb_s)
                                scm, scmb = TB.next()
                                S.op("dve", lambda e: e.tensor_tensor(out=scm[:, 0:128], in0=ps_s[:, 0:128], in1=maskP[:, h, :], op=ALU.mult),
                                     reads=pb_s.r() + Bcon.r(), writes=scmb.r())
                                oc_ = 128 * (ci - cg)
                                mm(ps_o[:, oc_:oc_ + 128], Vdec[:, ci, :], scm[:, 0:128], True, False, BVdec.r() + scmb.r(), pb_o)
                                mm(ps_o[:, oc_:oc_ + 128], Sbf[:, ci, :], QT[:, cc0:cc0 + 128], False, True,
                                   BSbf[ci].r() + BQT.r(cc0, cc0 + 128), pb_o)
                            w_ = 128 * cn
                            g0 = H + 128 * cg
                            qd_ap = qdP[:, h, :].unsqueeze(1).broadcast_to([128, cn, 128])

                            def dst_fn(o, g0=g0, w_=w_):
                                return mix[:, 4 + h, g0:g0 + w_], SG[:, g0:g0 + w_], o[:, 0:w_]
                            gn_block_3d(ps_o, pb_o, w_, cn, qd_ap, dst_fn, g0, g0 + w_, h, Bmix, BSG)
                        if hi == 0:
                            S.op("dve", lambda e: e.tensor_copy(out=cS[:, h, :], in_=Sf[:, :]), reads=BSf.r(), writes=BcS.r())
                        else:
                            S.dma("sp", o_pret[h, :, :], Sf[:, :], reads=BSf.r())
                        if has_s:
                            qs_v = tcols(QT, nch)
                            ks_v = tcols(KT, nch)
                            S.op("dve", lambda e: e.tensor_tensor(out=Qds[:, :], in0=qs_v, in1=qd8[:, h, :], op=ALU.mult),
                                 reads=BcmpS.r() + Bcon.r(), writes=BQds.r())
                            ps_s, pb_s = PS.next()
                            mm(ps_s[:, 0:128], ks_v, qs_v, True, True, BcmpS.r(), pb_s)
                            scm, scmb = TB.next()
                            S.op("dve", lambda e: e.tensor_tensor(out=scm[:, 0:128], in0=ps_s[:, 0:128], in1=maskS[:, h, :], op=ALU.mult),
                                 reads=pb_s.r() + Bcon.r(), writes=scmb.r())
                            ps_o, pb_o = PSL.next()
                            mm(ps_o[:, 0:128], Vdec[:, nch, :], scm[:, 0:128], True, False, BVdec.r() + scmb.r(), pb_o)
                            for s in range(NSQ):
                                mm(ps_o[:, 8 * s:8 * s + 8], S0b[:, s, :], Qds[:, 8 * s:8 * s + 8], False, s == NSQ - 1,
                                   BS0b.r() + BQds.r(), pb_o)

                            def dst_fn_s(o):
                                return (mix[:, 4 + h, sc0:nt].rearrange("p (s c) -> p s c", c=SW)[:, :, H:SW],
                                        SG[:, sc0:nt].rearrange("p (s c) -> p s c", c=SW)[:, :, H:SW],
                                        o[:, 0:128].rearrange("p (s c) -> p s c", c=TS))
                            gn_block_3d(ps_o, pb_o, 128, None, None, dst_fn_s, sc0, nt, h, Bmix, BSG)
                            S.op("dve", lambda e: e.tensor_tensor(out=Vbd[:, :, :], in0=Vdec[:, nch, :].unsqueeze(1).broadcast_to([128, NSQ, 128]),
                                                                  in1=rcon[:, 8:24].unsqueeze(2).broadcast_to([128, NSQ, 128]), op=ALU.mult),
                                 reads=BVdec.r() + Bcon.r(), writes=BVbd.r())
                            for q in range(4):
                                ps_u, pb_u = PS.next()
                                mm(ps_u[:, :], Kds[:, :], Vbd[:, 4 * q:4 * q + 4, :], True, True, BKds.r() + BVbd.r(), pb_u)
                                sn, snb = TF.next()
                                S.op("dve", lambda e: e.scalar_tensor_tensor(out=sn[:, :], in0=S0f[:, 4 * q:4 * q + 4, :].rearrange("p a b -> p (a b)"),
                                                                             scalar=g8[h], in1=ps_u[:, :], op0=ALU.mult, op1=ALU.add),
                                     reads=pb_u.r() + BS0f.r(), writes=snb.r())
                                S.dma("sp", o_sret[4 * q:4 * q + 4, h, :, :].rearrange("s d v -> d s v"),
                                      sn[:, :].rearrange("p (s v) -> p s v", v=128), reads=snb.r())
                    out_proj(hi, "wo0", mix, Bmix)
                    tail()
                    S.fence(fsc[:, 0:1], Bfsc)

        Buf.scoped = True
        Buf.live = []
        Buf.pending = None
        for hi in range(2):
            load_x(hi)
            rmsnorm(hi, "nmix0", None)
            even_mixer(hi, lambda hi=hi: rmsnorm(hi, "nffn0", 0))
            with ExitStack() as shared:
                SH.clear()
                SH["on"] = True
                SH["ph"] = shared
                ffn(hi, 0, lambda hi=hi: rmsnorm(hi, "nmix1", 1))
                lru_mixer(hi, lambda hi=hi: rmsnorm(hi, "nffn1", 2))
                ffn(hi, 1, lambda hi=hi: final_out(hi))
                SH["on"] = False
                S.fence(fsc[:, 0:1], Bfsc)
            if _FLUSH_HALF:
                S.flush()
        S.barrier()
        Buf.scoped = False
        build.marks = S.marks

    return nc


_NC_CACHE = {}


def _core_inputs(inp, params, consts, w_sw, c):
    m = {
        "xp": np.ascontiguousarray(inp["x_prompt"][c]),
        "xs": np.ascontiguousarray(inp["x_sample"][NSQ * c:NSQ * (c + 1)].reshape(NSQ * TS, D)),
        "st_conv_a": np.ascontiguousarray(inp["state_conv_a"][0, NSQ * c:NSQ * (c + 1)].reshape(NSQ * 30, 512)),
        "st_ret": np.ascontiguousarray(inp["state_ret"][0, NSQ * c:NSQ * (c + 1)]),
        "st_lru_conv": np.ascontiguousarray(inp["state_lru_conv"][0, NSQ * c:NSQ * (c + 1)].reshape(NSQ * 3, D)),
        "st_lru_h": np.ascontiguousarray(inp["state_lru_h"][0, NSQ * c:NSQ * (c + 1)]),
        "st_ffn": np.ascontiguousarray(inp["state_ffn_conv"][:, NSQ * c:NSQ * (c + 1)].reshape(2, NSQ * 2, 2 * FF)),
        "w_in_ab": inp["w_in_ab"][0],
        "w_out_ab": inp["w_out_ab"][0],
        "w_in_c": inp["w_in_c"][0],
        "w_lru_a": inp["w_lru_a"][0],
        "w_lru_x": inp["w_lru_x"][0],
        "w_out_c": inp["w_out_c"][0],
        "w_ffn_up": inp["w_ffn_up"],
        "w_ffn_down": inp["w_ffn_down"],
        "params": params,
    }
    m.update(consts)
    return m


def _prep(inputs):
    inp = {k: np.asarray(v, dtype=np.float32) for k, v in inputs.items()}
    params, npar = _pack_params(inp)
    consts = _host_consts()
    wqk = inp["w_in_ab"][0][:, 1024:2048].reshape(D, 8, 2, 64)
    w_sw = np.ascontiguousarray(wqk[:, :, ::-1, :].reshape(D, 1024))
    return inp, params, npar, consts, w_sw


def _assemble(res, ncores):
    def g(name):
        return [np.asarray(r[name]) for r in res]
    y_p = np.stack(g("y_p"), 0)
    y_s = np.concatenate([a.reshape(NSQ, TS, D) for a in g("y_s")], 0)
    p_ca = np.stack(g("p_conv_a"), 0)[None]
    p_ret = np.stack(g("p_ret"), 0)[None]
    p_lc = np.stack(g("p_lru_conv"), 0)[None]
    p_lh = np.stack([a.reshape(D) for a in g("p_lru_h")], 0)[None]
    p_ffn = np.stack(g("p_ffn"), 1)
    s_ca = np.concatenate([a.reshape(NSQ, 30, 512) for a in g("s_conv_a")], 0)[None]
    s_ret = np.concatenate(g("s_ret"), 0)[None]
    s_lc = np.concatenate([a.reshape(NSQ, 3, D) for a in g("s_lru_conv")], 0)[None]
    s_lh = np.concatenate(g("s_lru_h"), 0)[None]
    s_ffn = np.concatenate([a.reshape(2, NSQ, 2, 2 * FF) for a in g("s_ffn")], 1)
    outs = (y_p, y_s, p_ca, p_ret, p_lc, p_lh, p_ffn, s_ca, s_ret, s_lc, s_lh, s_ffn)
    return tuple(np.ascontiguousarray(o, dtype=np.float32) for o in outs)


def kernel(**inputs):
    inp, params, npar, consts, w_sw = _prep(inputs)
    ncores = 8
    nc = build(npar)
    in_maps = [_core_inputs(inp, params, consts, w_sw, c) for c in range(ncores)]
    res = run_bass_kernel_spmd(nc, in_maps, core_ids=list(range(ncores)))
    return _assemble(res.results, ncores)
```
